# Optimizing a Trainium2 kernel written in Bass

```python
import numpy as np
import jax
import jax.numpy as jnp
from jax import lax

D_MODEL = 1024
BATCH = 16
SEQ = 2048
DEPTH = 2

GRID_W = 64
CTX_LEN = 256
EPS = 1e-6
ROPE_BASE = 10000.0
Q_BLOCK = 128
D_FF = 4 * D_MODEL
MIX_W = D_MODEL

MLA_HEADS = D_MODEL // 128
MLA_NOPE = 64
MLA_ROPE = 32
MLA_QK = MLA_NOPE + MLA_ROPE
MLA_V = 64
MLA_Q_LORA = D_MODEL // 4
MLA_KV_LORA = D_MODEL // 8

LRU_W = MIX_W - MLA_HEADS * MLA_V
LRU_BLOCKS = 8
LRU_BS = LRU_W // LRU_BLOCKS
LRU_C = 8.0
CONV_W = 4
CONV_LEFT = CONV_W // 2

GLA_HEADS = D_MODEL // 256
GLA_DK = 64
GLA_DV = 128
GLA_LR = 16
GLA_TAU = 16.0
GLA_CHUNK = 64

NA_HD = 64
NA_HEADS = (MIX_W - GLA_HEADS * GLA_DV) // NA_HD
NA_WIN_ROWS = 8
NA_WIN_COLS = 16

EV_SIZES = (MLA_Q_LORA, MLA_KV_LORA, MLA_ROPE, LRU_W, LRU_W)
OD_SIZES = (GLA_HEADS * GLA_DK, GLA_HEADS * GLA_DK, GLA_HEADS * GLA_DV, GLA_HEADS * GLA_DV, 2 * GLA_LR,
            NA_HEADS * NA_HD, NA_HEADS * NA_HD, NA_HEADS * NA_HD)
EV_IN = sum(EV_SIZES)
OD_IN = sum(OD_SIZES)

kernel_name = 'hybrid_mla_rglru_gla_natten_dit'


def _split(t, sizes):
    return jnp.split(t, np.cumsum(sizes)[:-1].tolist(), axis=-1)


def rms_norm(x, g):
    xf = x.astype(jnp.float32)
    y = xf * lax.rsqrt(jnp.mean(xf * xf, axis=-1, keepdims=True) + EPS)
    return (y * g.astype(jnp.float32)).astype(x.dtype)


def _modulation(cv, ada_w, ada_b):
    return jnp.split(jax.nn.silu(cv) @ ada_w + ada_b, 6, axis=-1)


def _modulate(x, g, shift, scale):
    return rms_norm(x, g) * (1.0 + scale) + shift


def _sq_relu_mlp(h, w1, w2):
    return jnp.square(jax.nn.relu(h @ w1)) @ w2


def axial_rope(x):
    n = x.shape[1]
    pos = jnp.arange(n)
    half = x.shape[-1] // 2
    inv = ROPE_BASE ** (-jnp.arange(0, half, 2, dtype=jnp.float32) / half)

    def rot(xh, p):
        ang = p.astype(jnp.float32)[:, None] * inv[None, :]
        ang = jnp.concatenate([ang, ang], axis=-1)[None, :, None, :]
        xf = xh.astype(jnp.float32)
        x1, x2 = jnp.split(xf, 2, axis=-1)
        return (xf * jnp.cos(ang) + jnp.concatenate([-x2, x1], axis=-1) * jnp.sin(ang)).astype(x.dtype)

    return jnp.concatenate([rot(x[..., :half], pos // GRID_W), rot(x[..., half:], pos % GRID_W)], axis=-1)


def attend(q, k, v, scale):
    s = jnp.einsum('bqhd,bkhd->bhqk', q, k).astype(jnp.float32) * scale
    p = jax.nn.softmax(s, axis=-1).astype(v.dtype)
    return jnp.einsum('bhqk,bkhe->bqhe', p, v)


def block_attend(q, k, v, scale):
    b, n, h, d = q.shape
    qb = jnp.moveaxis(q.reshape(b, n // Q_BLOCK, Q_BLOCK, h, d), 1, 0)
    out = lax.map(lambda qq: attend(qq, k, v, scale), qb)
    return jnp.moveaxis(out, 0, 1).reshape(b, n, h, v.shape[-1])


def _mla(uq, ukv, ukr, uqc, ukvc, ukrc, q_norm, w_uq, kv_norm, w_ukv, q_gain, k_gain, need_ctx):
    def qkv(uq, ukv, ukr, use_rope):
        b, n, _ = uq.shape
        q = (rms_norm(uq, q_norm) @ w_uq).reshape(b, n, MLA_HEADS, MLA_QK)
        kv = (rms_norm(ukv, kv_norm) @ w_ukv).reshape(b, n, MLA_HEADS, MLA_NOPE + MLA_V)
        k_rope = jnp.broadcast_to(ukr[:, :, None, :], (b, n, MLA_HEADS, MLA_ROPE))
        q = rms_norm(q, q_gain)
        k = rms_norm(jnp.concatenate([kv[..., :MLA_NOPE], k_rope], axis=-1), k_gain)
        v = kv[..., MLA_NOPE:]
        if use_rope:
            q = jnp.concatenate([q[..., :MLA_NOPE], axial_rope(q[..., MLA_NOPE:])], axis=-1)
            k = jnp.concatenate([k[..., :MLA_NOPE], axial_rope(k[..., MLA_NOPE:])], axis=-1)
        return q, k, v

    scale = MLA_QK ** -0.5
    q, k, v = qkv(uq, ukv, ukr, True)
    qc, kc, vc = qkv(uqc, ukvc, ukrc, False)
    b, n = q.shape[:2]
    y = block_attend(q, jnp.concatenate([k, kc], axis=1), jnp.concatenate([v, vc], axis=1), scale)
    y = y.reshape(b, n, MLA_HEADS * MLA_V)
    yc = attend(qc, kc, vc, scale).reshape(qc.shape[0], qc.shape[1], MLA_HEADS * MLA_V) if need_ctx else None
    return y, yc


def _dwconv(x, w, b):
    n = x.shape[1]
    xp = jnp.pad(x, ((0, 0), (CONV_LEFT, CONV_W - 1 - CONV_LEFT), (0, 0)))
    y = b + xp[:, 0:n] * w[0]
    for j in range(1, CONV_W):
        y = y + xp[:, j:j + n] * w[j]
    return y


def _lru_coeffs(x, w_a, b_a, w_x, b_x, lam):
    b, n, _ = x.shape
    xb = x.reshape(b, n, LRU_BLOCKS, LRU_BS)
    r = jax.nn.sigmoid(jnp.einsum('bnki,kij->bnkj', xb, w_a).reshape(b, n, LRU_W) + b_a)
    i = jax.nn.sigmoid(jnp.einsum('bnki,kij->bnkj', xb, w_x).reshape(b, n, LRU_W) + b_x)
    log_a = -LRU_C * r * jax.nn.softplus(-lam.astype(jnp.float32))
    return jnp.exp(log_a), jnp.sqrt(-jnp.expm1(2.0 * log_a)) * (i * x)


def _linear_scan(a, bx, h0):
    def comb(l, r):
        return l[0] * r[0], r[0] * l[1] + r[1]
    acum, bcum = lax.associative_scan(comb, (a, bx), axis=1)
    return acum * h0[:, None, :] + bcum


def _rglru(ux, ug, uxc, ugc, conv_w, conv_b, w_a, b_a, w_x, b_x, lam, need_ctx):
    xl = _dwconv(ux, conv_w, conv_b).astype(jnp.float32)
    xc = _dwconv(uxc, conv_w, conv_b).astype(jnp.float32)
    flip = lambda t: jnp.flip(t, axis=1)
    h_sum, hc_sum = 0.0, 0.0
    for d in range(2):
        a, bx = _lru_coeffs(xl, w_a[d], b_a[d], w_x[d], b_x[d], lam[d])
        ac, bc = _lru_coeffs(xc, w_a[d], b_a[d], w_x[d], b_x[d], lam[d])
        if d == 1:
            a, bx, ac, bc = flip(a), flip(bx), flip(ac), flip(bc)
        hc = _linear_scan(ac, bc, jnp.zeros((xc.shape[0], LRU_W), jnp.float32))
        h = _linear_scan(a, bx, hc[:, -1])
        if d == 1:
            h, hc = flip(h), flip(hc)
        h_sum = h_sum + h
        hc_sum = hc_sum + hc
    y = (h_sum * jax.nn.gelu(ug.astype(jnp.float32))).astype(ux.dtype)
    yc = (hc_sum * jax.nn.gelu(ugc.astype(jnp.float32))).astype(ux.dtype) if need_ctx else None
    return y, yc


def _gla_chunk_scan(q, k, v, log_a, s0, with_output=True):
    b, n, h, dk = q.shape
    dv = v.shape[-1]
    nc = n // GLA_CHUNK
    q, k, v, log_a = [t.astype(jnp.float32).reshape(b, nc, GLA_CHUNK, h, t.shape[-1]) for t in (q, k, v, log_a)]
    cum = jnp.cumsum(log_a, axis=2)
    cum_last = cum[:, :, -1]
    d_state = jnp.einsum('bnshd,bnshe->bnhde', k * jnp.exp(cum_last[:, :, None] - cum), v)

    def step(s, inp):
        decay, ds = inp
        return decay[..., None] * s + ds, s

    s_fin, s_in = lax.scan(step, s0.astype(jnp.float32),
                           (jnp.moveaxis(jnp.exp(cum_last), 1, 0), jnp.moveaxis(d_state, 1, 0)))
    if not with_output:
        return None, s_fin
    s_in = jnp.moveaxis(s_in, 0, 1)
    q_dec = q * jnp.exp(cum)
    att = jnp.einsum('bnthd,bnshd->bnhts', q_dec, k * jnp.exp(-cum))
    att = jnp.where(jnp.tril(jnp.ones((GLA_CHUNK, GLA_CHUNK), dtype=bool)), att, 0.0)
    o = jnp.einsum('bnhts,bnshe->bnthe', att, v) + jnp.einsum('bnthd,bnhde->bnthe', q_dec, s_in)
    return o.reshape(b, n, h, dv), s_fin


def _gla(gq, gk, gv, gg, glr, gqc, gkc, gvc, ggc, glrc, w_a, b_a, o_gain, need_ctx):
    def heads(t, d):
        return t.reshape(t.shape[0], t.shape[1], -1, d)

    def log_decay(lr, d):
        z = lr[..., d * GLA_LR:(d + 1) * GLA_LR] @ w_a[d] + b_a[d]
        return heads(jax.nn.log_sigmoid(z.astype(jnp.float32)) / GLA_TAU, GLA_DK)

    sc = GLA_DK ** -0.5
    q, k, v = heads(gq, GLA_DK) * sc, heads(gk, GLA_DK), heads(gv, GLA_DV)
    qc, kc, vc = heads(gqc, GLA_DK) * sc, heads(gkc, GLA_DK), heads(gvc, GLA_DV)
    zero = jnp.zeros((qc.shape[0], GLA_HEADS, GLA_DK, GLA_DV), jnp.float32)
    flip = lambda t: jnp.flip(t, axis=1)
    oc_f, sc_f = _gla_chunk_scan(qc, kc, vc, log_decay(glrc, 0), zero, need_ctx)
    o_f, _ = _gla_chunk_scan(q, k, v, log_decay(glr, 0), sc_f)
    oc_b, sc_b = _gla_chunk_scan(flip(qc), flip(kc), flip(vc), flip(log_decay(glrc, 1)), zero, need_ctx)
    o_b, _ = _gla_chunk_scan(flip(q), flip(k), flip(v), flip(log_decay(glr, 1)), sc_b)

    def finish(o, g):
        return (rms_norm(o, o_gain).reshape(g.shape) * jax.nn.silu(g.astype(jnp.float32))).astype(g.dtype)

    y = finish(o_f + flip(o_b), gg)
    yc = finish(oc_f + flip(oc_b), ggc) if need_ctx else None
    return y, yc


def _natten(nq, nk, nv, nqc, nkc, nvc, q_gain, k_gain, rpb, need_ctx):
    def heads(t, g=None):
        t = t.reshape(t.shape[0], t.shape[1], NA_HEADS, NA_HD)
        return rms_norm(t, g) if g is not None else t

    q, k, v = heads(nq, q_gain), heads(nk, k_gain), heads(nv)
    qc, kc, vc = heads(nqc, q_gain), heads(nkc, k_gain), heads(nvc)
    b, n = q.shape[:2]
    rows = n // GRID_W
    wr = min(NA_WIN_ROWS, rows)
    scale = NA_HD ** -0.5
    qg = q.reshape(b, rows, GRID_W, NA_HEADS, NA_HD)
    kg = k.reshape(b, rows, GRID_W, NA_HEADS, NA_HD)
    vg = v.reshape(b, rows, GRID_W, NA_HEADS, NA_HD)
    cols = jnp.arange(GRID_W)
    c0 = jnp.clip(cols - NA_WIN_COLS // 2, 0, GRID_W - NA_WIN_COLS)
    col_idx = c0[:, None] + jnp.arange(NA_WIN_COLS)[None, :]
    dj = col_idx - cols[:, None] + (NA_WIN_COLS - 1)

    def row_block(args):
        q_r, r = args
        r0 = jnp.clip(r - wr // 2, 0, rows - wr)
        k_win = lax.dynamic_slice_in_dim(kg, r0, wr, axis=1)[:, :, col_idx]
        v_win = lax.dynamic_slice_in_dim(vg, r0, wr, axis=1)[:, :, col_idx]
        di = r0 + jnp.arange(wr) - r + (NA_WIN_ROWS - 1)
        bias = rpb[:, di[:, None, None], dj[None, :, :]]
        s_lat = jnp.einsum('bchd,bicjhd->bhcij', q_r, k_win).astype(jnp.float32) * scale
        s_lat = s_lat + jnp.transpose(bias, (0, 2, 1, 3))[None].astype(jnp.float32)
        s_ctx = jnp.einsum('bchd,bkhd->bhck', q_r, kc).astype(jnp.float32) * scale
        s = jnp.concatenate([s_lat.reshape(b, NA_HEADS, GRID_W, wr * NA_WIN_COLS), s_ctx], axis=-1)
        p = jax.nn.softmax(s, axis=-1).astype(v.dtype)
        p_lat = p[..., :wr * NA_WIN_COLS].reshape(b, NA_HEADS, GRID_W, wr, NA_WIN_COLS)
        return (jnp.einsum('bhcij,bicjhd->bchd', p_lat, v_win)
                + jnp.einsum('bhck,bkhd->bchd', p[..., wr * NA_WIN_COLS:], vc))

    out = lax.map(row_block, (jnp.moveaxis(qg, 1, 0), jnp.arange(rows)))
    y = jnp.moveaxis(out, 0, 1).reshape(b, n, NA_HEADS * NA_HD)
    yc = attend(qc, kc, vc, scale).reshape(qc.shape[0], qc.shape[1], NA_HEADS * NA_HD) if need_ctx else None
    return y, yc


def _even_mixer(h, hc, w_in, w_out, q_norm, w_uq, kv_norm, w_ukv, q_gain, k_gain,
                conv_w, conv_b, w_a, b_a, w_x, b_x, lam, need_ctx):
    uq, ukv, ukr, ux, ug = _split(h @ w_in, EV_SIZES)
    uqc, ukvc, ukrc, uxc, ugc = _split(hc @ w_in, EV_SIZES)
    y_a, yc_a = _mla(uq, ukv, ukr, uqc, ukvc, ukrc, q_norm, w_uq, kv_norm, w_ukv, q_gain, k_gain, need_ctx)
    y_b, yc_b = _rglru(ux, ug, uxc, ugc, conv_w, conv_b, w_a, b_a, w_x, b_x, lam, need_ctx)
    y = jnp.concatenate([y_a, y_b.astype(y_a.dtype)], axis=-1) @ w_out
    yc = jnp.concatenate([yc_a, yc_b.astype(yc_a.dtype)], axis=-1) @ w_out if need_ctx else None
    return y, yc


def _odd_mixer(h, hc, w_in, w_out, gla_w_a, gla_b_a, gla_o_gain, na_q_gain, na_k_gain, na_rpb, need_ctx):
    gq, gk, gv, gg, glr, nq, nk, nv = _split(h @ w_in, OD_SIZES)
    gqc, gkc, gvc, ggc, glrc, nqc, nkc, nvc = _split(hc @ w_in, OD_SIZES)
    y_c, yc_c = _gla(gq, gk, gv, gg, glr, gqc, gkc, gvc, ggc, glrc, gla_w_a, gla_b_a, gla_o_gain, need_ctx)
    y_d, yc_d = _natten(nq, nk, nv, nqc, nkc, nvc, na_q_gain, na_k_gain, na_rpb, need_ctx)
    y = jnp.concatenate([y_c, y_d], axis=-1) @ w_out
    yc = jnp.concatenate([yc_c, yc_d], axis=-1) @ w_out if need_ctx else None
    return y, yc


def setup_inputs(seed: int = 0) -> dict:
    key = jax.random.key(seed)
    ks = iter(jax.random.split(key, 40))
    ne = (DEPTH + 1) // 2
    no = DEPTH // 2
    D = D_MODEL

    def nrm(shape, scale):
        return jax.random.normal(next(ks), shape, jnp.float32) * scale

    def gain(shape):
        return 1.0 + nrm(shape, 0.05)

    a8 = jax.random.uniform(next(ks), (ne, 2, LRU_W), jnp.float32, minval=0.9, maxval=0.999)
    a_base = a8 ** (1.0 / LRU_C)
    return {
        'x': nrm((BATCH, SEQ, D), 1.0),
        'c': nrm((BATCH, D), 1.0),
        'ctx': nrm((BATCH, CTX_LEN, D), 1.0),
        'c_ctx': nrm((D,), 1.0),
        'ada_w': nrm((DEPTH, D, 6 * D), 0.5 * D ** -0.5),
        'ada_b': nrm((DEPTH, 6 * D), 0.02),
        'norm_mix': gain((DEPTH, D)),
        'norm_mlp': gain((DEPTH, D)),
        'w_out': nrm((DEPTH, MIX_W, D), MIX_W ** -0.5),
        'mlp_w1': nrm((DEPTH, D, D_FF), D ** -0.5),
        'mlp_w2': nrm((DEPTH, D_FF, D), D_FF ** -0.5),
        'ev_w_in': nrm((ne, D, EV_IN), D ** -0.5),
        'mla_q_norm': gain((ne, MLA_Q_LORA)),
        'mla_w_uq': nrm((ne, MLA_Q_LORA, MLA_HEADS * MLA_QK), MLA_Q_LORA ** -0.5),
        'mla_kv_norm': gain((ne, MLA_KV_LORA)),
        'mla_w_ukv': nrm((ne, MLA_KV_LORA, MLA_HEADS * (MLA_NOPE + MLA_V)), MLA_KV_LORA ** -0.5),
        'mla_q_gain': gain((ne, MLA_QK)),
        'mla_k_gain': gain((ne, MLA_QK)),
        'lru_conv_w': nrm((ne, CONV_W, LRU_W), CONV_W ** -0.5),
        'lru_conv_b': nrm((ne, LRU_W), 0.02),
        'lru_w_a': nrm((ne, 2, LRU_BLOCKS, LRU_BS, LRU_BS), LRU_BS ** -0.5),
        'lru_b_a': nrm((ne, 2, LRU_W), 0.02),
        'lru_w_x': nrm((ne, 2, LRU_BLOCKS, LRU_BS, LRU_BS), LRU_BS ** -0.5),
        'lru_b_x': nrm((ne, 2, LRU_W), 0.02),
        'lru_lam': jnp.log(a_base) - jnp.log1p(-a_base),
        'od_w_in': nrm((no, D, OD_IN), D ** -0.5),
        'gla_w_a': nrm((no, 2, GLA_LR, GLA_HEADS * GLA_DK), GLA_LR ** -0.5),
        'gla_b_a': nrm((no, 2, GLA_HEADS * GLA_DK), 0.02),
        'gla_o_gain': gain((no, GLA_DV)),
        'na_q_gain': gain((no, NA_HD)),
        'na_k_gain': gain((no, NA_HD)),
        'na_rpb': nrm((no, NA_HEADS, 2 * NA_WIN_ROWS - 1, 2 * NA_WIN_COLS - 1), 0.2),
    }


def reference(x, c, ctx, c_ctx, ada_w, ada_b, norm_mix, norm_mlp, w_out, mlp_w1, mlp_w2,
              ev_w_in, mla_q_norm, mla_w_uq, mla_kv_norm, mla_w_ukv, mla_q_gain, mla_k_gain,
              lru_conv_w, lru_conv_b, lru_w_a, lru_b_a, lru_w_x, lru_b_x, lru_lam,
              od_w_in, gla_w_a, gla_b_a, gla_o_gain, na_q_gain, na_k_gain, na_rpb):
    xc = ctx
    for l in range(DEPTH):
        last = l == DEPTH - 1
        j = l // 2
        m = [t[:, None, :] for t in _modulation(c, ada_w[l], ada_b[l])]
        mc = _modulation(c_ctx, ada_w[l], ada_b[l])
        h = _modulate(x, norm_mix[l], m[0], m[1])
        hc = _modulate(xc, norm_mix[l], mc[0], mc[1])
        if l % 2 == 0:
            y, yc = _even_mixer(h, hc, ev_w_in[j], w_out[l], mla_q_norm[j], mla_w_uq[j], mla_kv_norm[j],
                                mla_w_ukv[j], mla_q_gain[j], mla_k_gain[j], lru_conv_w[j], lru_conv_b[j],
                                lru_w_a[j], lru_b_a[j], lru_w_x[j], lru_b_x[j], lru_lam[j], not last)
        else:
            y, yc = _odd_mixer(h, hc, od_w_in[j], w_out[l], gla_w_a[j], gla_b_a[j], gla_o_gain[j],
                               na_q_gain[j], na_k_gain[j], na_rpb[j], not last)
        x = x + m[2] * y
        x = x + m[5] * _sq_relu_mlp(_modulate(x, norm_mlp[l], m[3], m[4]), mlp_w1[l], mlp_w2[l])
        if not last:
            xc = xc + mc[2] * yc
            xc = xc + mc[5] * _sq_relu_mlp(_modulate(xc, norm_mlp[l], mc[3], mc[4]), mlp_w1[l], mlp_w2[l])
    return x
```

```python
import numpy as np
import concourse.bass as bass
import concourse.mybir as mybir
from concourse.bass_utils import run_bass_kernel_spmd

F32 = mybir.dt.float32
BF16 = mybir.dt.bfloat16
AF = mybir.ActivationFunctionType
ALU = mybir.AluOpType

D = 1024
SEQ = 2048
CTX = 256
TOK = SEQ + CTX
NB = 2
EPS = 1e-6
NPOOL = 40


class Sched:
    def __init__(self, nc):
        self.nc = nc
        self.eng = {'pe': nc.tensor, 'act': nc.scalar, 'dve': nc.vector, 'pool': nc.gpsimd, 'sp': nc.sync}
        self.ins = []
        self.last_w = {}
        self.rd_eng = {}
        self.rd_dma = {}
        self.bar = {e: [] for e in self.eng}
        self.last_on = {e: None for e in self.eng}
        self.open_dma = set()
        self.ps_last = {}

    def _add(self, eng, meth, kw, r, w, dma):
        i = len(self.ins)
        deps = set()
        for k in list(r) + list(w):
            if k.startswith("ps"):
                pl = self.ps_last.setdefault(k, {})
                for e2, i2 in pl.items():
                    if e2 != eng:
                        deps.add(i2)
                pl[eng] = i
        for k in r:
            lw = self.last_w.get(k)
            if lw is not None:
                deps.add(lw)
        for k in w:
            lw = self.last_w.get(k)
            if lw is not None:
                deps.add(lw)
            deps.update(self.rd_eng.get(k, {}).values())
            deps.update(self.rd_dma.get(k, ()))
        if self.bar[eng]:
            deps.update(self.bar[eng])
            self.bar[eng] = []
        self.ins.append(dict(eng=eng, meth=meth, kw=kw, deps=deps, dma=dma))
        for k in r:
            if dma:
                self.rd_dma.setdefault(k, []).append(i)
            else:
                self.rd_eng.setdefault(k, {})[eng] = i
        for k in w:
            self.last_w[k] = i
            self.rd_eng[k] = {}
            self.rd_dma[k] = []
        if not dma:
            self.last_on[eng] = i
        for d in deps:
            self.open_dma.discard(d)
        if dma:
            self.open_dma.add(i)
        return i

    def op(self, eng, meth, r=(), w=(), **kw):
        return self._add(eng, meth, kw, r, w, False)

    def dma(self, q, out, in_, r=(), w=(), **kw):
        kw = dict(kw, out=out, in_=in_)
        return self._add(q, 'dma_start', kw, r, w, True)

    def barrier(self):
        deps = [i for i in self.last_on.values() if i is not None] + list(self.open_dma)
        for e in self.eng:
            self.bar[e] = list(deps)
        self.last_w.clear()
        self.rd_eng.clear()
        self.rd_dma.clear()
        self.ps_last.clear()

    def finish(self):
        self.barrier()
        self._add('sp', None, {}, (), (), False)

    def emit(self):
        nc = self.nc
        ins = self.ins
        need = [False] * len(ins)
        for i, I in enumerate(ins):
            for d in I['deps']:
                Dd = ins[d]
                if Dd['dma']:
                    continue
                if Dd['eng'] == 'pe' and I['eng'] == 'pe' and not I['dma']:
                    continue
                need[d] = True
        esem = {e: nc.alloc_semaphore("sem_" + e) for e in self.eng}
        npool = {'sp': 28, 'pool': 12, 'act': 2, 'pe': 2, 'dve': 2}
        dsem = {q: [nc.alloc_semaphore("dsem_%s%d" % (q, j)) for j in range(npool[q])] for q in ('sp', 'pool')}
        cnt = {e: 0 for e in self.eng}
        ndq = {'sp': 0, 'pool': 0}
        nd = 0
        for i, I in enumerate(ins):
            if I['dma']:
                q = I['eng']
                I['sem'] = dsem[q][ndq[q] % npool[q]]
                I['val'] = 16 * (ndq[q] // npool[q] + 1)
                ndq[q] += 1
                nd += 1
            elif need[i]:
                cnt[I['eng']] += 1
                I['sem'] = esem[I['eng']]
                I['val'] = cnt[I['eng']]
        known = {e: {} for e in self.eng}
        nwait = 0
        for i, I in enumerate(ins):
            e = I['eng']
            E = self.eng[e]
            waits = {}
            for d in I['deps']:
                Dd = ins[d]
                if (not Dd['dma']) and Dd['eng'] == 'pe' and e == 'pe' and not I['dma']:
                    continue
                s, v = Dd['sem'], Dd['val']
                if waits.get(id(s), (None, 0))[1] < v:
                    waits[id(s)] = (s, v)
            if I['dma'] and I['val'] > 16:
                s = I['sem']
                if waits.get(id(s), (None, 0))[1] < I['val'] - 16:
                    waits[id(s)] = (s, I['val'] - 16)
            for sid, (s, v) in waits.items():
                if known[e].get(sid, 0) >= v:
                    continue
                E.wait_ge(s, v)
                nwait += 1
                known[e][sid] = v
            if I['meth'] is None:
                continue
            inst = getattr(E, I['meth'])(**I['kw'])
            if I['dma']:
                inst.then_inc(I['sem'], 16)
            elif need[i]:
                inst.then_inc(I['sem'], 1)
        self.stats = dict(n=len(ins), nwait=nwait, nsig=sum(need), ndma=nd, cnt=cnt)


class StopBuild(Exception):
    pass


class Mem:
    LO = 16512
    HI = 229344

    def __init__(self, nc):
        self.nc = nc
        self.top = Mem.LO
        self.limit = Mem.HI
        self.n = 0

    def region(self, lo, hi):
        self.top = lo
        self.limit = hi

    def alloc(self, name, shape, dtype):
        nbytes = int(np.prod(shape[1:])) * (4 if dtype == F32 else 2)
        nbytes = (nbytes + 63) // 64 * 64
        off = self.top
        assert off + nbytes <= self.limit, (name, off, nbytes, self.limit)
        self.top += nbytes
        self.n += 1
        return self.nc.alloc_sbuf_tensor_at("%s_%d" % (name, self.n), list(shape), dtype, offset=off)

    def mark(self):
        return self.top

    def release(self, m):
        self.top = m


def tiles_for(ntok_lat=SEQ, with_ctx=True, b=0, tw=512):
    out = []
    for t0 in range(0, ntok_lat, tw):
        out.append((t0, tw, b))
    if with_ctx:
        out.append((SEQ, CTX, 2))
    return out


class Builder:
    def __init__(self, cfg):
        self.cfg = cfg
        self.nc = nc = bass.Bass("TRN2", target_bir_lowering=False)
        self.S = Sched(nc)
        self.M = Mem(nc)
        self.dr = {}
        self.dbg_outs = []
        self.pp = [nc.alloc_psum_tensor("pp%d" % i, [128, 1024], F32) for i in range(4)]
        self.ps = [self.pp[i // 2][:, (i % 2) * 512:(i % 2 + 1) * 512] for i in range(8)]
        v7 = self.ps[7].bitcast(BF16)
        self.psb = [v7[:, 0:512], v7[:, 512:1024]]
        self.ps_i = 0

    def din(self, name, shape, dtype=F32):
        t = self.nc.dram_tensor(name, list(shape), dtype, kind="ExternalInput").ap()
        self.dr[name] = t
        return t

    def dout(self, name, shape, dtype=F32):
        t = self.nc.dram_tensor(name, list(shape), dtype, kind="ExternalOutput").ap()
        self.dr[name] = t
        return t

    def psum(self, lo=0, hi=6):
        n = hi - lo
        i = lo + (self.ps_i % n)
        self.ps_i += 1
        return self.ps[i], "ps%d" % i

    def dump(self, name, ap, shape, rkeys):
        if not self.cfg.get('debug'):
            return
        t = self.dout("dbg_" + name, shape, ap.dtype)
        self.S.dma('sp', out=t, in_=ap, r=rkeys, w=["dbgout_" + name])
        self.dbg_outs.append("dbg_" + name)

    def modulate(self, xT, xkey, g, s, outT, okey, tiles, sc):
        S = self.S
        ones = self.ones_bf
        for (t0, tw, cv) in tiles:
            xk = xkey + ":%d" % t0
            S.op('pool', 'tensor_tensor', r=[xk], w=["sq"], out=sc['sq'][:, :, :tw], in0=xT[:, :, t0:t0 + tw], in1=xT[:, :, t0:t0 + tw],
                 op=ALU.mult)
            ps, pk = self.psum()
            for k in range(8):
                S.op('pe', 'matmul', r=["sq", "consts"], w=[pk], out=ps[:, :tw], lhsT=ones[:, :], rhs=sc['sq'][:, k, :tw],
                     start=(k == 0), stop=(k == 7))
            S.op('act', 'activation', r=[pk], w=["lnt"], out=sc['lnt'][:, :tw], in_=ps[:, :tw], func=AF.Ln,
                 scale=1.0 / D, bias=self.eps_t[:, 0:1])
            pr, prk = self.psum()
            S.op('act', 'activation', r=["lnt"], w=[prk], out=pr[:, :tw], in_=sc['lnt'][:, :tw], func=AF.Exp, scale=-0.5)
            for k in range(8):
                S.op('dve', 'tensor_tensor', r=[xk, prk], w=["mtmp%d" % k], out=sc['mtmp'][:, k, :tw], in0=xT[:, k, t0:t0 + tw],
                     in1=pr[:, :tw], op=ALU.mult)
                S.op('act', 'activation', r=["mtmp%d" % k, "mod"], w=[okey + ":%d" % t0],
                     out=outT[:, k, t0:t0 + tw], in_=sc['mtmp'][:, k, :tw], func=AF.Identity,
                     bias=s[:, k, cv:cv + 1], scale=g[:, k, cv:cv + 1])

    def mod_group(self, l, j, bufs, rowt, banks=(0, 6)):
        S = self.S
        for half in range(2):
            gi = self.mod_gi
            self.mod_gi += 1
            buf, bk = bufs[gi % 2], "awb%d" % (gi % 2)
            rt, rtk = rowt[gi % len(rowt)], "rowt%d" % (gi % len(rowt))
            c0 = j * 1024 + half * 512
            S.dma('pool', out=buf[:, :, :], in_=self.adaw_d[l][:, :, c0:c0 + 512], w=[bk])
            ps, pk = self.psum(*banks)
            for k in range(8):
                S.op('pe', 'matmul', r=[bk, "scv"], w=[pk], out=ps[0:4, :], lhsT=self.scvb[:, k, :], rhs=buf[:, k, :],
                     start=(k == 0), stop=(k == 7))
            S.op('act', 'activation', r=[pk], w=[rtk], out=rt[0:4, :], in_=ps[0:4, :], func=AF.Copy)
            pt_, ptk_ = self.psum(*banks)
            for blk in range(4):
                S.op('pe', 'transpose', r=[rtk, "consts"], w=[ptk_], out=pt_[:, blk * 4:(blk + 1) * 4], in_=rt[0:4, blk * 128:(blk + 1) * 128],
                     identity=self.ident_f[0:4, 0:4])
            f0 = j * 8 + half * 4
            for cv in range(4):
                S.op('dve', 'tensor_tensor', r=[ptk_, "nm"], w=["mod"], out=self.mod[:, l, f0:f0 + 4, cv],
                     in0=pt_[:, 0:16].rearrange("p (f c) -> p f c", c=4)[:, :, cv], in1=self.adab[:, l, f0:f0 + 4], op=ALU.add)
        if j == 1:
            for cv in range(4):
                S.op('dve', 'scalar_tensor_tensor', r=["mod", "nm"], w=["mod"], out=self.gmix[:, l, :, cv],
                     in0=self.mod[:, l, 8:16, cv], scalar=1.0, in1=self.nmix_t[:, l, :], op0=ALU.add, op1=ALU.mult)
        if j == 4:
            for cv in range(4):
                S.op('dve', 'scalar_tensor_tensor', r=["mod", "nm"], w=["mod"], out=self.gmlp[:, l, :, cv],
                     in0=self.mod[:, l, 32:40, cv], scalar=1.0, in1=self.nmlp_t[:, l, :], op0=ALU.add, op1=ALU.mult)

    def build(self):
        nc, S, M, cfg = self.nc, self.S, self.M, self.cfg
        layers = cfg.get('layers', [0, 1])
        nb = cfg.get('nb', NB)
        xT_d = self.din("xT", [NB, 128, 8, SEQ])
        cT_d = self.din("ctxT", [NB, 128, 8, CTX])
        cv_d = self.din("cvT", [128, 8, 4])
        adaw_d = [self.din("adaw%d" % l, [128, 8, 6 * D]) for l in range(2)]
        adab_d = self.din("adab", [128, 2, 48])
        nmix_d = self.din("nmix", [128, 2, 8])
        nmlp_d = self.din("nmlp", [128, 2, 8])
        wout_d = [self.din("wout%d" % l, [128, 8, D]) for l in range(2)]
        w1_d = [self.din("w1_%d" % l, [128, 8, 4 * D]) for l in range(2)]
        w2_d = [self.din("w2_%d" % l, [128, 32, D]) for l in range(2)]
        ident_d = self.din("ident", [128, 128])
        outT_d = self.dout("outT", [NB, 128, 8, SEQ])
        xs_d = self.nc.dram_tensor("xs", [128, 8, TOK], F32, kind="Internal").ap()
        if cfg.get('inject_y'):
            yinj_d = [self.din("yinj%d" % l, [NB, 128, 8, TOK]) for l in range(2)]
        self.declare_mixer_dram()

        self.ident_bf = M.alloc("ident_bf", [128, 128], BF16)
        self.ident_f = M.alloc("ident_f", [128, 128], F32)
        self.ones_bf = M.alloc("ones_bf", [128, 128], BF16)
        self.eps_t = M.alloc("eps_t", [128, 1], F32)
        self.mod = M.alloc("mod", [128, 2, 48, 4], F32)
        self.gmix = M.alloc("gmix", [128, 2, 8, 4], F32)
        self.gmlp = M.alloc("gmlp", [128, 2, 8, 4], F32)
        nmix = M.alloc("nmix", [128, 2, 8], F32)
        nmlp = M.alloc("nmlp", [128, 2, 8], F32)
        adab = M.alloc("adab", [128, 2, 48], F32)
        self.alloc_mixer_persistent()
        base = M.mark()

        S.dma('sp', out=self.ident_f[:, :], in_=ident_d[:, :], w=["consts"])
        S.dma('pool', out=self.ident_bf[:, :], in_=ident_d[:, :], w=["consts"])
        S.op('dve', 'memset', w=["consts"], ap=self.ones_bf[:, :], constant=1.0)
        S.op('dve', 'memset', w=["consts"], ap=self.eps_t[:, :], constant=EPS)
        S.dma('sp', out=nmix[:, :, :], in_=nmix_d[:, :, :], w=["nm"])
        S.dma('sp', out=nmlp[:, :, :], in_=nmlp_d[:, :, :], w=["nm"])
        S.dma('sp', out=adab[:, :, :], in_=adab_d[:, :, :], w=["nm"])
        self.load_mixer_persistent()

        cvs = M.alloc("cvs", [128, 8, 4], F32)
        self.scvb = M.alloc("scvb", [128, 8, 4], BF16)
        self.adab = adab
        self.nmix_t, self.nmlp_t = nmix, nmlp
        self.adaw_d = adaw_d
        S.dma('sp', out=cvs[:, :, :], in_=cv_d[:, :, :], w=["cvs"])
        S.op('act', 'activation', r=["cvs"], w=["scv"], out=self.scvb[:, :, :], in_=cvs[:, :, :], func=AF.Silu)
        base = M.mark()
        bufs = [M.alloc("awb%d" % i, [128, 8, 512], BF16) for i in range(2)]
        rowt = [M.alloc("rowt%d" % i, [4, 512], F32) for i in range(2)]
        self.mod_gi = 0
        for j in range(2):
            self.mod_group(0, j, bufs, rowt)
        self.mod_todo = [(0, j) for j in range(2, 6)] + [(1, j) for j in range(6)]
        if cfg.get('mod_all') or 0 not in layers:
            while self.mod_todo:
                self.mod_group(*self.mod_todo.pop(0), bufs, rowt)
        self.dump("mod", self.mod[:, :, :, :], [128, 2, 48, 4], ["mod"])
        S.barrier()
        M.release(base)

        self.offB = M.mark()
        hT = M.alloc("hT", [128, 8, TOK], BF16)
        self.offA = M.mark()
        xT = M.alloc("xT", [128, 8, TOK], F32)
        self.offC = M.mark()
        yT = M.alloc("yT", [128, 8, TOK], BF16)
        regD = M.mark()
        self.offD = regD

        for bi in range(nb):
            for t0 in range(0, SEQ, 512):
                S.dma('sp', out=xT[:, :, t0:t0 + 512], in_=xT_d[bi, :, :, t0:t0 + 512], w=["xT:%d" % t0])
            S.dma('sp', out=xT[:, :, SEQ:TOK], in_=cT_d[bi, :, :, :], w=["xT:%d" % SEQ])
            for l in layers:
                last = (l == 1)
                m0 = M.mark()
                sc = dict(sq=M.alloc("sq", [128, 8, 512], BF16), lnt=M.alloc("lnt", [128, 512], F32),
                          rstd=M.alloc("rstd", [128, 512], F32), mtmp=M.alloc("mtmp", [128, 8, 512], F32))
                self.modulate(xT, "xT", self.gmix[:, l], self.mod[:, l, 0:8], hT, "hT", tiles_for(b=bi), sc)
                if cfg.get('debug') and bi == 0:
                    self.dump("hT%d" % l, hT[:, :, :], [128, 8, TOK], ["hT:%d" % t for t in range(0, TOK, 512)])
                for (t0, tw, cv) in tiles_for(b=bi):
                    S.dma('sp', out=xs_d[:, :, t0:t0 + tw], in_=xT[:, :, t0:t0 + tw], r=["xT:%d" % t0], w=["xs:%d" % t0])
                S.barrier()
                M.release(m0)
                if cfg.get('inject_y'):
                    for k in range(8):
                        S.dma('pool', out=yT[:, k, :], in_=yinj_d[l][bi, :, k, :], w=["yT:%d" % t for t in range(0, TOK, 512)])
                else:
                    try:
                        if l == 0:
                            self.even_mixer(bi, hT, yT, xT, regD)
                        else:
                            self.odd_mixer(bi, hT, yT, xT, regD)
                    except StopBuild:
                        pass
                S.barrier()
                M.region(regD, Mem.HI)
                if cfg.get('debug') and bi == 0:
                    self.dump("yT%d" % l, yT[:, :, :], [128, 8, TOK], ["yT:%d" % t for t in range(0, TOK, 512)])
                for (t0, tw, cv) in tiles_for(b=bi, with_ctx=not last):
                    S.dma('sp', out=xT[:, :, t0:t0 + tw], in_=xs_d[:, :, t0:t0 + tw], r=["xs:%d" % t0], w=["xT:%d" % t0])
                m0 = M.mark()
                wo = M.alloc("wo", [128, 8, D], BF16)
                sc = dict(sq=M.alloc("sq", [128, 8, 512], BF16), lnt=M.alloc("lnt", [128, 512], F32),
                          mtmp=M.alloc("mtmp", [128, 8, 512], F32))
                assert M.top <= Mem.HI - 8 * D * 2
                M.region(Mem.HI - 8 * D * 2, Mem.HI)
                w1pre = M.alloc("w1pre", [128, 8, D], BF16)
                M.region(m0, Mem.HI)
                S.dma('pool', out=wo[:, :, :], in_=wout_d[l][:, :, :], w=["wo"])
                S.dma('pool', out=w1pre[:, :, :], in_=w1_d[l][:, :, 0:D], w=["w1g0"])
                tl = tiles_for(b=bi, with_ctx=not last)
                for (t0, tw, cv) in tl:
                    for fo in range(8):
                        ps, pk = self.psum()
                        for k in range(8):
                            S.op('pe', 'matmul', r=["wo", "yT:%d" % t0], w=[pk], out=ps[:, :tw],
                                 lhsT=wo[:, k, fo * 128:(fo + 1) * 128], rhs=yT[:, k, t0:t0 + tw],
                                 start=(k == 0), stop=(k == 7))
                        S.op('dve', 'scalar_tensor_tensor', r=[pk, "mod"], w=["xT:%d" % t0],
                             out=xT[:, fo, t0:t0 + tw], in0=ps[:, :tw], scalar=self.mod[:, l, 16 + fo, cv:cv + 1],
                             in1=xT[:, fo, t0:t0 + tw], op0=ALU.mult, op1=ALU.add)
                    self.modulate(xT, "xT", self.gmlp[:, l], self.mod[:, l, 24:32], hT, "hT", [(t0, tw, cv)], sc)
                if cfg.get('debug') and bi == 0:
                    self.dump("xmid%d" % l, xT[:, :, :], [128, 8, TOK], ["xT:%d" % t for t in range(0, TOK, 512)])
                S.barrier()
                M.release(m0)
                M.release(M.mark())
                M.region(self.offC, Mem.HI)
                m0 = M.mark()
                w1g = [w1pre, M.alloc("w1g1", [128, 8, D], BF16)]
                w2g = [M.alloc("w2g%d" % i, [128, 8, D], BF16) for i in range(2)]
                hid = [M.alloc("hid%d" % i, [128, 8, 512], BF16) for i in range(2)]
                rl = [M.alloc("rl%d" % i, [128, 512], BF16) for i in range(2)]
                hi_ = 0
                for fg in range(4):
                    wb = fg % 2
                    if fg > 0:
                        S.dma('pool', out=w1g[wb][:, :, :], in_=w1_d[l][:, :, fg * D:(fg + 1) * D], w=["w1g%d" % wb])
                    S.dma('pool', out=w2g[wb][:, :, :], in_=w2_d[l][:, fg * 8:(fg + 1) * 8, :], w=["w2g%d" % wb])
                    for (t0, tw, cv) in tl:
                        hb = hi_ % 2
                        hi_ += 1
                        for hc in range(8):
                            ps, pk = self.psum()
                            for k in range(8):
                                S.op('pe', 'matmul', r=["w1g%d" % wb, "hT:%d" % t0], w=[pk], out=ps[:, :tw],
                                     lhsT=w1g[wb][:, k, hc * 128:(hc + 1) * 128], rhs=hT[:, k, t0:t0 + tw],
                                     start=(k == 0), stop=(k == 7))
                            rb = hc % 2
                            S.op('act', 'activation', r=[pk], w=["rl%d" % rb], out=rl[rb][:, :tw], in_=ps[:, :tw],
                                 func=AF.Relu)
                            S.op('pool', 'tensor_tensor', r=["rl%d" % rb], w=["hid%d:%d" % (hb, hc)],
                                 out=hid[hb][:, hc, :tw], in0=rl[rb][:, :tw], in1=rl[rb][:, :tw], op=ALU.mult)
                        for fo in range(8):
                            ps, pk = self.psum()
                            for hc in range(8):
                                S.op('pe', 'matmul', r=["w2g%d" % wb, "hid%d:%d" % (hb, hc)], w=[pk], out=ps[:, :tw],
                                     lhsT=w2g[wb][:, hc, fo * 128:(fo + 1) * 128], rhs=hid[hb][:, hc, :tw],
                                     start=(hc == 0), stop=(hc == 7))
                            S.op('dve', 'scalar_tensor_tensor', r=[pk, "mod"], w=["xT:%d" % t0],
                                 out=xT[:, fo, t0:t0 + tw], in0=ps[:, :tw], scalar=self.mod[:, l, 40 + fo, cv:cv + 1],
                                 in1=xT[:, fo, t0:t0 + tw], op0=ALU.mult, op1=ALU.add)
                if cfg.get('debug') and bi == 0:
                    self.dump("x%d" % l, xT[:, :, :], [128, 8, TOK], ["xT:%d" % t for t in range(0, TOK, 512)])
                S.barrier()
                M.region(regD, Mem.HI)
            for t0 in range(0, SEQ, 512):
                S.dma('sp', out=outT_d[bi, :, :, t0:t0 + 512], in_=xT[:, :, t0:t0 + 512], r=["xT:%d" % t0], w=["outT:%d" % t0])
            S.barrier()
        S.finish()
        S.emit()
        return nc

    def cut(self, n):
        if self.cfg.get('cut') == n:
            raise StopBuild()

    def rot(self, name, shape, dtype, n):
        tiles = [self.M.alloc(name, shape, dtype) for _ in range(n)]
        uid = self.M.n
        st = {'i': 0}

        def nxt():
            i = st['i'] % n
            st['i'] += 1
            return tiles[i], "%s@%d#%d" % (name, uid, i)
        return nxt

    def rstd_of(self, ps_ap, pk, rows, tw, inv_n, nx_ln, nx_rs):
        S = self.S
        lnt, lk = nx_ln()
        rs, rk = nx_rs()
        S.op('act', 'activation', r=[pk], w=[lk], out=lnt[0:rows, :tw], in_=ps_ap, func=AF.Ln, scale=inv_n,
             bias=self.eps_t[0:rows, 0:1])
        S.op('act', 'activation', r=[lk], w=[rk], out=rs[0:rows, :tw], in_=lnt[0:rows, :tw], func=AF.Exp, scale=-0.5)
        return rs, rk

    def declare_mixer_dram(self):
        self.evwin_d = self.din("evwin", [128, 8, 1440])
        self.wuq_d = self.din("wuq", [128, 2, 768])
        self.wukvn_d = self.din("wukvn", [128, 8, 64])
        self.wukvv_d = self.din("wukvv", [128, 512])
        self.evvec_d = self.din("evvec", [128, 16])
        self.lruvec_d = self.din("lruvec", [128, 4, 12])
        self.lruw_d = self.din("lruw", [128, 4, 2, 2, 128])
        self.ropecs_d = self.din("ropecs", [128, 2, TOK])
        self.shk_d = self.din("shk", [128, 96])
        self.convwb_d = self.din("convwb", [128, 4, 512])
        self.odwin_d = self.din("odwin", [128, 8, 3104])
        self.glaw_d = self.din("glaw", [64, 2, 256])
        self.glau_d = self.din("glau", [128, 6, 128])
        self.odvec_d = self.din("odvec", [128, 4])
        self.tbig_d = self.din("tbig", [128, 8, 16, 64])

    def alloc_mixer_persistent(self):
        M = self.M
        self.evvec = M.alloc("evvec", [128, 16], F32)
        self.lruvec = M.alloc("lruvec", [128, 4, 12], F32)
        self.nsp = M.alloc("nsp", [128, 4, 2], F32)
        self.one_t = M.alloc("one_t", [128, 1], F32)
        self.spt = [M.alloc("spt%d" % i, [128, 4, 2], F32) for i in range(3)]
        self.odvec = M.alloc("odvec", [128, 4], F32)

    def load_mixer_persistent(self):
        S = self.S
        S.dma('sp', out=self.evvec[:, :], in_=self.evvec_d[:, :], w=["evvec"])
        S.dma('sp', out=self.lruvec[:, :, :], in_=self.lruvec_d[:, :, :], w=["lruvec"])
        S.op('dve', 'memset', w=["consts"], ap=self.one_t[:, :], constant=1.0)
        S.dma('sp', out=self.odvec[:, :], in_=self.odvec_d[:, :], w=["odvec"])
        e, t, u = self.spt
        S.op('act', 'activation', r=["lruvec"], w=["spe"], out=e[:, :, :], in_=self.lruvec[:, :, 9:11], func=AF.Exp, scale=-1.0)
        S.op('dve', 'tensor_scalar', r=["spe"], w=["spt"], out=t[:, :, :], in0=e[:, :, :], scalar1=-0.25, scalar2=1.0 / 3.0,
             op0=ALU.mult, op1=ALU.add)
        S.op('dve', 'tensor_tensor', r=["spe", "spt"], w=["spu"], out=u[:, :, :], in0=t[:, :, :], in1=e[:, :, :], op=ALU.mult)
        S.op('dve', 'tensor_scalar', r=["spu"], w=["spt"], out=t[:, :, :], in0=u[:, :, :], scalar1=-1.0, scalar2=0.5,
             op0=ALU.mult, op1=ALU.add)
        S.op('dve', 'tensor_tensor', r=["spe", "spt"], w=["spu"], out=u[:, :, :], in0=t[:, :, :], in1=e[:, :, :], op=ALU.mult)
        S.op('dve', 'tensor_scalar', r=["spu"], w=["spt"], out=t[:, :, :], in0=u[:, :, :], scalar1=8.0, scalar2=-8.0,
             op0=ALU.mult, op1=ALU.add)
        S.op('dve', 'tensor_tensor', r=["spe", "spt"], w=["nsp"], out=self.nsp[:, :, :], in0=t[:, :, :], in1=e[:, :, :], op=ALU.mult)

    def even_mixer(self, bi, hT, yT, xT, regD):
        nc, S, M, cfg = self.nc, self.S, self.M, self.cfg
        ev = self.evvec
        tl = tiles_for(b=bi)
        TK = ["%d" % t for t in range(0, TOK, 512)]
        M.region(self.offD, Mem.HI)
        qn = M.alloc("qn", [128, 2, TOK], BF16)
        kvn = M.alloc("kvn", [128, TOK], BF16)
        krb = M.alloc("krb", [128, TOK], BF16)
        sqkr = M.alloc("sqkr", [128, TOK], BF16)
        css = M.alloc("css", [128, 2, TOK], BF16)
        D1 = M.mark()
        winA = M.alloc("winA", [128, 8, 416], BF16)
        winR = M.alloc("winR", [128, 8, 32], BF16)
        nx_uqr = self.rot("uqr", [128, 2, 512], F32, 1)
        nx_sq2 = self.rot("sq2", [128, 2, 512], BF16, 1)
        nx_ln = self.rot("lnt", [128, 512], F32, 2)
        nx_rs = self.rot("rstd", [128, 512], F32, 2)
        nx_t = self.rot("tt", [128, 512], F32, 3)
        M.region(self.offA, self.offC)
        csf = M.alloc("csf", [128, 2, TOK], F32)
        M.region(D1, Mem.HI)
        for j in range(2):
            S.dma('sp', out=csf[:, j, :], in_=self.ropecs_d[:, j, :], w=["csf%d" % j])
            S.op('dve', 'tensor_scalar', r=["csf%d" % j, "evvec"], w=["css"], out=css[0:96, j, :], in0=csf[0:96, j, :],
                 scalar1=ev[0:96, 5 + j:6 + j], scalar2=None, op0=ALU.mult)
        S.dma('pool', out=winA[:, :, :], in_=self.evwin_d[:, :, 0:416], w=["winA"])
        for hb in range(2):
            o = hb * 16
            S.op('dve', 'tensor_scalar', r=["winA"], w=["winR"], out=winR[:, :, o:o + 8], in0=winA[:, :, 384 + o + 8:384 + o + 16],
                 scalar1=-1.0, scalar2=None, op0=ALU.mult)
            S.op('dve', 'tensor_copy', r=["winA"], w=["winR"], out=winR[:, :, o + 8:o + 16], in_=winA[:, :, 384 + o:384 + o + 8])
        self.cut(1)
        for (t0, tw, cv) in tl:
            hk = "hT:%d" % t0
            uqr, uk = nx_uqr()
            sq2, sk = nx_sq2()
            for c in range(2):
                ps, pk = self.psum()
                for k in range(8):
                    S.op('pe', 'matmul', r=["winA", hk], w=[pk], out=ps[:, :tw], lhsT=winA[:, k, c * 128:(c + 1) * 128],
                         rhs=hT[:, k, t0:t0 + tw], start=(k == 0), stop=(k == 7))
                S.op('act', 'activation', r=[pk], w=[sk + ":%d" % c], out=sq2[:, c, :tw], in_=ps[:, :tw], func=AF.Square)
                S.op('dve', 'tensor_copy', r=[pk], w=[uk + ":%d" % c], out=uqr[:, c, :tw], in_=ps[:, :tw])
            ps, pk = self.psum()
            for c in range(2):
                S.op('pe', 'matmul', r=[sk + ":%d" % c, "consts"], w=[pk], out=ps[:, :tw], lhsT=self.ones_bf[:, :],
                     rhs=sq2[:, c, :tw], start=(c == 0), stop=(c == 1))
            rs, rk = self.rstd_of(ps[:, :tw], pk, 128, tw, 1.0 / 256, nx_ln, nx_rs)
            for c in range(2):
                S.op('dve', 'scalar_tensor_tensor', r=[uk + ":%d" % c, rk, "evvec"], w=["qn:%d" % t0], out=qn[:, c, t0:t0 + tw],
                     in0=uqr[:, c, :tw], scalar=ev[:, c:c + 1], in1=rs[:, :tw], op0=ALU.mult, op1=ALU.mult)
            self.cut(2)
            ps, pk = self.psum()
            for k in range(8):
                S.op('pe', 'matmul', r=["winA", hk], w=[pk], out=ps[:, :tw], lhsT=winA[:, k, 256:384],
                     rhs=hT[:, k, t0:t0 + tw], start=(k == 0), stop=(k == 7))
            sq1, s1k = nx_sq2()
            ur, u1k = nx_t()
            S.op('act', 'activation', r=[pk], w=[s1k + ":0"], out=sq1[:, 0, :tw], in_=ps[:, :tw], func=AF.Square)
            S.op('dve', 'tensor_copy', r=[pk], w=[u1k], out=ur[:, :tw], in_=ps[:, :tw])
            ps2, pk2 = self.psum()
            S.op('pe', 'matmul', r=[s1k + ":0", "consts"], w=[pk2], out=ps2[:, :tw], lhsT=self.ones_bf[:, :], rhs=sq1[:, 0, :tw],
                 start=True, stop=True)
            rs, rk = self.rstd_of(ps2[:, :tw], pk2, 128, tw, 1.0 / 128, nx_ln, nx_rs)
            S.op('dve', 'scalar_tensor_tensor', r=[u1k, rk, "evvec"], w=["kvn:%d" % t0], out=kvn[:, t0:t0 + tw],
                 in0=ur[:, :tw], scalar=ev[:, 2:3], in1=rs[:, :tw], op0=ALU.mult, op1=ALU.mult)
            self.cut(3)
            psA, pkA = self.psum()
            for k in range(8):
                S.op('pe', 'matmul', r=["winA", hk], w=[pkA], out=psA[0:32, :tw], lhsT=winA[:, k, 384:416],
                     rhs=hT[:, k, t0:t0 + tw], start=(k == 0), stop=(k == 7))
            psB, pkB = self.psum()
            for k in range(8):
                S.op('pe', 'matmul', r=["winR", hk], w=[pkB], out=psB[0:32, :tw], lhsT=winR[:, k, :],
                     rhs=hT[:, k, t0:t0 + tw], start=(k == 0), stop=(k == 7))
            S.op('act', 'activation', r=[pkA], w=["sqkr:%d" % t0], out=sqkr[0:32, t0:t0 + tw], in_=psA[0:32, :tw], func=AF.Square)
            t1, t1k = nx_t()
            t2, t2k = nx_t()
            S.op('dve', 'tensor_tensor', r=[pkA, "css"], w=[t1k], out=t1[0:32, :tw], in0=psA[0:32, :tw],
                 in1=css[0:32, 0, t0:t0 + tw], op=ALU.mult)
            S.op('dve', 'tensor_tensor', r=[pkB, "css"], w=[t2k], out=t2[0:32, :tw], in0=psB[0:32, :tw],
                 in1=css[0:32, 1, t0:t0 + tw], op=ALU.mult)
            S.op('pool', 'tensor_tensor', r=[t1k, t2k], w=["krb:%d" % t0], out=krb[0:32, t0:t0 + tw], in0=t1[0:32, :tw],
                 in1=t2[0:32, :tw], op=ALU.add)
        self.cut(4)
        if cfg.get('debug') and bi == 0:
            self.dump("qn", qn[:, :, :], [128, 2, TOK], ["qn:" + t for t in TK])
            self.dump("kvn", kvn[:, :], [128, TOK], ["kvn:" + t for t in TK])
            self.dump("krb", krb[0:32, :], [32, TOK], ["krb:" + t for t in TK])
        S.barrier()
        mbufs = None
        if self.mod_todo:
            keep = (M.top, M.limit)
            M.region(self.offC, self.offC + 4 * TOK * 2)
            mbufs = ([M.alloc("awb%d" % i, [128, 8, 512], BF16) for i in range(2)], [M.alloc("rowt%d" % i, [4, 512], F32) for i in range(1)])
            M.region(*keep)
            for _ in range(3):
                self.mod_group(*self.mod_todo.pop(0), *mbufs)
        if cfg.get('even_stop') == 1:
            return
        M.region(D1, Mem.HI)
        wug = M.alloc("wug", [128, 8, 512], BF16)
        lw = M.alloc("lruw", [128, 4, 2, 2, 128], BF16)
        cwb = M.alloc("cwb", [128, 4, 512], BF16)
        wux = [M.alloc("wux%d" % i, [128, 8, 128], BF16) for i in range(2)]
        S.dma('pool', out=wug[:, :, :], in_=self.evwin_d[:, :, 928:1440], w=["wug"])
        S.dma('pool', out=lw[:, :, :, :, :], in_=self.lruw_d[:, :, :, :, :], w=["lruw"])
        S.dma('pool', out=cwb[:, :, :], in_=self.convwb_d[:, :, :], w=["cwb"])
        M.region(self.offA, self.offC)
        xl = M.alloc("xl", [128, TOK], F32)
        xlb = M.alloc("xlb", [128, TOK], BF16)
        at = M.alloc("at", [128, TOK], F32)
        bx = M.alloc("bx", [128, TOK], F32)
        hf = M.alloc("hf", [128, TOK], F32)
        hb_ = M.alloc("hb", [128, TOK], F32)
        gel = M.alloc("gel", [128, TOK], BF16)
        wj = M.alloc("wj", [128, 4, 8, 128], BF16)
        lv = self.lruvec
        for c in range(4):
            wx, wxk = wux[c % 2], "wux%d" % (c % 2)
            S.dma('pool', out=wx[:, :, :], in_=self.evwin_d[:, :, 416 + c * 128:416 + (c + 1) * 128], w=[wxk])
            for j in range(4):
                for k in range(8):
                    S.op('pool', 'tensor_tensor', r=[wxk, "cwb"], w=["wj"], out=wj[:, j, k, :], in0=wx[:, k, :],
                         in1=cwb[:, j, c * 128:(c + 1) * 128], op=ALU.mult)
            for (t0, tw, cv) in tl:
                s0, e0 = (0, SEQ) if t0 < SEQ else (SEQ, TOK)
                ps, pk = self.psum()
                mm = []
                for j in (2, 1, 0, 3):
                    sft = j - 2
                    lo = max(t0, s0 - sft)
                    hi = min(t0 + tw, e0 - sft)
                    for k in range(8):
                        mm.append((j, k, lo, hi, sft))
                for n, (j, k, lo, hi, sft) in enumerate(mm):
                    S.op('pe', 'matmul', r=["wj"] + ["hT:%d" % t for t in range(0, TOK, 512)], w=[pk], out=ps[:, lo - t0:hi - t0],
                         lhsT=wj[:, j, k, :], rhs=hT[:, k, lo + sft:hi + sft], start=(n == 0), stop=(n == len(mm) - 1))
                S.op('act', 'activation', r=[pk, "lruvec"], w=["xl"], out=xl[:, t0:t0 + tw], in_=ps[:, :tw], func=AF.Identity,
                     bias=lv[:, c, 4:5], scale=1.0)
            for (t0, tw, cv) in tl:
                ps, pk = self.psum()
                for k in range(8):
                    S.op('pe', 'matmul', r=["wug", "hT:%d" % t0], w=[pk], out=ps[:, :tw], lhsT=wug[:, k, c * 128:(c + 1) * 128],
                         rhs=hT[:, k, t0:t0 + tw], start=(k == 0), stop=(k == 7))
                S.op('act', 'activation', r=[pk], w=["gel"], out=gel[:, t0:t0 + tw], in_=ps[:, :tw], func=AF.Gelu_apprx_tanh)
            S.op('pool', 'tensor_copy', r=["xl"], w=["xlb:%d" % t for t in range(0, TOK, 512)], out=xlb[:, :], in_=xl[:, :])
            for d in range(2):
                hx, hxk = (hf, "hf") if d == 0 else (hb_, "hb")
                for (t0, tw, cv) in tl:
                    psa, pka = self.psum()
                    S.op('pe', 'matmul', r=["lruw", "xlb:%d" % t0], w=[pka], out=psa[:, :tw], lhsT=lw[:, c, d, 0, :], rhs=xlb[:, t0:t0 + tw],
                         start=True, stop=True)
                    S.op('act', 'activation', r=[pka, "lruvec"], w=["at"], out=at[:, t0:t0 + tw], in_=psa[:, :tw], func=AF.Sigmoid,
                         bias=lv[:, c, 5 + d:6 + d], scale=1.0)
                    psx, pkx = self.psum()
                    S.op('pe', 'matmul', r=["lruw", "xlb:%d" % t0], w=[pkx], out=psx[:, :tw], lhsT=lw[:, c, d, 1, :], rhs=xlb[:, t0:t0 + tw],
                         start=True, stop=True)
                    S.op('act', 'activation', r=[pkx, "lruvec"], w=["bx"], out=bx[:, t0:t0 + tw], in_=psx[:, :tw], func=AF.Sigmoid,
                         bias=lv[:, c, 7 + d:8 + d], scale=1.0)
                S.op('act', 'activation', r=["at", "nsp"], w=["at"], out=at[:, :], in_=at[:, :], func=AF.Exp, scale=self.nsp[:, c, d:d + 1])
                S.op('dve', 'tensor_tensor', r=["at"], w=[hxk], out=hx[:, :], in0=at[:, :], in1=at[:, :], op=ALU.mult)
                S.op('act', 'activation', r=[hxk], w=[hxk], out=hx[:, :], in_=hx[:, :], func=AF.Sqrt, scale=-1.0, bias=self.one_t[:, 0:1])
                S.op('dve', 'tensor_tensor', r=["bx", "xl"], w=["bx"], out=bx[:, :], in0=bx[:, :], in1=xl[:, :], op=ALU.mult)
                S.op('dve', 'tensor_tensor', r=["bx", hxk], w=["bx"], out=bx[:, :], in0=bx[:, :], in1=hx[:, :], op=ALU.mult)
                if d == 0:
                    S.op('dve', 'tensor_tensor_scan', r=["at", "bx"], w=[hxk], out=hf[:, SEQ:TOK], data0=at[:, SEQ:TOK],
                         data1=bx[:, SEQ:TOK], initial=0.0, op0=ALU.mult, op1=ALU.add)
                    S.op('dve', 'tensor_tensor_scan', r=["at", "bx", hxk], w=[hxk], out=hf[:, 0:SEQ], data0=at[:, 0:SEQ],
                         data1=bx[:, 0:SEQ], initial=hf[:, TOK - 1:TOK], op0=ALU.mult, op1=ALU.add)
                else:
                    S.op('dve', 'tensor_tensor_scan', r=["at", "bx"], w=[hxk], out=hb_[:, SEQ:TOK][:, ::-1],
                         data0=at[:, SEQ:TOK][:, ::-1], data1=bx[:, SEQ:TOK][:, ::-1], initial=0.0, op0=ALU.mult, op1=ALU.add)
                    S.op('dve', 'tensor_tensor_scan', r=["at", "bx", hxk], w=[hxk], out=hb_[:, 0:SEQ][:, ::-1],
                         data0=at[:, 0:SEQ][:, ::-1], data1=bx[:, 0:SEQ][:, ::-1], initial=hb_[:, SEQ:SEQ + 1],
                         op0=ALU.mult, op1=ALU.add)
            S.op('dve', 'tensor_tensor', r=["hf", "hb"], w=["hf"], out=hf[:, :], in0=hf[:, :], in1=hb_[:, :], op=ALU.add)
            S.op('dve', 'tensor_tensor', r=["hf", "gel"], w=["yT:" + t for t in TK], out=yT[:, 4 + c, :], in0=hf[:, :], in1=gel[:, :],
                 op=ALU.mult)
        S.barrier()
        if cfg.get('even_stop') == 2:
            return
        M.region(D1, Mem.HI)
        wuq = M.alloc("wuq", [128, 2, 768], BF16)
        wuqr = M.alloc("wuqr", [128, 2, 8, 96], BF16)
        wkn = M.alloc("wkn", [128, 8, 96], BF16)
        wv = M.alloc("wv", [128, 512], BF16)
        shk = M.alloc("shk", [128, 96], BF16)
        nx_sq = self.rot("sq", [128, 512], BF16, 2)
        nx_ln = self.rot("lnt", [128, 512], F32, 2)
        nx_rs = self.rot("rstd", [128, 512], F32, 2)
        nx_t = self.rot("tt", [128, 512], F32, 2)
        nx_raw = self.rot("qkr", [128, 512], F32, 2)
        nx_rec = self.rot("rec", [128, 1], F32, 4)
        S.dma('pool', out=wuq[:, :, :], in_=self.wuq_d[:, :, :], w=["wuq"])
        S.dma('pool', out=wkn[:, :, 0:64], in_=self.wukvn_d[:, :, :], w=["wkn"])
        S.op('dve', 'memset', w=["wkn"], ap=wkn[:, :, 64:96], constant=0.0)
        S.dma('pool', out=wv[:, :], in_=self.wukvv_d[:, :], w=["wv"])
        S.dma('pool', out=shk[:, :], in_=self.shk_d[:, :], w=["shk"])
        S.op('dve', 'memset', w=["wuqr"], ap=wuqr[:, :, :, 0:64], constant=0.0)
        wuq4 = wuq[:, :, :].rearrange("p c (h e) -> p c h e", e=96)
        for hb in range(2):
            o = 64 + hb * 16
            S.op('dve', 'tensor_scalar', r=["wuq"], w=["wuqr"], out=wuqr[:, :, :, o:o + 8], in0=wuq4[:, :, :, o + 8:o + 16],
                 scalar1=-1.0, scalar2=None, op0=ALU.mult)
            S.op('dve', 'tensor_copy', r=["wuq"], w=["wuqr"], out=wuqr[:, :, :, o + 8:o + 16], in_=wuq4[:, :, :, o:o + 8])
        M.region(self.offB, self.offC)
        PT = [M.alloc("PT%d" % i, [128, 18, 512], BF16) for i in range(2)]
        ytok = M.alloc("ytok", [128, 18, 512], BF16)
        Vx = M.alloc("Vx", [128, 18, 8, 65], BF16)
        QT = [M.alloc("QT%d" % i, [128, TOK], BF16) for i in range(2)]
        KT = [M.alloc("KT%d" % i, [128, TOK], BF16) for i in range(2)]
        KR = 128 if cfg.get('mla_k128', 0) else 96
        if KR == 128:
            for i in range(2):
                S.op('pool', 'memset', w=["QT%d:%d" % (i, t) for t in range(0, TOK, 512)], ap=QT[i][96:128, :], constant=0.0)
                S.op('pool', 'memset', w=["KT%d:%d" % (i, t) for t in range(0, TOK, 512)], ap=KT[i][96:128, :], constant=0.0)
        S.op('pool', 'memset', w=["Vx"], ap=Vx[:, :, :, 64:65], constant=1.0)
        for kc in range(18):
            ps, pk = self.psum(0, 3)
            S.op('pe', 'matmul', r=["kvn:%d" % (kc // 4 * 512), "wv"], w=[pk], out=ps[:, :], lhsT=kvn[:, kc * 128:(kc + 1) * 128],
                 rhs=wv[:, :], start=True, stop=True)
            S.op('dve', 'tensor_copy', r=[pk], w=["Vx"], out=Vx[:, kc, :, 0:64], in_=ps[:, :].rearrange("p (h e) -> p h e", e=64))
        scale = 96.0 ** -0.5

        def proj(h, only=None):
            qt, kt = QT[h % 2], KT[h % 2]
            for ti, (t0, tw, cv) in enumerate(tl):
                if only is not None and ti != only:
                    continue
                psq, pkq = self.psum(6, 8)
                for c in range(2):
                    S.op('pe', 'matmul', r=["wuq", "qn:%d" % t0], w=[pkq], out=psq[0:96, :tw], lhsT=wuq[:, c, h * 96:(h + 1) * 96],
                         rhs=qn[:, c, t0:t0 + tw], start=(c == 0), stop=(c == 1))
                sq, sk = nx_sq()
                S.op('act', 'activation', r=[pkq], w=[sk], out=sq[0:96, :tw], in_=psq[0:96, :tw], func=AF.Square)
                qr, qrk = nx_raw()
                S.op('dve', 'tensor_copy', r=[pkq], w=[qrk], out=qr[0:96, :tw], in_=psq[0:96, :tw])
                psr, pkr = self.psum(6, 8)
                for c in range(2):
                    S.op('pe', 'matmul', r=["wuqr", "qn:%d" % t0], w=[pkr], out=psr[0:96, :tw], lhsT=wuqr[:, c, h, :],
                         rhs=qn[:, c, t0:t0 + tw], start=(c == 0), stop=(c == 1))
                t2, t2k = nx_t()
                S.op('dve', 'tensor_tensor', r=[pkr, "css"], w=[t2k], out=t2[64:96, :tw], in0=psr[64:96, :tw],
                     in1=css[64:96, 1, t0:t0 + tw], op=ALU.mult)
                pss, pks = self.psum(6, 8)
                S.op('pe', 'matmul', r=[sk, "consts"], w=[pks], out=pss[0:96, :tw], lhsT=self.ones_bf[0:96, 0:96], rhs=sq[0:96, :tw],
                     start=True, stop=True)
                rs, rk = self.rstd_of(pss[0:96, :tw], pks, 96, tw, 1.0 / 96, nx_ln, nx_rs)
                S.op('dve', 'scalar_tensor_tensor', r=[qrk, rk, "evvec"], w=["QT%d:%d" % (h % 2, t0)], out=qt[0:64, t0:t0 + tw],
                     in0=qr[0:64, :tw], scalar=ev[0:64, 3:4], in1=rs[0:64, :tw], op0=ALU.mult, op1=ALU.mult)
                t1, t1k = nx_t()
                S.op('pool', 'tensor_tensor', r=[qrk, "css"], w=[t1k], out=t1[64:96, :tw], in0=qr[64:96, :tw],
                     in1=css[64:96, 0, t0:t0 + tw], op=ALU.mult)
                S.op('pool', 'tensor_tensor', r=[t1k, t2k], w=[t1k], out=t1[64:96, :tw], in0=t1[64:96, :tw], in1=t2[64:96, :tw],
                     op=ALU.add)
                S.op('pool', 'tensor_tensor', r=[t1k, rk], w=["QT%d:%d" % (h % 2, t0)], out=qt[64:96, t0:t0 + tw], in0=t1[64:96, :tw],
                     in1=rs[64:96, :tw], op=ALU.mult)
                psk, pkk = self.psum(6, 8)
                S.op('pe', 'matmul', r=["wkn", "kvn:%d" % t0], w=[pkk], out=psk[0:96, :tw], lhsT=wkn[:, h, :], rhs=kvn[:, t0:t0 + tw],
                     start=True, stop=False)
                S.op('pe', 'matmul', r=["shk", "krb:%d" % t0], w=[pkk], out=psk[0:96, :tw], lhsT=shk[0:32, :], rhs=krb[0:32, t0:t0 + tw],
                     start=False, stop=True)
                sq, sk = nx_sq()
                S.op('act', 'activation', r=[pkk], w=[sk], out=sq[0:64, :tw], in_=psk[0:64, :tw], func=AF.Square)
                kr_, krk = nx_raw()
                S.op('dve', 'tensor_copy', r=[pkk], w=[krk], out=kr_[0:96, :tw], in_=psk[0:96, :tw])
                pss, pks = self.psum(6, 8)
                S.op('pe', 'matmul', r=[sk, "consts"], w=[pks], out=pss[0:96, :tw], lhsT=self.ones_bf[0:64, 0:96], rhs=sq[0:64, :tw],
                     start=True, stop=False)
                S.op('pe', 'matmul', r=["sqkr:%d" % t0, "consts"], w=[pks], out=pss[0:96, :tw], lhsT=self.ones_bf[0:32, 0:96],
                     rhs=sqkr[0:32, t0:t0 + tw], start=False, stop=True)
                rs, rk = self.rstd_of(pss[0:96, :tw], pks, 96, tw, 1.0 / 96, nx_ln, nx_rs)
                S.op('dve', 'scalar_tensor_tensor', r=[krk, rk, "evvec"], w=["KT%d:%d" % (h % 2, t0)], out=kt[0:96, t0:t0 + tw],
                     in0=kr_[0:96, :tw], scalar=ev[0:96, 4:5], in1=rs[0:96, :tw], op0=ALU.mult, op1=ALU.mult)

        qtiles = [(t0, tw, list(range(18))) for (t0, tw, cv) in tiles_for(b=bi, with_ctx=False)] + [(SEQ, CTX, [16, 17])]

        NH = cfg.get('mla_heads', 8)
        seq_ = [(h, qi) for h in range(NH) for qi in range(len(qtiles))]
        steps = []
        for it, (h, qi) in enumerate(seq_):
            kcs = qtiles[qi][2]
            for m in range(len(kcs) // 2):
                steps.append((it, h, qi, m, kcs[2 * m], kcs[2 * m + 1], len(kcs) // 2))

        def emit_S(si, step):
            it, h, qi, m, kc0, kc1, npair = step
            t0, tw, kcs = qtiles[qi]
            pt, ptk = PT[it % 2], "PT%d" % (it % 2)
            if m == 0 and qi == 0 and h + 1 < NH:
                proj(h + 1)
            if m == 0 and qi in (1, 3) and mbufs is not None and self.mod_todo:
                self.mod_group(*self.mod_todo.pop(0), *mbufs, banks=(6, 8))
            P = self.pp[si % 2]
            pks = ["ps%d" % (2 * (si % 2)), "ps%d" % (2 * (si % 2) + 1)]
            for j, kc in enumerate((kc0, kc1)):
                S.op('pe', 'matmul', r=["KT%d:%d" % (h % 2, kc // 4 * 512), "QT%d:%d" % (h % 2, t0)], w=[pks[j]], out=P[:, j * 512:j * 512 + tw],
                     lhsT=KT[h % 2][0:96, kc * 128:(kc + 1) * 128], rhs=QT[h % 2][0:96, t0:t0 + tw], start=True, stop=True)
            S.op('act', 'activation', r=pks, w=[ptk + ":%d" % kc0, ptk + ":%d" % kc1], out=pt[:, kc0:kc0 + 2, :tw],
                 in_=P[:, :].rearrange("p (b c) -> p b c", c=512)[:, :, :tw], func=AF.Exp, scale=scale)

        def emit_PV(step):
            it, h, qi, m, kc0, kc1, npair = step
            t0, tw, kcs = qtiles[qi]
            pt, ptk = PT[it % 2], "PT%d" % (it % 2)
            pso, pko = self.ps[4 + it % 2], "ps%d" % (4 + it % 2)
            for j, kc in enumerate((kc0, kc1)):
                n = 2 * m + j
                for sub in range(tw // 128):
                    S.op('pe', 'matmul', r=[ptk + ":%d" % kc, "Vx"], w=[pko], out=pso[:, sub * 65:(sub + 1) * 65],
                         lhsT=pt[:, kc, sub * 128:(sub + 1) * 128], rhs=Vx[:, kc, h, :], start=(n == 0 and sub == 0),
                         stop=(n == 2 * npair - 1), skip_group_check=True)
            if m == npair - 1:
                for sub in range(tw // 128):
                    st = (t0 // 128) + sub
                    rec, rck = nx_rec()
                    S.op('dve', 'reciprocal', r=[pko], w=[rck], out=rec[:, 0:1], in_=pso[:, sub * 65 + 64:sub * 65 + 65])
                    S.op('dve', 'tensor_scalar', r=[pko, rck], w=["ytok:%d" % st], out=ytok[:, st, h * 64:(h + 1) * 64],
                         in0=pso[:, sub * 65:sub * 65 + 64], scalar1=rec[:, 0:1], scalar2=None, op0=ALU.mult)

        LAG = cfg.get('lag', 2)
        proj(0)
        for i, stp in enumerate(steps):
            emit_S(i, stp)
            if i >= LAG:
                emit_PV(steps[i - LAG])
        for stp in steps[max(0, len(steps) - LAG):]:
            emit_PV(stp)
        while mbufs is not None and self.mod_todo:
            self.mod_group(*self.mod_todo.pop(0), *mbufs, banks=(6, 8))
        if cfg.get('debug') and bi == 0:
            self.dump("ytok", ytok[:, :, :], [128, 18, 512], ["ytok:%d" % st for st in range(18)])
        for st in range(18):
            pb = self.psb[st % 2]
            pbk = "ps7"
            for c in range(4):
                S.op('pe', 'transpose', r=["ytok:%d" % st, "consts"], w=[pbk], out=pb[:, c * 128:(c + 1) * 128],
                     in_=ytok[:, st, c * 128:(c + 1) * 128], identity=self.ident_bf[:, :])
            S.op('act', 'activation', r=[pbk], w=["yT:%d" % (st // 4 * 512)], out=yT[:, 0:4, st * 128:(st + 1) * 128],
                 in_=pb[:, 0:512].rearrange("p (c t) -> p c t", t=128), func=AF.Copy)

    def odd_mixer(self, bi, hT, yT, xT, regD):
        nc, S, M, cfg = self.nc, self.S, self.M, self.cfg
        tl = tiles_for(b=bi)
        TK = ["%d" % t for t in range(0, TOK, 512)]
        ov = self.odvec
        for p in range(2 if not cfg.get('skip_gla') else 0):
            M.region(self.offD, Mem.HI)
            wq = M.alloc("gwq", [128, 8, 128], BF16)
            wk = M.alloc("gwk", [128, 8, 128], BF16)
            wvv = M.alloc("gwv", [128, 8, 256], BF16)
            wg = M.alloc("gwg", [128, 8, 256], BF16)
            wlr = M.alloc("gwlr", [128, 8, 32], BF16)
            glau = M.alloc("glau", [128, 4, 128], F32)
            gmask = M.alloc("gmask", [128, 2, 128], BF16)
            glaw = M.alloc("glaw", [64, 2, 256], F32)
            nx_qk = self.rot("qkraw", [128, 512], BF16, 2)
            lrT = M.alloc("lrT", [64, 512], F32)
            gmask4 = M.alloc("gmask4", [128, 2, 4, 128], BF16)
            nx_e = self.rot("ge", [128, 512], F32, 2)
            nx_l = self.rot("gl", [128, 512], F32, 2)
            nx_x = self.rot("gx", [128, 512], F32, 4)
            nx_kt = self.rot("gkt", [128, 512], F32, 2)
            nx_am = self.rot("attm", [128, 512], BF16, 4)
            nx_sq = self.rot("gsq", [128, 512], BF16, 2)
            nx_ln = self.rot("glnt", [128, 512], F32, 2)
            od = self.odwin_d
            S.dma('pool', out=wq[:, :, :], in_=od[:, :, p * 128:(p + 1) * 128], w=["gwq"])
            S.dma('pool', out=wk[:, :, :], in_=od[:, :, 256 + p * 128:256 + (p + 1) * 128], w=["gwk"])
            S.dma('pool', out=wvv[:, :, :], in_=od[:, :, 512 + p * 256:512 + (p + 1) * 256], w=["gwv"])
            S.dma('pool', out=wg[:, :, :], in_=od[:, :, 1024 + p * 256:1024 + (p + 1) * 256], w=["gwg"])
            S.dma('pool', out=wlr[:, :, :], in_=od[:, :, 1536:1568], w=["gwlr"])
            S.dma('sp', out=glau[:, :, :], in_=self.glau_d[:, 0:4, :], w=["glau"])
            S.dma('pool', out=gmask[:, :, :], in_=self.glau_d[:, 4:6, :], w=["gmask"])
            for j in range(4):
                S.dma('pool', out=gmask4[:, :, j, :], in_=self.glau_d[:, 4:6, :], w=["gmask4"])
            S.dma('sp', out=glaw[:, :, :], in_=self.glaw_d[:, :, :], w=["glaw"])
            S.op('dve', 'memset', w=["lrT"], ap=lrT[:, :], constant=1.0)
            M.region(self.offA, self.offC)
            kd = [M.alloc("kd%d" % d, [128, 18, 128], BF16) for d in range(2)]
            vt = M.alloc("vtok", [128, 18, 256], BF16)
            qd = [M.alloc("qdec%d" % d, [128, SEQ], BF16) for d in range(2)]
            ki = [M.alloc("kinv%d" % d, [128, SEQ], BF16) for d in range(2)]
            sh = [M.alloc("shist%d" % d, [128, 32, 128], BF16) for d in range(2)]
            sg = M.alloc("sg", [128, 2, SEQ], BF16)
            dec = M.alloc("dec", [128, 2, 36], F32)
            sst = [[M.alloc("sst%d%d" % (d, i), [128, 128], F32) for i in range(2)] for d in range(2)]
            nx_rs = self.rot("grstd", [128, 512], F32, 2)
            nx_t = self.rot("gt1", [128, 512], F32, 2)
            for (t0, tw, cv) in tl:
                hk = "hT:%d" % t0
                lat = t0 < SEQ
                if lat:
                    qraw, qk_ = nx_qk()
                    ps, pk = self.psum()
                    for k in range(8):
                        S.op('pe', 'matmul', r=["gwq", hk], w=[pk], out=ps[:, :tw], lhsT=wq[:, k, :], rhs=hT[:, k, t0:t0 + tw],
                             start=(k == 0), stop=(k == 7))
                    S.op('act', 'activation', r=[pk], w=[qk_], out=qraw[:, :tw], in_=ps[:, :tw], func=AF.Copy, scale=0.125)
                    kraw, kk_ = nx_qk()
                    ps, pk = self.psum()
                    for k in range(8):
                        S.op('pe', 'matmul', r=["gwk", hk], w=[pk], out=ps[:, :tw], lhsT=wk[:, k, :], rhs=hT[:, k, t0:t0 + tw],
                             start=(k == 0), stop=(k == 7))
                    S.op('act', 'activation', r=[pk], w=[kk_], out=kraw[:, :tw], in_=ps[:, :tw], func=AF.Copy)
                    for hl in range(2):
                        ps, pk = self.psum()
                        for k in range(8):
                            S.op('pe', 'matmul', r=["gwg", hk], w=[pk], out=ps[:, :tw], lhsT=wg[:, k, hl * 128:(hl + 1) * 128],
                                 rhs=hT[:, k, t0:t0 + tw], start=(k == 0), stop=(k == 7))
                        S.op('act', 'activation', r=[pk], w=["sg:%d" % t0], out=sg[:, hl, t0:t0 + tw], in_=ps[:, :tw], func=AF.Silu)
                ps, pk = self.psum()
                for k in range(8):
                    S.op('pe', 'matmul', r=["gwlr", hk], w=[pk], out=ps[0:32, :tw], lhsT=wlr[:, k, :],
                         rhs=hT[:, k, t0:t0 + tw], start=(k == 0), stop=(k == 7))
                S.op('act', 'activation', r=[pk], w=["lrT"], out=lrT[0:32, :tw], in_=ps[0:32, :tw], func=AF.Copy)
                nsub = tw // 128
                st0 = t0 // 128
                psk, pkk = self.psum()
                for sub in range(nsub):
                    c0 = sub * 128
                    for k in range(8):
                        S.op('pe', 'matmul', r=["gwk", hk], w=[pkk], out=psk[:, c0:c0 + 128], lhsT=hT[:, k, t0 + c0:t0 + c0 + 128], rhs=wk[:, k, :],
                             start=(k == 0), stop=(k == 7))
                ktk, ktkk = nx_kt()
                S.op('act', 'activation', r=[pkk], w=[ktkk], out=ktk[:, :tw], in_=psk[:, :tw], func=AF.Copy)
                for s2 in range(0, nsub, 2):
                    psv, pkv = self.psum()
                    for sub in range(s2, min(s2 + 2, nsub)):
                        c0 = sub * 128
                        for k in range(8):
                            S.op('pe', 'matmul', r=["gwv", hk], w=[pkv], out=psv[:, (sub - s2) * 256:(sub - s2 + 1) * 256],
                                 lhsT=hT[:, k, t0 + c0:t0 + c0 + 128], rhs=wvv[:, k, :], start=(k == 0), stop=(k == 7))
                    ns = min(2, nsub - s2)
                    S.op('act', 'activation', r=[pkv], w=["vtok:%d" % (st0 + s2 + i) for i in range(ns)],
                         out=vt[:, st0 + s2:st0 + s2 + ns, :], in_=psv[:, 0:ns * 256].rearrange("p (a b) -> p a b", b=256), func=AF.Copy)
                for d in range(2):
                    psz, pkz = self.psum()
                    for sub in range(nsub):
                        c0 = sub * 128
                        S.op('pe', 'matmul', r=["lrT", "glaw"], w=[pkz], out=psz[:, c0:c0 + 128], lhsT=lrT[0:33, c0:c0 + 128],
                             rhs=glaw[0:33, d, p * 128:(p + 1) * 128], start=True, stop=True)
                    e, ek = nx_e()
                    l, lk = nx_l()
                    S.op('act', 'activation', r=[pkz], w=[ek], out=e[:, :tw], in_=psz[:, :tw], func=AF.Exp, scale=-1.0)
                    S.op('act', 'activation', r=[ek], w=[lk], out=l[:, :tw], in_=e[:, :tw], func=AF.Ln, bias=self.one_t[:, 0:1], scale=1.0)
                    psr, pkr = self.psum()
                    for sub in range(nsub):
                        c0 = sub * 128
                        S.op('pe', 'matmul', r=[lk, "glau"], w=[pkr], out=psr[:, c0:c0 + 128], lhsT=glau[:, 2 + d, :], rhs=l[:, c0:c0 + 128],
                             start=True, stop=True)
                    er, erk = nx_x()
                    S.op('act', 'activation', r=[pkr], w=[erk], out=er[:, :tw], in_=psr[:, :tw], func=AF.Exp)
                    S.op('dve', 'tensor_tensor', r=[ktkk, erk], w=["kd%d:%d" % (d, st0 + i) for i in range(nsub)],
                         out=kd[d][:, st0:st0 + nsub, :], in0=ktk[:, :tw].rearrange("p (a b) -> p a b", b=128),
                         in1=er[:, :tw].rearrange("p (a b) -> p a b", b=128), op=ALU.mult)
                    psc, pkc = self.psum()
                    for sub in range(nsub):
                        c0 = sub * 128
                        S.op('pe', 'matmul', r=[lk, "glau"], w=[pkc], out=psc[:, c0:c0 + 128], lhsT=l[:, c0:c0 + 128], rhs=glau[:, d, :],
                             start=True, stop=True)
                    ep, epk = nx_x()
                    S.op('act', 'activation', r=[pkc], w=[epk], out=ep[:, :tw], in_=psc[:, :tw], func=AF.Exp)
                    g0_ = 2 * st0
                    S.op('dve', 'tensor_copy', r=[epk], w=["dec"], out=dec[:, d, g0_:g0_ + 2 * nsub],
                         in_=ep[:, :tw].rearrange("p (a b) -> p a b", b=64)[:, :, 63 if d == 0 else 0])
                    if lat:
                        em, emk = nx_x()
                        S.op('act', 'activation', r=[pkc], w=[emk], out=em[:, :tw], in_=psc[:, :tw], func=AF.Exp, scale=-1.0)
                        S.op('dve', 'tensor_tensor', r=[qk_, epk], w=["qdec%d:%d" % (d, st0 + i) for i in range(nsub)], out=qd[d][:, t0:t0 + tw],
                             in0=qraw[:, :tw], in1=ep[:, :tw], op=ALU.mult)
                        S.op('pool', 'tensor_tensor', r=[kk_, emk], w=["kinv%d:%d" % (d, st0 + i) for i in range(nsub)], out=ki[d][:, t0:t0 + tw],
                             in0=kraw[:, :tw], in1=em[:, :tw], op=ALU.mult)
            if cfg.get('debug') and bi == 0 and p == 0:
                self.dump("dec", dec[:, :, :], [128, 2, 36], ["dec"])
                for d in range(2):
                    self.dump("kd%d" % d, kd[d][:, :, :], [128, 18, 128], ["kd%d:%d" % (d, st) for st in range(18)])
                    self.dump("qd%d" % d, qd[d][:, :], [128, SEQ], ["qdec%d:%d" % (d, st) for st in range(16)])
                    self.dump("ki%d" % d, ki[d][:, :], [128, SEQ], ["kinv%d:%d" % (d, st) for st in range(16)])
                self.dump("vt", vt[:, :, :], [128, 18, 256], ["vtok:%d" % st for st in range(18)])
            self.cut(11)
            orders = [[32, 33, 34, 35] + list(range(32)), [35, 34, 33, 32] + list(range(31, -1, -1))]
            for d in range(2):
                S.op('dve', 'memset', w=["sst%d0" % d], ap=sst[d][0][:, :], constant=0.0)
            for n in range(36):
                for d in range(2):
                    g = orders[d][n]
                    st, half = g // 2, g % 2
                    cur, nxt_ = sst[d][n % 2], sst[d][(n + 1) % 2]
                    ck, nk_ = "sst%d%d" % (d, n % 2), "sst%d%d" % (d, (n + 1) % 2)
                    if g < 32:
                        S.op('pool', 'tensor_copy', r=[ck], w=["shist%d:%d" % (d, g)], out=sh[d][:, g, :], in_=cur[:, :])
                    if n == 35:
                        continue
                    ps, pk = self.psum()
                    r0_ = half * 64
                    S.op('pe', 'matmul', r=["kd%d:%d" % (d, st), "vtok:%d" % st], w=[pk], out=ps[:, 0:256],
                         lhsT=kd[d][r0_:r0_ + 64, st, :], rhs=vt[r0_:r0_ + 64, st, :], start=True, stop=True)
                    for hl in range(2):
                        rr = slice(hl * 64, (hl + 1) * 64)
                        S.op('dve', 'scalar_tensor_tensor', r=[ck, pk, "dec"], w=[nk_], out=nxt_[rr, :], in0=cur[rr, :],
                             scalar=dec[rr, d, g:g + 1], in1=ps[rr, hl * 128:(hl + 1) * 128], op0=ALU.mult, op1=ALU.add)
            if cfg.get('debug') and bi == 0 and p == 0:
                for d in range(2):
                    self.dump("sh%d" % d, sh[d][:, :, :], [128, 32, 128], ["shist%d:%d" % (d, g) for g in range(32)])
            self.cut(12)
            for grp in range(4):
                T0 = grp * 512
                for hl in range(2):
                    rr = slice(hl * 64, (hl + 1) * 64)
                    ams = []
                    for d in range(2):
                        psa, pka = self.psum()
                        for j in range(4):
                            st = grp * 4 + j
                            tt0 = st * 128
                            S.op('pe', 'matmul', r=["kinv%d:%d" % (d, st), "qdec%d:%d" % (d, st)], w=[pka], out=psa[:, j * 128:(j + 1) * 128],
                                 lhsT=ki[d][rr, tt0:tt0 + 128], rhs=qd[d][rr, tt0:tt0 + 128], start=True, stop=True)
                        am, amk = nx_am()
                        S.op('dve', 'tensor_tensor', r=[pka, "gmask4"], w=[amk], out=am[:, :], in0=psa[:, :],
                             in1=gmask4[:, d, :, :].rearrange("p a b -> p (a b)"), op=ALU.mult)
                        ams.append((am, amk))
                    pso, pko = self.psum()
                    for j in range(4):
                        st = grp * 4 + j
                        tt0 = st * 128
                        for d in range(2):
                            S.op('pe', 'matmul', r=[ams[d][1], "vtok:%d" % st], w=[pko], out=pso[:, j * 128:(j + 1) * 128],
                                 lhsT=vt[:, st, hl * 128:(hl + 1) * 128], rhs=ams[d][0][:, j * 128:(j + 1) * 128], start=(d == 0), stop=False)
                        for d in range(2):
                            for half in range(2):
                                g = 2 * st + half
                                S.op('pe', 'matmul', r=["shist%d:%d" % (d, g), "qdec%d:%d" % (d, st)], w=[pko],
                                     out=pso[:, j * 128 + half * 64:j * 128 + (half + 1) * 64], lhsT=sh[d][rr, g, :],
                                     rhs=qd[d][rr, tt0 + half * 64:tt0 + (half + 1) * 64], start=False, stop=(d == 1 and half == 1))
                    sq, sk = nx_sq()
                    S.op('act', 'activation', r=[pko], w=[sk], out=sq[:, :], in_=pso[:, :], func=AF.Square)
                    pss, pks = self.psum()
                    S.op('pe', 'matmul', r=[sk, "consts"], w=[pks], out=pss[:, :], lhsT=self.ones_bf[:, :], rhs=sq[:, :], start=True, stop=True)
                    rs, rk = self.rstd_of(pss[:, :], pks, 128, 512, 1.0 / 128, nx_ln, nx_rs)
                    t1, t1k = nx_t()
                    S.op('dve', 'scalar_tensor_tensor', r=[pko, rk, "odvec"], w=[t1k], out=t1[:, :], in0=pso[:, :], scalar=ov[:, 0:1],
                         in1=rs[:, :], op0=ALU.mult, op1=ALU.mult)
                    S.op('pool', 'tensor_tensor', r=[t1k, "sg:%d" % T0], w=["yT:%d" % T0],
                         out=yT[:, 2 * p + hl, T0:T0 + 512], in0=t1[:, :], in1=sg[:, hl, T0:T0 + 512], op=ALU.mult)
            S.barrier()
        if cfg.get('skip_nat'):
            return
        M.region(self.offD, Mem.HI)
        tb = M.alloc("tbig", [128, 8, 16, 64], BF16)
        wnq_ = [M.alloc("wnq%d" % i, [128, 8, 128], BF16) for i in range(2)]
        wnk_ = [M.alloc("wnk%d" % i, [128, 8, 128], BF16) for i in range(2)]
        wnv = M.alloc("wnv", [128, 8, 512], BF16)
        ones2 = M.alloc("ones2", [128, 128], BF16)
        PTn = [M.alloc("PTn%d" % i, [128, 7, 512], BF16) for i in range(2)]
        nx_sb = self.rot("nsb", [128, 512], F32, 2)
        nx_sq = self.rot("nsq", [128, 512], BF16, 2)
        nx_ln = self.rot("nlnt", [128, 512], F32, 1)
        nx_rs = self.rot("nrstd", [128, 512], F32, 2)
        nx_rec = self.rot("nrec", [128, 4], F32, 4)
        M.region(self.offA, self.offC)
        QnT = M.alloc("QnT", [128, 4, SEQ], BF16)
        KnT = M.alloc("KnT", [128, 4, TOK], BF16)
        Vn = M.alloc("Vn", [128, 18, 8, 65], BF16)
        ytok = M.alloc("nytok", [128, 16, 512], BF16)
        od = self.odwin_d
        for hh in range(4):
            S.dma('pool', out=tb[:, 2 * hh:2 * hh + 2, :, :], in_=self.tbig_d[:, 2 * hh:2 * hh + 2, :, :], w=["tbig"])
        S.op('dve', 'tensor_scalar', r=["tbig"], w=["tbig"], out=tb[:, :, :, :], in0=tb[:, :, :, :], scalar1=8.0, scalar2=None, op0=ALU.mult)
        S.dma('pool', out=wnv[:, :, :], in_=od[:, :, 2592:3104], w=["wnv"])
        S.op('dve', 'memset', w=["ones2"], ap=ones2[:, :], constant=0.0)
        S.op('dve', 'memset', w=["ones2"], ap=ones2[0:64, 0:64], constant=1.0)
        S.op('dve', 'memset', w=["ones2"], ap=ones2[64:128, 64:128], constant=1.0)
        S.op('pool', 'memset', w=["Vn"], ap=Vn[:, :, :, :], constant=1.0)
        for st in range(18):
            ps, pk = self.psum()
            for k in range(8):
                S.op('pe', 'matmul', r=["wnv", "hT:%d" % (st // 4 * 512)], w=[pk], out=ps[:, :], lhsT=hT[:, k, st * 128:(st + 1) * 128],
                     rhs=wnv[:, k, :], start=(k == 0), stop=(k == 7))
            S.op('act', 'activation', r=[pk], w=["Vn"], out=Vn[:, st, :, 0:64], in_=ps[:, :].rearrange("p (h e) -> p h e", e=64), func=AF.Copy)
        for pr in range(4):
            wnq, wnk = wnq_[pr % 2], wnk_[pr % 2]
            S.dma('pool', out=wnq[:, :, :], in_=od[:, :, 1568 + pr * 128:1568 + (pr + 1) * 128], w=["wnq%d" % (pr % 2)])
            S.dma('pool', out=wnk[:, :, :], in_=od[:, :, 2080 + pr * 128:2080 + (pr + 1) * 128], w=["wnk%d" % (pr % 2)])
            for (t0, tw, cv) in tl:
                for which in range(2):
                    if which == 0 and t0 >= SEQ:
                        continue
                    wt, wkey = (wnq, "wnq%d" % (pr % 2)) if which == 0 else (wnk, "wnk%d" % (pr % 2))
                    ps, pk = self.psum()
                    for k in range(8):
                        S.op('pe', 'matmul', r=[wkey, "hT:%d" % t0], w=[pk], out=ps[:, :tw], lhsT=wt[:, k, :], rhs=hT[:, k, t0:t0 + tw],
                             start=(k == 0), stop=(k == 7))
                    sq, sk = nx_sq()
                    S.op('act', 'activation', r=[pk], w=[sk], out=sq[:, :tw], in_=ps[:, :tw], func=AF.Square)
                    pss, pks = self.psum()
                    S.op('pe', 'matmul', r=[sk, "ones2"], w=[pks], out=pss[:, :tw], lhsT=ones2[:, :], rhs=sq[:, :tw], start=True, stop=True)
                    rs, rk = self.rstd_of(pss[:, :tw], pks, 128, tw, 1.0 / 64, nx_ln, nx_rs)
                    if which == 0:
                        S.op('dve', 'scalar_tensor_tensor', r=[pk, rk, "odvec"], w=["QnT:%d" % t0], out=QnT[:, pr, t0:t0 + tw], in0=ps[:, :tw],
                             scalar=ov[:, 1:2], in1=rs[:, :tw], op0=ALU.mult, op1=ALU.mult)
                    else:
                        S.op('dve', 'scalar_tensor_tensor', r=[pk, rk, "odvec"], w=["KnT:%d" % t0], out=KnT[:, pr, t0:t0 + tw], in0=ps[:, :tw],
                             scalar=ov[:, 2:3], in1=rs[:, :tw], op0=ALU.mult, op1=ALU.mult)
        self.cut(21)

        def r0f(r):
            return min(max(r - 4, 0), 24)
        iters = [(qt, hg) for qt in range(cfg.get('nat_tiles', 16)) for hg in range(2)]

        def kcs_of(qt):
            rlo = r0f(2 * qt)
            rhi = r0f(2 * qt + 1) + 8
            return list(range(rlo // 2, (rhi - 1) // 2 + 1)) + [16, 17]

        def nat_S(it):
            qt, hg = iters[it]
            kcs = kcs_of(qt)
            pt, ptk = PTn[it % 2], "PTn%d" % (it % 2)
            for j, kc in enumerate(kcs):
                ps, pk = self.psum(0, 3)
                for hh in range(4):
                    h = hh * 2 + hg
                    pr, rr = h // 2, slice((h % 2) * 64, (h % 2) * 64 + 64)
                    S.op('pe', 'matmul', r=["KnT:%d" % (kc // 4 * 512), "QnT:%d" % (qt // 4 * 512)], w=[pk], out=ps[:, hh * 128:(hh + 1) * 128],
                         lhsT=KnT[rr, pr, kc * 128:(kc + 1) * 128], rhs=QnT[rr, pr, qt * 128:(qt + 1) * 128], start=True, stop=True)
                if kc < 16:
                    Dd = 2 * kc - 2 * qt + 7
                    sb, sbk = nx_sb()
                    S.op('dve', 'tensor_tensor', r=[pk, "tbig"], w=[sbk], out=sb[:, :].rearrange("p (h a c) -> p h a c", a=2, c=64),
                         in0=ps[:, :].rearrange("p (h a c) -> p h a c", a=2, c=64),
                         in1=tb[:, hg:8:2, Dd - 1:Dd + 1, :][:, :, ::-1, :], op=ALU.add)
                    S.op('act', 'activation', r=[sbk], w=[ptk + ":%d" % j], out=pt[:, j, :], in_=sb[:, :], func=AF.Exp, scale=0.125)
                    for i in range(2):
                        for a in range(2):
                            krow, qrow = 2 * kc + i, 2 * qt + a
                            if not (r0f(qrow) <= krow < r0f(qrow) + 8) and not cfg.get('nat_nomemset'):
                                S.op('pool', 'memset', r=[], w=[ptk + ":%d" % j],
                                     ap=pt[i * 64:(i + 1) * 64, j, :].rearrange("p (h q) -> p h q", q=128)[:, :, a * 64:(a + 1) * 64], constant=0.0)
                else:
                    S.op('act', 'activation', r=[pk], w=[ptk + ":%d" % j], out=pt[:, j, :], in_=ps[:, :], func=AF.Exp, scale=0.125)

        def nat_PV(it):
            qt, hg = iters[it]
            if cfg.get('nat_nopv'):
                return
            kcs = kcs_of(qt)
            pt, ptk = PTn[it % 2], "PTn%d" % (it % 2)
            pso, pko = self.psum(3, 6)
            for hh in range(4):
                h = hh * 2 + hg
                for j, kc in enumerate(kcs):
                    S.op('pe', 'matmul', r=[ptk + ":%d" % j, "Vn"], w=[pko], out=pso[:, hh * 65:(hh + 1) * 65], lhsT=pt[:, j, hh * 128:(hh + 1) * 128],
                         rhs=Vn[:, kc, h, :], start=(j == 0), stop=(j == len(kcs) - 1))
            rec, rck = nx_rec()
            S.op('dve', 'reciprocal', r=[pko], w=[rck], out=rec[:, 0:4], in_=pso[:, 0:260].rearrange("p (h e) -> p h e", e=65)[:, :, 64])
            for hh in range(4):
                h = hh * 2 + hg
                S.op('dve', 'tensor_scalar', r=[pko, rck], w=["nytok:%d" % qt], out=ytok[:, qt, h * 64:(h + 1) * 64], in0=pso[:, hh * 65:hh * 65 + 64],
                     scalar1=rec[:, hh:hh + 1], scalar2=None, op0=ALU.mult)
            if hg == 1:
                pb = self.psb[qt % 2]
                pbk = "ps7"
                for c in range(4):
                    S.op('pe', 'transpose', r=["nytok:%d" % qt, "consts"], w=[pbk], out=pb[:, c * 128:(c + 1) * 128],
                         in_=ytok[:, qt, c * 128:(c + 1) * 128], identity=self.ident_bf[:, :])
                S.op('act', 'activation', r=[pbk], w=["yT:%d" % (qt // 4 * 512)], out=yT[:, 4:8, qt * 128:(qt + 1) * 128],
                     in_=pb[:, 0:512].rearrange("p (c t) -> p c t", t=128), func=AF.Copy)

        if iters:
            nat_S(0)
        for it in range(len(iters)):
            if it + 1 < len(iters):
                nat_S(it + 1)
            nat_PV(it)


def fm(w):
    K = w.shape[0] // 128
    return np.ascontiguousarray(w.reshape(K, 128, -1).transpose(1, 0, 2))


def vec_fm(v):
    return np.ascontiguousarray(v.reshape(-1, 128).T)


def host_inputs(inp, core, extra=None):
    b0 = core * NB
    f32 = np.float32
    m = {}
    x = inp['x'][b0:b0 + NB]
    m['xT'] = np.ascontiguousarray(x.transpose(0, 2, 1).reshape(NB, 8, 128, SEQ).transpose(0, 2, 1, 3)).astype(f32)
    cx = inp['ctx'][b0:b0 + NB]
    m['ctxT'] = np.ascontiguousarray(cx.transpose(0, 2, 1).reshape(NB, 8, 128, CTX).transpose(0, 2, 1, 3)).astype(f32)
    cv = np.stack([inp['c'][b0], inp['c'][b0 + 1], inp['c_ctx'], inp['c_ctx']], axis=-1)
    m['cvT'] = fm(cv).astype(f32)
    for l in range(2):
        m['adaw%d' % l] = fm(inp['ada_w'][l]).astype(f32)
        m['wout%d' % l] = fm(inp['w_out'][l]).astype(f32)
        m['w1_%d' % l] = fm(inp['mlp_w1'][l]).astype(f32)
        m['w2_%d' % l] = fm(inp['mlp_w2'][l]).astype(f32)
    m['adab'] = np.ascontiguousarray(np.stack([vec_fm(inp['ada_b'][l]) for l in range(2)], axis=1)).astype(f32)
    m['nmix'] = np.ascontiguousarray(np.stack([vec_fm(inp['norm_mix'][l]) for l in range(2)], axis=1)).astype(f32)
    m['nmlp'] = np.ascontiguousarray(np.stack([vec_fm(inp['norm_mlp'][l]) for l in range(2)], axis=1)).astype(f32)
    m['ident'] = np.eye(128, dtype=f32)
    m['evwin'] = fm(inp['ev_w_in'][0]).astype(f32)
    m['wuq'] = fm(inp['mla_w_uq'][0]).astype(f32)
    wukv = inp['mla_w_ukv'][0].reshape(128, 8, 128)
    m['wukvn'] = np.ascontiguousarray(wukv[:, :, 0:64]).astype(f32)
    m['wukvv'] = np.ascontiguousarray(wukv[:, :, 64:128].reshape(128, 512)).astype(f32)
    qg = inp['mla_q_gain'][0]
    kg = inp['mla_k_gain'][0]
    perm = np.array([(i + 8) if (i % 16) < 8 else (i - 8) for i in range(32)])
    ev = np.ones((128, 16), f32)
    ev[:, 0:2] = vec_fm(inp['mla_q_norm'][0])
    ev[:, 2] = inp['mla_kv_norm'][0]
    ev[0:96, 3] = qg
    ev[0:64, 4] = kg[0:64]
    ev[0:32, 5] = kg[64:96]
    ev[64:96, 5] = qg[64:96]
    ev[0:32, 6] = kg[64:96][perm]
    ev[64:96, 6] = qg[64:96][perm]
    m['evvec'] = ev
    lv = np.zeros((128, 4, 12), f32)
    for j in range(4):
        lv[:, :, j] = vec_fm(inp['lru_conv_w'][0][j])
    lv[:, :, 4] = vec_fm(inp['lru_conv_b'][0])
    for d in range(2):
        lv[:, :, 5 + d] = vec_fm(inp['lru_b_a'][0][d])
        lv[:, :, 7 + d] = vec_fm(inp['lru_b_x'][0][d])
        lv[:, :, 9 + d] = vec_fm(inp['lru_lam'][0][d])
    m['lruvec'] = lv
    m['convwb'] = np.ascontiguousarray(np.broadcast_to(inp['lru_conv_w'][0][None, :, :], (128, 4, 512))).astype(f32)
    lw = np.zeros((128, 4, 2, 2, 128), f32)
    for c in range(4):
        for d in range(2):
            for g, nm in enumerate(('lru_w_a', 'lru_w_x')):
                for hb in range(2):
                    lw[hb * 64:(hb + 1) * 64, c, d, g, hb * 64:(hb + 1) * 64] = inp[nm][0][d][2 * c + hb]
    m['lruw'] = lw
    m['ropecs'] = rope_tables()
    shk = np.zeros((128, 96), f32)
    for k in range(32):
        shk[k, 64 + k] = 1.0
    m['shk'] = shk
    m['odwin'] = fm(inp['od_w_in'][0]).astype(f32)
    gw = np.zeros((64, 2, 256), f32)
    for d in range(2):
        gw[d * 16:(d + 1) * 16, d, :] = inp['gla_w_a'][0][d]
        gw[32, d, :] = inp['gla_b_a'][0][d]
    m['glaw'] = gw
    m['glau'] = gla_consts()
    ovv = np.ones((128, 4), f32)
    ovv[:, 0] = inp['gla_o_gain'][0]
    ovv[:, 1] = np.tile(inp['na_q_gain'][0], 2)
    ovv[:, 2] = np.tile(inp['na_k_gain'][0], 2)
    m['odvec'] = ovv
    m['tbig'] = natten_table(inp['na_rpb'][0])
    return m


def gla_consts():
    f32 = np.float32
    u = np.zeros((128, 6, 128), f32)
    sp = np.arange(128)[:, None]
    t = np.arange(128)[None, :]
    same = (sp // 64) == (t // 64)
    u[:, 0, :] = np.where(same & (sp <= t), -1.0 / 16, 0.0)
    u[:, 1, :] = np.where(same & (sp >= t), -1.0 / 16, 0.0)
    u[:, 2, :] = np.where(same & (sp > t), -1.0 / 16, 0.0)
    u[:, 3, :] = np.where(same & (sp < t), -1.0 / 16, 0.0)
    u[:, 4, :] = np.where(same & (sp <= t), 1.0, 0.0)
    u[:, 5, :] = np.where(same & (sp >= t), 1.0, 0.0)
    return u


def natten_table(rpb):
    f32 = np.float32
    tbl = np.full((128, 8, 16, 64), -30000.0, f32)
    p = np.arange(128)
    i = p // 64
    kcol = p % 64
    c = np.arange(64)
    c0 = np.clip(c - 8, 0, 48)
    valid = (kcol[:, None] >= c0[None, :]) & (kcol[:, None] < c0[None, :] + 16)
    dj = np.clip(kcol[:, None] - c[None, :] + 15, 0, 30)
    for e in range(16):
        di = e + i
        ok = valid & (di[:, None] <= 14)
        dic = np.clip(di, 0, 14)
        for h in range(8):
            g = rpb[h][dic[:, None], dj]
            tbl[:, h, e, :] = np.where(ok, g, tbl[:, h, e, :])
    return tbl


def rope_tables():
    f32 = np.float32
    half = 16
    inv = 10000.0 ** (-np.arange(0, half, 2, dtype=np.float64) / half)
    pos = np.arange(SEQ)
    cs = np.zeros((128, 2, TOK), f32)
    tab = np.zeros((32, 2, TOK), np.float64)
    tab[:, 0, :] = 1.0
    for mm in range(32):
        p = (pos // 64) if mm < 16 else (pos % 64)
        ang = p.astype(np.float64) * inv[(mm % 16) % 8]
        tab[mm, 0, :SEQ] = np.cos(ang)
        tab[mm, 1, :SEQ] = np.sin(ang)
    cs[0:32] = tab.astype(f32)
    cs[64:96] = tab.astype(f32)
    return cs


def unpack_out(outT):
    return np.ascontiguousarray(outT.transpose(0, 3, 2, 1).reshape(NB, SEQ, D))


def kernel(**inputs):
    inp = {k: np.asarray(v) for k, v in inputs.items()}
    bld = Builder(dict())
    nc = bld.build()
    in_maps = [host_inputs(inp, c) for c in range(8)]
    res = run_bass_kernel_spmd(nc, in_maps, core_ids=list(range(8)))
    outs = [unpack_out(r["outT"]) for r in res.results]
    return np.concatenate(outs, axis=0).astype(np.float32)
```

```python
import numpy as np
import concourse.bass as bass
import concourse.mybir as mybir
from concourse.bass_utils import run_bass_kernel_spmd

F32 = mybir.dt.float32
BF16 = mybir.dt.bfloat16
AF = mybir.ActivationFunctionType
ALU = mybir.AluOpType

D = 1024
SEQ = 2048
CTX = 256
TOK = SEQ + CTX
NB = 2
EPS = 1e-6
NPOOL = 40


class Sched:
    def __init__(self, nc):
        self.nc = nc
        self.eng = {'pe': nc.tensor, 'act': nc.scalar, 'dve': nc.vector, 'pool': nc.gpsimd, 'sp': nc.sync}
        self.ins = []
        self.last_w = {}
        self.rd_eng = {}
        self.rd_dma = {}
        self.bar = {e: [] for e in self.eng}
        self.last_on = {e: None for e in self.eng}
        self.open_dma = set()
        self.ps_last = {}

    def _add(self, eng, meth, kw, r, w, dma):
        i = len(self.ins)
        deps = set()
        for k in list(r) + list(w):
            if k.startswith("ps"):
                pl = self.ps_last.setdefault(k, {})
                for e2, i2 in pl.items():
                    if e2 != eng:
                        deps.add(i2)
                pl[eng] = i
        for k in r:
            lw = self.last_w.get(k)
            if lw is not None:
                deps.add(lw)
        for k in w:
            lw = self.last_w.get(k)
            if lw is not None:
                deps.add(lw)
            deps.update(self.rd_eng.get(k, {}).values())
            deps.update(self.rd_dma.get(k, ()))
        if self.bar[eng]:
            deps.update(self.bar[eng])
            self.bar[eng] = []
        self.ins.append(dict(eng=eng, meth=meth, kw=kw, deps=deps, dma=dma))
        for k in r:
            if dma:
                self.rd_dma.setdefault(k, []).append(i)
            else:
                self.rd_eng.setdefault(k, {})[eng] = i
        for k in w:
            self.last_w[k] = i
            self.rd_eng[k] = {}
            self.rd_dma[k] = []
        if not dma:
            self.last_on[eng] = i
        for d in deps:
            self.open_dma.discard(d)
        if dma:
            self.open_dma.add(i)
        return i

    def op(self, eng, meth, r=(), w=(), **kw):
        return self._add(eng, meth, kw, r, w, False)

    def dma(self, q, out, in_, r=(), w=(), **kw):
        kw = dict(kw, out=out, in_=in_)
        return self._add(q, 'dma_start', kw, r, w, True)

    def barrier(self):
        deps = [i for i in self.last_on.values() if i is not None] + list(self.open_dma)
        for e in self.eng:
            self.bar[e] = list(deps)
        self.last_w.clear()
        self.rd_eng.clear()
        self.rd_dma.clear()
        self.ps_last.clear()

    def finish(self):
        self.barrier()
        self._add('sp', None, {}, (), (), False)

    def emit(self):
        nc = self.nc
        ins = self.ins
        need = [False] * len(ins)
        for i, I in enumerate(ins):
            for d in I['deps']:
                Dd = ins[d]
                if Dd['dma']:
                    continue
                if Dd['eng'] == 'pe' and I['eng'] == 'pe' and not I['dma']:
                    continue
                need[d] = True
        esem = {e: nc.alloc_semaphore("sem_" + e) for e in self.eng}
        npool = {'sp': 28, 'pool': 12, 'act': 2, 'pe': 2, 'dve': 2}
        dsem = {q: [nc.alloc_semaphore("dsem_%s%d" % (q, j)) for j in range(npool[q])] for q in ('sp', 'pool')}
        cnt = {e: 0 for e in self.eng}
        ndq = {'sp': 0, 'pool': 0}
        nd = 0
        for i, I in enumerate(ins):
            if I['dma']:
                q = I['eng']
                I['sem'] = dsem[q][ndq[q] % npool[q]]
                I['val'] = 16 * (ndq[q] // npool[q] + 1)
                ndq[q] += 1
                nd += 1
            elif need[i]:
                cnt[I['eng']] += 1
                I['sem'] = esem[I['eng']]
                I['val'] = cnt[I['eng']]
        known = {e: {} for e in self.eng}
        nwait = 0
        for i, I in enumerate(ins):
            e = I['eng']
            E = self.eng[e]
            waits = {}
            for d in I['deps']:
                Dd = ins[d]
                if (not Dd['dma']) and Dd['eng'] == 'pe' and e == 'pe' and not I['dma']:
                    continue
                s, v = Dd['sem'], Dd['val']
                if waits.get(id(s), (None, 0))[1] < v:
                    waits[id(s)] = (s, v)
            if I['dma'] and I['val'] > 16:
                s = I['sem']
                if waits.get(id(s), (None, 0))[1] < I['val'] - 16:
                    waits[id(s)] = (s, I['val'] - 16)
            for sid, (s, v) in waits.items():
                if known[e].get(sid, 0) >= v:
                    continue
                E.wait_ge(s, v)
                nwait += 1
                known[e][sid] = v
            if I['meth'] is None:
                continue
            inst = getattr(E, I['meth'])(**I['kw'])
            if I['dma']:
                inst.then_inc(I['sem'], 16)
            elif need[i]:
                inst.then_inc(I['sem'], 1)
        self.stats = dict(n=len(ins), nwait=nwait, nsig=sum(need), ndma=nd, cnt=cnt)


class StopBuild(Exception):
    pass


class Mem:
    LO = 16512
    HI = 229344

    def __init__(self, nc):
        self.nc = nc
        self.top = Mem.LO
        self.limit = Mem.HI
        self.n = 0

    def region(self, lo, hi):
        self.top = lo
        self.limit = hi

    def alloc(self, name, shape, dtype):
        nbytes = int(np.prod(shape[1:])) * (4 if dtype == F32 else 2)
        nbytes = (nbytes + 63) // 64 * 64
        off = self.top
        assert off + nbytes <= self.limit, (name, off, nbytes, self.limit)
        self.top += nbytes
        self.n += 1
        return self.nc.alloc_sbuf_tensor_at("%s_%d" % (name, self.n), list(shape), dtype, offset=off)

    def mark(self):
        return self.top

    def release(self, m):
        self.top = m


def tiles_for(ntok_lat=SEQ, with_ctx=True, b=0, tw=512):
    out = []
    for t0 in range(0, ntok_lat, tw):
        out.append((t0, tw, b))
    if with_ctx:
        out.append((SEQ, CTX, 2))
    return out


class Builder:
    def __init__(self, cfg):
        self.cfg = cfg
        self.nc = nc = bass.Bass("TRN2", target_bir_lowering=False)
        self.S = Sched(nc)
        self.M = Mem(nc)
        self.dr = {}
        self.dbg_outs = []
        self.pp = [nc.alloc_psum_tensor("pp%d" % i, [128, 1024], F32) for i in range(4)]
        self.ps = [self.pp[i // 2][:, (i % 2) * 512:(i % 2 + 1) * 512] for i in range(8)]
        v7 = self.ps[7].bitcast(BF16)
        self.psb = [v7[:, 0:512], v7[:, 512:1024]]
        self.ps_i = 0

    def din(self, name, shape, dtype=F32):
        t = self.nc.dram_tensor(name, list(shape), dtype, kind="ExternalInput").ap()
        self.dr[name] = t
        return t

    def dout(self, name, shape, dtype=F32):
        t = self.nc.dram_tensor(name, list(shape), dtype, kind="ExternalOutput").ap()
        self.dr[name] = t
        return t

    def psum(self, lo=0, hi=6):
        n = hi - lo
        i = lo + (self.ps_i % n)
        self.ps_i += 1
        return self.ps[i], "ps%d" % i

    def dump(self, name, ap, shape, rkeys):
        if not self.cfg.get('debug'):
            return
        t = self.dout("dbg_" + name, shape, ap.dtype)
        self.S.dma('sp', out=t, in_=ap, r=rkeys, w=["dbgout_" + name])
        self.dbg_outs.append("dbg_" + name)

    def modulate(self, xT, xkey, g, s, outT, okey, tiles, sc):
        S = self.S
        ones = self.ones_bf
        for (t0, tw, cv) in tiles:
            xk = xkey + ":%d" % t0
            S.op('pool', 'tensor_tensor', r=[xk], w=["sq"], out=sc['sq'][:, :, :tw], in0=xT[:, :, t0:t0 + tw], in1=xT[:, :, t0:t0 + tw],
                 op=ALU.mult)
            ps, pk = self.psum()
            for k in range(8):
                S.op('pe', 'matmul', r=["sq", "consts"], w=[pk], out=ps[:, :tw], lhsT=ones[:, :], rhs=sc['sq'][:, k, :tw],
                     start=(k == 0), stop=(k == 7))
            S.op('act', 'activation', r=[pk], w=["lnt"], out=sc['lnt'][:, :tw], in_=ps[:, :tw], func=AF.Ln,
                 scale=1.0 / D, bias=self.eps_t[:, 0:1])
            pr, prk = self.psum()
            S.op('act', 'activation', r=["lnt"], w=[prk], out=pr[:, :tw], in_=sc['lnt'][:, :tw], func=AF.Exp, scale=-0.5)
            for k in range(8):
                S.op('dve', 'tensor_tensor', r=[xk, prk], w=["mtmp%d" % k], out=sc['mtmp'][:, k, :tw], in0=xT[:, k, t0:t0 + tw],
                     in1=pr[:, :tw], op=ALU.mult)
                S.op('act', 'activation', r=["mtmp%d" % k, "mod"], w=[okey + ":%d" % t0],
                     out=outT[:, k, t0:t0 + tw], in_=sc['mtmp'][:, k, :tw], func=AF.Identity,
                     bias=s[:, k, cv:cv + 1], scale=g[:, k, cv:cv + 1])

    def mod_group(self, l, j, bufs, rowt, banks=(0, 6)):
        S = self.S
        for half in range(2):
            gi = self.mod_gi
            self.mod_gi += 1
            buf, bk = bufs[gi % 2], "awb%d" % (gi % 2)
            rt, rtk = rowt[gi % len(rowt)], "rowt%d" % (gi % len(rowt))
            c0 = j * 1024 + half * 512
            S.dma('pool', out=buf[:, :, :], in_=self.adaw_d[l][:, :, c0:c0 + 512], w=[bk])
            ps, pk = self.psum(*banks)
            for k in range(8):
                S.op('pe', 'matmul', r=[bk, "scv"], w=[pk], out=ps[0:4, :], lhsT=self.scvb[:, k, :], rhs=buf[:, k, :],
                     start=(k == 0), stop=(k == 7))
            S.op('act', 'activation', r=[pk], w=[rtk], out=rt[0:4, :], in_=ps[0:4, :], func=AF.Copy)
            pt_, ptk_ = self.psum(*banks)
            for blk in range(4):
                S.op('pe', 'transpose', r=[rtk, "consts"], w=[ptk_], out=pt_[:, blk * 4:(blk + 1) * 4], in_=rt[0:4, blk * 128:(blk + 1) * 128],
                     identity=self.ident_f[0:4, 0:4])
            f0 = j * 8 + half * 4
            for cv in range(4):
                S.op('dve', 'tensor_tensor', r=[ptk_, "nm"], w=["mod"], out=self.mod[:, l, f0:f0 + 4, cv],
                     in0=pt_[:, 0:16].rearrange("p (f c) -> p f c", c=4)[:, :, cv], in1=self.adab[:, l, f0:f0 + 4], op=ALU.add)
        if j == 1:
            for cv in range(4):
                S.op('dve', 'scalar_tensor_tensor', r=["mod", "nm"], w=["mod"], out=self.gmix[:, l, :, cv],
                     in0=self.mod[:, l, 8:16, cv], scalar=1.0, in1=self.nmix_t[:, l, :], op0=ALU.add, op1=ALU.mult)
        if j == 4:
            for cv in range(4):
                S.op('dve', 'scalar_tensor_tensor', r=["mod", "nm"], w=["mod"], out=self.gmlp[:, l, :, cv],
                     in0=self.mod[:, l, 32:40, cv], scalar=1.0, in1=self.nmlp_t[:, l, :], op0=ALU.add, op1=ALU.mult)

    def build(self):
        nc, S, M, cfg = self.nc, self.S, self.M, self.cfg
        layers = cfg.get('layers', [0, 1])
        nb = cfg.get('nb', NB)
        xT_d = self.din("xT", [NB, 128, 8, SEQ])
        cT_d = self.din("ctxT", [NB, 128, 8, CTX])
        cv_d = self.din("cvT", [128, 8, 4])
        adaw_d = [self.din("adaw%d" % l, [128, 8, 6 * D]) for l in range(2)]
        adab_d = self.din("adab", [128, 2, 48])
        nmix_d = self.din("nmix", [128, 2, 8])
        nmlp_d = self.din("nmlp", [128, 2, 8])
        wout_d = [self.din("wout%d" % l, [128, 8, D]) for l in range(2)]
        w1_d = [self.din("w1_%d" % l, [128, 8, 4 * D]) for l in range(2)]
        w2_d = [self.din("w2_%d" % l, [128, 32, D]) for l in range(2)]
        ident_d = self.din("ident", [128, 128])
        outT_d = self.dout("outT", [NB, 128, 8, SEQ])
        xs_d = self.nc.dram_tensor("xs", [128, 8, TOK], F32, kind="Internal").ap()
        if cfg.get('inject_y'):
            yinj_d = [self.din("yinj%d" % l, [NB, 128, 8, TOK]) for l in range(2)]
        self.declare_mixer_dram()

        self.ident_bf = M.alloc("ident_bf", [128, 128], BF16)
        self.ident_f = M.alloc("ident_f", [128, 128], F32)
        self.ones_bf = M.alloc("ones_bf", [128, 128], BF16)
        self.eps_t = M.alloc("eps_t", [128, 1], F32)
        self.mod = M.alloc("mod", [128, 2, 48, 4], F32)
        self.gmix = M.alloc("gmix", [128, 2, 8, 4], F32)
        self.gmlp = M.alloc("gmlp", [128, 2, 8, 4], F32)
        nmix = M.alloc("nmix", [128, 2, 8], F32)
        nmlp = M.alloc("nmlp", [128, 2, 8], F32)
        adab = M.alloc("adab", [128, 2, 48], F32)
        self.alloc_mixer_persistent()
        base = M.mark()

        S.dma('sp', out=self.ident_f[:, :], in_=ident_d[:, :], w=["consts"])
        S.dma('pool', out=self.ident_bf[:, :], in_=ident_d[:, :], w=["consts"])
        S.op('dve', 'memset', w=["consts"], ap=self.ones_bf[:, :], constant=1.0)
        S.op('dve', 'memset', w=["consts"], ap=self.eps_t[:, :], constant=EPS)
        S.dma('sp', out=nmix[:, :, :], in_=nmix_d[:, :, :], w=["nm"])
        S.dma('sp', out=nmlp[:, :, :], in_=nmlp_d[:, :, :], w=["nm"])
        S.dma('sp', out=adab[:, :, :], in_=adab_d[:, :, :], w=["nm"])
        self.load_mixer_persistent()

        cvs = M.alloc("cvs", [128, 8, 4], F32)
        self.scvb = M.alloc("scvb", [128, 8, 4], BF16)
        self.adab = adab
        self.nmix_t, self.nmlp_t = nmix, nmlp
        self.adaw_d = adaw_d
        S.dma('sp', out=cvs[:, :, :], in_=cv_d[:, :, :], w=["cvs"])
        S.op('act', 'activation', r=["cvs"], w=["scv"], out=self.scvb[:, :, :], in_=cvs[:, :, :], func=AF.Silu)
        base = M.mark()
        bufs = [M.alloc("awb%d" % i, [128, 8, 512], BF16) for i in range(2)]
        rowt = [M.alloc("rowt%d" % i, [4, 512], F32) for i in range(2)]
        self.mod_gi = 0
        for j in range(2):
            self.mod_group(0, j, bufs, rowt)
        self.mod_todo = [(0, j) for j in range(2, 6)] + [(1, j) for j in range(6)]
        if cfg.get('mod_all') or 0 not in layers:
            while self.mod_todo:
                self.mod_group(*self.mod_todo.pop(0), bufs, rowt)
        self.dump("mod", self.mod[:, :, :, :], [128, 2, 48, 4], ["mod"])
        S.barrier()
        M.release(base)

        self.offB = M.mark()
        hT = M.alloc("hT", [128, 8, TOK], BF16)
        self.offA = M.mark()
        xT = M.alloc("xT", [128, 8, TOK], F32)
        self.offC = M.mark()
        yT = M.alloc("yT", [128, 8, TOK], BF16)
        regD = M.mark()
        self.offD = regD

        for bi in range(nb):
            for t0 in range(0, SEQ, 512):
                S.dma('sp', out=xT[:, :, t0:t0 + 512], in_=xT_d[bi, :, :, t0:t0 + 512], w=["xT:%d" % t0])
            S.dma('sp', out=xT[:, :, SEQ:TOK], in_=cT_d[bi, :, :, :], w=["xT:%d" % SEQ])
            for l in layers:
                last = (l == 1)
                m0 = M.mark()
                sc = dict(sq=M.alloc("sq", [128, 8, 512], BF16), lnt=M.alloc("lnt", [128, 512], F32),
                          rstd=M.alloc("rstd", [128, 512], F32), mtmp=M.alloc("mtmp", [128, 8, 512], F32))
                self.modulate(xT, "xT", self.gmix[:, l], self.mod[:, l, 0:8], hT, "hT", tiles_for(b=bi), sc)
                if cfg.get('debug') and bi == 0:
                    self.dump("hT%d" % l, hT[:, :, :], [128, 8, TOK], ["hT:%d" % t for t in range(0, TOK, 512)])
                for (t0, tw, cv) in tiles_for(b=bi):
                    S.dma('sp', out=xs_d[:, :, t0:t0 + tw], in_=xT[:, :, t0:t0 + tw], r=["xT:%d" % t0], w=["xs:%d" % t0])
                S.barrier()
                M.release(m0)
                if cfg.get('inject_y'):
                    for k in range(8):
                        S.dma('pool', out=yT[:, k, :], in_=yinj_d[l][bi, :, k, :], w=["yT:%d" % t for t in range(0, TOK, 512)])
                else:
                    try:
                        if l == 0:
                            self.even_mixer(bi, hT, yT, xT, regD)
                        else:
                            self.odd_mixer(bi, hT, yT, xT, regD)
                    except StopBuild:
                        pass
                S.barrier()
                M.region(regD, Mem.HI)
                if cfg.get('debug') and bi == 0:
                    self.dump("yT%d" % l, yT[:, :, :], [128, 8, TOK], ["yT:%d" % t for t in range(0, TOK, 512)])
                for (t0, tw, cv) in tiles_for(b=bi, with_ctx=not last):
                    S.dma('sp', out=xT[:, :, t0:t0 + tw], in_=xs_d[:, :, t0:t0 + tw], r=["xs:%d" % t0], w=["xT:%d" % t0])
                m0 = M.mark()
                wo = M.alloc("wo", [128, 8, D], BF16)
                sc = dict(sq=M.alloc("sq", [128, 8, 512], BF16), lnt=M.alloc("lnt", [128, 512], F32),
                          mtmp=M.alloc("mtmp", [128, 8, 512], F32))
                assert M.top <= Mem.HI - 8 * D * 2
                M.region(Mem.HI - 8 * D * 2, Mem.HI)
                w1pre = M.alloc("w1pre", [128, 8, D], BF16)
                M.region(m0, Mem.HI)
                S.dma('pool', out=wo[:, :, :], in_=wout_d[l][:, :, :], w=["wo"])
                S.dma('pool', out=w1pre[:, :, :], in_=w1_d[l][:, :, 0:D], w=["w1g0"])
                tl = tiles_for(b=bi, with_ctx=not last)

                def wout_tile(t0, tw, cv):
                    for fo in range(8):
                        ps, pk = self.psum()
                        for k in range(8):
                            S.op('pe', 'matmul', r=["wo", "yT:%d" % t0], w=[pk], out=ps[:, :tw],
                                 lhsT=wo[:, k, fo * 128:(fo + 1) * 128], rhs=yT[:, k, t0:t0 + tw],
                                 start=(k == 0), stop=(k == 7))
                        S.op('dve', 'scalar_tensor_tensor', r=[pk, "mod"], w=["xT:%d" % t0],
                             out=xT[:, fo, t0:t0 + tw], in0=ps[:, :tw], scalar=self.mod[:, l, 16 + fo, cv:cv + 1],
                             in1=xT[:, fo, t0:t0 + tw], op0=ALU.mult, op1=ALU.add)

                wout_tile(*tl[0])
                for ti, (t0, tw, cv) in enumerate(tl):
                    if ti + 1 < len(tl):
                        wout_tile(*tl[ti + 1])
                    self.modulate(xT, "xT", self.gmlp[:, l], self.mod[:, l, 24:32], hT, "hT", [(t0, tw, cv)], sc)
                if cfg.get('debug') and bi == 0:
                    self.dump("xmid%d" % l, xT[:, :, :], [128, 8, TOK], ["xT:%d" % t for t in range(0, TOK, 512)])
                S.barrier()
                M.release(m0)
                M.release(M.mark())
                M.region(self.offC, Mem.HI)
                m0 = M.mark()
                w1g = [w1pre, M.alloc("w1g1", [128, 8, D], BF16)]
                w2g = [M.alloc("w2g%d" % i, [128, 8, D], BF16) for i in range(2)]
                hid = [M.alloc("hid%d" % i, [128, 8, 512], BF16) for i in range(2)]
                rl = [M.alloc("rl%d" % i, [128, 512], BF16) for i in range(2)]
                hi_ = 0
                for fg in range(4):
                    wb = fg % 2
                    if fg > 0:
                        S.dma('pool', out=w1g[wb][:, :, :], in_=w1_d[l][:, :, fg * D:(fg + 1) * D], w=["w1g%d" % wb])
                    S.dma('pool', out=w2g[wb][:, :, :], in_=w2_d[l][:, fg * 8:(fg + 1) * 8, :], w=["w2g%d" % wb])
                    for (t0, tw, cv) in tl:
                        hb = hi_ % 2
                        hi_ += 1
                        for hc in range(8):
                            ps, pk = self.psum()
                            for k in range(8):
                                S.op('pe', 'matmul', r=["w1g%d" % wb, "hT:%d" % t0], w=[pk], out=ps[:, :tw],
                                     lhsT=w1g[wb][:, k, hc * 128:(hc + 1) * 128], rhs=hT[:, k, t0:t0 + tw],
                                     start=(k == 0), stop=(k == 7))
                            rb = hc % 2
                            S.op('act', 'activation', r=[pk], w=["rl%d" % rb], out=rl[rb][:, :tw], in_=ps[:, :tw],
                                 func=AF.Relu)
                            S.op('pool', 'tensor_tensor', r=["rl%d" % rb], w=["hid%d:%d" % (hb, hc)],
                                 out=hid[hb][:, hc, :tw], in0=rl[rb][:, :tw], in1=rl[rb][:, :tw], op=ALU.mult)
                        for fo in range(8):
                            ps, pk = self.psum()
                            for hc in range(8):
                                S.op('pe', 'matmul', r=["w2g%d" % wb, "hid%d:%d" % (hb, hc)], w=[pk], out=ps[:, :tw],
                                     lhsT=w2g[wb][:, hc, fo * 128:(fo + 1) * 128], rhs=hid[hb][:, hc, :tw],
                                     start=(hc == 0), stop=(hc == 7))
                            S.op('dve', 'scalar_tensor_tensor', r=[pk, "mod"], w=["xT:%d" % t0],
                                 out=xT[:, fo, t0:t0 + tw], in0=ps[:, :tw], scalar=self.mod[:, l, 40 + fo, cv:cv + 1],
                                 in1=xT[:, fo, t0:t0 + tw], op0=ALU.mult, op1=ALU.add)
                if cfg.get('debug') and bi == 0:
                    self.dump("x%d" % l, xT[:, :, :], [128, 8, TOK], ["xT:%d" % t for t in range(0, TOK, 512)])
                S.barrier()
                M.region(regD, Mem.HI)
            for t0 in range(0, SEQ, 512):
                S.dma('sp', out=outT_d[bi, :, :, t0:t0 + 512], in_=xT[:, :, t0:t0 + 512], r=["xT:%d" % t0], w=["outT:%d" % t0])
            S.barrier()
        S.finish()
        S.emit()
        return nc

    def cut(self, n):
        if self.cfg.get('cut') == n:
            raise StopBuild()

    def rot(self, name, shape, dtype, n):
        tiles = [self.M.alloc(name, shape, dtype) for _ in range(n)]
        uid = self.M.n
        st = {'i': 0}

        def nxt():
            i = st['i'] % n
            st['i'] += 1
            return tiles[i], "%s@%d#%d" % (name, uid, i)
        return nxt

    def rstd_of(self, ps_ap, pk, rows, tw, inv_n, nx_ln, nx_rs):
        S = self.S
        lnt, lk = nx_ln()
        rs, rk = nx_rs()
        S.op('act', 'activation', r=[pk], w=[lk], out=lnt[0:rows, :tw], in_=ps_ap, func=AF.Ln, scale=inv_n,
             bias=self.eps_t[0:rows, 0:1])
        S.op('act', 'activation', r=[lk], w=[rk], out=rs[0:rows, :tw], in_=lnt[0:rows, :tw], func=AF.Exp, scale=-0.5)
        return rs, rk

    def declare_mixer_dram(self):
        self.evwin_d = self.din("evwin", [128, 8, 1440])
        self.wuq_d = self.din("wuq", [128, 2, 768])
        self.wukvn_d = self.din("wukvn", [128, 8, 64])
        self.wukvv_d = self.din("wukvv", [128, 512])
        self.evvec_d = self.din("evvec", [128, 16])
        self.lruvec_d = self.din("lruvec", [128, 4, 12])
        self.lruw_d = self.din("lruw", [128, 4, 2, 2, 128])
        self.ropecs_d = self.din("ropecs", [128, 2, TOK])
        self.shk_d = self.din("shk", [128, 96])
        self.convwb_d = self.din("convwb", [128, 4, 512])
        self.odwin_d = self.din("odwin", [128, 8, 3104])
        self.glaw_d = self.din("glaw", [64, 2, 256])
        self.glau_d = self.din("glau", [128, 6, 128])
        self.odvec_d = self.din("odvec", [128, 4])
        self.tbig_d = self.din("tbig", [128, 8, 16, 64])

    def alloc_mixer_persistent(self):
        M = self.M
        self.evvec = M.alloc("evvec", [128, 16], F32)
        self.lruvec = M.alloc("lruvec", [128, 4, 12], F32)
        self.nsp = M.alloc("nsp", [128, 4, 2], F32)
        self.one_t = M.alloc("one_t", [128, 1], F32)
        self.spt = [M.alloc("spt%d" % i, [128, 4, 2], F32) for i in range(3)]
        self.odvec = M.alloc("odvec", [128, 4], F32)

    def load_mixer_persistent(self):
        S = self.S
        S.dma('sp', out=self.evvec[:, :], in_=self.evvec_d[:, :], w=["evvec"])
        S.dma('sp', out=self.lruvec[:, :, :], in_=self.lruvec_d[:, :, :], w=["lruvec"])
        S.op('dve', 'memset', w=["consts"], ap=self.one_t[:, :], constant=1.0)
        S.dma('sp', out=self.odvec[:, :], in_=self.odvec_d[:, :], w=["odvec"])
        e, t, u = self.spt
        S.op('act', 'activation', r=["lruvec"], w=["spe"], out=e[:, :, :], in_=self.lruvec[:, :, 9:11], func=AF.Exp, scale=-1.0)
        S.op('dve', 'tensor_scalar', r=["spe"], w=["spt"], out=t[:, :, :], in0=e[:, :, :], scalar1=-0.25, scalar2=1.0 / 3.0,
             op0=ALU.mult, op1=ALU.add)
        S.op('dve', 'tensor_tensor', r=["spe", "spt"], w=["spu"], out=u[:, :, :], in0=t[:, :, :], in1=e[:, :, :], op=ALU.mult)
        S.op('dve', 'tensor_scalar', r=["spu"], w=["spt"], out=t[:, :, :], in0=u[:, :, :], scalar1=-1.0, scalar2=0.5,
             op0=ALU.mult, op1=ALU.add)
        S.op('dve', 'tensor_tensor', r=["spe", "spt"], w=["spu"], out=u[:, :, :], in0=t[:, :, :], in1=e[:, :, :], op=ALU.mult)
        S.op('dve', 'tensor_scalar', r=["spu"], w=["spt"], out=t[:, :, :], in0=u[:, :, :], scalar1=8.0, scalar2=-8.0,
             op0=ALU.mult, op1=ALU.add)
        S.op('dve', 'tensor_tensor', r=["spe", "spt"], w=["nsp"], out=self.nsp[:, :, :], in0=t[:, :, :], in1=e[:, :, :], op=ALU.mult)

    def even_mixer(self, bi, hT, yT, xT, regD):
        nc, S, M, cfg = self.nc, self.S, self.M, self.cfg
        ev = self.evvec
        tl = tiles_for(b=bi)
        TK = ["%d" % t for t in range(0, TOK, 512)]
        M.region(self.offD, Mem.HI)
        qn = M.alloc("qn", [128, 2, TOK], BF16)
        kvn = M.alloc("kvn", [128, TOK], BF16)
        krb = M.alloc("krb", [128, TOK], BF16)
        sqkr = M.alloc("sqkr", [128, TOK], BF16)
        css = M.alloc("css", [128, 2, TOK], BF16)
        D1 = M.mark()
        winA = M.alloc("winA", [128, 8, 416], BF16)
        winR = M.alloc("winR", [128, 8, 32], BF16)
        nx_uqr = self.rot("uqr", [128, 2, 512], F32, 1)
        nx_sq2 = self.rot("sq2", [128, 2, 512], BF16, 1)
        nx_ln = self.rot("lnt", [128, 512], F32, 2)
        nx_rs = self.rot("rstd", [128, 512], F32, 2)
        nx_t = self.rot("tt", [128, 512], F32, 3)
        M.region(self.offA, self.offC)
        csf = M.alloc("csf", [128, 2, TOK], F32)
        M.region(D1, Mem.HI)
        for j in range(2):
            S.dma('sp', out=csf[:, j, :], in_=self.ropecs_d[:, j, :], w=["csf%d" % j])
            S.op('dve', 'tensor_scalar', r=["csf%d" % j, "evvec"], w=["css"], out=css[0:96, j, :], in0=csf[0:96, j, :],
                 scalar1=ev[0:96, 5 + j:6 + j], scalar2=None, op0=ALU.mult)
        S.dma('pool', out=winA[:, :, :], in_=self.evwin_d[:, :, 0:416], w=["winA"])
        for hb in range(2):
            o = hb * 16
            S.op('dve', 'tensor_scalar', r=["winA"], w=["winR"], out=winR[:, :, o:o + 8], in0=winA[:, :, 384 + o + 8:384 + o + 16],
                 scalar1=-1.0, scalar2=None, op0=ALU.mult)
            S.op('dve', 'tensor_copy', r=["winA"], w=["winR"], out=winR[:, :, o + 8:o + 16], in_=winA[:, :, 384 + o:384 + o + 8])
        self.cut(1)
        for (t0, tw, cv) in tl:
            hk = "hT:%d" % t0
            uqr, uk = nx_uqr()
            sq2, sk = nx_sq2()
            for c in range(2):
                ps, pk = self.psum()
                for k in range(8):
                    S.op('pe', 'matmul', r=["winA", hk], w=[pk], out=ps[:, :tw], lhsT=winA[:, k, c * 128:(c + 1) * 128],
                         rhs=hT[:, k, t0:t0 + tw], start=(k == 0), stop=(k == 7))
                S.op('act', 'activation', r=[pk], w=[sk + ":%d" % c], out=sq2[:, c, :tw], in_=ps[:, :tw], func=AF.Square)
                S.op('dve', 'tensor_copy', r=[pk], w=[uk + ":%d" % c], out=uqr[:, c, :tw], in_=ps[:, :tw])
            ps, pk = self.psum()
            for c in range(2):
                S.op('pe', 'matmul', r=[sk + ":%d" % c, "consts"], w=[pk], out=ps[:, :tw], lhsT=self.ones_bf[:, :],
                     rhs=sq2[:, c, :tw], start=(c == 0), stop=(c == 1))
            rs, rk = self.rstd_of(ps[:, :tw], pk, 128, tw, 1.0 / 256, nx_ln, nx_rs)
            for c in range(2):
                S.op('dve', 'scalar_tensor_tensor', r=[uk + ":%d" % c, rk, "evvec"], w=["qn:%d" % t0], out=qn[:, c, t0:t0 + tw],
                     in0=uqr[:, c, :tw], scalar=ev[:, c:c + 1], in1=rs[:, :tw], op0=ALU.mult, op1=ALU.mult)
            self.cut(2)
            ps, pk = self.psum()
            for k in range(8):
                S.op('pe', 'matmul', r=["winA", hk], w=[pk], out=ps[:, :tw], lhsT=winA[:, k, 256:384],
                     rhs=hT[:, k, t0:t0 + tw], start=(k == 0), stop=(k == 7))
            sq1, s1k = nx_sq2()
            ur, u1k = nx_t()
            S.op('act', 'activation', r=[pk], w=[s1k + ":0"], out=sq1[:, 0, :tw], in_=ps[:, :tw], func=AF.Square)
            S.op('dve', 'tensor_copy', r=[pk], w=[u1k], out=ur[:, :tw], in_=ps[:, :tw])
            ps2, pk2 = self.psum()
            S.op('pe', 'matmul', r=[s1k + ":0", "consts"], w=[pk2], out=ps2[:, :tw], lhsT=self.ones_bf[:, :], rhs=sq1[:, 0, :tw],
                 start=True, stop=True)
            rs, rk = self.rstd_of(ps2[:, :tw], pk2, 128, tw, 1.0 / 128, nx_ln, nx_rs)
            S.op('dve', 'scalar_tensor_tensor', r=[u1k, rk, "evvec"], w=["kvn:%d" % t0], out=kvn[:, t0:t0 + tw],
                 in0=ur[:, :tw], scalar=ev[:, 2:3], in1=rs[:, :tw], op0=ALU.mult, op1=ALU.mult)
            self.cut(3)
            psA, pkA = self.psum()
            for k in range(8):
                S.op('pe', 'matmul', r=["winA", hk], w=[pkA], out=psA[0:32, :tw], lhsT=winA[:, k, 384:416],
                     rhs=hT[:, k, t0:t0 + tw], start=(k == 0), stop=(k == 7))
            psB, pkB = self.psum()
            for k in range(8):
                S.op('pe', 'matmul', r=["winR", hk], w=[pkB], out=psB[0:32, :tw], lhsT=winR[:, k, :],
                     rhs=hT[:, k, t0:t0 + tw], start=(k == 0), stop=(k == 7))
            S.op('act', 'activation', r=[pkA], w=["sqkr:%d" % t0], out=sqkr[0:32, t0:t0 + tw], in_=psA[0:32, :tw], func=AF.Square)
            t1, t1k = nx_t()
            t2, t2k = nx_t()
            S.op('dve', 'tensor_tensor', r=[pkA, "css"], w=[t1k], out=t1[0:32, :tw], in0=psA[0:32, :tw],
                 in1=css[0:32, 0, t0:t0 + tw], op=ALU.mult)
            S.op('dve', 'tensor_tensor', r=[pkB, "css"], w=[t2k], out=t2[0:32, :tw], in0=psB[0:32, :tw],
                 in1=css[0:32, 1, t0:t0 + tw], op=ALU.mult)
            S.op('pool', 'tensor_tensor', r=[t1k, t2k], w=["krb:%d" % t0], out=krb[0:32, t0:t0 + tw], in0=t1[0:32, :tw],
                 in1=t2[0:32, :tw], op=ALU.add)
        self.cut(4)
        if cfg.get('debug') and bi == 0:
            self.dump("qn", qn[:, :, :], [128, 2, TOK], ["qn:" + t for t in TK])
            self.dump("kvn", kvn[:, :], [128, TOK], ["kvn:" + t for t in TK])
            self.dump("krb", krb[0:32, :], [32, TOK], ["krb:" + t for t in TK])
        S.barrier()
        mbufs = None
        if self.mod_todo:
            keep = (M.top, M.limit)
            M.region(self.offC, self.offC + 4 * TOK * 2)
            mbufs = ([M.alloc("awb%d" % i, [128, 8, 512], BF16) for i in range(2)], [M.alloc("rowt%d" % i, [4, 512], F32) for i in range(1)])
            M.region(*keep)
            for _ in range(3):
                self.mod_group(*self.mod_todo.pop(0), *mbufs)
        if cfg.get('even_stop') == 1:
            return
        M.region(D1, Mem.HI)
        wug = M.alloc("wug", [128, 8, 512], BF16)
        lw = M.alloc("lruw", [128, 4, 2, 2, 128], BF16)
        cwb = M.alloc("cwb", [128, 4, 512], BF16)
        wux = [M.alloc("wux%d" % i, [128, 8, 128], BF16) for i in range(2)]
        S.dma('pool', out=wug[:, :, :], in_=self.evwin_d[:, :, 928:1440], w=["wug"])
        S.dma('pool', out=lw[:, :, :, :, :], in_=self.lruw_d[:, :, :, :, :], w=["lruw"])
        S.dma('pool', out=cwb[:, :, :], in_=self.convwb_d[:, :, :], w=["cwb"])
        M.region(self.offA, self.offC)
        xl = M.alloc("xl", [128, TOK], F32)
        xlb = M.alloc("xlb", [128, TOK], BF16)
        at = M.alloc("at", [128, TOK], F32)
        bx = M.alloc("bx", [128, TOK], F32)
        hf = M.alloc("hf", [128, TOK], F32)
        hb_ = M.alloc("hb", [128, TOK], F32)
        gel = M.alloc("gel", [128, TOK], BF16)
        wj = M.alloc("wj", [128, 4, 8, 128], BF16)
        lv = self.lruvec
        for c in range(4):
            wx, wxk = wux[c % 2], "wux%d" % (c % 2)
            S.dma('pool', out=wx[:, :, :], in_=self.evwin_d[:, :, 416 + c * 128:416 + (c + 1) * 128], w=[wxk])
            for j in range(4):
                for k in range(8):
                    S.op('pool', 'tensor_tensor', r=[wxk, "cwb"], w=["wj"], out=wj[:, j, k, :], in0=wx[:, k, :],
                         in1=cwb[:, j, c * 128:(c + 1) * 128], op=ALU.mult)
            for (t0, tw, cv) in tl:
                s0, e0 = (0, SEQ) if t0 < SEQ else (SEQ, TOK)
                ps, pk = self.psum()
                mm = []
                for j in (2, 1, 0, 3):
                    sft = j - 2
                    lo = max(t0, s0 - sft)
                    hi = min(t0 + tw, e0 - sft)
                    for k in range(8):
                        mm.append((j, k, lo, hi, sft))
                for n, (j, k, lo, hi, sft) in enumerate(mm):
                    S.op('pe', 'matmul', r=["wj"] + ["hT:%d" % t for t in range(0, TOK, 512)], w=[pk], out=ps[:, lo - t0:hi - t0],
                         lhsT=wj[:, j, k, :], rhs=hT[:, k, lo + sft:hi + sft], start=(n == 0), stop=(n == len(mm) - 1))
                S.op('act', 'activation', r=[pk, "lruvec"], w=["xl"], out=xl[:, t0:t0 + tw], in_=ps[:, :tw], func=AF.Identity,
                     bias=lv[:, c, 4:5], scale=1.0)
            for (t0, tw, cv) in tl:
                ps, pk = self.psum()
                for k in range(8):
                    S.op('pe', 'matmul', r=["wug", "hT:%d" % t0], w=[pk], out=ps[:, :tw], lhsT=wug[:, k, c * 128:(c + 1) * 128],
                         rhs=hT[:, k, t0:t0 + tw], start=(k == 0), stop=(k == 7))
                S.op('act', 'activation', r=[pk], w=["gel"], out=gel[:, t0:t0 + tw], in_=ps[:, :tw], func=AF.Gelu_apprx_tanh)
            S.op('pool', 'tensor_copy', r=["xl"], w=["xlb:%d" % t for t in range(0, TOK, 512)], out=xlb[:, :], in_=xl[:, :])
            for d in range(2):
                hx, hxk = (hf, "hf") if d == 0 else (hb_, "hb")
                for (t0, tw, cv) in tl:
                    psa, pka = self.psum()
                    S.op('pe', 'matmul', r=["lruw", "xlb:%d" % t0], w=[pka], out=psa[:, :tw], lhsT=lw[:, c, d, 0, :], rhs=xlb[:, t0:t0 + tw],
                         start=True, stop=True)
                    S.op('act', 'activation', r=[pka, "lruvec"], w=["at"], out=at[:, t0:t0 + tw], in_=psa[:, :tw], func=AF.Sigmoid,
                         bias=lv[:, c, 5 + d:6 + d], scale=1.0)
                    psx, pkx = self.psum()
                    S.op('pe', 'matmul', r=["lruw", "xlb:%d" % t0], w=[pkx], out=psx[:, :tw], lhsT=lw[:, c, d, 1, :], rhs=xlb[:, t0:t0 + tw],
                         start=True, stop=True)
                    S.op('act', 'activation', r=[pkx, "lruvec"], w=["bx"], out=bx[:, t0:t0 + tw], in_=psx[:, :tw], func=AF.Sigmoid,
                         bias=lv[:, c, 7 + d:8 + d], scale=1.0)
                S.op('act', 'activation', r=["at", "nsp"], w=["at"], out=at[:, :], in_=at[:, :], func=AF.Exp, scale=self.nsp[:, c, d:d + 1])
                S.op('dve', 'tensor_tensor', r=["at"], w=[hxk], out=hx[:, :], in0=at[:, :], in1=at[:, :], op=ALU.mult)
                S.op('act', 'activation', r=[hxk], w=[hxk], out=hx[:, :], in_=hx[:, :], func=AF.Sqrt, scale=-1.0, bias=self.one_t[:, 0:1])
                S.op('dve', 'tensor_tensor', r=["bx", "xl"], w=["bx"], out=bx[:, :], in0=bx[:, :], in1=xl[:, :], op=ALU.mult)
                S.op('dve', 'tensor_tensor', r=["bx", hxk], w=["bx"], out=bx[:, :], in0=bx[:, :], in1=hx[:, :], op=ALU.mult)
                if d == 0:
                    S.op('dve', 'tensor_tensor_scan', r=["at", "bx"], w=[hxk], out=hf[:, SEQ:TOK], data0=at[:, SEQ:TOK],
                         data1=bx[:, SEQ:TOK], initial=0.0, op0=ALU.mult, op1=ALU.add)
                    S.op('dve', 'tensor_tensor_scan', r=["at", "bx", hxk], w=[hxk], out=hf[:, 0:SEQ], data0=at[:, 0:SEQ],
                         data1=bx[:, 0:SEQ], initial=hf[:, TOK - 1:TOK], op0=ALU.mult, op1=ALU.add)
                else:
                    S.op('dve', 'tensor_tensor_scan', r=["at", "bx"], w=[hxk], out=hb_[:, SEQ:TOK][:, ::-1],
                         data0=at[:, SEQ:TOK][:, ::-1], data1=bx[:, SEQ:TOK][:, ::-1], initial=0.0, op0=ALU.mult, op1=ALU.add)
                    S.op('dve', 'tensor_tensor_scan', r=["at", "bx", hxk], w=[hxk], out=hb_[:, 0:SEQ][:, ::-1],
                         data0=at[:, 0:SEQ][:, ::-1], data1=bx[:, 0:SEQ][:, ::-1], initial=hb_[:, SEQ:SEQ + 1],
                         op0=ALU.mult, op1=ALU.add)
            S.op('dve', 'tensor_tensor', r=["hf", "hb"], w=["hf"], out=hf[:, :], in0=hf[:, :], in1=hb_[:, :], op=ALU.add)
            S.op('dve', 'tensor_tensor', r=["hf", "gel"], w=["yT:" + t for t in TK], out=yT[:, 4 + c, :], in0=hf[:, :], in1=gel[:, :],
                 op=ALU.mult)
        S.barrier()
        if cfg.get('even_stop') == 2:
            return
        M.region(D1, Mem.HI)
        wuq = M.alloc("wuq", [128, 2, 768], BF16)
        wuqr = M.alloc("wuqr", [128, 2, 8, 96], BF16)
        wkn = M.alloc("wkn", [128, 8, 96], BF16)
        wv = M.alloc("wv", [128, 512], BF16)
        shk = M.alloc("shk", [128, 96], BF16)
        nx_sq = self.rot("sq", [128, 512], BF16, 2)
        nx_ln = self.rot("lnt", [128, 512], F32, 2)
        nx_rs = self.rot("rstd", [128, 512], F32, 2)
        nx_t = self.rot("tt", [128, 512], F32, 2)
        nx_raw = self.rot("qkr", [128, 512], F32, 2)
        nx_rec = self.rot("rec", [128, 1], F32, 4)
        S.dma('pool', out=wuq[:, :, :], in_=self.wuq_d[:, :, :], w=["wuq"])
        S.dma('pool', out=wkn[:, :, 0:64], in_=self.wukvn_d[:, :, :], w=["wkn"])
        S.op('dve', 'memset', w=["wkn"], ap=wkn[:, :, 64:96], constant=0.0)
        S.dma('pool', out=wv[:, :], in_=self.wukvv_d[:, :], w=["wv"])
        S.dma('pool', out=shk[:, :], in_=self.shk_d[:, :], w=["shk"])
        S.op('dve', 'memset', w=["wuqr"], ap=wuqr[:, :, :, 0:64], constant=0.0)
        wuq4 = wuq[:, :, :].rearrange("p c (h e) -> p c h e", e=96)
        for hb in range(2):
            o = 64 + hb * 16
            S.op('dve', 'tensor_scalar', r=["wuq"], w=["wuqr"], out=wuqr[:, :, :, o:o + 8], in0=wuq4[:, :, :, o + 8:o + 16],
                 scalar1=-1.0, scalar2=None, op0=ALU.mult)
            S.op('dve', 'tensor_copy', r=["wuq"], w=["wuqr"], out=wuqr[:, :, :, o + 8:o + 16], in_=wuq4[:, :, :, o:o + 8])
        M.region(self.offB, self.offC)
        PT = [M.alloc("PT%d" % i, [128, 18, 512], BF16) for i in range(2)]
        ytok = M.alloc("ytok", [128, 18, 512], BF16)
        Vx = M.alloc("Vx", [128, 18, 8, 65], BF16)
        QT = [M.alloc("QT%d" % i, [128, TOK], BF16) for i in range(2)]
        KT = [M.alloc("KT%d" % i, [128, TOK], BF16) for i in range(2)]
        KR = 128 if cfg.get('mla_k128', 0) else 96
        if KR == 128:
            for i in range(2):
                S.op('pool', 'memset', w=["QT%d:%d" % (i, t) for t in range(0, TOK, 512)], ap=QT[i][96:128, :], constant=0.0)
                S.op('pool', 'memset', w=["KT%d:%d" % (i, t) for t in range(0, TOK, 512)], ap=KT[i][96:128, :], constant=0.0)
        S.op('pool', 'memset', w=["Vx"], ap=Vx[:, :, :, 64:65], constant=1.0)
        for kc in range(18):
            ps, pk = self.psum(0, 3)
            S.op('pe', 'matmul', r=["kvn:%d" % (kc // 4 * 512), "wv"], w=[pk], out=ps[:, :], lhsT=kvn[:, kc * 128:(kc + 1) * 128],
                 rhs=wv[:, :], start=True, stop=True)
            S.op('dve', 'tensor_copy', r=[pk], w=["Vx"], out=Vx[:, kc, :, 0:64], in_=ps[:, :].rearrange("p (h e) -> p h e", e=64))
        scale = 96.0 ** -0.5

        def proj(h, only=None):
            qt, kt = QT[h % 2], KT[h % 2]
            for ti, (t0, tw, cv) in enumerate(tl):
                if only is not None and ti != only:
                    continue
                psq, pkq = self.psum(6, 8)
                for c in range(2):
                    S.op('pe', 'matmul', r=["wuq", "qn:%d" % t0], w=[pkq], out=psq[0:96, :tw], lhsT=wuq[:, c, h * 96:(h + 1) * 96],
                         rhs=qn[:, c, t0:t0 + tw], start=(c == 0), stop=(c == 1))
                sq, sk = nx_sq()
                S.op('act', 'activation', r=[pkq], w=[sk], out=sq[0:96, :tw], in_=psq[0:96, :tw], func=AF.Square)
                qr, qrk = nx_raw()
                S.op('dve', 'tensor_copy', r=[pkq], w=[qrk], out=qr[0:96, :tw], in_=psq[0:96, :tw])
                psr, pkr = self.psum(6, 8)
                for c in range(2):
                    S.op('pe', 'matmul', r=["wuqr", "qn:%d" % t0], w=[pkr], out=psr[0:96, :tw], lhsT=wuqr[:, c, h, :],
                         rhs=qn[:, c, t0:t0 + tw], start=(c == 0), stop=(c == 1))
                t2, t2k = nx_t()
                S.op('dve', 'tensor_tensor', r=[pkr, "css"], w=[t2k], out=t2[64:96, :tw], in0=psr[64:96, :tw],
                     in1=css[64:96, 1, t0:t0 + tw], op=ALU.mult)
                pss, pks = self.psum(6, 8)
                S.op('pe', 'matmul', r=[sk, "consts"], w=[pks], out=pss[0:96, :tw], lhsT=self.ones_bf[0:96, 0:96], rhs=sq[0:96, :tw],
                     start=True, stop=True)
                rs, rk = self.rstd_of(pss[0:96, :tw], pks, 96, tw, 1.0 / 96, nx_ln, nx_rs)
                S.op('dve', 'scalar_tensor_tensor', r=[qrk, rk, "evvec"], w=["QT%d:%d" % (h % 2, t0)], out=qt[0:64, t0:t0 + tw],
                     in0=qr[0:64, :tw], scalar=ev[0:64, 3:4], in1=rs[0:64, :tw], op0=ALU.mult, op1=ALU.mult)
                t1, t1k = nx_t()
                S.op('pool', 'tensor_tensor', r=[qrk, "css"], w=[t1k], out=t1[64:96, :tw], in0=qr[64:96, :tw],
                     in1=css[64:96, 0, t0:t0 + tw], op=ALU.mult)
                S.op('pool', 'tensor_tensor', r=[t1k, t2k], w=[t1k], out=t1[64:96, :tw], in0=t1[64:96, :tw], in1=t2[64:96, :tw],
                     op=ALU.add)
                S.op('pool', 'tensor_tensor', r=[t1k, rk], w=["QT%d:%d" % (h % 2, t0)], out=qt[64:96, t0:t0 + tw], in0=t1[64:96, :tw],
                     in1=rs[64:96, :tw], op=ALU.mult)
                psk, pkk = self.psum(6, 8)
                S.op('pe', 'matmul', r=["wkn", "kvn:%d" % t0], w=[pkk], out=psk[0:96, :tw], lhsT=wkn[:, h, :], rhs=kvn[:, t0:t0 + tw],
                     start=True, stop=False)
                S.op('pe', 'matmul', r=["shk", "krb:%d" % t0], w=[pkk], out=psk[0:96, :tw], lhsT=shk[0:32, :], rhs=krb[0:32, t0:t0 + tw],
                     start=False, stop=True)
                sq, sk = nx_sq()
                S.op('act', 'activation', r=[pkk], w=[sk], out=sq[0:64, :tw], in_=psk[0:64, :tw], func=AF.Square)
                kr_, krk = nx_raw()
                S.op('dve', 'tensor_copy', r=[pkk], w=[krk], out=kr_[0:96, :tw], in_=psk[0:96, :tw])
                pss, pks = self.psum(6, 8)
                S.op('pe', 'matmul', r=[sk, "consts"], w=[pks], out=pss[0:96, :tw], lhsT=self.ones_bf[0:64, 0:96], rhs=sq[0:64, :tw],
                     start=True, stop=False)
                S.op('pe', 'matmul', r=["sqkr:%d" % t0, "consts"], w=[pks], out=pss[0:96, :tw], lhsT=self.ones_bf[0:32, 0:96],
                     rhs=sqkr[0:32, t0:t0 + tw], start=False, stop=True)
                rs, rk = self.rstd_of(pss[0:96, :tw], pks, 96, tw, 1.0 / 96, nx_ln, nx_rs)
                S.op('dve', 'scalar_tensor_tensor', r=[krk, rk, "evvec"], w=["KT%d:%d" % (h % 2, t0)], out=kt[0:96, t0:t0 + tw],
                     in0=kr_[0:96, :tw], scalar=ev[0:96, 4:5], in1=rs[0:96, :tw], op0=ALU.mult, op1=ALU.mult)

        qtiles = [(t0, tw, list(range(18))) for (t0, tw, cv) in tiles_for(b=bi, with_ctx=False)] + [(SEQ, CTX, [16, 17])]

        NH = cfg.get('mla_heads', 8)
        seq_ = [(h, qi) for h in range(NH) for qi in range(len(qtiles))]
        steps = []
        for it, (h, qi) in enumerate(seq_):
            kcs = qtiles[qi][2]
            for m in range(len(kcs) // 2):
                steps.append((it, h, qi, m, kcs[2 * m], kcs[2 * m + 1], len(kcs) // 2))

        def emit_S(si, step):
            it, h, qi, m, kc0, kc1, npair = step
            t0, tw, kcs = qtiles[qi]
            pt, ptk = PT[it % 2], "PT%d" % (it % 2)
            if m == 0 and qi == 0 and h + 1 < NH:
                proj(h + 1)
            if m == 0 and qi in (1, 3) and mbufs is not None and self.mod_todo:
                self.mod_group(*self.mod_todo.pop(0), *mbufs, banks=(6, 8))
            P = self.pp[si % 2]
            pks = ["ps%d" % (2 * (si % 2)), "ps%d" % (2 * (si % 2) + 1)]
            for j, kc in enumerate((kc0, kc1)):
                S.op('pe', 'matmul', r=["KT%d:%d" % (h % 2, kc // 4 * 512), "QT%d:%d" % (h % 2, t0)], w=[pks[j]], out=P[:, j * 512:j * 512 + tw],
                     lhsT=KT[h % 2][0:96, kc * 128:(kc + 1) * 128], rhs=QT[h % 2][0:96, t0:t0 + tw], start=True, stop=True)
            S.op('act', 'activation', r=pks, w=[ptk + ":%d" % kc0, ptk + ":%d" % kc1], out=pt[:, kc0:kc0 + 2, :tw],
                 in_=P[:, :].rearrange("p (b c) -> p b c", c=512)[:, :, :tw], func=AF.Exp, scale=scale)

        def emit_PV(step):
            it, h, qi, m, kc0, kc1, npair = step
            t0, tw, kcs = qtiles[qi]
            pt, ptk = PT[it % 2], "PT%d" % (it % 2)
            pso, pko = self.ps[4 + it % 2], "ps%d" % (4 + it % 2)
            for j, kc in enumerate((kc0, kc1)):
                n = 2 * m + j
                for sub in range(tw // 128):
                    S.op('pe', 'matmul', r=[ptk + ":%d" % kc, "Vx"], w=[pko], out=pso[:, sub * 65:(sub + 1) * 65],
                         lhsT=pt[:, kc, sub * 128:(sub + 1) * 128], rhs=Vx[:, kc, h, :], start=(n == 0 and sub == 0),
                         stop=(n == 2 * npair - 1), skip_group_check=True)
            if m == npair - 1:
                for sub in range(tw // 128):
                    st = (t0 // 128) + sub
                    rec, rck = nx_rec()
                    S.op('dve', 'reciprocal', r=[pko], w=[rck], out=rec[:, 0:1], in_=pso[:, sub * 65 + 64:sub * 65 + 65])
                    S.op('dve', 'tensor_scalar', r=[pko, rck], w=["ytok:%d" % st], out=ytok[:, st, h * 64:(h + 1) * 64],
                         in0=pso[:, sub * 65:sub * 65 + 64], scalar1=rec[:, 0:1], scalar2=None, op0=ALU.mult)

        LAG = cfg.get('lag', 2)
        proj(0)
        for i, stp in enumerate(steps):
            emit_S(i, stp)
            if i >= LAG:
                emit_PV(steps[i - LAG])
        for stp in steps[max(0, len(steps) - LAG):]:
            emit_PV(stp)
        while mbufs is not None and self.mod_todo:
            self.mod_group(*self.mod_todo.pop(0), *mbufs, banks=(6, 8))
        if cfg.get('debug') and bi == 0:
            self.dump("ytok", ytok[:, :, :], [128, 18, 512], ["ytok:%d" % st for st in range(18)])
        for st in range(18):
            pb = self.psb[st % 2]
            pbk = "ps7"
            for c in range(4):
                S.op('pe', 'transpose', r=["ytok:%d" % st, "consts"], w=[pbk], out=pb[:, c * 128:(c + 1) * 128],
                     in_=ytok[:, st, c * 128:(c + 1) * 128], identity=self.ident_bf[:, :])
            S.op('act', 'activation', r=[pbk], w=["yT:%d" % (st // 4 * 512)], out=yT[:, 0:4, st * 128:(st + 1) * 128],
                 in_=pb[:, 0:512].rearrange("p (c t) -> p c t", t=128), func=AF.Copy)

    def odd_mixer(self, bi, hT, yT, xT, regD):
        nc, S, M, cfg = self.nc, self.S, self.M, self.cfg
        tl = tiles_for(b=bi)
        TK = ["%d" % t for t in range(0, TOK, 512)]
        ov = self.odvec
        for p in range(2 if not cfg.get('skip_gla') else 0):
            M.region(self.offD, Mem.HI)
            wq = M.alloc("gwq", [128, 8, 128], BF16)
            wk = M.alloc("gwk", [128, 8, 128], BF16)
            wvv = M.alloc("gwv", [128, 8, 256], BF16)
            wg = M.alloc("gwg", [128, 8, 256], BF16)
            wlr = M.alloc("gwlr", [128, 8, 32], BF16)
            glau = M.alloc("glau", [128, 4, 128], F32)
            gmask = M.alloc("gmask", [128, 2, 128], BF16)
            glaw = M.alloc("glaw", [64, 2, 256], F32)
            nx_qk = self.rot("qkraw", [128, 512], BF16, 2)
            lrT = M.alloc("lrT", [64, 512], F32)
            gmask4 = M.alloc("gmask4", [128, 2, 4, 128], BF16)
            nx_e = self.rot("ge", [128, 512], F32, 2)
            nx_l = self.rot("gl", [128, 512], F32, 2)
            nx_x = self.rot("gx", [128, 512], F32, 4)
            nx_kt = self.rot("gkt", [128, 512], F32, 2)
            nx_am = self.rot("attm", [128, 512], BF16, 4)
            nx_sq = self.rot("gsq", [128, 512], BF16, 2)
            nx_ln = self.rot("glnt", [128, 512], F32, 2)
            od = self.odwin_d
            S.dma('pool', out=wq[:, :, :], in_=od[:, :, p * 128:(p + 1) * 128], w=["gwq"])
            S.dma('pool', out=wk[:, :, :], in_=od[:, :, 256 + p * 128:256 + (p + 1) * 128], w=["gwk"])
            S.dma('pool', out=wvv[:, :, :], in_=od[:, :, 512 + p * 256:512 + (p + 1) * 256], w=["gwv"])
            S.dma('pool', out=wg[:, :, :], in_=od[:, :, 1024 + p * 256:1024 + (p + 1) * 256], w=["gwg"])
            S.dma('pool', out=wlr[:, :, :], in_=od[:, :, 1536:1568], w=["gwlr"])
            S.dma('sp', out=glau[:, :, :], in_=self.glau_d[:, 0:4, :], w=["glau"])
            S.dma('pool', out=gmask[:, :, :], in_=self.glau_d[:, 4:6, :], w=["gmask"])
            for j in range(4):
                S.dma('pool', out=gmask4[:, :, j, :], in_=self.glau_d[:, 4:6, :], w=["gmask4"])
            S.dma('sp', out=glaw[:, :, :], in_=self.glaw_d[:, :, :], w=["glaw"])
            S.op('dve', 'memset', w=["lrT"], ap=lrT[:, :], constant=1.0)
            M.region(self.offA, self.offC)
            kd = [M.alloc("kd%d" % d, [128, 18, 128], BF16) for d in range(2)]
            vt = M.alloc("vtok", [128, 18, 256], BF16)
            qd = [M.alloc("qdec%d" % d, [128, SEQ], BF16) for d in range(2)]
            ki = [M.alloc("kinv%d" % d, [128, SEQ], BF16) for d in range(2)]
            sh = [M.alloc("shist%d" % d, [128, 32, 128], BF16) for d in range(2)]
            sg = M.alloc("sg", [128, 2, SEQ], BF16)
            dec = M.alloc("dec", [128, 2, 36], F32)
            sst = [[M.alloc("sst%d%d" % (d, i), [128, 128], F32) for i in range(2)] for d in range(2)]
            nx_rs = self.rot("grstd", [128, 512], F32, 2)
            nx_t = self.rot("gt1", [128, 512], F32, 2)
            for (t0, tw, cv) in tl:
                hk = "hT:%d" % t0
                lat = t0 < SEQ
                if lat:
                    qraw, qk_ = nx_qk()
                    ps, pk = self.psum()
                    for k in range(8):
                        S.op('pe', 'matmul', r=["gwq", hk], w=[pk], out=ps[:, :tw], lhsT=wq[:, k, :], rhs=hT[:, k, t0:t0 + tw],
                             start=(k == 0), stop=(k == 7))
                    S.op('act', 'activation', r=[pk], w=[qk_], out=qraw[:, :tw], in_=ps[:, :tw], func=AF.Copy, scale=0.125)
                    kraw, kk_ = nx_qk()
                    ps, pk = self.psum()
                    for k in range(8):
                        S.op('pe', 'matmul', r=["gwk", hk], w=[pk], out=ps[:, :tw], lhsT=wk[:, k, :], rhs=hT[:, k, t0:t0 + tw],
                             start=(k == 0), stop=(k == 7))
                    S.op('act', 'activation', r=[pk], w=[kk_], out=kraw[:, :tw], in_=ps[:, :tw], func=AF.Copy)
                    for hl in range(2):
                        ps, pk = self.psum()
                        for k in range(8):
                            S.op('pe', 'matmul', r=["gwg", hk], w=[pk], out=ps[:, :tw], lhsT=wg[:, k, hl * 128:(hl + 1) * 128],
                                 rhs=hT[:, k, t0:t0 + tw], start=(k == 0), stop=(k == 7))
                        S.op('act', 'activation', r=[pk], w=["sg:%d" % t0], out=sg[:, hl, t0:t0 + tw], in_=ps[:, :tw], func=AF.Silu)
                ps, pk = self.psum()
                for k in range(8):
                    S.op('pe', 'matmul', r=["gwlr", hk], w=[pk], out=ps[0:32, :tw], lhsT=wlr[:, k, :],
                         rhs=hT[:, k, t0:t0 + tw], start=(k == 0), stop=(k == 7))
                S.op('act', 'activation', r=[pk], w=["lrT"], out=lrT[0:32, :tw], in_=ps[0:32, :tw], func=AF.Copy)
                nsub = tw // 128
                st0 = t0 // 128
                psk, pkk = self.psum()
                for sub in range(nsub):
                    c0 = sub * 128
                    for k in range(8):
                        S.op('pe', 'matmul', r=["gwk", hk], w=[pkk], out=psk[:, c0:c0 + 128], lhsT=hT[:, k, t0 + c0:t0 + c0 + 128], rhs=wk[:, k, :],
                             start=(k == 0), stop=(k == 7))
                ktk, ktkk = nx_kt()
                S.op('act', 'activation', r=[pkk], w=[ktkk], out=ktk[:, :tw], in_=psk[:, :tw], func=AF.Copy)
                for s2 in range(0, nsub, 2):
                    psv, pkv = self.psum()
                    for sub in range(s2, min(s2 + 2, nsub)):
                        c0 = sub * 128
                        for k in range(8):
                            S.op('pe', 'matmul', r=["gwv", hk], w=[pkv], out=psv[:, (sub - s2) * 256:(sub - s2 + 1) * 256],
                                 lhsT=hT[:, k, t0 + c0:t0 + c0 + 128], rhs=wvv[:, k, :], start=(k == 0), stop=(k == 7))
                    ns = min(2, nsub - s2)
                    S.op('act', 'activation', r=[pkv], w=["vtok:%d" % (st0 + s2 + i) for i in range(ns)],
                         out=vt[:, st0 + s2:st0 + s2 + ns, :], in_=psv[:, 0:ns * 256].rearrange("p (a b) -> p a b", b=256), func=AF.Copy)
                for d in range(2):
                    psz, pkz = self.psum()
                    for sub in range(nsub):
                        c0 = sub * 128
                        S.op('pe', 'matmul', r=["lrT", "glaw"], w=[pkz], out=psz[:, c0:c0 + 128], lhsT=lrT[0:33, c0:c0 + 128],
                             rhs=glaw[0:33, d, p * 128:(p + 1) * 128], start=True, stop=True)
                    e, ek = nx_e()
                    l, lk = nx_l()
                    S.op('act', 'activation', r=[pkz], w=[ek], out=e[:, :tw], in_=psz[:, :tw], func=AF.Exp, scale=-1.0)
                    S.op('act', 'activation', r=[ek], w=[lk], out=l[:, :tw], in_=e[:, :tw], func=AF.Ln, bias=self.one_t[:, 0:1], scale=1.0)
                    psr, pkr = self.psum()
                    for sub in range(nsub):
                        c0 = sub * 128
                        S.op('pe', 'matmul', r=[lk, "glau"], w=[pkr], out=psr[:, c0:c0 + 128], lhsT=glau[:, 2 + d, :], rhs=l[:, c0:c0 + 128],
                             start=True, stop=True)
                    er, erk = nx_x()
                    S.op('act', 'activation', r=[pkr], w=[erk], out=er[:, :tw], in_=psr[:, :tw], func=AF.Exp)
                    S.op('dve', 'tensor_tensor', r=[ktkk, erk], w=["kd%d:%d" % (d, st0 + i) for i in range(nsub)],
                         out=kd[d][:, st0:st0 + nsub, :], in0=ktk[:, :tw].rearrange("p (a b) -> p a b", b=128),
                         in1=er[:, :tw].rearrange("p (a b) -> p a b", b=128), op=ALU.mult)
                    psc, pkc = self.psum()
                    for sub in range(nsub):
                        c0 = sub * 128
                        S.op('pe', 'matmul', r=[lk, "glau"], w=[pkc], out=psc[:, c0:c0 + 128], lhsT=l[:, c0:c0 + 128], rhs=glau[:, d, :],
                             start=True, stop=True)
                    ep, epk = nx_x()
                    S.op('act', 'activation', r=[pkc], w=[epk], out=ep[:, :tw], in_=psc[:, :tw], func=AF.Exp)
                    g0_ = 2 * st0
                    S.op('dve', 'tensor_copy', r=[epk], w=["dec"], out=dec[:, d, g0_:g0_ + 2 * nsub],
                         in_=ep[:, :tw].rearrange("p (a b) -> p a b", b=64)[:, :, 63 if d == 0 else 0])
                    if lat:
                        em, emk = nx_x()
                        S.op('act', 'activation', r=[pkc], w=[emk], out=em[:, :tw], in_=psc[:, :tw], func=AF.Exp, scale=-1.0)
                        S.op('dve', 'tensor_tensor', r=[qk_, epk], w=["qdec%d:%d" % (d, st0 + i) for i in range(nsub)], out=qd[d][:, t0:t0 + tw],
                             in0=qraw[:, :tw], in1=ep[:, :tw], op=ALU.mult)
                        S.op('pool', 'tensor_tensor', r=[kk_, emk], w=["kinv%d:%d" % (d, st0 + i) for i in range(nsub)], out=ki[d][:, t0:t0 + tw],
                             in0=kraw[:, :tw], in1=em[:, :tw], op=ALU.mult)
            if cfg.get('debug') and bi == 0 and p == 0:
                self.dump("dec", dec[:, :, :], [128, 2, 36], ["dec"])
                for d in range(2):
                    self.dump("kd%d" % d, kd[d][:, :, :], [128, 18, 128], ["kd%d:%d" % (d, st) for st in range(18)])
                    self.dump("qd%d" % d, qd[d][:, :], [128, SEQ], ["qdec%d:%d" % (d, st) for st in range(16)])
                    self.dump("ki%d" % d, ki[d][:, :], [128, SEQ], ["kinv%d:%d" % (d, st) for st in range(16)])
                self.dump("vt", vt[:, :, :], [128, 18, 256], ["vtok:%d" % st for st in range(18)])
            self.cut(11)
            orders = [[32, 33, 34, 35] + list(range(32)), [35, 34, 33, 32] + list(range(31, -1, -1))]
            for d in range(2):
                S.op('dve', 'memset', w=["sst%d0" % d], ap=sst[d][0][:, :], constant=0.0)
            for n in range(36):
                for d in range(2):
                    g = orders[d][n]
                    st, half = g // 2, g % 2
                    cur, nxt_ = sst[d][n % 2], sst[d][(n + 1) % 2]
                    ck, nk_ = "sst%d%d" % (d, n % 2), "sst%d%d" % (d, (n + 1) % 2)
                    if g < 32:
                        S.op('pool', 'tensor_copy', r=[ck], w=["shist%d:%d" % (d, g)], out=sh[d][:, g, :], in_=cur[:, :])
                    if n == 35:
                        continue
                    ps, pk = self.psum()
                    r0_ = half * 64
                    S.op('pe', 'matmul', r=["kd%d:%d" % (d, st), "vtok:%d" % st], w=[pk], out=ps[:, 0:256],
                         lhsT=kd[d][r0_:r0_ + 64, st, :], rhs=vt[r0_:r0_ + 64, st, :], start=True, stop=True)
                    for hl in range(2):
                        rr = slice(hl * 64, (hl + 1) * 64)
                        S.op('dve', 'scalar_tensor_tensor', r=[ck, pk, "dec"], w=[nk_], out=nxt_[rr, :], in0=cur[rr, :],
                             scalar=dec[rr, d, g:g + 1], in1=ps[rr, hl * 128:(hl + 1) * 128], op0=ALU.mult, op1=ALU.add)
            if cfg.get('debug') and bi == 0 and p == 0:
                for d in range(2):
                    self.dump("sh%d" % d, sh[d][:, :, :], [128, 32, 128], ["shist%d:%d" % (d, g) for g in range(32)])
            self.cut(12)
            for grp in range(4):
                T0 = grp * 512
                for hl in range(2):
                    rr = slice(hl * 64, (hl + 1) * 64)
                    ams = []
                    for d in range(2):
                        psa, pka = self.psum()
                        for j in range(4):
                            st = grp * 4 + j
                            tt0 = st * 128
                            S.op('pe', 'matmul', r=["kinv%d:%d" % (d, st), "qdec%d:%d" % (d, st)], w=[pka], out=psa[:, j * 128:(j + 1) * 128],
                                 lhsT=ki[d][rr, tt0:tt0 + 128], rhs=qd[d][rr, tt0:tt0 + 128], start=True, stop=True)
                        am, amk = nx_am()
                        S.op('dve', 'tensor_tensor', r=[pka, "gmask4"], w=[amk], out=am[:, :], in0=psa[:, :],
                             in1=gmask4[:, d, :, :].rearrange("p a b -> p (a b)"), op=ALU.mult)
                        ams.append((am, amk))
                    pso, pko = self.psum()
                    for j in range(4):
                        st = grp * 4 + j
                        tt0 = st * 128
                        for d in range(2):
                            S.op('pe', 'matmul', r=[ams[d][1], "vtok:%d" % st], w=[pko], out=pso[:, j * 128:(j + 1) * 128],
                                 lhsT=vt[:, st, hl * 128:(hl + 1) * 128], rhs=ams[d][0][:, j * 128:(j + 1) * 128], start=(d == 0), stop=False)
                        for d in range(2):
                            for half in range(2):
                                g = 2 * st + half
                                S.op('pe', 'matmul', r=["shist%d:%d" % (d, g), "qdec%d:%d" % (d, st)], w=[pko],
                                     out=pso[:, j * 128 + half * 64:j * 128 + (half + 1) * 64], lhsT=sh[d][rr, g, :],
                                     rhs=qd[d][rr, tt0 + half * 64:tt0 + (half + 1) * 64], start=False, stop=(d == 1 and half == 1))
                    sq, sk = nx_sq()
                    S.op('act', 'activation', r=[pko], w=[sk], out=sq[:, :], in_=pso[:, :], func=AF.Square)
                    pss, pks = self.psum()
                    S.op('pe', 'matmul', r=[sk, "consts"], w=[pks], out=pss[:, :], lhsT=self.ones_bf[:, :], rhs=sq[:, :], start=True, stop=True)
                    rs, rk = self.rstd_of(pss[:, :], pks, 128, 512, 1.0 / 128, nx_ln, nx_rs)
                    t1, t1k = nx_t()
                    S.op('dve', 'scalar_tensor_tensor', r=[pko, rk, "odvec"], w=[t1k], out=t1[:, :], in0=pso[:, :], scalar=ov[:, 0:1],
                         in1=rs[:, :], op0=ALU.mult, op1=ALU.mult)
                    S.op('pool', 'tensor_tensor', r=[t1k, "sg:%d" % T0], w=["yT:%d" % T0],
                         out=yT[:, 2 * p + hl, T0:T0 + 512], in0=t1[:, :], in1=sg[:, hl, T0:T0 + 512], op=ALU.mult)
            S.barrier()
        if cfg.get('skip_nat'):
            return
        M.region(self.offD, Mem.HI)
        tb = M.alloc("tbig", [128, 8, 16, 64], BF16)
        wnq_ = [M.alloc("wnq%d" % i, [128, 8, 128], BF16) for i in range(2)]
        wnk_ = [M.alloc("wnk%d" % i, [128, 8, 128], BF16) for i in range(2)]
        wnv = M.alloc("wnv", [128, 8, 512], BF16)
        ones2 = M.alloc("ones2", [128, 128], BF16)
        PTn = [M.alloc("PTn%d" % i, [128, 7, 512], BF16) for i in range(2)]
        nx_sb = self.rot("nsb", [128, 512], F32, 2)
        nx_sq = self.rot("nsq", [128, 512], BF16, 2)
        nx_ln = self.rot("nlnt", [128, 512], F32, 1)
        nx_rs = self.rot("nrstd", [128, 512], F32, 2)
        nx_rec = self.rot("nrec", [128, 4], F32, 4)
        M.region(self.offA, self.offC)
        QnT = M.alloc("QnT", [128, 4, SEQ], BF16)
        KnT = M.alloc("KnT", [128, 4, TOK], BF16)
        Vn = M.alloc("Vn", [128, 18, 8, 65], BF16)
        ytok = M.alloc("nytok", [128, 16, 512], BF16)
        od = self.odwin_d
        for hh in range(4):
            S.dma('pool', out=tb[:, 2 * hh:2 * hh + 2, :, :], in_=self.tbig_d[:, 2 * hh:2 * hh + 2, :, :], w=["tbig"])
        S.op('dve', 'tensor_scalar', r=["tbig"], w=["tbig"], out=tb[:, :, :, :], in0=tb[:, :, :, :], scalar1=8.0, scalar2=None, op0=ALU.mult)
        S.dma('pool', out=wnv[:, :, :], in_=od[:, :, 2592:3104], w=["wnv"])
        S.op('dve', 'memset', w=["ones2"], ap=ones2[:, :], constant=0.0)
        S.op('dve', 'memset', w=["ones2"], ap=ones2[0:64, 0:64], constant=1.0)
        S.op('dve', 'memset', w=["ones2"], ap=ones2[64:128, 64:128], constant=1.0)
        S.op('pool', 'memset', w=["Vn"], ap=Vn[:, :, :, :], constant=1.0)
        for st in range(18):
            ps, pk = self.psum()
            for k in range(8):
                S.op('pe', 'matmul', r=["wnv", "hT:%d" % (st // 4 * 512)], w=[pk], out=ps[:, :], lhsT=hT[:, k, st * 128:(st + 1) * 128],
                     rhs=wnv[:, k, :], start=(k == 0), stop=(k == 7))
            S.op('act', 'activation', r=[pk], w=["Vn"], out=Vn[:, st, :, 0:64], in_=ps[:, :].rearrange("p (h e) -> p h e", e=64), func=AF.Copy)
        groups = []
        for pr in range(4):
            for (t0, tw, cv) in tl:
                for which in range(2):
                    if which == 0 and t0 >= SEQ:
                        continue
                    groups.append((pr, t0, tw, which))
        loaded = set()

        def nproj_a(g):
            pr, t0, tw, which = g
            wnq, wnk = wnq_[pr % 2], wnk_[pr % 2]
            if pr not in loaded:
                loaded.add(pr)
                S.dma('pool', out=wnq[:, :, :], in_=od[:, :, 1568 + pr * 128:1568 + (pr + 1) * 128], w=["wnq%d" % (pr % 2)])
                S.dma('pool', out=wnk[:, :, :], in_=od[:, :, 2080 + pr * 128:2080 + (pr + 1) * 128], w=["wnk%d" % (pr % 2)])
            wt, wkey = (wnq, "wnq%d" % (pr % 2)) if which == 0 else (wnk, "wnk%d" % (pr % 2))
            ps, pk = self.psum()
            for k in range(8):
                S.op('pe', 'matmul', r=[wkey, "hT:%d" % t0], w=[pk], out=ps[:, :tw], lhsT=wt[:, k, :], rhs=hT[:, k, t0:t0 + tw],
                     start=(k == 0), stop=(k == 7))
            sq, sk = nx_sq()
            S.op('act', 'activation', r=[pk], w=[sk], out=sq[:, :tw], in_=ps[:, :tw], func=AF.Square)
            return ps, pk, sq, sk

        def nproj_b(g, st_):
            pr, t0, tw, which = g
            ps, pk, sq, sk = st_
            pss, pks = self.psum()
            S.op('pe', 'matmul', r=[sk, "ones2"], w=[pks], out=pss[:, :tw], lhsT=ones2[:, :], rhs=sq[:, :tw], start=True, stop=True)
            rs, rk = self.rstd_of(pss[:, :tw], pks, 128, tw, 1.0 / 64, nx_ln, nx_rs)
            if which == 0:
                S.op('dve', 'scalar_tensor_tensor', r=[pk, rk, "odvec"], w=["QnT:%d" % t0], out=QnT[:, pr, t0:t0 + tw], in0=ps[:, :tw],
                     scalar=ov[:, 1:2], in1=rs[:, :tw], op0=ALU.mult, op1=ALU.mult)
            else:
                S.op('dve', 'scalar_tensor_tensor', r=[pk, rk, "odvec"], w=["KnT:%d" % t0], out=KnT[:, pr, t0:t0 + tw], in0=ps[:, :tw],
                     scalar=ov[:, 2:3], in1=rs[:, :tw], op0=ALU.mult, op1=ALU.mult)

        pend = nproj_a(groups[0])
        for gi_, g in enumerate(groups):
            nxt_ = nproj_a(groups[gi_ + 1]) if gi_ + 1 < len(groups) else None
            nproj_b(g, pend)
            pend = nxt_
        self.cut(21)

        def r0f(r):
            return min(max(r - 4, 0), 24)
        iters = [(qt, hg) for qt in range(cfg.get('nat_tiles', 16)) for hg in range(2)]

        def kcs_of(qt):
            rlo = r0f(2 * qt)
            rhi = r0f(2 * qt + 1) + 8
            return list(range(rlo // 2, (rhi - 1) // 2 + 1)) + [16, 17]

        def nat_S(it):
            qt, hg = iters[it]
            kcs = kcs_of(qt)
            pt, ptk = PTn[it % 2], "PTn%d" % (it % 2)
            for j, kc in enumerate(kcs):
                ps, pk = self.psum(0, 3)
                for hh in range(4):
                    h = hh * 2 + hg
                    pr, rr = h // 2, slice((h % 2) * 64, (h % 2) * 64 + 64)
                    S.op('pe', 'matmul', r=["KnT:%d" % (kc // 4 * 512), "QnT:%d" % (qt // 4 * 512)], w=[pk], out=ps[:, hh * 128:(hh + 1) * 128],
                         lhsT=KnT[rr, pr, kc * 128:(kc + 1) * 128], rhs=QnT[rr, pr, qt * 128:(qt + 1) * 128], start=True, stop=True)
                if kc < 16:
                    Dd = 2 * kc - 2 * qt + 7
                    sb, sbk = nx_sb()
                    S.op('dve', 'tensor_tensor', r=[pk, "tbig"], w=[sbk], out=sb[:, :].rearrange("p (h a c) -> p h a c", a=2, c=64),
                         in0=ps[:, :].rearrange("p (h a c) -> p h a c", a=2, c=64),
                         in1=tb[:, hg:8:2, Dd - 1:Dd + 1, :][:, :, ::-1, :], op=ALU.add)
                    S.op('act', 'activation', r=[sbk], w=[ptk + ":%d" % j], out=pt[:, j, :], in_=sb[:, :], func=AF.Exp, scale=0.125)
                    for i in range(2):
                        for a in range(2):
                            krow, qrow = 2 * kc + i, 2 * qt + a
                            if not (r0f(qrow) <= krow < r0f(qrow) + 8) and not cfg.get('nat_nomemset'):
                                S.op('pool', 'memset', r=[], w=[ptk + ":%d" % j],
                                     ap=pt[i * 64:(i + 1) * 64, j, :].rearrange("p (h q) -> p h q", q=128)[:, :, a * 64:(a + 1) * 64], constant=0.0)
                else:
                    S.op('act', 'activation', r=[pk], w=[ptk + ":%d" % j], out=pt[:, j, :], in_=ps[:, :], func=AF.Exp, scale=0.125)

        def nat_PV(it):
            qt, hg = iters[it]
            if cfg.get('nat_nopv'):
                return
            kcs = kcs_of(qt)
            pt, ptk = PTn[it % 2], "PTn%d" % (it % 2)
            pso, pko = self.psum(3, 6)
            for hh in range(4):
                h = hh * 2 + hg
                for j, kc in enumerate(kcs):
                    S.op('pe', 'matmul', r=[ptk + ":%d" % j, "Vn"], w=[pko], out=pso[:, hh * 65:(hh + 1) * 65], lhsT=pt[:, j, hh * 128:(hh + 1) * 128],
                         rhs=Vn[:, kc, h, :], start=(j == 0), stop=(j == len(kcs) - 1))
            rec, rck = nx_rec()
            S.op('dve', 'reciprocal', r=[pko], w=[rck], out=rec[:, 0:4], in_=pso[:, 0:260].rearrange("p (h e) -> p h e", e=65)[:, :, 64])
            for hh in range(4):
                h = hh * 2 + hg
                S.op('dve', 'tensor_scalar', r=[pko, rck], w=["nytok:%d" % qt], out=ytok[:, qt, h * 64:(h + 1) * 64], in0=pso[:, hh * 65:hh * 65 + 64],
                     scalar1=rec[:, hh:hh + 1], scalar2=None, op0=ALU.mult)
            if hg == 1:
                pb = self.psb[qt % 2]
                pbk = "ps7"
                for c in range(4):
                    S.op('pe', 'transpose', r=["nytok:%d" % qt, "consts"], w=[pbk], out=pb[:, c * 128:(c + 1) * 128],
                         in_=ytok[:, qt, c * 128:(c + 1) * 128], identity=self.ident_bf[:, :])
                S.op('act', 'activation', r=[pbk], w=["yT:%d" % (qt // 4 * 512)], out=yT[:, 4:8, qt * 128:(qt + 1) * 128],
                     in_=pb[:, 0:512].rearrange("p (c t) -> p c t", t=128), func=AF.Copy)

        if iters:
            nat_S(0)
        for it in range(len(iters)):
            if it + 1 < len(iters):
                nat_S(it + 1)
            nat_PV(it)


def fm(w):
    K = w.shape[0] // 128
    return np.ascontiguousarray(w.reshape(K, 128, -1).transpose(1, 0, 2))


def vec_fm(v):
    return np.ascontiguousarray(v.reshape(-1, 128).T)


def host_inputs(inp, core, extra=None):
    b0 = core * NB
    f32 = np.float32
    m = {}
    x = inp['x'][b0:b0 + NB]
    m['xT'] = np.ascontiguousarray(x.transpose(0, 2, 1).reshape(NB, 8, 128, SEQ).transpose(0, 2, 1, 3)).astype(f32)
    cx = inp['ctx'][b0:b0 + NB]
    m['ctxT'] = np.ascontiguousarray(cx.transpose(0, 2, 1).reshape(NB, 8, 128, CTX).transpose(0, 2, 1, 3)).astype(f32)
    cv = np.stack([inp['c'][b0], inp['c'][b0 + 1], inp['c_ctx'], inp['c_ctx']], axis=-1)
    m['cvT'] = fm(cv).astype(f32)
    for l in range(2):
        m['adaw%d' % l] = fm(inp['ada_w'][l]).astype(f32)
        m['wout%d' % l] = fm(inp['w_out'][l]).astype(f32)
        m['w1_%d' % l] = fm(inp['mlp_w1'][l]).astype(f32)
        m['w2_%d' % l] = fm(inp['mlp_w2'][l]).astype(f32)
    m['adab'] = np.ascontiguousarray(np.stack([vec_fm(inp['ada_b'][l]) for l in range(2)], axis=1)).astype(f32)
    m['nmix'] = np.ascontiguousarray(np.stack([vec_fm(inp['norm_mix'][l]) for l in range(2)], axis=1)).astype(f32)
    m['nmlp'] = np.ascontiguousarray(np.stack([vec_fm(inp['norm_mlp'][l]) for l in range(2)], axis=1)).astype(f32)
    m['ident'] = np.eye(128, dtype=f32)
    m['evwin'] = fm(inp['ev_w_in'][0]).astype(f32)
    m['wuq'] = fm(inp['mla_w_uq'][0]).astype(f32)
    wukv = inp['mla_w_ukv'][0].reshape(128, 8, 128)
    m['wukvn'] = np.ascontiguousarray(wukv[:, :, 0:64]).astype(f32)
    m['wukvv'] = np.ascontiguousarray(wukv[:, :, 64:128].reshape(128, 512)).astype(f32)
    qg = inp['mla_q_gain'][0]
    kg = inp['mla_k_gain'][0]
    perm = np.array([(i + 8) if (i % 16) < 8 else (i - 8) for i in range(32)])
    ev = np.ones((128, 16), f32)
    ev[:, 0:2] = vec_fm(inp['mla_q_norm'][0])
    ev[:, 2] = inp['mla_kv_norm'][0]
    ev[0:96, 3] = qg
    ev[0:64, 4] = kg[0:64]
    ev[0:32, 5] = kg[64:96]
    ev[64:96, 5] = qg[64:96]
    ev[0:32, 6] = kg[64:96][perm]
    ev[64:96, 6] = qg[64:96][perm]
    m['evvec'] = ev
    lv = np.zeros((128, 4, 12), f32)
    for j in range(4):
        lv[:, :, j] = vec_fm(inp['lru_conv_w'][0][j])
    lv[:, :, 4] = vec_fm(inp['lru_conv_b'][0])
    for d in range(2):
        lv[:, :, 5 + d] = vec_fm(inp['lru_b_a'][0][d])
        lv[:, :, 7 + d] = vec_fm(inp['lru_b_x'][0][d])
        lv[:, :, 9 + d] = vec_fm(inp['lru_lam'][0][d])
    m['lruvec'] = lv
    m['convwb'] = np.ascontiguousarray(np.broadcast_to(inp['lru_conv_w'][0][None, :, :], (128, 4, 512))).astype(f32)
    lw = np.zeros((128, 4, 2, 2, 128), f32)
    for c in range(4):
        for d in range(2):
            for g, nm in enumerate(('lru_w_a', 'lru_w_x')):
                for hb in range(2):
                    lw[hb * 64:(hb + 1) * 64, c, d, g, hb * 64:(hb + 1) * 64] = inp[nm][0][d][2 * c + hb]
    m['lruw'] = lw
    m['ropecs'] = rope_tables()
    shk = np.zeros((128, 96), f32)
    for k in range(32):
        shk[k, 64 + k] = 1.0
    m['shk'] = shk
    m['odwin'] = fm(inp['od_w_in'][0]).astype(f32)
    gw = np.zeros((64, 2, 256), f32)
    for d in range(2):
        gw[d * 16:(d + 1) * 16, d, :] = inp['gla_w_a'][0][d]
        gw[32, d, :] = inp['gla_b_a'][0][d]
    m['glaw'] = gw
    m['glau'] = gla_consts()
    ovv = np.ones((128, 4), f32)
    ovv[:, 0] = inp['gla_o_gain'][0]
    ovv[:, 1] = np.tile(inp['na_q_gain'][0], 2)
    ovv[:, 2] = np.tile(inp['na_k_gain'][0], 2)
    m['odvec'] = ovv
    m['tbig'] = natten_table(inp['na_rpb'][0])
    return m


def gla_consts():
    f32 = np.float32
    u = np.zeros((128, 6, 128), f32)
    sp = np.arange(128)[:, None]
    t = np.arange(128)[None, :]
    same = (sp // 64) == (t // 64)
    u[:, 0, :] = np.where(same & (sp <= t), -1.0 / 16, 0.0)
    u[:, 1, :] = np.where(same & (sp >= t), -1.0 / 16, 0.0)
    u[:, 2, :] = np.where(same & (sp > t), -1.0 / 16, 0.0)
    u[:, 3, :] = np.where(same & (sp < t), -1.0 / 16, 0.0)
    u[:, 4, :] = np.where(same & (sp <= t), 1.0, 0.0)
    u[:, 5, :] = np.where(same & (sp >= t), 1.0, 0.0)
    return u


def natten_table(rpb):
    f32 = np.float32
    tbl = np.full((128, 8, 16, 64), -30000.0, f32)
    p = np.arange(128)
    i = p // 64
    kcol = p % 64
    c = np.arange(64)
    c0 = np.clip(c - 8, 0, 48)
    valid = (kcol[:, None] >= c0[None, :]) & (kcol[:, None] < c0[None, :] + 16)
    dj = np.clip(kcol[:, None] - c[None, :] + 15, 0, 30)
    for e in range(16):
        di = e + i
        ok = valid & (di[:, None] <= 14)
        dic = np.clip(di, 0, 14)
        for h in range(8):
            g = rpb[h][dic[:, None], dj]
            tbl[:, h, e, :] = np.where(ok, g, tbl[:, h, e, :])
    return tbl


def rope_tables():
    f32 = np.float32
    half = 16
    inv = 10000.0 ** (-np.arange(0, half, 2, dtype=np.float64) / half)
    pos = np.arange(SEQ)
    cs = np.zeros((128, 2, TOK), f32)
    tab = np.zeros((32, 2, TOK), np.float64)
    tab[:, 0, :] = 1.0
    for mm in range(32):
        p = (pos // 64) if mm < 16 else (pos % 64)
        ang = p.astype(np.float64) * inv[(mm % 16) % 8]
        tab[mm, 0, :SEQ] = np.cos(ang)
        tab[mm, 1, :SEQ] = np.sin(ang)
    cs[0:32] = tab.astype(f32)
    cs[64:96] = tab.astype(f32)
    return cs


def unpack_out(outT):
    return np.ascontiguousarray(outT.transpose(0, 3, 2, 1).reshape(NB, SEQ, D))


def kernel(**inputs):
    inp = {k: np.asarray(v) for k, v in inputs.items()}
    bld = Builder(dict())
    nc = bld.build()
    in_maps = [host_inputs(inp, c) for c in range(8)]
    res = run_bass_kernel_spmd(nc, in_maps, core_ids=list(range(8)))
    outs = [unpack_out(r["outT"]) for r in res.results]
    return np.concatenate(outs, axis=0).astype(np.float32)
```

```python
import numpy as np
import concourse.bass as bass
import concourse.mybir as mybir
from concourse.bass_utils import run_bass_kernel_spmd

F32 = mybir.dt.float32
BF16 = mybir.dt.bfloat16
AF = mybir.ActivationFunctionType
ALU = mybir.AluOpType

D = 1024
SEQ = 2048
CTX = 256
TOK = SEQ + CTX
NB = 2
EPS = 1e-6
NPOOL = 40


class Sched:
    def __init__(self, nc):
        self.nc = nc
        self.eng = {'pe': nc.tensor, 'act': nc.scalar, 'dve': nc.vector, 'pool': nc.gpsimd, 'sp': nc.sync}
        self.ins = []
        self.last_w = {}
        self.rd_eng = {}
        self.rd_dma = {}
        self.bar = {e: [] for e in self.eng}
        self.last_on = {e: None for e in self.eng}
        self.open_dma = set()
        self.ps_last = {}

    def _add(self, eng, meth, kw, r, w, dma):
        i = len(self.ins)
        deps = set()
        for k in list(r) + list(w):
            if k.startswith("ps"):
                pl = self.ps_last.setdefault(k, {})
                for e2, i2 in pl.items():
                    if e2 != eng:
                        deps.add(i2)
                pl[eng] = i
        for k in r:
            lw = self.last_w.get(k)
            if lw is not None:
                deps.add(lw)
        for k in w:
            lw = self.last_w.get(k)
            if lw is not None:
                deps.add(lw)
            deps.update(self.rd_eng.get(k, {}).values())
            deps.update(self.rd_dma.get(k, ()))
        if self.bar[eng]:
            deps.update(self.bar[eng])
            self.bar[eng] = []
        self.ins.append(dict(eng=eng, meth=meth, kw=kw, deps=deps, dma=dma))
        for k in r:
            if dma:
                self.rd_dma.setdefault(k, []).append(i)
            else:
                self.rd_eng.setdefault(k, {})[eng] = i
        for k in w:
            self.last_w[k] = i
            self.rd_eng[k] = {}
            self.rd_dma[k] = []
        if not dma:
            self.last_on[eng] = i
        for d in deps:
            self.open_dma.discard(d)
        if dma:
            self.open_dma.add(i)
        return i

    def op(self, eng, meth, r=(), w=(), **kw):
        return self._add(eng, meth, kw, r, w, False)

    def dma(self, q, out, in_, r=(), w=(), **kw):
        kw = dict(kw, out=out, in_=in_)
        return self._add(q, 'dma_start', kw, r, w, True)

    def barrier(self):
        deps = [i for i in self.last_on.values() if i is not None] + list(self.open_dma)
        for e in self.eng:
            self.bar[e] = list(deps)
        self.last_w.clear()
        self.rd_eng.clear()
        self.rd_dma.clear()
        self.ps_last.clear()

    def finish(self):
        self.barrier()
        self._add('sp', None, {}, (), (), False)

    def emit(self):
        nc = self.nc
        ins = self.ins
        need = [False] * len(ins)
        for i, I in enumerate(ins):
            for d in I['deps']:
                Dd = ins[d]
                if Dd['dma']:
                    continue
                if Dd['eng'] == 'pe' and I['eng'] == 'pe' and not I['dma']:
                    continue
                need[d] = True
        esem = {e: nc.alloc_semaphore("sem_" + e) for e in self.eng}
        npool = {'sp': 28, 'pool': 12, 'act': 2, 'pe': 2, 'dve': 2}
        dsem = {q: [nc.alloc_semaphore("dsem_%s%d" % (q, j)) for j in range(npool[q])] for q in ('sp', 'pool')}
        cnt = {e: 0 for e in self.eng}
        ndq = {'sp': 0, 'pool': 0}
        nd = 0
        for i, I in enumerate(ins):
            if I['dma']:
                q = I['eng']
                I['sem'] = dsem[q][ndq[q] % npool[q]]
                I['val'] = 16 * (ndq[q] // npool[q] + 1)
                ndq[q] += 1
                nd += 1
            elif need[i]:
                cnt[I['eng']] += 1
                I['sem'] = esem[I['eng']]
                I['val'] = cnt[I['eng']]
        known = {e: {} for e in self.eng}
        nwait = 0
        for i, I in enumerate(ins):
            e = I['eng']
            E = self.eng[e]
            waits = {}
            for d in I['deps']:
                Dd = ins[d]
                if (not Dd['dma']) and Dd['eng'] == 'pe' and e == 'pe' and not I['dma']:
                    continue
                s, v = Dd['sem'], Dd['val']
                if waits.get(id(s), (None, 0))[1] < v:
                    waits[id(s)] = (s, v)
            if I['dma'] and I['val'] > 16:
                s = I['sem']
                if waits.get(id(s), (None, 0))[1] < I['val'] - 16:
                    waits[id(s)] = (s, I['val'] - 16)
            for sid, (s, v) in waits.items():
                if known[e].get(sid, 0) >= v:
                    continue
                E.wait_ge(s, v)
                nwait += 1
                known[e][sid] = v
            if I['meth'] is None:
                continue
            inst = getattr(E, I['meth'])(**I['kw'])
            if I['dma']:
                inst.then_inc(I['sem'], 16)
            elif need[i]:
                inst.then_inc(I['sem'], 1)
        self.stats = dict(n=len(ins), nwait=nwait, nsig=sum(need), ndma=nd, cnt=cnt)


class StopBuild(Exception):
    pass


class Mem:
    LO = 16512
    HI = 229344

    def __init__(self, nc):
        self.nc = nc
        self.top = Mem.LO
        self.limit = Mem.HI
        self.n = 0

    def region(self, lo, hi):
        self.top = lo
        self.limit = hi

    def alloc(self, name, shape, dtype):
        nbytes = int(np.prod(shape[1:])) * (4 if dtype == F32 else 2)
        nbytes = (nbytes + 63) // 64 * 64
        off = self.top
        assert off + nbytes <= self.limit, (name, off, nbytes, self.limit)
        self.top += nbytes
        self.n += 1
        return self.nc.alloc_sbuf_tensor_at("%s_%d" % (name, self.n), list(shape), dtype, offset=off)

    def mark(self):
        return self.top

    def release(self, m):
        self.top = m


def tiles_for(ntok_lat=SEQ, with_ctx=True, b=0, tw=512):
    out = []
    for t0 in range(0, ntok_lat, tw):
        out.append((t0, tw, b))
    if with_ctx:
        out.append((SEQ, CTX, 2))
    return out


class Builder:
    def __init__(self, cfg):
        self.cfg = cfg
        self.nc = nc = bass.Bass("TRN2", target_bir_lowering=False)
        self.S = Sched(nc)
        self.M = Mem(nc)
        self.dr = {}
        self.dbg_outs = []
        self.pp = [nc.alloc_psum_tensor("pp%d" % i, [128, 1024], F32) for i in range(4)]
        self.ps = [self.pp[i // 2][:, (i % 2) * 512:(i % 2 + 1) * 512] for i in range(8)]
        v7 = self.ps[7].bitcast(BF16)
        self.psb = [v7[:, 0:512], v7[:, 512:1024]]
        self.ps_i = 0

    def din(self, name, shape, dtype=F32):
        t = self.nc.dram_tensor(name, list(shape), dtype, kind="ExternalInput").ap()
        self.dr[name] = t
        return t

    def dout(self, name, shape, dtype=F32):
        t = self.nc.dram_tensor(name, list(shape), dtype, kind="ExternalOutput").ap()
        self.dr[name] = t
        return t

    def psum(self, lo=0, hi=6):
        n = hi - lo
        i = lo + (self.ps_i % n)
        self.ps_i += 1
        return self.ps[i], "ps%d" % i

    def dump(self, name, ap, shape, rkeys):
        if not self.cfg.get('debug'):
            return
        t = self.dout("dbg_" + name, shape, ap.dtype)
        self.S.dma('sp', out=t, in_=ap, r=rkeys, w=["dbgout_" + name])
        self.dbg_outs.append("dbg_" + name)

    def modulate(self, xT, xkey, g, s, outT, okey, tiles, sc):
        S = self.S
        ones = self.ones_bf
        for (t0, tw, cv) in tiles:
            xk = xkey + ":%d" % t0
            S.op('pool', 'tensor_tensor', r=[xk], w=["sq"], out=sc['sq'][:, :, :tw], in0=xT[:, :, t0:t0 + tw], in1=xT[:, :, t0:t0 + tw],
                 op=ALU.mult)
            ps, pk = self.psum()
            for k in range(8):
                S.op('pe', 'matmul', r=["sq", "consts"], w=[pk], out=ps[:, :tw], lhsT=ones[:, :], rhs=sc['sq'][:, k, :tw],
                     start=(k == 0), stop=(k == 7))
            S.op('act', 'activation', r=[pk], w=["lnt"], out=sc['lnt'][:, :tw], in_=ps[:, :tw], func=AF.Ln,
                 scale=1.0 / D, bias=self.eps_t[:, 0:1])
            pr, prk = self.psum()
            S.op('act', 'activation', r=["lnt"], w=[prk], out=pr[:, :tw], in_=sc['lnt'][:, :tw], func=AF.Exp, scale=-0.5)
            for k in range(8):
                S.op('dve', 'tensor_tensor', r=[xk, prk], w=["mtmp%d" % k], out=sc['mtmp'][:, k, :tw], in0=xT[:, k, t0:t0 + tw],
                     in1=pr[:, :tw], op=ALU.mult)
                S.op('act', 'activation', r=["mtmp%d" % k, "mod"], w=[okey + ":%d" % t0],
                     out=outT[:, k, t0:t0 + tw], in_=sc['mtmp'][:, k, :tw], func=AF.Identity,
                     bias=s[:, k, cv:cv + 1], scale=g[:, k, cv:cv + 1])

    def mod_group(self, l, j, bufs, rowt, banks=(0, 6)):
        S = self.S
        for half in range(2):
            gi = self.mod_gi
            self.mod_gi += 1
            buf, bk = bufs[gi % 2], "awb%d" % (gi % 2)
            rt, rtk = rowt[gi % len(rowt)], "rowt%d" % (gi % len(rowt))
            c0 = j * 1024 + half * 512
            S.dma('pool', out=buf[:, :, :], in_=self.adaw_d[l][:, :, c0:c0 + 512], w=[bk])
            ps, pk = self.psum(*banks)
            for k in range(8):
                S.op('pe', 'matmul', r=[bk, "scv"], w=[pk], out=ps[0:4, :], lhsT=self.scvb[:, k, :], rhs=buf[:, k, :],
                     start=(k == 0), stop=(k == 7))
            S.op('act', 'activation', r=[pk], w=[rtk], out=rt[0:4, :], in_=ps[0:4, :], func=AF.Copy)
            pt_, ptk_ = self.psum(*banks)
            for blk in range(4):
                S.op('pe', 'transpose', r=[rtk, "consts"], w=[ptk_], out=pt_[:, blk * 4:(blk + 1) * 4], in_=rt[0:4, blk * 128:(blk + 1) * 128],
                     identity=self.ident_f[0:4, 0:4])
            f0 = j * 8 + half * 4
            for cv in range(4):
                S.op('dve', 'tensor_tensor', r=[ptk_, "nm"], w=["mod"], out=self.mod[:, l, f0:f0 + 4, cv],
                     in0=pt_[:, 0:16].rearrange("p (f c) -> p f c", c=4)[:, :, cv], in1=self.adab[:, l, f0:f0 + 4], op=ALU.add)
        if j == 1:
            for cv in range(4):
                S.op('dve', 'scalar_tensor_tensor', r=["mod", "nm"], w=["mod"], out=self.gmix[:, l, :, cv],
                     in0=self.mod[:, l, 8:16, cv], scalar=1.0, in1=self.nmix_t[:, l, :], op0=ALU.add, op1=ALU.mult)
        if j == 4:
            for cv in range(4):
                S.op('dve', 'scalar_tensor_tensor', r=["mod", "nm"], w=["mod"], out=self.gmlp[:, l, :, cv],
                     in0=self.mod[:, l, 32:40, cv], scalar=1.0, in1=self.nmlp_t[:, l, :], op0=ALU.add, op1=ALU.mult)

    def build(self):
        nc, S, M, cfg = self.nc, self.S, self.M, self.cfg
        layers = cfg.get('layers', [0, 1])
        nb = cfg.get('nb', NB)
        xT_d = self.din("xT", [NB, 128, 8, SEQ])
        cT_d = self.din("ctxT", [NB, 128, 8, CTX])
        cv_d = self.din("cvT", [128, 8, 4])
        adaw_d = [self.din("adaw%d" % l, [128, 8, 6 * D]) for l in range(2)]
        adab_d = self.din("adab", [128, 2, 48])
        nmix_d = self.din("nmix", [128, 2, 8])
        nmlp_d = self.din("nmlp", [128, 2, 8])
        wout_d = [self.din("wout%d" % l, [128, 8, D]) for l in range(2)]
        w1_d = [self.din("w1_%d" % l, [128, 8, 4 * D]) for l in range(2)]
        w2_d = [self.din("w2_%d" % l, [128, 32, D]) for l in range(2)]
        ident_d = self.din("ident", [128, 128])
        outT_d = self.dout("outT", [NB, 128, 8, SEQ])
        xs_d = self.nc.dram_tensor("xs", [128, 8, TOK], F32, kind="Internal").ap()
        if cfg.get('inject_y'):
            yinj_d = [self.din("yinj%d" % l, [NB, 128, 8, TOK]) for l in range(2)]
        self.declare_mixer_dram()

        self.ident_bf = M.alloc("ident_bf", [128, 128], BF16)
        self.ident_f = M.alloc("ident_f", [128, 128], F32)
        self.ones_bf = M.alloc("ones_bf", [128, 128], BF16)
        self.eps_t = M.alloc("eps_t", [128, 1], F32)
        self.mod = M.alloc("mod", [128, 2, 48, 4], F32)
        self.gmix = M.alloc("gmix", [128, 2, 8, 4], F32)
        self.gmlp = M.alloc("gmlp", [128, 2, 8, 4], F32)
        nmix = M.alloc("nmix", [128, 2, 8], F32)
        nmlp = M.alloc("nmlp", [128, 2, 8], F32)
        adab = M.alloc("adab", [128, 2, 48], F32)
        self.alloc_mixer_persistent()
        base = M.mark()

        S.dma('sp', out=self.ident_f[:, :], in_=ident_d[:, :], w=["consts"])
        S.dma('pool', out=self.ident_bf[:, :], in_=ident_d[:, :], w=["consts"])
        S.op('dve', 'memset', w=["consts"], ap=self.ones_bf[:, :], constant=1.0)
        S.op('dve', 'memset', w=["consts"], ap=self.eps_t[:, :], constant=EPS)
        S.dma('sp', out=nmix[:, :, :], in_=nmix_d[:, :, :], w=["nm"])
        S.dma('sp', out=nmlp[:, :, :], in_=nmlp_d[:, :, :], w=["nm"])
        S.dma('sp', out=adab[:, :, :], in_=adab_d[:, :, :], w=["nm"])
        self.load_mixer_persistent()

        cvs = M.alloc("cvs", [128, 8, 4], F32)
        self.scvb = M.alloc("scvb", [128, 8, 4], BF16)
        self.adab = adab
        self.nmix_t, self.nmlp_t = nmix, nmlp
        self.adaw_d = adaw_d
        S.dma('sp', out=cvs[:, :, :], in_=cv_d[:, :, :], w=["cvs"])
        S.op('act', 'activation', r=["cvs"], w=["scv"], out=self.scvb[:, :, :], in_=cvs[:, :, :], func=AF.Silu)
        base = M.mark()
        bufs = [M.alloc("awb%d" % i, [128, 8, 512], BF16) for i in range(2)]
        rowt = [M.alloc("rowt%d" % i, [4, 512], F32) for i in range(2)]
        self.mod_gi = 0
        for j in range(2):
            self.mod_group(0, j, bufs, rowt)
        self.mod_todo = [(0, j) for j in range(2, 6)] + [(1, j) for j in range(6)]
        if cfg.get('mod_all') or 0 not in layers:
            while self.mod_todo:
                self.mod_group(*self.mod_todo.pop(0), bufs, rowt)
        self.dump("mod", self.mod[:, :, :, :], [128, 2, 48, 4], ["mod"])
        S.barrier()
        M.release(base)

        self.offB = M.mark()
        hT = M.alloc("hT", [128, 8, TOK], BF16)
        self.offA = M.mark()
        xT = M.alloc("xT", [128, 8, TOK], F32)
        self.offC = M.mark()
        yT = M.alloc("yT", [128, 8, TOK], BF16)
        regD = M.mark()
        self.offD = regD

        for bi in range(nb):
            for t0 in range(0, SEQ, 512):
                S.dma('sp', out=xT[:, :, t0:t0 + 512], in_=xT_d[bi, :, :, t0:t0 + 512], w=["xT:%d" % t0])
            S.dma('sp', out=xT[:, :, SEQ:TOK], in_=cT_d[bi, :, :, :], w=["xT:%d" % SEQ])
            for l in layers:
                last = (l == 1)
                m0 = M.mark()
                sc = dict(sq=M.alloc("sq", [128, 8, 512], BF16), lnt=M.alloc("lnt", [128, 512], F32),
                          rstd=M.alloc("rstd", [128, 512], F32), mtmp=M.alloc("mtmp", [128, 8, 512], F32))
                self.modulate(xT, "xT", self.gmix[:, l], self.mod[:, l, 0:8], hT, "hT", tiles_for(b=bi), sc)
                if cfg.get('debug') and bi == 0:
                    self.dump("hT%d" % l, hT[:, :, :], [128, 8, TOK], ["hT:%d" % t for t in range(0, TOK, 512)])
                for (t0, tw, cv) in tiles_for(b=bi):
                    S.dma('sp', out=xs_d[:, :, t0:t0 + tw], in_=xT[:, :, t0:t0 + tw], r=["xT:%d" % t0], w=["xs:%d" % t0])
                S.barrier()
                M.release(m0)
                if cfg.get('inject_y'):
                    for k in range(8):
                        S.dma('pool', out=yT[:, k, :], in_=yinj_d[l][bi, :, k, :], w=["yT:%d" % t for t in range(0, TOK, 512)])
                else:
                    try:
                        if l == 0:
                            self.even_mixer(bi, hT, yT, xT, regD)
                        else:
                            self.odd_mixer(bi, hT, yT, xT, regD)
                    except StopBuild:
                        pass
                S.barrier()
                M.region(regD, Mem.HI)
                if cfg.get('debug') and bi == 0:
                    self.dump("yT%d" % l, yT[:, :, :], [128, 8, TOK], ["yT:%d" % t for t in range(0, TOK, 512)])
                for (t0, tw, cv) in tiles_for(b=bi, with_ctx=not last):
                    S.dma('sp', out=xT[:, :, t0:t0 + tw], in_=xs_d[:, :, t0:t0 + tw], r=["xs:%d" % t0], w=["xT:%d" % t0])
                m0 = M.mark()
                wo = M.alloc("wo", [128, 8, D], BF16)
                sc = dict(sq=M.alloc("sq", [128, 8, 512], BF16), lnt=M.alloc("lnt", [128, 512], F32),
                          mtmp=M.alloc("mtmp", [128, 8, 512], F32))
                assert M.top <= Mem.HI - 8 * D * 2
                M.region(Mem.HI - 8 * D * 2, Mem.HI)
                w1pre = M.alloc("w1pre", [128, 8, D], BF16)
                M.region(m0, Mem.HI)
                S.dma('pool', out=wo[:, :, :], in_=wout_d[l][:, :, :], w=["wo"])
                S.dma('pool', out=w1pre[:, :, :], in_=w1_d[l][:, :, 0:D], w=["w1g0"])
                tl = tiles_for(b=bi, with_ctx=not last)

                def wout_tile(t0, tw, cv):
                    for fo in range(8):
                        ps, pk = self.psum()
                        for k in range(8):
                            S.op('pe', 'matmul', r=["wo", "yT:%d" % t0], w=[pk], out=ps[:, :tw],
                                 lhsT=wo[:, k, fo * 128:(fo + 1) * 128], rhs=yT[:, k, t0:t0 + tw],
                                 start=(k == 0), stop=(k == 7))
                        S.op('dve', 'scalar_tensor_tensor', r=[pk, "mod"], w=["xT:%d" % t0],
                             out=xT[:, fo, t0:t0 + tw], in0=ps[:, :tw], scalar=self.mod[:, l, 16 + fo, cv:cv + 1],
                             in1=xT[:, fo, t0:t0 + tw], op0=ALU.mult, op1=ALU.add)

                wout_tile(*tl[0])
                for ti, (t0, tw, cv) in enumerate(tl):
                    if ti + 1 < len(tl):
                        wout_tile(*tl[ti + 1])
                    self.modulate(xT, "xT", self.gmlp[:, l], self.mod[:, l, 24:32], hT, "hT", [(t0, tw, cv)], sc)
                if cfg.get('debug') and bi == 0:
                    self.dump("xmid%d" % l, xT[:, :, :], [128, 8, TOK], ["xT:%d" % t for t in range(0, TOK, 512)])
                S.barrier()
                M.release(m0)
                M.release(M.mark())
                M.region(self.offC, Mem.HI)
                m0 = M.mark()
                w1g = [w1pre, M.alloc("w1g1", [128, 8, D], BF16)]
                w2g = [M.alloc("w2g%d" % i, [128, 8, D], BF16) for i in range(2)]
                hid = [M.alloc("hid%d" % i, [128, 8, 512], BF16) for i in range(2)]
                rl = [M.alloc("rl%d" % i, [128, 512], BF16) for i in range(2)]
                hi_ = 0
                S.dma('pool', out=w2g[0][:, :, :], in_=w2_d[l][:, 0:8, :], w=["w2g0"])
                for fg in range(4):
                    wb = fg % 2
                    if fg + 1 < 4:
                        nb_ = (fg + 1) % 2
                        S.dma('pool', out=w1g[nb_][:, :, :], in_=w1_d[l][:, :, (fg + 1) * D:(fg + 2) * D], w=["w1g%d" % nb_])
                        S.dma('pool', out=w2g[nb_][:, :, :], in_=w2_d[l][:, (fg + 1) * 8:(fg + 2) * 8, :], w=["w2g%d" % nb_])
                    for (t0, tw, cv) in tl:
                        hb = hi_ % 2
                        hi_ += 1
                        for hc in range(8):
                            ps, pk = self.psum()
                            for k in range(8):
                                S.op('pe', 'matmul', r=["w1g%d" % wb, "hT:%d" % t0], w=[pk], out=ps[:, :tw],
                                     lhsT=w1g[wb][:, k, hc * 128:(hc + 1) * 128], rhs=hT[:, k, t0:t0 + tw],
                                     start=(k == 0), stop=(k == 7))
                            rb = hc % 2
                            S.op('act', 'activation', r=[pk], w=["rl%d" % rb], out=rl[rb][:, :tw], in_=ps[:, :tw],
                                 func=AF.Relu)
                            S.op('pool', 'tensor_tensor', r=["rl%d" % rb], w=["hid%d:%d" % (hb, hc)],
                                 out=hid[hb][:, hc, :tw], in0=rl[rb][:, :tw], in1=rl[rb][:, :tw], op=ALU.mult)
                        for fo in range(8):
                            ps, pk = self.psum()
                            for hc in range(8):
                                S.op('pe', 'matmul', r=["w2g%d" % wb, "hid%d:%d" % (hb, hc)], w=[pk], out=ps[:, :tw],
                                     lhsT=w2g[wb][:, hc, fo * 128:(fo + 1) * 128], rhs=hid[hb][:, hc, :tw],
                                     start=(hc == 0), stop=(hc == 7))
                            S.op('dve', 'scalar_tensor_tensor', r=[pk, "mod"], w=["xT:%d" % t0],
                                 out=xT[:, fo, t0:t0 + tw], in0=ps[:, :tw], scalar=self.mod[:, l, 40 + fo, cv:cv + 1],
                                 in1=xT[:, fo, t0:t0 + tw], op0=ALU.mult, op1=ALU.add)
                if cfg.get('debug') and bi == 0:
                    self.dump("x%d" % l, xT[:, :, :], [128, 8, TOK], ["xT:%d" % t for t in range(0, TOK, 512)])
                S.barrier()
                M.region(regD, Mem.HI)
            for t0 in range(0, SEQ, 512):
                S.dma('sp', out=outT_d[bi, :, :, t0:t0 + 512], in_=xT[:, :, t0:t0 + 512], r=["xT:%d" % t0], w=["outT:%d" % t0])
            S.barrier()
        S.finish()
        S.emit()
        return nc

    def cut(self, n):
        if self.cfg.get('cut') == n:
            raise StopBuild()

    def rot(self, name, shape, dtype, n):
        tiles = [self.M.alloc(name, shape, dtype) for _ in range(n)]
        uid = self.M.n
        st = {'i': 0}

        def nxt():
            i = st['i'] % n
            st['i'] += 1
            return tiles[i], "%s@%d#%d" % (name, uid, i)
        return nxt

    def rstd_of(self, ps_ap, pk, rows, tw, inv_n, nx_ln, nx_rs):
        S = self.S
        lnt, lk = nx_ln()
        rs, rk = nx_rs()
        S.op('act', 'activation', r=[pk], w=[lk], out=lnt[0:rows, :tw], in_=ps_ap, func=AF.Ln, scale=inv_n,
             bias=self.eps_t[0:rows, 0:1])
        S.op('act', 'activation', r=[lk], w=[rk], out=rs[0:rows, :tw], in_=lnt[0:rows, :tw], func=AF.Exp, scale=-0.5)
        return rs, rk

    def declare_mixer_dram(self):
        self.evwin_d = self.din("evwin", [128, 8, 1440])
        self.wuq_d = self.din("wuq", [128, 2, 768])
        self.wukvn_d = self.din("wukvn", [128, 8, 64])
        self.wukvv_d = self.din("wukvv", [128, 512])
        self.evvec_d = self.din("evvec", [128, 16])
        self.lruvec_d = self.din("lruvec", [128, 4, 12])
        self.lruw_d = self.din("lruw", [128, 4, 2, 2, 128])
        self.ropecs_d = self.din("ropecs", [128, 2, TOK])
        self.shk_d = self.din("shk", [128, 96])
        self.convwb_d = self.din("convwb", [128, 4, 512])
        self.odwin_d = self.din("odwin", [128, 8, 3104])
        self.glaw_d = self.din("glaw", [64, 2, 256])
        self.glau_d = self.din("glau", [128, 6, 128])
        self.odvec_d = self.din("odvec", [128, 4])
        self.tbig_d = self.din("tbig", [128, 8, 16, 64])

    def alloc_mixer_persistent(self):
        M = self.M
        self.evvec = M.alloc("evvec", [128, 16], F32)
        self.lruvec = M.alloc("lruvec", [128, 4, 12], F32)
        self.nsp = M.alloc("nsp", [128, 4, 2], F32)
        self.one_t = M.alloc("one_t", [128, 1], F32)
        self.spt = [M.alloc("spt%d" % i, [128, 4, 2], F32) for i in range(3)]
        self.odvec = M.alloc("odvec", [128, 4], F32)

    def load_mixer_persistent(self):
        S = self.S
        S.dma('sp', out=self.evvec[:, :], in_=self.evvec_d[:, :], w=["evvec"])
        S.dma('sp', out=self.lruvec[:, :, :], in_=self.lruvec_d[:, :, :], w=["lruvec"])
        S.op('dve', 'memset', w=["consts"], ap=self.one_t[:, :], constant=1.0)
        S.dma('sp', out=self.odvec[:, :], in_=self.odvec_d[:, :], w=["odvec"])
        e, t, u = self.spt
        S.op('act', 'activation', r=["lruvec"], w=["spe"], out=e[:, :, :], in_=self.lruvec[:, :, 9:11], func=AF.Exp, scale=-1.0)
        S.op('dve', 'tensor_scalar', r=["spe"], w=["spt"], out=t[:, :, :], in0=e[:, :, :], scalar1=-0.25, scalar2=1.0 / 3.0,
             op0=ALU.mult, op1=ALU.add)
        S.op('dve', 'tensor_tensor', r=["spe", "spt"], w=["spu"], out=u[:, :, :], in0=t[:, :, :], in1=e[:, :, :], op=ALU.mult)
        S.op('dve', 'tensor_scalar', r=["spu"], w=["spt"], out=t[:, :, :], in0=u[:, :, :], scalar1=-1.0, scalar2=0.5,
             op0=ALU.mult, op1=ALU.add)
        S.op('dve', 'tensor_tensor', r=["spe", "spt"], w=["spu"], out=u[:, :, :], in0=t[:, :, :], in1=e[:, :, :], op=ALU.mult)
        S.op('dve', 'tensor_scalar', r=["spu"], w=["spt"], out=t[:, :, :], in0=u[:, :, :], scalar1=8.0, scalar2=-8.0,
             op0=ALU.mult, op1=ALU.add)
        S.op('dve', 'tensor_tensor', r=["spe", "spt"], w=["nsp"], out=self.nsp[:, :, :], in0=t[:, :, :], in1=e[:, :, :], op=ALU.mult)

    def even_mixer(self, bi, hT, yT, xT, regD):
        nc, S, M, cfg = self.nc, self.S, self.M, self.cfg
        ev = self.evvec
        tl = tiles_for(b=bi)
        TK = ["%d" % t for t in range(0, TOK, 512)]
        M.region(self.offD, Mem.HI)
        qn = M.alloc("qn", [128, 2, TOK], BF16)
        kvn = M.alloc("kvn", [128, TOK], BF16)
        krb = M.alloc("krb", [128, TOK], BF16)
        sqkr = M.alloc("sqkr", [128, TOK], BF16)
        css = M.alloc("css", [128, 2, TOK], BF16)
        D1 = M.mark()
        winA = M.alloc("winA", [128, 8, 416], BF16)
        winR = M.alloc("winR", [128, 8, 32], BF16)
        nx_uqr = self.rot("uqr", [128, 2, 512], F32, 1)
        nx_sq2 = self.rot("sq2", [128, 2, 512], BF16, 1)
        nx_ln = self.rot("lnt", [128, 512], F32, 2)
        nx_rs = self.rot("rstd", [128, 512], F32, 2)
        nx_t = self.rot("tt", [128, 512], F32, 3)
        M.region(self.offA, self.offC)
        csf = M.alloc("csf", [128, 2, TOK], F32)
        M.region(D1, Mem.HI)
        for j in range(2):
            S.dma('sp', out=csf[:, j, :], in_=self.ropecs_d[:, j, :], w=["csf%d" % j])
            S.op('dve', 'tensor_scalar', r=["csf%d" % j, "evvec"], w=["css"], out=css[0:96, j, :], in0=csf[0:96, j, :],
                 scalar1=ev[0:96, 5 + j:6 + j], scalar2=None, op0=ALU.mult)
        S.dma('pool', out=winA[:, :, :], in_=self.evwin_d[:, :, 0:416], w=["winA"])
        for hb in range(2):
            o = hb * 16
            S.op('dve', 'tensor_scalar', r=["winA"], w=["winR"], out=winR[:, :, o:o + 8], in0=winA[:, :, 384 + o + 8:384 + o + 16],
                 scalar1=-1.0, scalar2=None, op0=ALU.mult)
            S.op('dve', 'tensor_copy', r=["winA"], w=["winR"], out=winR[:, :, o + 8:o + 16], in_=winA[:, :, 384 + o:384 + o + 8])
        self.cut(1)
        for (t0, tw, cv) in tl:
            hk = "hT:%d" % t0
            uqr, uk = nx_uqr()
            sq2, sk = nx_sq2()
            for c in range(2):
                ps, pk = self.psum()
                for k in range(8):
                    S.op('pe', 'matmul', r=["winA", hk], w=[pk], out=ps[:, :tw], lhsT=winA[:, k, c * 128:(c + 1) * 128],
                         rhs=hT[:, k, t0:t0 + tw], start=(k == 0), stop=(k == 7))
                S.op('act', 'activation', r=[pk], w=[sk + ":%d" % c], out=sq2[:, c, :tw], in_=ps[:, :tw], func=AF.Square)
                S.op('dve', 'tensor_copy', r=[pk], w=[uk + ":%d" % c], out=uqr[:, c, :tw], in_=ps[:, :tw])
            ps, pk = self.psum()
            for c in range(2):
                S.op('pe', 'matmul', r=[sk + ":%d" % c, "consts"], w=[pk], out=ps[:, :tw], lhsT=self.ones_bf[:, :],
                     rhs=sq2[:, c, :tw], start=(c == 0), stop=(c == 1))
            rs, rk = self.rstd_of(ps[:, :tw], pk, 128, tw, 1.0 / 256, nx_ln, nx_rs)
            for c in range(2):
                S.op('dve', 'scalar_tensor_tensor', r=[uk + ":%d" % c, rk, "evvec"], w=["qn:%d" % t0], out=qn[:, c, t0:t0 + tw],
                     in0=uqr[:, c, :tw], scalar=ev[:, c:c + 1], in1=rs[:, :tw], op0=ALU.mult, op1=ALU.mult)
            self.cut(2)
            ps, pk = self.psum()
            for k in range(8):
                S.op('pe', 'matmul', r=["winA", hk], w=[pk], out=ps[:, :tw], lhsT=winA[:, k, 256:384],
                     rhs=hT[:, k, t0:t0 + tw], start=(k == 0), stop=(k == 7))
            sq1, s1k = nx_sq2()
            ur, u1k = nx_t()
            S.op('act', 'activation', r=[pk], w=[s1k + ":0"], out=sq1[:, 0, :tw], in_=ps[:, :tw], func=AF.Square)
            S.op('dve', 'tensor_copy', r=[pk], w=[u1k], out=ur[:, :tw], in_=ps[:, :tw])
            ps2, pk2 = self.psum()
            S.op('pe', 'matmul', r=[s1k + ":0", "consts"], w=[pk2], out=ps2[:, :tw], lhsT=self.ones_bf[:, :], rhs=sq1[:, 0, :tw],
                 start=True, stop=True)
            rs, rk = self.rstd_of(ps2[:, :tw], pk2, 128, tw, 1.0 / 128, nx_ln, nx_rs)
            S.op('dve', 'scalar_tensor_tensor', r=[u1k, rk, "evvec"], w=["kvn:%d" % t0], out=kvn[:, t0:t0 + tw],
                 in0=ur[:, :tw], scalar=ev[:, 2:3], in1=rs[:, :tw], op0=ALU.mult, op1=ALU.mult)
            self.cut(3)
            psA, pkA = self.psum()
            for k in range(8):
                S.op('pe', 'matmul', r=["winA", hk], w=[pkA], out=psA[0:32, :tw], lhsT=winA[:, k, 384:416],
                     rhs=hT[:, k, t0:t0 + tw], start=(k == 0), stop=(k == 7))
            psB, pkB = self.psum()
            for k in range(8):
                S.op('pe', 'matmul', r=["winR", hk], w=[pkB], out=psB[0:32, :tw], lhsT=winR[:, k, :],
                     rhs=hT[:, k, t0:t0 + tw], start=(k == 0), stop=(k == 7))
            S.op('act', 'activation', r=[pkA], w=["sqkr:%d" % t0], out=sqkr[0:32, t0:t0 + tw], in_=psA[0:32, :tw], func=AF.Square)
            t1, t1k = nx_t()
            t2, t2k = nx_t()
            S.op('dve', 'tensor_tensor', r=[pkA, "css"], w=[t1k], out=t1[0:32, :tw], in0=psA[0:32, :tw],
                 in1=css[0:32, 0, t0:t0 + tw], op=ALU.mult)
            S.op('dve', 'tensor_tensor', r=[pkB, "css"], w=[t2k], out=t2[0:32, :tw], in0=psB[0:32, :tw],
                 in1=css[0:32, 1, t0:t0 + tw], op=ALU.mult)
            S.op('pool', 'tensor_tensor', r=[t1k, t2k], w=["krb:%d" % t0], out=krb[0:32, t0:t0 + tw], in0=t1[0:32, :tw],
                 in1=t2[0:32, :tw], op=ALU.add)
        self.cut(4)
        if cfg.get('debug') and bi == 0:
            self.dump("qn", qn[:, :, :], [128, 2, TOK], ["qn:" + t for t in TK])
            self.dump("kvn", kvn[:, :], [128, TOK], ["kvn:" + t for t in TK])
            self.dump("krb", krb[0:32, :], [32, TOK], ["krb:" + t for t in TK])
        S.barrier()
        mbufs = None
        if self.mod_todo:
            keep = (M.top, M.limit)
            M.region(self.offC, self.offC + 4 * TOK * 2)
            mbufs = ([M.alloc("awb%d" % i, [128, 8, 512], BF16) for i in range(2)], [M.alloc("rowt%d" % i, [4, 512], F32) for i in range(1)])
            M.region(*keep)
            for _ in range(3):
                self.mod_group(*self.mod_todo.pop(0), *mbufs)
        if cfg.get('even_stop') == 1:
            return
        M.region(D1, Mem.HI)
        wug = M.alloc("wug", [128, 8, 512], BF16)
        lw = M.alloc("lruw", [128, 4, 2, 2, 128], BF16)
        cwb = M.alloc("cwb", [128, 4, 512], BF16)
        wux = [M.alloc("wux%d" % i, [128, 8, 128], BF16) for i in range(2)]
        S.dma('pool', out=wug[:, :, :], in_=self.evwin_d[:, :, 928:1440], w=["wug"])
        S.dma('pool', out=lw[:, :, :, :, :], in_=self.lruw_d[:, :, :, :, :], w=["lruw"])
        S.dma('pool', out=cwb[:, :, :], in_=self.convwb_d[:, :, :], w=["cwb"])
        M.region(self.offA, self.offC)
        xl = M.alloc("xl", [128, TOK], F32)
        xlb = M.alloc("xlb", [128, TOK], BF16)
        at = M.alloc("at", [128, TOK], F32)
        bx = M.alloc("bx", [128, TOK], F32)
        hf = M.alloc("hf", [128, TOK], F32)
        hb_ = M.alloc("hb", [128, TOK], F32)
        gel = M.alloc("gel", [128, TOK], BF16)
        wj = M.alloc("wj", [128, 4, 8, 128], BF16)
        lv = self.lruvec
        S.dma('pool', out=wux[0][:, :, :], in_=self.evwin_d[:, :, 416:416 + 128], w=["wux0"])
        for c in range(4):
            wx, wxk = wux[c % 2], "wux%d" % (c % 2)
            if c + 1 < 4:
                S.dma('pool', out=wux[(c + 1) % 2][:, :, :], in_=self.evwin_d[:, :, 416 + (c + 1) * 128:416 + (c + 2) * 128],
                      w=["wux%d" % ((c + 1) % 2)])
            for j in range(4):
                for k in range(8):
                    S.op('pool', 'tensor_tensor', r=[wxk, "cwb"], w=["wj"], out=wj[:, j, k, :], in0=wx[:, k, :],
                         in1=cwb[:, j, c * 128:(c + 1) * 128], op=ALU.mult)
            for (t0, tw, cv) in tl:
                s0, e0 = (0, SEQ) if t0 < SEQ else (SEQ, TOK)
                ps, pk = self.psum()
                mm = []
                for j in (2, 1, 0, 3):
                    sft = j - 2
                    lo = max(t0, s0 - sft)
                    hi = min(t0 + tw, e0 - sft)
                    for k in range(8):
                        mm.append((j, k, lo, hi, sft))
                for n, (j, k, lo, hi, sft) in enumerate(mm):
                    S.op('pe', 'matmul', r=["wj"] + ["hT:%d" % t for t in range(0, TOK, 512)], w=[pk], out=ps[:, lo - t0:hi - t0],
                         lhsT=wj[:, j, k, :], rhs=hT[:, k, lo + sft:hi + sft], start=(n == 0), stop=(n == len(mm) - 1))
                S.op('act', 'activation', r=[pk, "lruvec"], w=["xl"], out=xl[:, t0:t0 + tw], in_=ps[:, :tw], func=AF.Identity,
                     bias=lv[:, c, 4:5], scale=1.0)
            for (t0, tw, cv) in tl:
                ps, pk = self.psum()
                for k in range(8):
                    S.op('pe', 'matmul', r=["wug", "hT:%d" % t0], w=[pk], out=ps[:, :tw], lhsT=wug[:, k, c * 128:(c + 1) * 128],
                         rhs=hT[:, k, t0:t0 + tw], start=(k == 0), stop=(k == 7))
                S.op('act', 'activation', r=[pk], w=["gel"], out=gel[:, t0:t0 + tw], in_=ps[:, :tw], func=AF.Gelu_apprx_tanh)
            S.op('pool', 'tensor_copy', r=["xl"], w=["xlb:%d" % t for t in range(0, TOK, 512)], out=xlb[:, :], in_=xl[:, :])
            for d in range(2):
                hx, hxk = (hf, "hf") if d == 0 else (hb_, "hb")
                for (t0, tw, cv) in tl:
                    psa, pka = self.psum()
                    S.op('pe', 'matmul', r=["lruw", "xlb:%d" % t0], w=[pka], out=psa[:, :tw], lhsT=lw[:, c, d, 0, :], rhs=xlb[:, t0:t0 + tw],
                         start=True, stop=True)
                    S.op('act', 'activation', r=[pka, "lruvec"], w=["at"], out=at[:, t0:t0 + tw], in_=psa[:, :tw], func=AF.Sigmoid,
                         bias=lv[:, c, 5 + d:6 + d], scale=1.0)
                    psx, pkx = self.psum()
                    S.op('pe', 'matmul', r=["lruw", "xlb:%d" % t0], w=[pkx], out=psx[:, :tw], lhsT=lw[:, c, d, 1, :], rhs=xlb[:, t0:t0 + tw],
                         start=True, stop=True)
                    S.op('act', 'activation', r=[pkx, "lruvec"], w=["bx"], out=bx[:, t0:t0 + tw], in_=psx[:, :tw], func=AF.Sigmoid,
                         bias=lv[:, c, 7 + d:8 + d], scale=1.0)
                S.op('act', 'activation', r=["at", "nsp"], w=["at"], out=at[:, :], in_=at[:, :], func=AF.Exp, scale=self.nsp[:, c, d:d + 1])
                S.op('dve', 'tensor_tensor', r=["at"], w=[hxk], out=hx[:, :], in0=at[:, :], in1=at[:, :], op=ALU.mult)
                S.op('act', 'activation', r=[hxk], w=[hxk], out=hx[:, :], in_=hx[:, :], func=AF.Sqrt, scale=-1.0, bias=self.one_t[:, 0:1])
                S.op('dve', 'tensor_tensor', r=["bx", "xl"], w=["bx"], out=bx[:, :], in0=bx[:, :], in1=xl[:, :], op=ALU.mult)
                S.op('dve', 'tensor_tensor', r=["bx", hxk], w=["bx"], out=bx[:, :], in0=bx[:, :], in1=hx[:, :], op=ALU.mult)
                if d == 0:
                    S.op('dve', 'tensor_tensor_scan', r=["at", "bx"], w=[hxk], out=hf[:, SEQ:TOK], data0=at[:, SEQ:TOK],
                         data1=bx[:, SEQ:TOK], initial=0.0, op0=ALU.mult, op1=ALU.add)
                    S.op('dve', 'tensor_tensor_scan', r=["at", "bx", hxk], w=[hxk], out=hf[:, 0:SEQ], data0=at[:, 0:SEQ],
                         data1=bx[:, 0:SEQ], initial=hf[:, TOK - 1:TOK], op0=ALU.mult, op1=ALU.add)
                else:
                    S.op('dve', 'tensor_tensor_scan', r=["at", "bx"], w=[hxk], out=hb_[:, SEQ:TOK][:, ::-1],
                         data0=at[:, SEQ:TOK][:, ::-1], data1=bx[:, SEQ:TOK][:, ::-1], initial=0.0, op0=ALU.mult, op1=ALU.add)
                    S.op('dve', 'tensor_tensor_scan', r=["at", "bx", hxk], w=[hxk], out=hb_[:, 0:SEQ][:, ::-1],
                         data0=at[:, 0:SEQ][:, ::-1], data1=bx[:, 0:SEQ][:, ::-1], initial=hb_[:, SEQ:SEQ + 1],
                         op0=ALU.mult, op1=ALU.add)
            S.op('dve', 'tensor_tensor', r=["hf", "hb"], w=["hf"], out=hf[:, :], in0=hf[:, :], in1=hb_[:, :], op=ALU.add)
            S.op('dve', 'tensor_tensor', r=["hf", "gel"], w=["yT:" + t for t in TK], out=yT[:, 4 + c, :], in0=hf[:, :], in1=gel[:, :],
                 op=ALU.mult)
        S.barrier()
        if cfg.get('even_stop') == 2:
            return
        M.region(D1, Mem.HI)
        wuq = M.alloc("wuq", [128, 2, 768], BF16)
        wuqr = M.alloc("wuqr", [128, 2, 8, 96], BF16)
        wkn = M.alloc("wkn", [128, 8, 96], BF16)
        wv = M.alloc("wv", [128, 512], BF16)
        shk = M.alloc("shk", [128, 96], BF16)
        nx_sq = self.rot("sq", [128, 512], BF16, 2)
        nx_ln = self.rot("lnt", [128, 512], F32, 2)
        nx_rs = self.rot("rstd", [128, 512], F32, 2)
        nx_t = self.rot("tt", [128, 512], F32, 2)
        nx_raw = self.rot("qkr", [128, 512], F32, 2)
        nx_rec = self.rot("rec", [128, 1], F32, 4)
        S.dma('pool', out=wuq[:, :, :], in_=self.wuq_d[:, :, :], w=["wuq"])
        S.dma('pool', out=wkn[:, :, 0:64], in_=self.wukvn_d[:, :, :], w=["wkn"])
        S.op('dve', 'memset', w=["wkn"], ap=wkn[:, :, 64:96], constant=0.0)
        S.dma('pool', out=wv[:, :], in_=self.wukvv_d[:, :], w=["wv"])
        S.dma('pool', out=shk[:, :], in_=self.shk_d[:, :], w=["shk"])
        S.op('dve', 'memset', w=["wuqr"], ap=wuqr[:, :, :, 0:64], constant=0.0)
        wuq4 = wuq[:, :, :].rearrange("p c (h e) -> p c h e", e=96)
        for hb in range(2):
            o = 64 + hb * 16
            S.op('dve', 'tensor_scalar', r=["wuq"], w=["wuqr"], out=wuqr[:, :, :, o:o + 8], in0=wuq4[:, :, :, o + 8:o + 16],
                 scalar1=-1.0, scalar2=None, op0=ALU.mult)
            S.op('dve', 'tensor_copy', r=["wuq"], w=["wuqr"], out=wuqr[:, :, :, o + 8:o + 16], in_=wuq4[:, :, :, o:o + 8])
        M.region(self.offB, self.offC)
        PT = [M.alloc("PT%d" % i, [128, 18, 512], BF16) for i in range(2)]
        ytok = M.alloc("ytok", [128, 18, 512], BF16)
        Vx = M.alloc("Vx", [128, 18, 8, 65], BF16)
        QT = [M.alloc("QT%d" % i, [128, TOK], BF16) for i in range(2)]
        KT = [M.alloc("KT%d" % i, [128, TOK], BF16) for i in range(2)]
        KR = 128 if cfg.get('mla_k128', 0) else 96
        if KR == 128:
            for i in range(2):
                S.op('pool', 'memset', w=["QT%d:%d" % (i, t) for t in range(0, TOK, 512)], ap=QT[i][96:128, :], constant=0.0)
                S.op('pool', 'memset', w=["KT%d:%d" % (i, t) for t in range(0, TOK, 512)], ap=KT[i][96:128, :], constant=0.0)
        S.op('pool', 'memset', w=["Vx"], ap=Vx[:, :, :, 64:65], constant=1.0)
        for kc in range(18):
            ps, pk = self.psum(0, 3)
            S.op('pe', 'matmul', r=["kvn:%d" % (kc // 4 * 512), "wv"], w=[pk], out=ps[:, :], lhsT=kvn[:, kc * 128:(kc + 1) * 128],
                 rhs=wv[:, :], start=True, stop=True)
            S.op('dve', 'tensor_copy', r=[pk], w=["Vx"], out=Vx[:, kc, :, 0:64], in_=ps[:, :].rearrange("p (h e) -> p h e", e=64))
        scale = 96.0 ** -0.5

        def proj(h, only=None):
            qt, kt = QT[h % 2], KT[h % 2]
            for ti, (t0, tw, cv) in enumerate(tl):
                if only is not None and ti != only:
                    continue
                psq, pkq = self.psum(6, 8)
                for c in range(2):
                    S.op('pe', 'matmul', r=["wuq", "qn:%d" % t0], w=[pkq], out=psq[0:96, :tw], lhsT=wuq[:, c, h * 96:(h + 1) * 96],
                         rhs=qn[:, c, t0:t0 + tw], start=(c == 0), stop=(c == 1))
                sq, sk = nx_sq()
                S.op('act', 'activation', r=[pkq], w=[sk], out=sq[0:96, :tw], in_=psq[0:96, :tw], func=AF.Square)
                qr, qrk = nx_raw()
                S.op('dve', 'tensor_copy', r=[pkq], w=[qrk], out=qr[0:96, :tw], in_=psq[0:96, :tw])
                psr, pkr = self.psum(6, 8)
                for c in range(2):
                    S.op('pe', 'matmul', r=["wuqr", "qn:%d" % t0], w=[pkr], out=psr[0:96, :tw], lhsT=wuqr[:, c, h, :],
                         rhs=qn[:, c, t0:t0 + tw], start=(c == 0), stop=(c == 1))
                t2, t2k = nx_t()
                S.op('dve', 'tensor_tensor', r=[pkr, "css"], w=[t2k], out=t2[64:96, :tw], in0=psr[64:96, :tw],
                     in1=css[64:96, 1, t0:t0 + tw], op=ALU.mult)
                pss, pks = self.psum(6, 8)
                S.op('pe', 'matmul', r=[sk, "consts"], w=[pks], out=pss[0:96, :tw], lhsT=self.ones_bf[0:96, 0:96], rhs=sq[0:96, :tw],
                     start=True, stop=True)
                rs, rk = self.rstd_of(pss[0:96, :tw], pks, 96, tw, 1.0 / 96, nx_ln, nx_rs)
                S.op('dve', 'scalar_tensor_tensor', r=[qrk, rk, "evvec"], w=["QT%d:%d" % (h % 2, t0)], out=qt[0:64, t0:t0 + tw],
                     in0=qr[0:64, :tw], scalar=ev[0:64, 3:4], in1=rs[0:64, :tw], op0=ALU.mult, op1=ALU.mult)
                t1, t1k = nx_t()
                S.op('pool', 'tensor_tensor', r=[qrk, "css"], w=[t1k], out=t1[64:96, :tw], in0=qr[64:96, :tw],
                     in1=css[64:96, 0, t0:t0 + tw], op=ALU.mult)
                S.op('pool', 'tensor_tensor', r=[t1k, t2k], w=[t1k], out=t1[64:96, :tw], in0=t1[64:96, :tw], in1=t2[64:96, :tw],
                     op=ALU.add)
                S.op('pool', 'tensor_tensor', r=[t1k, rk], w=["QT%d:%d" % (h % 2, t0)], out=qt[64:96, t0:t0 + tw], in0=t1[64:96, :tw],
                     in1=rs[64:96, :tw], op=ALU.mult)
                psk, pkk = self.psum(6, 8)
                S.op('pe', 'matmul', r=["wkn", "kvn:%d" % t0], w=[pkk], out=psk[0:96, :tw], lhsT=wkn[:, h, :], rhs=kvn[:, t0:t0 + tw],
                     start=True, stop=False)
                S.op('pe', 'matmul', r=["shk", "krb:%d" % t0], w=[pkk], out=psk[0:96, :tw], lhsT=shk[0:32, :], rhs=krb[0:32, t0:t0 + tw],
                     start=False, stop=True)
                sq, sk = nx_sq()
                S.op('act', 'activation', r=[pkk], w=[sk], out=sq[0:64, :tw], in_=psk[0:64, :tw], func=AF.Square)
                kr_, krk = nx_raw()
                S.op('dve', 'tensor_copy', r=[pkk], w=[krk], out=kr_[0:96, :tw], in_=psk[0:96, :tw])
                pss, pks = self.psum(6, 8)
                S.op('pe', 'matmul', r=[sk, "consts"], w=[pks], out=pss[0:96, :tw], lhsT=self.ones_bf[0:64, 0:96], rhs=sq[0:64, :tw],
                     start=True, stop=False)
                S.op('pe', 'matmul', r=["sqkr:%d" % t0, "consts"], w=[pks], out=pss[0:96, :tw], lhsT=self.ones_bf[0:32, 0:96],
                     rhs=sqkr[0:32, t0:t0 + tw], start=False, stop=True)
                rs, rk = self.rstd_of(pss[0:96, :tw], pks, 96, tw, 1.0 / 96, nx_ln, nx_rs)
                S.op('dve', 'scalar_tensor_tensor', r=[krk, rk, "evvec"], w=["KT%d:%d" % (h % 2, t0)], out=kt[0:96, t0:t0 + tw],
                     in0=kr_[0:96, :tw], scalar=ev[0:96, 4:5], in1=rs[0:96, :tw], op0=ALU.mult, op1=ALU.mult)

        qtiles = [(t0, tw, list(range(18))) for (t0, tw, cv) in tiles_for(b=bi, with_ctx=False)] + [(SEQ, CTX, [16, 17])]

        NH = cfg.get('mla_heads', 8)
        seq_ = [(h, qi) for h in range(NH) for qi in range(len(qtiles))]
        steps = []
        for it, (h, qi) in enumerate(seq_):
            kcs = qtiles[qi][2]
            for m in range(len(kcs) // 2):
                steps.append((it, h, qi, m, kcs[2 * m], kcs[2 * m + 1], len(kcs) // 2))

        def emit_S(si, step):
            it, h, qi, m, kc0, kc1, npair = step
            t0, tw, kcs = qtiles[qi]
            pt, ptk = PT[it % 2], "PT%d" % (it % 2)
            if m == 0 and qi == 0 and h + 1 < NH:
                proj(h + 1)
            if m == 0 and qi in (1, 3) and mbufs is not None and self.mod_todo:
                self.mod_group(*self.mod_todo.pop(0), *mbufs, banks=(6, 8))
            P = self.pp[si % 2]
            pks = ["ps%d" % (2 * (si % 2)), "ps%d" % (2 * (si % 2) + 1)]
            for j, kc in enumerate((kc0, kc1)):
                S.op('pe', 'matmul', r=["KT%d:%d" % (h % 2, kc // 4 * 512), "QT%d:%d" % (h % 2, t0)], w=[pks[j]], out=P[:, j * 512:j * 512 + tw],
                     lhsT=KT[h % 2][0:96, kc * 128:(kc + 1) * 128], rhs=QT[h % 2][0:96, t0:t0 + tw], start=True, stop=True)
            S.op('act', 'activation', r=pks, w=[ptk + ":%d" % kc0, ptk + ":%d" % kc1], out=pt[:, kc0:kc0 + 2, :tw],
                 in_=P[:, :].rearrange("p (b c) -> p b c", c=512)[:, :, :tw], func=AF.Exp, scale=scale)

        def emit_PV(step):
            it, h, qi, m, kc0, kc1, npair = step
            t0, tw, kcs = qtiles[qi]
            pt, ptk = PT[it % 2], "PT%d" % (it % 2)
            pso, pko = self.ps[4 + it % 2], "ps%d" % (4 + it % 2)
            for j, kc in enumerate((kc0, kc1)):
                n = 2 * m + j
                for sub in range(tw // 128):
                    S.op('pe', 'matmul', r=[ptk + ":%d" % kc, "Vx"], w=[pko], out=pso[:, sub * 65:(sub + 1) * 65],
                         lhsT=pt[:, kc, sub * 128:(sub + 1) * 128], rhs=Vx[:, kc, h, :], start=(n == 0 and sub == 0),
                         stop=(n == 2 * npair - 1), skip_group_check=True)
            if m == npair - 1:
                for sub in range(tw // 128):
                    st = (t0 // 128) + sub
                    rec, rck = nx_rec()
                    S.op('dve', 'reciprocal', r=[pko], w=[rck], out=rec[:, 0:1], in_=pso[:, sub * 65 + 64:sub * 65 + 65])
                    S.op('dve', 'tensor_scalar', r=[pko, rck], w=["ytok:%d" % st], out=ytok[:, st, h * 64:(h + 1) * 64],
                         in0=pso[:, sub * 65:sub * 65 + 64], scalar1=rec[:, 0:1], scalar2=None, op0=ALU.mult)

        LAG = cfg.get('lag', 2)
        proj(0)
        for i, stp in enumerate(steps):
            emit_S(i, stp)
            if i >= LAG:
                emit_PV(steps[i - LAG])
        for stp in steps[max(0, len(steps) - LAG):]:
            emit_PV(stp)
        while mbufs is not None and self.mod_todo:
            self.mod_group(*self.mod_todo.pop(0), *mbufs, banks=(6, 8))
        if cfg.get('debug') and bi == 0:
            self.dump("ytok", ytok[:, :, :], [128, 18, 512], ["ytok:%d" % st for st in range(18)])
        for st in range(18):
            pb = self.psb[st % 2]
            pbk = "ps7"
            for c in range(4):
                S.op('pe', 'transpose', r=["ytok:%d" % st, "consts"], w=[pbk], out=pb[:, c * 128:(c + 1) * 128],
                     in_=ytok[:, st, c * 128:(c + 1) * 128], identity=self.ident_bf[:, :])
            S.op('act', 'activation', r=[pbk], w=["yT:%d" % (st // 4 * 512)], out=yT[:, 0:4, st * 128:(st + 1) * 128],
                 in_=pb[:, 0:512].rearrange("p (c t) -> p c t", t=128), func=AF.Copy)

    def odd_mixer(self, bi, hT, yT, xT, regD):
        nc, S, M, cfg = self.nc, self.S, self.M, self.cfg
        tl = tiles_for(b=bi)
        TK = ["%d" % t for t in range(0, TOK, 512)]
        ov = self.odvec
        for p in range(2 if not cfg.get('skip_gla') else 0):
            M.region(self.offD, Mem.HI)
            wq = M.alloc("gwq", [128, 8, 128], BF16)
            wk = M.alloc("gwk", [128, 8, 128], BF16)
            wvv = M.alloc("gwv", [128, 8, 256], BF16)
            wg = M.alloc("gwg", [128, 8, 256], BF16)
            wlr = M.alloc("gwlr", [128, 8, 32], BF16)
            glau = M.alloc("glau", [128, 4, 128], F32)
            gmask = M.alloc("gmask", [128, 2, 128], BF16)
            glaw = M.alloc("glaw", [64, 2, 256], F32)
            nx_qk = self.rot("qkraw", [128, 512], BF16, 4)
            lrTs = [M.alloc("lrT%d" % i, [64, 512], F32) for i in range(2)]
            gmask4 = M.alloc("gmask4", [128, 2, 4, 128], BF16)
            nx_e = self.rot("ge", [128, 512], F32, 2)
            nx_l = self.rot("gl", [128, 512], F32, 2)
            nx_x = self.rot("gx", [128, 512], F32, 4)
            nx_kt = self.rot("gkt", [128, 512], F32, 2)
            nx_am = self.rot("attm", [128, 512], BF16, 4)
            nx_sq = self.rot("gsq", [128, 512], BF16, 2)
            nx_ln = self.rot("glnt", [128, 512], F32, 2)
            od = self.odwin_d
            S.dma('pool', out=wq[:, :, :], in_=od[:, :, p * 128:(p + 1) * 128], w=["gwq"])
            S.dma('pool', out=wk[:, :, :], in_=od[:, :, 256 + p * 128:256 + (p + 1) * 128], w=["gwk"])
            S.dma('pool', out=wvv[:, :, :], in_=od[:, :, 512 + p * 256:512 + (p + 1) * 256], w=["gwv"])
            S.dma('pool', out=wg[:, :, :], in_=od[:, :, 1024 + p * 256:1024 + (p + 1) * 256], w=["gwg"])
            S.dma('pool', out=wlr[:, :, :], in_=od[:, :, 1536:1568], w=["gwlr"])
            S.dma('sp', out=glau[:, :, :], in_=self.glau_d[:, 0:4, :], w=["glau"])
            S.dma('pool', out=gmask[:, :, :], in_=self.glau_d[:, 4:6, :], w=["gmask"])
            for j in range(4):
                S.dma('pool', out=gmask4[:, :, j, :], in_=self.glau_d[:, 4:6, :], w=["gmask4"])
            S.dma('sp', out=glaw[:, :, :], in_=self.glaw_d[:, :, :], w=["glaw"])
            for i_ in range(2):
                S.op('dve', 'memset', w=["lrT%d" % i_], ap=lrTs[i_][:, :], constant=1.0)
            M.region(self.offA, self.offC)
            kd = [M.alloc("kd%d" % d, [128, 18, 128], BF16) for d in range(2)]
            vt = M.alloc("vtok", [128, 18, 256], BF16)
            qd = [M.alloc("qdec%d" % d, [128, SEQ], BF16) for d in range(2)]
            ki = [M.alloc("kinv%d" % d, [128, SEQ], BF16) for d in range(2)]
            sh = [M.alloc("shist%d" % d, [128, 32, 128], BF16) for d in range(2)]
            sg = M.alloc("sg", [128, 2, SEQ], BF16)
            dec = M.alloc("dec", [128, 2, 36], F32)
            sst = [[M.alloc("sst%d%d" % (d, i), [128, 128], F32) for i in range(2)] for d in range(2)]
            nx_rs = self.rot("grstd", [128, 512], F32, 2)
            nx_t = self.rot("gt1", [128, 512], F32, 2)
            def g1_a(t0, tw, cv):
                lrT = lrTs[(t0 // 512) % 2]
                lrk = "lrT%d" % ((t0 // 512) % 2)
                qraw = kraw = qk_ = kk_ = None
                hk = "hT:%d" % t0
                lat = t0 < SEQ
                if lat:
                    qraw, qk_ = nx_qk()
                    ps, pk = self.psum()
                    for k in range(8):
                        S.op('pe', 'matmul', r=["gwq", hk], w=[pk], out=ps[:, :tw], lhsT=wq[:, k, :], rhs=hT[:, k, t0:t0 + tw],
                             start=(k == 0), stop=(k == 7))
                    S.op('act', 'activation', r=[pk], w=[qk_], out=qraw[:, :tw], in_=ps[:, :tw], func=AF.Copy, scale=0.125)
                    kraw, kk_ = nx_qk()
                    ps, pk = self.psum()
                    for k in range(8):
                        S.op('pe', 'matmul', r=["gwk", hk], w=[pk], out=ps[:, :tw], lhsT=wk[:, k, :], rhs=hT[:, k, t0:t0 + tw],
                             start=(k == 0), stop=(k == 7))
                    S.op('act', 'activation', r=[pk], w=[kk_], out=kraw[:, :tw], in_=ps[:, :tw], func=AF.Copy)
                    for hl in range(2):
                        ps, pk = self.psum()
                        for k in range(8):
                            S.op('pe', 'matmul', r=["gwg", hk], w=[pk], out=ps[:, :tw], lhsT=wg[:, k, hl * 128:(hl + 1) * 128],
                                 rhs=hT[:, k, t0:t0 + tw], start=(k == 0), stop=(k == 7))
                        S.op('act', 'activation', r=[pk], w=["sg:%d" % t0], out=sg[:, hl, t0:t0 + tw], in_=ps[:, :tw], func=AF.Silu)
                ps, pk = self.psum()
                for k in range(8):
                    S.op('pe', 'matmul', r=["gwlr", hk], w=[pk], out=ps[0:32, :tw], lhsT=wlr[:, k, :],
                         rhs=hT[:, k, t0:t0 + tw], start=(k == 0), stop=(k == 7))
                S.op('act', 'activation', r=[pk], w=[lrk], out=lrT[0:32, :tw], in_=ps[0:32, :tw], func=AF.Copy)
                nsub = tw // 128
                st0 = t0 // 128
                psk, pkk = self.psum()
                for sub in range(nsub):
                    c0 = sub * 128
                    for k in range(8):
                        S.op('pe', 'matmul', r=["gwk", hk], w=[pkk], out=psk[:, c0:c0 + 128], lhsT=hT[:, k, t0 + c0:t0 + c0 + 128], rhs=wk[:, k, :],
                             start=(k == 0), stop=(k == 7))
                ktk, ktkk = nx_kt()
                S.op('act', 'activation', r=[pkk], w=[ktkk], out=ktk[:, :tw], in_=psk[:, :tw], func=AF.Copy)
                for s2 in range(0, nsub, 2):
                    psv, pkv = self.psum()
                    for sub in range(s2, min(s2 + 2, nsub)):
                        c0 = sub * 128
                        for k in range(8):
                            S.op('pe', 'matmul', r=["gwv", hk], w=[pkv], out=psv[:, (sub - s2) * 256:(sub - s2 + 1) * 256],
                                 lhsT=hT[:, k, t0 + c0:t0 + c0 + 128], rhs=wvv[:, k, :], start=(k == 0), stop=(k == 7))
                    ns = min(2, nsub - s2)
                    S.op('act', 'activation', r=[pkv], w=["vtok:%d" % (st0 + s2 + i) for i in range(ns)],
                         out=vt[:, st0 + s2:st0 + s2 + ns, :], in_=psv[:, 0:ns * 256].rearrange("p (a b) -> p a b", b=256), func=AF.Copy)
                return dict(qraw=qraw, kraw=kraw, qk_=qk_, kk_=kk_, lrT=lrT, lrk=lrk, ktk=ktk, ktkk=ktkk, hk=hk, lat=lat, nsub=nsub, st0=st0)

            def g1_b(t0, tw, cv, st_):
                qraw, kraw, qk_, kk_, lrT, lrk, ktk, ktkk, hk, lat, nsub, st0 = (st_[k_] for k_ in ('qraw', 'kraw', 'qk_', 'kk_', 'lrT', 'lrk', 'ktk', 'ktkk', 'hk', 'lat', 'nsub', 'st0'))
                for d in range(2):
                    psz, pkz = self.psum()
                    for sub in range(nsub):
                        c0 = sub * 128
                        S.op('pe', 'matmul', r=[lrk, "glaw"], w=[pkz], out=psz[:, c0:c0 + 128], lhsT=lrT[0:33, c0:c0 + 128],
                             rhs=glaw[0:33, d, p * 128:(p + 1) * 128], start=True, stop=True)
                    e, ek = nx_e()
                    l, lk = nx_l()
                    S.op('act', 'activation', r=[pkz], w=[ek], out=e[:, :tw], in_=psz[:, :tw], func=AF.Exp, scale=-1.0)
                    S.op('act', 'activation', r=[ek], w=[lk], out=l[:, :tw], in_=e[:, :tw], func=AF.Ln, bias=self.one_t[:, 0:1], scale=1.0)
                    psr, pkr = self.psum()
                    for sub in range(nsub):
                        c0 = sub * 128
                        S.op('pe', 'matmul', r=[lk, "glau"], w=[pkr], out=psr[:, c0:c0 + 128], lhsT=glau[:, 2 + d, :], rhs=l[:, c0:c0 + 128],
                             start=True, stop=True)
                    er, erk = nx_x()
                    S.op('act', 'activation', r=[pkr], w=[erk], out=er[:, :tw], in_=psr[:, :tw], func=AF.Exp)
                    S.op('dve', 'tensor_tensor', r=[ktkk, erk], w=["kd%d:%d" % (d, st0 + i) for i in range(nsub)],
                         out=kd[d][:, st0:st0 + nsub, :], in0=ktk[:, :tw].rearrange("p (a b) -> p a b", b=128),
                         in1=er[:, :tw].rearrange("p (a b) -> p a b", b=128), op=ALU.mult)
                    psc, pkc = self.psum()
                    for sub in range(nsub):
                        c0 = sub * 128
                        S.op('pe', 'matmul', r=[lk, "glau"], w=[pkc], out=psc[:, c0:c0 + 128], lhsT=l[:, c0:c0 + 128], rhs=glau[:, d, :],
                             start=True, stop=True)
                    ep, epk = nx_x()
                    S.op('act', 'activation', r=[pkc], w=[epk], out=ep[:, :tw], in_=psc[:, :tw], func=AF.Exp)
                    g0_ = 2 * st0
                    S.op('dve', 'tensor_copy', r=[epk], w=["dec"], out=dec[:, d, g0_:g0_ + 2 * nsub],
                         in_=ep[:, :tw].rearrange("p (a b) -> p a b", b=64)[:, :, 63 if d == 0 else 0])
                    if lat:
                        em, emk = nx_x()
                        S.op('act', 'activation', r=[pkc], w=[emk], out=em[:, :tw], in_=psc[:, :tw], func=AF.Exp, scale=-1.0)
                        S.op('dve', 'tensor_tensor', r=[qk_, epk], w=["qdec%d:%d" % (d, st0 + i) for i in range(nsub)], out=qd[d][:, t0:t0 + tw],
                             in0=qraw[:, :tw], in1=ep[:, :tw], op=ALU.mult)
                        S.op('pool', 'tensor_tensor', r=[kk_, emk], w=["kinv%d:%d" % (d, st0 + i) for i in range(nsub)], out=ki[d][:, t0:t0 + tw],
                             in0=kraw[:, :tw], in1=em[:, :tw], op=ALU.mult)

            pend_ = g1_a(*tl[0])
            for ti_, tile_ in enumerate(tl):
                nxt2_ = g1_a(*tl[ti_ + 1]) if ti_ + 1 < len(tl) else None
                g1_b(*tile_, pend_)
                pend_ = nxt2_
            if cfg.get('debug') and bi == 0 and p == 0:
                self.dump("dec", dec[:, :, :], [128, 2, 36], ["dec"])
                for d in range(2):
                    self.dump("kd%d" % d, kd[d][:, :, :], [128, 18, 128], ["kd%d:%d" % (d, st) for st in range(18)])
                    self.dump("qd%d" % d, qd[d][:, :], [128, SEQ], ["qdec%d:%d" % (d, st) for st in range(16)])
                    self.dump("ki%d" % d, ki[d][:, :], [128, SEQ], ["kinv%d:%d" % (d, st) for st in range(16)])
                self.dump("vt", vt[:, :, :], [128, 18, 256], ["vtok:%d" % st for st in range(18)])
            self.cut(11)
            orders = [[32, 33, 34, 35] + list(range(32)), [35, 34, 33, 32] + list(range(31, -1, -1))]
            for d in range(2):
                S.op('dve', 'memset', w=["sst%d0" % d], ap=sst[d][0][:, :], constant=0.0)
            for n in range(36):
                for d in range(2):
                    g = orders[d][n]
                    st, half = g // 2, g % 2
                    cur, nxt_ = sst[d][n % 2], sst[d][(n + 1) % 2]
                    ck, nk_ = "sst%d%d" % (d, n % 2), "sst%d%d" % (d, (n + 1) % 2)
                    if g < 32:
                        S.op('pool', 'tensor_copy', r=[ck], w=["shist%d:%d" % (d, g)], out=sh[d][:, g, :], in_=cur[:, :])
                    if n == 35:
                        continue
                    ps, pk = self.psum()
                    r0_ = half * 64
                    S.op('pe', 'matmul', r=["kd%d:%d" % (d, st), "vtok:%d" % st], w=[pk], out=ps[:, 0:256],
                         lhsT=kd[d][r0_:r0_ + 64, st, :], rhs=vt[r0_:r0_ + 64, st, :], start=True, stop=True)
                    for hl in range(2):
                        rr = slice(hl * 64, (hl + 1) * 64)
                        S.op('dve', 'scalar_tensor_tensor', r=[ck, pk, "dec"], w=[nk_], out=nxt_[rr, :], in0=cur[rr, :],
                             scalar=dec[rr, d, g:g + 1], in1=ps[rr, hl * 128:(hl + 1) * 128], op0=ALU.mult, op1=ALU.add)
            if cfg.get('debug') and bi == 0 and p == 0:
                for d in range(2):
                    self.dump("sh%d" % d, sh[d][:, :, :], [128, 32, 128], ["shist%d:%d" % (d, g) for g in range(32)])
            self.cut(12)
            for grp in range(4):
                T0 = grp * 512
                for hl in range(2):
                    rr = slice(hl * 64, (hl + 1) * 64)
                    ams = []
                    for d in range(2):
                        psa, pka = self.psum()
                        for j in range(4):
                            st = grp * 4 + j
                            tt0 = st * 128
                            S.op('pe', 'matmul', r=["kinv%d:%d" % (d, st), "qdec%d:%d" % (d, st)], w=[pka], out=psa[:, j * 128:(j + 1) * 128],
                                 lhsT=ki[d][rr, tt0:tt0 + 128], rhs=qd[d][rr, tt0:tt0 + 128], start=True, stop=True)
                        am, amk = nx_am()
                        S.op('dve', 'tensor_tensor', r=[pka, "gmask4"], w=[amk], out=am[:, :], in0=psa[:, :],
                             in1=gmask4[:, d, :, :].rearrange("p a b -> p (a b)"), op=ALU.mult)
                        ams.append((am, amk))
                    pso, pko = self.psum()
                    for j in range(4):
                        st = grp * 4 + j
                        tt0 = st * 128
                        for d in range(2):
                            S.op('pe', 'matmul', r=[ams[d][1], "vtok:%d" % st], w=[pko], out=pso[:, j * 128:(j + 1) * 128],
                                 lhsT=vt[:, st, hl * 128:(hl + 1) * 128], rhs=ams[d][0][:, j * 128:(j + 1) * 128], start=(d == 0), stop=False)
                        for d in range(2):
                            for half in range(2):
                                g = 2 * st + half
                                S.op('pe', 'matmul', r=["shist%d:%d" % (d, g), "qdec%d:%d" % (d, st)], w=[pko],
                                     out=pso[:, j * 128 + half * 64:j * 128 + (half + 1) * 64], lhsT=sh[d][rr, g, :],
                                     rhs=qd[d][rr, tt0 + half * 64:tt0 + (half + 1) * 64], start=False, stop=(d == 1 and half == 1))
                    sq, sk = nx_sq()
                    S.op('act', 'activation', r=[pko], w=[sk], out=sq[:, :], in_=pso[:, :], func=AF.Square)
                    pss, pks = self.psum()
                    S.op('pe', 'matmul', r=[sk, "consts"], w=[pks], out=pss[:, :], lhsT=self.ones_bf[:, :], rhs=sq[:, :], start=True, stop=True)
                    rs, rk = self.rstd_of(pss[:, :], pks, 128, 512, 1.0 / 128, nx_ln, nx_rs)
                    t1, t1k = nx_t()
                    S.op('dve', 'scalar_tensor_tensor', r=[pko, rk, "odvec"], w=[t1k], out=t1[:, :], in0=pso[:, :], scalar=ov[:, 0:1],
                         in1=rs[:, :], op0=ALU.mult, op1=ALU.mult)
                    S.op('pool', 'tensor_tensor', r=[t1k, "sg:%d" % T0], w=["yT:%d" % T0],
                         out=yT[:, 2 * p + hl, T0:T0 + 512], in0=t1[:, :], in1=sg[:, hl, T0:T0 + 512], op=ALU.mult)
            S.barrier()
        if cfg.get('skip_nat'):
            return
        M.region(self.offD, Mem.HI)
        tb = M.alloc("tbig", [128, 8, 16, 64], BF16)
        wnq_ = [M.alloc("wnq%d" % i, [128, 8, 128], BF16) for i in range(2)]
        wnk_ = [M.alloc("wnk%d" % i, [128, 8, 128], BF16) for i in range(2)]
        wnv = M.alloc("wnv", [128, 8, 512], BF16)
        ones2 = M.alloc("ones2", [128, 128], BF16)
        PTn = [M.alloc("PTn%d" % i, [128, 7, 512], BF16) for i in range(2)]
        nx_sb = self.rot("nsb", [128, 512], F32, 2)
        nx_sq = self.rot("nsq", [128, 512], BF16, 2)
        nx_ln = self.rot("nlnt", [128, 512], F32, 1)
        nx_rs = self.rot("nrstd", [128, 512], F32, 2)
        nx_rec = self.rot("nrec", [128, 4], F32, 4)
        M.region(self.offA, self.offC)
        QnT = M.alloc("QnT", [128, 4, SEQ], BF16)
        KnT = M.alloc("KnT", [128, 4, TOK], BF16)
        Vn = M.alloc("Vn", [128, 18, 8, 65], BF16)
        ytok = M.alloc("nytok", [128, 16, 512], BF16)
        od = self.odwin_d
        for hh in range(4):
            S.dma('pool', out=tb[:, 2 * hh:2 * hh + 2, :, :], in_=self.tbig_d[:, 2 * hh:2 * hh + 2, :, :], w=["tbig"])
        S.op('dve', 'tensor_scalar', r=["tbig"], w=["tbig"], out=tb[:, :, :, :], in0=tb[:, :, :, :], scalar1=8.0, scalar2=None, op0=ALU.mult)
        S.dma('pool', out=wnv[:, :, :], in_=od[:, :, 2592:3104], w=["wnv"])
        S.op('dve', 'memset', w=["ones2"], ap=ones2[:, :], constant=0.0)
        S.op('dve', 'memset', w=["ones2"], ap=ones2[0:64, 0:64], constant=1.0)
        S.op('dve', 'memset', w=["ones2"], ap=ones2[64:128, 64:128], constant=1.0)
        S.op('pool', 'memset', w=["Vn"], ap=Vn[:, :, :, :], constant=1.0)
        for st in range(18):
            ps, pk = self.psum()
            for k in range(8):
                S.op('pe', 'matmul', r=["wnv", "hT:%d" % (st // 4 * 512)], w=[pk], out=ps[:, :], lhsT=hT[:, k, st * 128:(st + 1) * 128],
                     rhs=wnv[:, k, :], start=(k == 0), stop=(k == 7))
            S.op('act', 'activation', r=[pk], w=["Vn"], out=Vn[:, st, :, 0:64], in_=ps[:, :].rearrange("p (h e) -> p h e", e=64), func=AF.Copy)
        groups = []
        for pr in range(4):
            for (t0, tw, cv) in tl:
                for which in range(2):
                    if which == 0 and t0 >= SEQ:
                        continue
                    groups.append((pr, t0, tw, which))
        loaded = set()

        def nproj_a(g):
            pr, t0, tw, which = g
            wnq, wnk = wnq_[pr % 2], wnk_[pr % 2]
            if pr not in loaded:
                loaded.add(pr)
                S.dma('pool', out=wnq[:, :, :], in_=od[:, :, 1568 + pr * 128:1568 + (pr + 1) * 128], w=["wnq%d" % (pr % 2)])
                S.dma('pool', out=wnk[:, :, :], in_=od[:, :, 2080 + pr * 128:2080 + (pr + 1) * 128], w=["wnk%d" % (pr % 2)])
            wt, wkey = (wnq, "wnq%d" % (pr % 2)) if which == 0 else (wnk, "wnk%d" % (pr % 2))
            ps, pk = self.psum()
            for k in range(8):
                S.op('pe', 'matmul', r=[wkey, "hT:%d" % t0], w=[pk], out=ps[:, :tw], lhsT=wt[:, k, :], rhs=hT[:, k, t0:t0 + tw],
                     start=(k == 0), stop=(k == 7))
            sq, sk = nx_sq()
            S.op('act', 'activation', r=[pk], w=[sk], out=sq[:, :tw], in_=ps[:, :tw], func=AF.Square)
            return ps, pk, sq, sk

        def nproj_b(g, st_):
            pr, t0, tw, which = g
            ps, pk, sq, sk = st_
            pss, pks = self.psum()
            S.op('pe', 'matmul', r=[sk, "ones2"], w=[pks], out=pss[:, :tw], lhsT=ones2[:, :], rhs=sq[:, :tw], start=True, stop=True)
            rs, rk = self.rstd_of(pss[:, :tw], pks, 128, tw, 1.0 / 64, nx_ln, nx_rs)
            if which == 0:
                S.op('dve', 'scalar_tensor_tensor', r=[pk, rk, "odvec"], w=["QnT:%d" % t0], out=QnT[:, pr, t0:t0 + tw], in0=ps[:, :tw],
                     scalar=ov[:, 1:2], in1=rs[:, :tw], op0=ALU.mult, op1=ALU.mult)
            else:
                S.op('dve', 'scalar_tensor_tensor', r=[pk, rk, "odvec"], w=["KnT:%d" % t0], out=KnT[:, pr, t0:t0 + tw], in0=ps[:, :tw],
                     scalar=ov[:, 2:3], in1=rs[:, :tw], op0=ALU.mult, op1=ALU.mult)

        pend = nproj_a(groups[0])
        for gi_, g in enumerate(groups):
            nxt_ = nproj_a(groups[gi_ + 1]) if gi_ + 1 < len(groups) else None
            nproj_b(g, pend)
            pend = nxt_
        self.cut(21)

        def r0f(r):
            return min(max(r - 4, 0), 24)
        iters = [(qt, hg) for qt in range(cfg.get('nat_tiles', 16)) for hg in range(2)]

        def kcs_of(qt):
            rlo = r0f(2 * qt)
            rhi = r0f(2 * qt + 1) + 8
            return list(range(rlo // 2, (rhi - 1) // 2 + 1)) + [16, 17]

        def nat_S(it):
            qt, hg = iters[it]
            kcs = kcs_of(qt)
            pt, ptk = PTn[it % 2], "PTn%d" % (it % 2)
            for j, kc in enumerate(kcs):
                ps, pk = self.psum(0, 3)
                for hh in range(4):
                    h = hh * 2 + hg
                    pr, rr = h // 2, slice((h % 2) * 64, (h % 2) * 64 + 64)
                    S.op('pe', 'matmul', r=["KnT:%d" % (kc // 4 * 512), "QnT:%d" % (qt // 4 * 512)], w=[pk], out=ps[:, hh * 128:(hh + 1) * 128],
                         lhsT=KnT[rr, pr, kc * 128:(kc + 1) * 128], rhs=QnT[rr, pr, qt * 128:(qt + 1) * 128], start=True, stop=True)
                if kc < 16:
                    Dd = 2 * kc - 2 * qt + 7
                    sb, sbk = nx_sb()
                    S.op('dve', 'tensor_tensor', r=[pk, "tbig"], w=[sbk], out=sb[:, :].rearrange("p (h a c) -> p h a c", a=2, c=64),
                         in0=ps[:, :].rearrange("p (h a c) -> p h a c", a=2, c=64),
                         in1=tb[:, hg:8:2, Dd - 1:Dd + 1, :][:, :, ::-1, :], op=ALU.add)
                    S.op('act', 'activation', r=[sbk], w=[ptk + ":%d" % j], out=pt[:, j, :], in_=sb[:, :], func=AF.Exp, scale=0.125)
                    for i in range(2):
                        for a in range(2):
                            krow, qrow = 2 * kc + i, 2 * qt + a
                            if not (r0f(qrow) <= krow < r0f(qrow) + 8) and not cfg.get('nat_nomemset'):
                                S.op('pool', 'memset', r=[], w=[ptk + ":%d" % j],
                                     ap=pt[i * 64:(i + 1) * 64, j, :].rearrange("p (h q) -> p h q", q=128)[:, :, a * 64:(a + 1) * 64], constant=0.0)
                else:
                    S.op('act', 'activation', r=[pk], w=[ptk + ":%d" % j], out=pt[:, j, :], in_=ps[:, :], func=AF.Exp, scale=0.125)

        def nat_PV(it):
            qt, hg = iters[it]
            if cfg.get('nat_nopv'):
                return
            kcs = kcs_of(qt)
            pt, ptk = PTn[it % 2], "PTn%d" % (it % 2)
            pso, pko = self.psum(3, 6)
            for hh in range(4):
                h = hh * 2 + hg
                for j, kc in enumerate(kcs):
                    S.op('pe', 'matmul', r=[ptk + ":%d" % j, "Vn"], w=[pko], out=pso[:, hh * 65:(hh + 1) * 65], lhsT=pt[:, j, hh * 128:(hh + 1) * 128],
                         rhs=Vn[:, kc, h, :], start=(j == 0), stop=(j == len(kcs) - 1))
            rec, rck = nx_rec()
            S.op('dve', 'reciprocal', r=[pko], w=[rck], out=rec[:, 0:4], in_=pso[:, 0:260].rearrange("p (h e) -> p h e", e=65)[:, :, 64])
            for hh in range(4):
                h = hh * 2 + hg
                S.op('dve', 'tensor_scalar', r=[pko, rck], w=["nytok:%d" % qt], out=ytok[:, qt, h * 64:(h + 1) * 64], in0=pso[:, hh * 65:hh * 65 + 64],
                     scalar1=rec[:, hh:hh + 1], scalar2=None, op0=ALU.mult)
            if hg == 1:
                pb = self.psb[qt % 2]
                pbk = "ps7"
                for c in range(4):
                    S.op('pe', 'transpose', r=["nytok:%d" % qt, "consts"], w=[pbk], out=pb[:, c * 128:(c + 1) * 128],
                         in_=ytok[:, qt, c * 128:(c + 1) * 128], identity=self.ident_bf[:, :])
                S.op('act', 'activation', r=[pbk], w=["yT:%d" % (qt // 4 * 512)], out=yT[:, 4:8, qt * 128:(qt + 1) * 128],
                     in_=pb[:, 0:512].rearrange("p (c t) -> p c t", t=128), func=AF.Copy)

        if iters:
            nat_S(0)
        for it in range(len(iters)):
            if it + 1 < len(iters):
                nat_S(it + 1)
            nat_PV(it)


def fm(w):
    K = w.shape[0] // 128
    return np.ascontiguousarray(w.reshape(K, 128, -1).transpose(1, 0, 2))


def vec_fm(v):
    return np.ascontiguousarray(v.reshape(-1, 128).T)


def host_inputs(inp, core, extra=None):
    b0 = core * NB
    f32 = np.float32
    m = {}
    x = inp['x'][b0:b0 + NB]
    m['xT'] = np.ascontiguousarray(x.transpose(0, 2, 1).reshape(NB, 8, 128, SEQ).transpose(0, 2, 1, 3)).astype(f32)
    cx = inp['ctx'][b0:b0 + NB]
    m['ctxT'] = np.ascontiguousarray(cx.transpose(0, 2, 1).reshape(NB, 8, 128, CTX).transpose(0, 2, 1, 3)).astype(f32)
    cv = np.stack([inp['c'][b0], inp['c'][b0 + 1], inp['c_ctx'], inp['c_ctx']], axis=-1)
    m['cvT'] = fm(cv).astype(f32)
    for l in range(2):
        m['adaw%d' % l] = fm(inp['ada_w'][l]).astype(f32)
        m['wout%d' % l] = fm(inp['w_out'][l]).astype(f32)
        m['w1_%d' % l] = fm(inp['mlp_w1'][l]).astype(f32)
        m['w2_%d' % l] = fm(inp['mlp_w2'][l]).astype(f32)
    m['adab'] = np.ascontiguousarray(np.stack([vec_fm(inp['ada_b'][l]) for l in range(2)], axis=1)).astype(f32)
    m['nmix'] = np.ascontiguousarray(np.stack([vec_fm(inp['norm_mix'][l]) for l in range(2)], axis=1)).astype(f32)
    m['nmlp'] = np.ascontiguousarray(np.stack([vec_fm(inp['norm_mlp'][l]) for l in range(2)], axis=1)).astype(f32)
    m['ident'] = np.eye(128, dtype=f32)
    m['evwin'] = fm(inp['ev_w_in'][0]).astype(f32)
    m['wuq'] = fm(inp['mla_w_uq'][0]).astype(f32)
    wukv = inp['mla_w_ukv'][0].reshape(128, 8, 128)
    m['wukvn'] = np.ascontiguousarray(wukv[:, :, 0:64]).astype(f32)
    m['wukvv'] = np.ascontiguousarray(wukv[:, :, 64:128].reshape(128, 512)).astype(f32)
    qg = inp['mla_q_gain'][0]
    kg = inp['mla_k_gain'][0]
    perm = np.array([(i + 8) if (i % 16) < 8 else (i - 8) for i in range(32)])
    ev = np.ones((128, 16), f32)
    ev[:, 0:2] = vec_fm(inp['mla_q_norm'][0])
    ev[:, 2] = inp['mla_kv_norm'][0]
    ev[0:96, 3] = qg
    ev[0:64, 4] = kg[0:64]
    ev[0:32, 5] = kg[64:96]
    ev[64:96, 5] = qg[64:96]
    ev[0:32, 6] = kg[64:96][perm]
    ev[64:96, 6] = qg[64:96][perm]
    m['evvec'] = ev
    lv = np.zeros((128, 4, 12), f32)
    for j in range(4):
        lv[:, :, j] = vec_fm(inp['lru_conv_w'][0][j])
    lv[:, :, 4] = vec_fm(inp['lru_conv_b'][0])
    for d in range(2):
        lv[:, :, 5 + d] = vec_fm(inp['lru_b_a'][0][d])
        lv[:, :, 7 + d] = vec_fm(inp['lru_b_x'][0][d])
        lv[:, :, 9 + d] = vec_fm(inp['lru_lam'][0][d])
    m['lruvec'] = lv
    m['convwb'] = np.ascontiguousarray(np.broadcast_to(inp['lru_conv_w'][0][None, :, :], (128, 4, 512))).astype(f32)
    lw = np.zeros((128, 4, 2, 2, 128), f32)
    for c in range(4):
        for d in range(2):
            for g, nm in enumerate(('lru_w_a', 'lru_w_x')):
                for hb in range(2):
                    lw[hb * 64:(hb + 1) * 64, c, d, g, hb * 64:(hb + 1) * 64] = inp[nm][0][d][2 * c + hb]
    m['lruw'] = lw
    m['ropecs'] = rope_tables()
    shk = np.zeros((128, 96), f32)
    for k in range(32):
        shk[k, 64 + k] = 1.0
    m['shk'] = shk
    m['odwin'] = fm(inp['od_w_in'][0]).astype(f32)
    gw = np.zeros((64, 2, 256), f32)
    for d in range(2):
        gw[d * 16:(d + 1) * 16, d, :] = inp['gla_w_a'][0][d]
        gw[32, d, :] = inp['gla_b_a'][0][d]
    m['glaw'] = gw
    m['glau'] = gla_consts()
    ovv = np.ones((128, 4), f32)
    ovv[:, 0] = inp['gla_o_gain'][0]
    ovv[:, 1] = np.tile(inp['na_q_gain'][0], 2)
    ovv[:, 2] = np.tile(inp['na_k_gain'][0], 2)
    m['odvec'] = ovv
    m['tbig'] = natten_table(inp['na_rpb'][0])
    return m


def gla_consts():
    f32 = np.float32
    u = np.zeros((128, 6, 128), f32)
    sp = np.arange(128)[:, None]
    t = np.arange(128)[None, :]
    same = (sp // 64) == (t // 64)
    u[:, 0, :] = np.where(same & (sp <= t), -1.0 / 16, 0.0)
    u[:, 1, :] = np.where(same & (sp >= t), -1.0 / 16, 0.0)
    u[:, 2, :] = np.where(same & (sp > t), -1.0 / 16, 0.0)
    u[:, 3, :] = np.where(same & (sp < t), -1.0 / 16, 0.0)
    u[:, 4, :] = np.where(same & (sp <= t), 1.0, 0.0)
    u[:, 5, :] = np.where(same & (sp >= t), 1.0, 0.0)
    return u


def natten_table(rpb):
    f32 = np.float32
    tbl = np.full((128, 8, 16, 64), -30000.0, f32)
    p = np.arange(128)
    i = p // 64
    kcol = p % 64
    c = np.arange(64)
    c0 = np.clip(c - 8, 0, 48)
    valid = (kcol[:, None] >= c0[None, :]) & (kcol[:, None] < c0[None, :] + 16)
    dj = np.clip(kcol[:, None] - c[None, :] + 15, 0, 30)
    for e in range(16):
        di = e + i
        ok = valid & (di[:, None] <= 14)
        dic = np.clip(di, 0, 14)
        for h in range(8):
            g = rpb[h][dic[:, None], dj]
            tbl[:, h, e, :] = np.where(ok, g, tbl[:, h, e, :])
    return tbl


def rope_tables():
    f32 = np.float32
    half = 16
    inv = 10000.0 ** (-np.arange(0, half, 2, dtype=np.float64) / half)
    pos = np.arange(SEQ)
    cs = np.zeros((128, 2, TOK), f32)
    tab = np.zeros((32, 2, TOK), np.float64)
    tab[:, 0, :] = 1.0
    for mm in range(32):
        p = (pos // 64) if mm < 16 else (pos % 64)
        ang = p.astype(np.float64) * inv[(mm % 16) % 8]
        tab[mm, 0, :SEQ] = np.cos(ang)
        tab[mm, 1, :SEQ] = np.sin(ang)
    cs[0:32] = tab.astype(f32)
    cs[64:96] = tab.astype(f32)
    return cs


def unpack_out(outT):
    return np.ascontiguousarray(outT.transpose(0, 3, 2, 1).reshape(NB, SEQ, D))


def kernel(**inputs):
    inp = {k: np.asarray(v) for k, v in inputs.items()}
    bld = Builder(dict())
    nc = bld.build()
    in_maps = [host_inputs(inp, c) for c in range(8)]
    res = run_bass_kernel_spmd(nc, in_maps, core_ids=list(range(8)))
    outs = [unpack_out(r["outT"]) for r in res.results]
    return np.concatenate(outs, axis=0).astype(np.float32)
```

```python
import numpy as np
import concourse.bass as bass
import concourse.mybir as mybir
from concourse.bass_utils import run_bass_kernel_spmd

F32 = mybir.dt.float32
BF16 = mybir.dt.bfloat16
AF = mybir.ActivationFunctionType
ALU = mybir.AluOpType

D = 1024
SEQ = 2048
CTX = 256
TOK = SEQ + CTX
NB = 2
EPS = 1e-6
NPOOL = 40


class Sched:
    def __init__(self, nc):
        self.nc = nc
        self.eng = {'pe': nc.tensor, 'act': nc.scalar, 'dve': nc.vector, 'pool': nc.gpsimd, 'sp': nc.sync}
        self.ins = []
        self.last_w = {}
        self.rd_eng = {}
        self.rd_dma = {}
        self.bar = {e: [] for e in self.eng}
        self.last_on = {e: None for e in self.eng}
        self.open_dma = set()
        self.ps_last = {}

    def _add(self, eng, meth, kw, r, w, dma):
        i = len(self.ins)
        deps = set()
        for k in list(r) + list(w):
            if k.startswith("ps"):
                pl = self.ps_last.setdefault(k, {})
                for e2, i2 in pl.items():
                    if e2 != eng:
                        deps.add(i2)
                pl[eng] = i
        for k in r:
            lw = self.last_w.get(k)
            if lw is not None:
                deps.add(lw)
        for k in w:
            lw = self.last_w.get(k)
            if lw is not None:
                deps.add(lw)
            deps.update(self.rd_eng.get(k, {}).values())
            deps.update(self.rd_dma.get(k, ()))
        if self.bar[eng]:
            deps.update(self.bar[eng])
            self.bar[eng] = []
        self.ins.append(dict(eng=eng, meth=meth, kw=kw, deps=deps, dma=dma))
        for k in r:
            if dma:
                self.rd_dma.setdefault(k, []).append(i)
            else:
                self.rd_eng.setdefault(k, {})[eng] = i
        for k in w:
            self.last_w[k] = i
            self.rd_eng[k] = {}
            self.rd_dma[k] = []
        if not dma:
            self.last_on[eng] = i
        for d in deps:
            self.open_dma.discard(d)
        if dma:
            self.open_dma.add(i)
        return i

    def op(self, eng, meth, r=(), w=(), **kw):
        return self._add(eng, meth, kw, r, w, False)

    def dma(self, q, out, in_, r=(), w=(), **kw):
        kw = dict(kw, out=out, in_=in_)
        return self._add(q, 'dma_start', kw, r, w, True)

    def barrier(self):
        deps = [i for i in self.last_on.values() if i is not None] + list(self.open_dma)
        for e in self.eng:
            self.bar[e] = list(deps)
        self.last_w.clear()
        self.rd_eng.clear()
        self.rd_dma.clear()
        self.ps_last.clear()

    def finish(self):
        self.barrier()
        self._add('sp', None, {}, (), (), False)

    def emit(self):
        nc = self.nc
        ins = self.ins
        need = [False] * len(ins)
        for i, I in enumerate(ins):
            for d in I['deps']:
                Dd = ins[d]
                if Dd['dma']:
                    continue
                if Dd['eng'] == 'pe' and I['eng'] == 'pe' and not I['dma']:
                    continue
                need[d] = True
        esem = {e: nc.alloc_semaphore("sem_" + e) for e in self.eng}
        npool = {'sp': 28, 'pool': 12, 'act': 2, 'pe': 2, 'dve': 2}
        dsem = {q: [nc.alloc_semaphore("dsem_%s%d" % (q, j)) for j in range(npool[q])] for q in ('sp', 'pool')}
        cnt = {e: 0 for e in self.eng}
        ndq = {'sp': 0, 'pool': 0}
        nd = 0
        for i, I in enumerate(ins):
            if I['dma']:
                q = I['eng']
                I['sem'] = dsem[q][ndq[q] % npool[q]]
                I['val'] = 16 * (ndq[q] // npool[q] + 1)
                ndq[q] += 1
                nd += 1
            elif need[i]:
                cnt[I['eng']] += 1
                I['sem'] = esem[I['eng']]
                I['val'] = cnt[I['eng']]
        known = {e: {} for e in self.eng}
        nwait = 0
        for i, I in enumerate(ins):
            e = I['eng']
            E = self.eng[e]
            waits = {}
            for d in I['deps']:
                Dd = ins[d]
                if (not Dd['dma']) and Dd['eng'] == 'pe' and e == 'pe' and not I['dma']:
                    continue
                s, v = Dd['sem'], Dd['val']
                if waits.get(id(s), (None, 0))[1] < v:
                    waits[id(s)] = (s, v)
            if I['dma'] and I['val'] > 16:
                s = I['sem']
                if waits.get(id(s), (None, 0))[1] < I['val'] - 16:
                    waits[id(s)] = (s, I['val'] - 16)
            for sid, (s, v) in waits.items():
                if known[e].get(sid, 0) >= v:
                    continue
                E.wait_ge(s, v)
                nwait += 1
                known[e][sid] = v
            if I['meth'] is None:
                continue
            inst = getattr(E, I['meth'])(**I['kw'])
            if I['dma']:
                inst.then_inc(I['sem'], 16)
            elif need[i]:
                inst.then_inc(I['sem'], 1)
        self.stats = dict(n=len(ins), nwait=nwait, nsig=sum(need), ndma=nd, cnt=cnt)


class StopBuild(Exception):
    pass


class Mem:
    LO = 16512
    HI = 229344

    def __init__(self, nc):
        self.nc = nc
        self.top = Mem.LO
        self.limit = Mem.HI
        self.n = 0

    def region(self, lo, hi):
        self.top = lo
        self.limit = hi

    def alloc(self, name, shape, dtype):
        nbytes = int(np.prod(shape[1:])) * (4 if dtype == F32 else 2)
        nbytes = (nbytes + 63) // 64 * 64
        off = self.top
        assert off + nbytes <= self.limit, (name, off, nbytes, self.limit)
        self.top += nbytes
        self.n += 1
        return self.nc.alloc_sbuf_tensor_at("%s_%d" % (name, self.n), list(shape), dtype, offset=off)

    def mark(self):
        return self.top

    def release(self, m):
        self.top = m


def tiles_for(ntok_lat=SEQ, with_ctx=True, b=0, tw=512):
    out = []
    for t0 in range(0, ntok_lat, tw):
        out.append((t0, tw, b))
    if with_ctx:
        out.append((SEQ, CTX, 2))
    return out


class Builder:
    def __init__(self, cfg):
        self.cfg = cfg
        self.nc = nc = bass.Bass("TRN2", target_bir_lowering=False)
        self.S = Sched(nc)
        self.M = Mem(nc)
        self.dr = {}
        self.dbg_outs = []
        self.pp = [nc.alloc_psum_tensor("pp%d" % i, [128, 1024], F32) for i in range(4)]
        self.ps = [self.pp[i // 2][:, (i % 2) * 512:(i % 2 + 1) * 512] for i in range(8)]
        v7 = self.ps[7].bitcast(BF16)
        self.psb = [v7[:, 0:512], v7[:, 512:1024]]
        self.ps_i = 0

    def din(self, name, shape, dtype=F32):
        t = self.nc.dram_tensor(name, list(shape), dtype, kind="ExternalInput").ap()
        self.dr[name] = t
        return t

    def dout(self, name, shape, dtype=F32):
        t = self.nc.dram_tensor(name, list(shape), dtype, kind="ExternalOutput").ap()
        self.dr[name] = t
        return t

    def psum(self, lo=0, hi=6):
        n = hi - lo
        i = lo + (self.ps_i % n)
        self.ps_i += 1
        return self.ps[i], "ps%d" % i

    def dump(self, name, ap, shape, rkeys):
        if not self.cfg.get('debug'):
            return
        t = self.dout("dbg_" + name, shape, ap.dtype)
        self.S.dma('sp', out=t, in_=ap, r=rkeys, w=["dbgout_" + name])
        self.dbg_outs.append("dbg_" + name)

    def modulate(self, xT, xkey, g, s, outT, okey, tiles, sc):
        S = self.S
        ones = self.ones_bf
        for (t0, tw, cv) in tiles:
            xk = xkey + ":%d" % t0
            S.op('pool', 'tensor_tensor', r=[xk], w=["sq"], out=sc['sq'][:, :, :tw], in0=xT[:, :, t0:t0 + tw], in1=xT[:, :, t0:t0 + tw],
                 op=ALU.mult)
            ps, pk = self.psum()
            for k in range(8):
                S.op('pe', 'matmul', r=["sq", "consts"], w=[pk], out=ps[:, :tw], lhsT=ones[:, :], rhs=sc['sq'][:, k, :tw],
                     start=(k == 0), stop=(k == 7))
            S.op('act', 'activation', r=[pk], w=["lnt"], out=sc['lnt'][:, :tw], in_=ps[:, :tw], func=AF.Ln,
                 scale=1.0 / D, bias=self.eps_t[:, 0:1])
            pr, prk = self.psum()
            S.op('act', 'activation', r=["lnt"], w=[prk], out=pr[:, :tw], in_=sc['lnt'][:, :tw], func=AF.Exp, scale=-0.5)
            for k in range(8):
                S.op('dve', 'tensor_tensor', r=[xk, prk], w=["mtmp%d" % k], out=sc['mtmp'][:, k, :tw], in0=xT[:, k, t0:t0 + tw],
                     in1=pr[:, :tw], op=ALU.mult)
                S.op('act', 'activation', r=["mtmp%d" % k, "mod"], w=[okey + ":%d" % t0],
                     out=outT[:, k, t0:t0 + tw], in_=sc['mtmp'][:, k, :tw], func=AF.Identity,
                     bias=s[:, k, cv:cv + 1], scale=g[:, k, cv:cv + 1])

    def mod_group(self, l, j, bufs, rowt, banks=(0, 6)):
        S = self.S
        for half in range(2):
            gi = self.mod_gi
            self.mod_gi += 1
            buf, bk = bufs[gi % 2], "awb%d" % (gi % 2)
            rt, rtk = rowt[gi % len(rowt)], "rowt%d" % (gi % len(rowt))
            c0 = j * 1024 + half * 512
            S.dma('pool', out=buf[:, :, :], in_=self.adaw_d[l][:, :, c0:c0 + 512], w=[bk])
            ps, pk = self.psum(*banks)
            for k in range(8):
                S.op('pe', 'matmul', r=[bk, "scv"], w=[pk], out=ps[0:4, :], lhsT=self.scvb[:, k, :], rhs=buf[:, k, :],
                     start=(k == 0), stop=(k == 7))
            S.op('act', 'activation', r=[pk], w=[rtk], out=rt[0:4, :], in_=ps[0:4, :], func=AF.Copy)
            pt_, ptk_ = self.psum(*banks)
            for blk in range(4):
                S.op('pe', 'transpose', r=[rtk, "consts"], w=[ptk_], out=pt_[:, blk * 4:(blk + 1) * 4], in_=rt[0:4, blk * 128:(blk + 1) * 128],
                     identity=self.ident_f[0:4, 0:4])
            f0 = j * 8 + half * 4
            for cv in range(4):
                S.op('dve', 'tensor_tensor', r=[ptk_, "nm"], w=["mod"], out=self.mod[:, l, f0:f0 + 4, cv],
                     in0=pt_[:, 0:16].rearrange("p (f c) -> p f c", c=4)[:, :, cv], in1=self.adab[:, l, f0:f0 + 4], op=ALU.add)
        if j == 1:
            for cv in range(4):
                S.op('dve', 'scalar_tensor_tensor', r=["mod", "nm"], w=["mod"], out=self.gmix[:, l, :, cv],
                     in0=self.mod[:, l, 8:16, cv], scalar=1.0, in1=self.nmix_t[:, l, :], op0=ALU.add, op1=ALU.mult)
        if j == 4:
            for cv in range(4):
                S.op('dve', 'scalar_tensor_tensor', r=["mod", "nm"], w=["mod"], out=self.gmlp[:, l, :, cv],
                     in0=self.mod[:, l, 32:40, cv], scalar=1.0, in1=self.nmlp_t[:, l, :], op0=ALU.add, op1=ALU.mult)

    def build(self):
        nc, S, M, cfg = self.nc, self.S, self.M, self.cfg
        layers = cfg.get('layers', [0, 1])
        nb = cfg.get('nb', NB)
        xT_d = self.din("xT", [NB, 128, 8, SEQ])
        cT_d = self.din("ctxT", [NB, 128, 8, CTX])
        cv_d = self.din("cvT", [128, 8, 4])
        adaw_d = [self.din("adaw%d" % l, [128, 8, 6 * D]) for l in range(2)]
        adab_d = self.din("adab", [128, 2, 48])
        nmix_d = self.din("nmix", [128, 2, 8])
        nmlp_d = self.din("nmlp", [128, 2, 8])
        wout_d = [self.din("wout%d" % l, [128, 8, D]) for l in range(2)]
        w1_d = [self.din("w1_%d" % l, [128, 8, 4 * D]) for l in range(2)]
        w2_d = [self.din("w2_%d" % l, [128, 32, D]) for l in range(2)]
        ident_d = self.din("ident", [128, 128])
        outT_d = self.dout("outT", [NB, 128, 8, SEQ])
        xs_d = self.nc.dram_tensor("xs", [128, 8, TOK], F32, kind="Internal").ap()
        if cfg.get('inject_y'):
            yinj_d = [self.din("yinj%d" % l, [NB, 128, 8, TOK]) for l in range(2)]
        self.declare_mixer_dram()

        self.ident_bf = M.alloc("ident_bf", [128, 128], BF16)
        self.ident_f = M.alloc("ident_f", [128, 128], F32)
        self.ones_bf = M.alloc("ones_bf", [128, 128], BF16)
        self.eps_t = M.alloc("eps_t", [128, 1], F32)
        self.mod = M.alloc("mod", [128, 2, 48, 4], F32)
        self.gmix = M.alloc("gmix", [128, 2, 8, 4], F32)
        self.gmlp = M.alloc("gmlp", [128, 2, 8, 4], F32)
        nmix = M.alloc("nmix", [128, 2, 8], F32)
        nmlp = M.alloc("nmlp", [128, 2, 8], F32)
        adab = M.alloc("adab", [128, 2, 48], F32)
        self.alloc_mixer_persistent()
        base = M.mark()

        S.dma('sp', out=self.ident_f[:, :], in_=ident_d[:, :], w=["consts"])
        S.dma('pool', out=self.ident_bf[:, :], in_=ident_d[:, :], w=["consts"])
        S.op('dve', 'memset', w=["consts"], ap=self.ones_bf[:, :], constant=1.0)
        S.op('dve', 'memset', w=["consts"], ap=self.eps_t[:, :], constant=EPS)
        S.dma('sp', out=nmix[:, :, :], in_=nmix_d[:, :, :], w=["nm"])
        S.dma('sp', out=nmlp[:, :, :], in_=nmlp_d[:, :, :], w=["nm"])
        S.dma('sp', out=adab[:, :, :], in_=adab_d[:, :, :], w=["nm"])
        self.load_mixer_persistent()

        cvs = M.alloc("cvs", [128, 8, 4], F32)
        self.scvb = M.alloc("scvb", [128, 8, 4], BF16)
        self.adab = adab
        self.nmix_t, self.nmlp_t = nmix, nmlp
        self.adaw_d = adaw_d
        S.dma('sp', out=cvs[:, :, :], in_=cv_d[:, :, :], w=["cvs"])
        S.op('act', 'activation', r=["cvs"], w=["scv"], out=self.scvb[:, :, :], in_=cvs[:, :, :], func=AF.Silu)
        base = M.mark()
        bufs = [M.alloc("awb%d" % i, [128, 8, 512], BF16) for i in range(2)]
        rowt = [M.alloc("rowt%d" % i, [4, 512], F32) for i in range(2)]
        self.mod_gi = 0
        for j in range(2):
            self.mod_group(0, j, bufs, rowt)
        self.mod_todo = [(0, j) for j in range(2, 6)] + [(1, j) for j in range(6)]
        if cfg.get('mod_all') or 0 not in layers:
            while self.mod_todo:
                self.mod_group(*self.mod_todo.pop(0), bufs, rowt)
        self.dump("mod", self.mod[:, :, :, :], [128, 2, 48, 4], ["mod"])
        S.barrier()
        M.release(base)

        self.offB = M.mark()
        hT = M.alloc("hT", [128, 8, TOK], BF16)
        self.offA = M.mark()
        xT = M.alloc("xT", [128, 8, TOK], F32)
        self.offC = M.mark()
        yT = M.alloc("yT", [128, 8, TOK], BF16)
        regD = M.mark()
        self.offD = regD

        for bi in range(nb):
            for t0 in range(0, SEQ, 512):
                S.dma('sp', out=xT[:, :, t0:t0 + 512], in_=xT_d[bi, :, :, t0:t0 + 512], w=["xT:%d" % t0])
            S.dma('sp', out=xT[:, :, SEQ:TOK], in_=cT_d[bi, :, :, :], w=["xT:%d" % SEQ])
            for l in layers:
                last = (l == 1)
                m0 = M.mark()
                sc = dict(sq=M.alloc("sq", [128, 8, 512], BF16), lnt=M.alloc("lnt", [128, 512], F32),
                          rstd=M.alloc("rstd", [128, 512], F32), mtmp=M.alloc("mtmp", [128, 8, 512], F32))
                self.modulate(xT, "xT", self.gmix[:, l], self.mod[:, l, 0:8], hT, "hT", tiles_for(b=bi), sc)
                if cfg.get('debug') and bi == 0:
                    self.dump("hT%d" % l, hT[:, :, :], [128, 8, TOK], ["hT:%d" % t for t in range(0, TOK, 512)])
                for (t0, tw, cv) in tiles_for(b=bi):
                    S.dma('sp', out=xs_d[:, :, t0:t0 + tw], in_=xT[:, :, t0:t0 + tw], r=["xT:%d" % t0], w=["xs:%d" % t0])
                S.barrier()
                M.release(m0)
                if cfg.get('inject_y'):
                    for k in range(8):
                        S.dma('pool', out=yT[:, k, :], in_=yinj_d[l][bi, :, k, :], w=["yT:%d" % t for t in range(0, TOK, 512)])
                else:
                    try:
                        if l == 0:
                            self.even_mixer(bi, hT, yT, xT, regD)
                        else:
                            self.odd_mixer(bi, hT, yT, xT, regD)
                    except StopBuild:
                        pass
                S.barrier()
                M.region(regD, Mem.HI)
                if cfg.get('debug') and bi == 0:
                    self.dump("yT%d" % l, yT[:, :, :], [128, 8, TOK], ["yT:%d" % t for t in range(0, TOK, 512)])
                for (t0, tw, cv) in tiles_for(b=bi, with_ctx=not last):
                    S.dma('sp', out=xT[:, :, t0:t0 + tw], in_=xs_d[:, :, t0:t0 + tw], r=["xs:%d" % t0], w=["xT:%d" % t0])
                m0 = M.mark()
                wo = M.alloc("wo", [128, 8, D], BF16)
                sc = dict(sq=M.alloc("sq", [128, 8, 512], BF16), lnt=M.alloc("lnt", [128, 512], F32),
                          mtmp=M.alloc("mtmp", [128, 8, 512], F32))
                assert M.top <= Mem.HI - 8 * D * 2
                M.region(Mem.HI - 8 * D * 2, Mem.HI)
                w1pre = M.alloc("w1pre", [128, 8, D], BF16)
                M.region(m0, Mem.HI)
                S.dma('pool', out=wo[:, :, :], in_=wout_d[l][:, :, :], w=["wo"])
                S.dma('pool', out=w1pre[:, :, :], in_=w1_d[l][:, :, 0:D], w=["w1g0"])
                tl = tiles_for(b=bi, with_ctx=not last)

                def wout_tile(t0, tw, cv):
                    for fo in range(8):
                        ps, pk = self.psum()
                        for k in range(8):
                            S.op('pe', 'matmul', r=["wo", "yT:%d" % t0], w=[pk], out=ps[:, :tw],
                                 lhsT=wo[:, k, fo * 128:(fo + 1) * 128], rhs=yT[:, k, t0:t0 + tw],
                                 start=(k == 0), stop=(k == 7))
                        S.op('dve', 'scalar_tensor_tensor', r=[pk, "mod"], w=["xT:%d" % t0],
                             out=xT[:, fo, t0:t0 + tw], in0=ps[:, :tw], scalar=self.mod[:, l, 16 + fo, cv:cv + 1],
                             in1=xT[:, fo, t0:t0 + tw], op0=ALU.mult, op1=ALU.add)

                wout_tile(*tl[0])
                for ti, (t0, tw, cv) in enumerate(tl):
                    if ti + 1 < len(tl):
                        wout_tile(*tl[ti + 1])
                    self.modulate(xT, "xT", self.gmlp[:, l], self.mod[:, l, 24:32], hT, "hT", [(t0, tw, cv)], sc)
                if cfg.get('debug') and bi == 0:
                    self.dump("xmid%d" % l, xT[:, :, :], [128, 8, TOK], ["xT:%d" % t for t in range(0, TOK, 512)])
                S.barrier()
                M.release(m0)
                M.release(M.mark())
                M.region(self.offC, Mem.HI)
                m0 = M.mark()
                w1g = [w1pre, M.alloc("w1g1", [128, 8, D], BF16)]
                w2g = [M.alloc("w2g%d" % i, [128, 8, D], BF16) for i in range(2)]
                hid = [M.alloc("hid%d" % i, [128, 8, 512], BF16) for i in range(2)]
                rl = [M.alloc("rl%d" % i, [128, 512], BF16) for i in range(2)]
                hi_ = 0
                S.dma('pool', out=w2g[0][:, :, :], in_=w2_d[l][:, 0:8, :], w=["w2g0"])
                for fg in range(4):
                    wb = fg % 2
                    if fg + 1 < 4:
                        nb_ = (fg + 1) % 2
                        S.dma('pool', out=w1g[nb_][:, :, :], in_=w1_d[l][:, :, (fg + 1) * D:(fg + 2) * D], w=["w1g%d" % nb_])
                        S.dma('pool', out=w2g[nb_][:, :, :], in_=w2_d[l][:, (fg + 1) * 8:(fg + 2) * 8, :], w=["w2g%d" % nb_])
                    for (t0, tw, cv) in tl:
                        hb = hi_ % 2
                        hi_ += 1
                        for hc in range(8):
                            ps, pk = self.psum()
                            for k in range(8):
                                S.op('pe', 'matmul', r=["w1g%d" % wb, "hT:%d" % t0], w=[pk], out=ps[:, :tw],
                                     lhsT=w1g[wb][:, k, hc * 128:(hc + 1) * 128], rhs=hT[:, k, t0:t0 + tw],
                                     start=(k == 0), stop=(k == 7))
                            rb = hc % 2
                            S.op('act', 'activation', r=[pk], w=["rl%d" % rb], out=rl[rb][:, :tw], in_=ps[:, :tw],
                                 func=AF.Relu)
                            S.op('pool', 'tensor_tensor', r=["rl%d" % rb], w=["hid%d:%d" % (hb, hc)],
                                 out=hid[hb][:, hc, :tw], in0=rl[rb][:, :tw], in1=rl[rb][:, :tw], op=ALU.mult)
                        for fo in range(8):
                            ps, pk = self.psum()
                            for hc in range(8):
                                S.op('pe', 'matmul', r=["w2g%d" % wb, "hid%d:%d" % (hb, hc)], w=[pk], out=ps[:, :tw],
                                     lhsT=w2g[wb][:, hc, fo * 128:(fo + 1) * 128], rhs=hid[hb][:, hc, :tw],
                                     start=(hc == 0), stop=(hc == 7))
                            S.op('dve', 'scalar_tensor_tensor', r=[pk, "mod"], w=["xT:%d" % t0],
                                 out=xT[:, fo, t0:t0 + tw], in0=ps[:, :tw], scalar=self.mod[:, l, 40 + fo, cv:cv + 1],
                                 in1=xT[:, fo, t0:t0 + tw], op0=ALU.mult, op1=ALU.add)
                        if fg == 3 and last and l == layers[-1]:
                            S.dma('sp', out=outT_d[bi, :, :, t0:t0 + tw], in_=xT[:, :, t0:t0 + tw], r=["xT:%d" % t0], w=["outT:%d" % t0])
                if cfg.get('debug') and bi == 0:
                    self.dump("x%d" % l, xT[:, :, :], [128, 8, TOK], ["xT:%d" % t for t in range(0, TOK, 512)])
                S.barrier()
                M.region(regD, Mem.HI)
            if not (layers and layers[-1] == 1):
                for t0 in range(0, SEQ, 512):
                    S.dma('sp', out=outT_d[bi, :, :, t0:t0 + 512], in_=xT[:, :, t0:t0 + 512], r=["xT:%d" % t0], w=["outT:%d" % t0])
            S.barrier()
        S.finish()
        S.emit()
        return nc

    def cut(self, n):
        if self.cfg.get('cut') == n:
            raise StopBuild()

    def rot(self, name, shape, dtype, n):
        tiles = [self.M.alloc(name, shape, dtype) for _ in range(n)]
        uid = self.M.n
        st = {'i': 0}

        def nxt():
            i = st['i'] % n
            st['i'] += 1
            return tiles[i], "%s@%d#%d" % (name, uid, i)
        return nxt

    def rstd_of(self, ps_ap, pk, rows, tw, inv_n, nx_ln, nx_rs):
        S = self.S
        lnt, lk = nx_ln()
        rs, rk = nx_rs()
        S.op('act', 'activation', r=[pk], w=[lk], out=lnt[0:rows, :tw], in_=ps_ap, func=AF.Ln, scale=inv_n,
             bias=self.eps_t[0:rows, 0:1])
        S.op('act', 'activation', r=[lk], w=[rk], out=rs[0:rows, :tw], in_=lnt[0:rows, :tw], func=AF.Exp, scale=-0.5)
        return rs, rk

    def declare_mixer_dram(self):
        self.evwin_d = self.din("evwin", [128, 8, 1440])
        self.wuq_d = self.din("wuq", [128, 2, 768])
        self.wukvn_d = self.din("wukvn", [128, 8, 64])
        self.wukvv_d = self.din("wukvv", [128, 512])
        self.evvec_d = self.din("evvec", [128, 16])
        self.lruvec_d = self.din("lruvec", [128, 4, 12])
        self.lruw_d = self.din("lruw", [128, 4, 2, 2, 128])
        self.ropecs_d = self.din("ropecs", [128, 2, TOK])
        self.shk_d = self.din("shk", [128, 96])
        self.convwb_d = self.din("convwb", [128, 4, 512])
        self.odwin_d = self.din("odwin", [128, 8, 3104])
        self.glaw_d = self.din("glaw", [64, 2, 256])
        self.glau_d = self.din("glau", [128, 6, 128])
        self.odvec_d = self.din("odvec", [128, 4])
        self.tbig_d = self.din("tbig", [128, 8, 16, 64])

    def alloc_mixer_persistent(self):
        M = self.M
        self.evvec = M.alloc("evvec", [128, 16], F32)
        self.lruvec = M.alloc("lruvec", [128, 4, 12], F32)
        self.nsp = M.alloc("nsp", [128, 4, 2], F32)
        self.one_t = M.alloc("one_t", [128, 1], F32)
        self.spt = [M.alloc("spt%d" % i, [128, 4, 2], F32) for i in range(3)]
        self.odvec = M.alloc("odvec", [128, 4], F32)

    def load_mixer_persistent(self):
        S = self.S
        S.dma('sp', out=self.evvec[:, :], in_=self.evvec_d[:, :], w=["evvec"])
        S.dma('sp', out=self.lruvec[:, :, :], in_=self.lruvec_d[:, :, :], w=["lruvec"])
        S.op('dve', 'memset', w=["consts"], ap=self.one_t[:, :], constant=1.0)
        S.dma('sp', out=self.odvec[:, :], in_=self.odvec_d[:, :], w=["odvec"])
        e, t, u = self.spt
        S.op('act', 'activation', r=["lruvec"], w=["spe"], out=e[:, :, :], in_=self.lruvec[:, :, 9:11], func=AF.Exp, scale=-1.0)
        S.op('dve', 'tensor_scalar', r=["spe"], w=["spt"], out=t[:, :, :], in0=e[:, :, :], scalar1=-0.25, scalar2=1.0 / 3.0,
             op0=ALU.mult, op1=ALU.add)
        S.op('dve', 'tensor_tensor', r=["spe", "spt"], w=["spu"], out=u[:, :, :], in0=t[:, :, :], in1=e[:, :, :], op=ALU.mult)
        S.op('dve', 'tensor_scalar', r=["spu"], w=["spt"], out=t[:, :, :], in0=u[:, :, :], scalar1=-1.0, scalar2=0.5,
             op0=ALU.mult, op1=ALU.add)
        S.op('dve', 'tensor_tensor', r=["spe", "spt"], w=["spu"], out=u[:, :, :], in0=t[:, :, :], in1=e[:, :, :], op=ALU.mult)
        S.op('dve', 'tensor_scalar', r=["spu"], w=["spt"], out=t[:, :, :], in0=u[:, :, :], scalar1=8.0, scalar2=-8.0,
             op0=ALU.mult, op1=ALU.add)
        S.op('dve', 'tensor_tensor', r=["spe", "spt"], w=["nsp"], out=self.nsp[:, :, :], in0=t[:, :, :], in1=e[:, :, :], op=ALU.mult)

    def even_mixer(self, bi, hT, yT, xT, regD):
        nc, S, M, cfg = self.nc, self.S, self.M, self.cfg
        ev = self.evvec
        tl = tiles_for(b=bi)
        TK = ["%d" % t for t in range(0, TOK, 512)]
        M.region(self.offD, Mem.HI)
        qn = M.alloc("qn", [128, 2, TOK], BF16)
        kvn = M.alloc("kvn", [128, TOK], BF16)
        krb = M.alloc("krb", [128, TOK], BF16)
        sqkr = M.alloc("sqkr", [128, TOK], BF16)
        css = M.alloc("css", [128, 2, TOK], BF16)
        D1 = M.mark()
        winA = M.alloc("winA", [128, 8, 416], BF16)
        winR = M.alloc("winR", [128, 8, 32], BF16)
        nx_uqr = self.rot("uqr", [128, 2, 512], F32, 1)
        nx_sq2 = self.rot("sq2", [128, 2, 512], BF16, 1)
        nx_ln = self.rot("lnt", [128, 512], F32, 2)
        nx_rs = self.rot("rstd", [128, 512], F32, 2)
        nx_t = self.rot("tt", [128, 512], F32, 3)
        M.region(self.offA, self.offC)
        csf = M.alloc("csf", [128, 2, TOK], F32)
        M.region(D1, Mem.HI)
        for j in range(2):
            S.dma('sp', out=csf[:, j, :], in_=self.ropecs_d[:, j, :], w=["csf%d" % j])
            S.op('dve', 'tensor_scalar', r=["csf%d" % j, "evvec"], w=["css"], out=css[0:96, j, :], in0=csf[0:96, j, :],
                 scalar1=ev[0:96, 5 + j:6 + j], scalar2=None, op0=ALU.mult)
        S.dma('pool', out=winA[:, :, :], in_=self.evwin_d[:, :, 0:416], w=["winA"])
        for hb in range(2):
            o = hb * 16
            S.op('dve', 'tensor_scalar', r=["winA"], w=["winR"], out=winR[:, :, o:o + 8], in0=winA[:, :, 384 + o + 8:384 + o + 16],
                 scalar1=-1.0, scalar2=None, op0=ALU.mult)
            S.op('dve', 'tensor_copy', r=["winA"], w=["winR"], out=winR[:, :, o + 8:o + 16], in_=winA[:, :, 384 + o:384 + o + 8])
        self.cut(1)
        for (t0, tw, cv) in tl:
            hk = "hT:%d" % t0
            uqr, uk = nx_uqr()
            sq2, sk = nx_sq2()
            for c in range(2):
                ps, pk = self.psum()
                for k in range(8):
                    S.op('pe', 'matmul', r=["winA", hk], w=[pk], out=ps[:, :tw], lhsT=winA[:, k, c * 128:(c + 1) * 128],
                         rhs=hT[:, k, t0:t0 + tw], start=(k == 0), stop=(k == 7))
                S.op('act', 'activation', r=[pk], w=[sk + ":%d" % c], out=sq2[:, c, :tw], in_=ps[:, :tw], func=AF.Square)
                S.op('dve', 'tensor_copy', r=[pk], w=[uk + ":%d" % c], out=uqr[:, c, :tw], in_=ps[:, :tw])
            ps, pk = self.psum()
            for c in range(2):
                S.op('pe', 'matmul', r=[sk + ":%d" % c, "consts"], w=[pk], out=ps[:, :tw], lhsT=self.ones_bf[:, :],
                     rhs=sq2[:, c, :tw], start=(c == 0), stop=(c == 1))
            rs, rk = self.rstd_of(ps[:, :tw], pk, 128, tw, 1.0 / 256, nx_ln, nx_rs)
            for c in range(2):
                S.op('dve', 'scalar_tensor_tensor', r=[uk + ":%d" % c, rk, "evvec"], w=["qn:%d" % t0], out=qn[:, c, t0:t0 + tw],
                     in0=uqr[:, c, :tw], scalar=ev[:, c:c + 1], in1=rs[:, :tw], op0=ALU.mult, op1=ALU.mult)
            self.cut(2)
            ps, pk = self.psum()
            for k in range(8):
                S.op('pe', 'matmul', r=["winA", hk], w=[pk], out=ps[:, :tw], lhsT=winA[:, k, 256:384],
                     rhs=hT[:, k, t0:t0 + tw], start=(k == 0), stop=(k == 7))
            sq1, s1k = nx_sq2()
            ur, u1k = nx_t()
            S.op('act', 'activation', r=[pk], w=[s1k + ":0"], out=sq1[:, 0, :tw], in_=ps[:, :tw], func=AF.Square)
            S.op('dve', 'tensor_copy', r=[pk], w=[u1k], out=ur[:, :tw], in_=ps[:, :tw])
            ps2, pk2 = self.psum()
            S.op('pe', 'matmul', r=[s1k + ":0", "consts"], w=[pk2], out=ps2[:, :tw], lhsT=self.ones_bf[:, :], rhs=sq1[:, 0, :tw],
                 start=True, stop=True)
            rs, rk = self.rstd_of(ps2[:, :tw], pk2, 128, tw, 1.0 / 128, nx_ln, nx_rs)
            S.op('dve', 'scalar_tensor_tensor', r=[u1k, rk, "evvec"], w=["kvn:%d" % t0], out=kvn[:, t0:t0 + tw],
                 in0=ur[:, :tw], scalar=ev[:, 2:3], in1=rs[:, :tw], op0=ALU.mult, op1=ALU.mult)
            self.cut(3)
            psA, pkA = self.psum()
            for k in range(8):
                S.op('pe', 'matmul', r=["winA", hk], w=[pkA], out=psA[0:32, :tw], lhsT=winA[:, k, 384:416],
                     rhs=hT[:, k, t0:t0 + tw], start=(k == 0), stop=(k == 7))
            psB, pkB = self.psum()
            for k in range(8):
                S.op('pe', 'matmul', r=["winR", hk], w=[pkB], out=psB[0:32, :tw], lhsT=winR[:, k, :],
                     rhs=hT[:, k, t0:t0 + tw], start=(k == 0), stop=(k == 7))
            S.op('act', 'activation', r=[pkA], w=["sqkr:%d" % t0], out=sqkr[0:32, t0:t0 + tw], in_=psA[0:32, :tw], func=AF.Square)
            t1, t1k = nx_t()
            t2, t2k = nx_t()
            S.op('dve', 'tensor_tensor', r=[pkA, "css"], w=[t1k], out=t1[0:32, :tw], in0=psA[0:32, :tw],
                 in1=css[0:32, 0, t0:t0 + tw], op=ALU.mult)
            S.op('dve', 'tensor_tensor', r=[pkB, "css"], w=[t2k], out=t2[0:32, :tw], in0=psB[0:32, :tw],
                 in1=css[0:32, 1, t0:t0 + tw], op=ALU.mult)
            S.op('pool', 'tensor_tensor', r=[t1k, t2k], w=["krb:%d" % t0], out=krb[0:32, t0:t0 + tw], in0=t1[0:32, :tw],
                 in1=t2[0:32, :tw], op=ALU.add)
        self.cut(4)
        if cfg.get('debug') and bi == 0:
            self.dump("qn", qn[:, :, :], [128, 2, TOK], ["qn:" + t for t in TK])
            self.dump("kvn", kvn[:, :], [128, TOK], ["kvn:" + t for t in TK])
            self.dump("krb", krb[0:32, :], [32, TOK], ["krb:" + t for t in TK])
        S.barrier()
        mbufs = None
        if self.mod_todo:
            keep = (M.top, M.limit)
            M.region(self.offC, self.offC + 4 * TOK * 2)
            mbufs = ([M.alloc("awb%d" % i, [128, 8, 512], BF16) for i in range(2)], [M.alloc("rowt%d" % i, [4, 512], F32) for i in range(1)])
            M.region(*keep)
            for _ in range(3):
                self.mod_group(*self.mod_todo.pop(0), *mbufs)
        if cfg.get('even_stop') == 1:
            return
        M.region(D1, Mem.HI)
        wug = M.alloc("wug", [128, 8, 512], BF16)
        lw = M.alloc("lruw", [128, 4, 2, 2, 128], BF16)
        cwb = M.alloc("cwb", [128, 4, 512], BF16)
        wux = [M.alloc("wux%d" % i, [128, 8, 128], BF16) for i in range(2)]
        S.dma('pool', out=wug[:, :, :], in_=self.evwin_d[:, :, 928:1440], w=["wug"])
        S.dma('pool', out=lw[:, :, :, :, :], in_=self.lruw_d[:, :, :, :, :], w=["lruw"])
        S.dma('pool', out=cwb[:, :, :], in_=self.convwb_d[:, :, :], w=["cwb"])
        M.region(self.offA, self.offC)
        xl = M.alloc("xl", [128, TOK], F32)
        xlb = M.alloc("xlb", [128, TOK], BF16)
        at = M.alloc("at", [128, TOK], F32)
        bx = M.alloc("bx", [128, TOK], F32)
        hf = M.alloc("hf", [128, TOK], F32)
        hb_ = M.alloc("hb", [128, TOK], F32)
        gel = M.alloc("gel", [128, TOK], BF16)
        wj = M.alloc("wj", [128, 4, 8, 128], BF16)
        lv = self.lruvec
        S.dma('pool', out=wux[0][:, :, :], in_=self.evwin_d[:, :, 416:416 + 128], w=["wux0"])
        for c in range(4):
            wx, wxk = wux[c % 2], "wux%d" % (c % 2)
            if c + 1 < 4:
                S.dma('pool', out=wux[(c + 1) % 2][:, :, :], in_=self.evwin_d[:, :, 416 + (c + 1) * 128:416 + (c + 2) * 128],
                      w=["wux%d" % ((c + 1) % 2)])
            for j in range(4):
                for k in range(8):
                    S.op('pool', 'tensor_tensor', r=[wxk, "cwb"], w=["wj"], out=wj[:, j, k, :], in0=wx[:, k, :],
                         in1=cwb[:, j, c * 128:(c + 1) * 128], op=ALU.mult)
            for (t0, tw, cv) in tl:
                s0, e0 = (0, SEQ) if t0 < SEQ else (SEQ, TOK)
                ps, pk = self.psum()
                mm = []
                for j in (2, 1, 0, 3):
                    sft = j - 2
                    lo = max(t0, s0 - sft)
                    hi = min(t0 + tw, e0 - sft)
                    for k in range(8):
                        mm.append((j, k, lo, hi, sft))
                for n, (j, k, lo, hi, sft) in enumerate(mm):
                    S.op('pe', 'matmul', r=["wj"] + ["hT:%d" % t for t in range(0, TOK, 512)], w=[pk], out=ps[:, lo - t0:hi - t0],
                         lhsT=wj[:, j, k, :], rhs=hT[:, k, lo + sft:hi + sft], start=(n == 0), stop=(n == len(mm) - 1))
                S.op('act', 'activation', r=[pk, "lruvec"], w=["xl"], out=xl[:, t0:t0 + tw], in_=ps[:, :tw], func=AF.Identity,
                     bias=lv[:, c, 4:5], scale=1.0)
            for (t0, tw, cv) in tl:
                ps, pk = self.psum()
                for k in range(8):
                    S.op('pe', 'matmul', r=["wug", "hT:%d" % t0], w=[pk], out=ps[:, :tw], lhsT=wug[:, k, c * 128:(c + 1) * 128],
                         rhs=hT[:, k, t0:t0 + tw], start=(k == 0), stop=(k == 7))
                S.op('act', 'activation', r=[pk], w=["gel"], out=gel[:, t0:t0 + tw], in_=ps[:, :tw], func=AF.Gelu_apprx_tanh)
            S.op('pool', 'tensor_copy', r=["xl"], w=["xlb:%d" % t for t in range(0, TOK, 512)], out=xlb[:, :], in_=xl[:, :])
            for d in range(2):
                hx, hxk = (hf, "hf") if d == 0 else (hb_, "hb")
                for (t0, tw, cv) in tl:
                    psa, pka = self.psum()
                    S.op('pe', 'matmul', r=["lruw", "xlb:%d" % t0], w=[pka], out=psa[:, :tw], lhsT=lw[:, c, d, 0, :], rhs=xlb[:, t0:t0 + tw],
                         start=True, stop=True)
                    S.op('act', 'activation', r=[pka, "lruvec"], w=["at"], out=at[:, t0:t0 + tw], in_=psa[:, :tw], func=AF.Sigmoid,
                         bias=lv[:, c, 5 + d:6 + d], scale=1.0)
                    psx, pkx = self.psum()
                    S.op('pe', 'matmul', r=["lruw", "xlb:%d" % t0], w=[pkx], out=psx[:, :tw], lhsT=lw[:, c, d, 1, :], rhs=xlb[:, t0:t0 + tw],
                         start=True, stop=True)
                    S.op('act', 'activation', r=[pkx, "lruvec"], w=["bx"], out=bx[:, t0:t0 + tw], in_=psx[:, :tw], func=AF.Sigmoid,
                         bias=lv[:, c, 7 + d:8 + d], scale=1.0)
                S.op('act', 'activation', r=["at", "nsp"], w=["at"], out=at[:, :], in_=at[:, :], func=AF.Exp, scale=self.nsp[:, c, d:d + 1])
                S.op('dve', 'tensor_tensor', r=["at"], w=[hxk], out=hx[:, :], in0=at[:, :], in1=at[:, :], op=ALU.mult)
                S.op('act', 'activation', r=[hxk], w=[hxk], out=hx[:, :], in_=hx[:, :], func=AF.Sqrt, scale=-1.0, bias=self.one_t[:, 0:1])
                S.op('dve', 'tensor_tensor', r=["bx", "xl"], w=["bx"], out=bx[:, :], in0=bx[:, :], in1=xl[:, :], op=ALU.mult)
                S.op('dve', 'tensor_tensor', r=["bx", hxk], w=["bx"], out=bx[:, :], in0=bx[:, :], in1=hx[:, :], op=ALU.mult)
                if d == 0:
                    S.op('dve', 'tensor_tensor_scan', r=["at", "bx"], w=[hxk], out=hf[:, SEQ:TOK], data0=at[:, SEQ:TOK],
                         data1=bx[:, SEQ:TOK], initial=0.0, op0=ALU.mult, op1=ALU.add)
                    S.op('dve', 'tensor_tensor_scan', r=["at", "bx", hxk], w=[hxk], out=hf[:, 0:SEQ], data0=at[:, 0:SEQ],
                         data1=bx[:, 0:SEQ], initial=hf[:, TOK - 1:TOK], op0=ALU.mult, op1=ALU.add)
                else:
                    S.op('dve', 'tensor_tensor_scan', r=["at", "bx"], w=[hxk], out=hb_[:, SEQ:TOK][:, ::-1],
                         data0=at[:, SEQ:TOK][:, ::-1], data1=bx[:, SEQ:TOK][:, ::-1], initial=0.0, op0=ALU.mult, op1=ALU.add)
                    S.op('dve', 'tensor_tensor_scan', r=["at", "bx", hxk], w=[hxk], out=hb_[:, 0:SEQ][:, ::-1],
                         data0=at[:, 0:SEQ][:, ::-1], data1=bx[:, 0:SEQ][:, ::-1], initial=hb_[:, SEQ:SEQ + 1],
                         op0=ALU.mult, op1=ALU.add)
            S.op('dve', 'tensor_tensor', r=["hf", "hb"], w=["hf"], out=hf[:, :], in0=hf[:, :], in1=hb_[:, :], op=ALU.add)
            S.op('dve', 'tensor_tensor', r=["hf", "gel"], w=["yT:" + t for t in TK], out=yT[:, 4 + c, :], in0=hf[:, :], in1=gel[:, :],
                 op=ALU.mult)
        S.barrier()
        if cfg.get('even_stop') == 2:
            return
        M.region(D1, Mem.HI)
        wuq = M.alloc("wuq", [128, 2, 768], BF16)
        wuqr = M.alloc("wuqr", [128, 2, 8, 96], BF16)
        wkn = M.alloc("wkn", [128, 8, 96], BF16)
        wv = M.alloc("wv", [128, 512], BF16)
        shk = M.alloc("shk", [128, 96], BF16)
        nx_sq = self.rot("sq", [128, 512], BF16, 2)
        nx_ln = self.rot("lnt", [128, 512], F32, 2)
        nx_rs = self.rot("rstd", [128, 512], F32, 2)
        nx_t = self.rot("tt", [128, 512], F32, 2)
        nx_raw = self.rot("qkr", [128, 512], F32, 2)
        nx_rec = self.rot("rec", [128, 1], F32, 4)
        S.dma('pool', out=wuq[:, :, :], in_=self.wuq_d[:, :, :], w=["wuq"])
        S.dma('pool', out=wkn[:, :, 0:64], in_=self.wukvn_d[:, :, :], w=["wkn"])
        S.op('dve', 'memset', w=["wkn"], ap=wkn[:, :, 64:96], constant=0.0)
        S.dma('pool', out=wv[:, :], in_=self.wukvv_d[:, :], w=["wv"])
        S.dma('pool', out=shk[:, :], in_=self.shk_d[:, :], w=["shk"])
        S.op('dve', 'memset', w=["wuqr"], ap=wuqr[:, :, :, 0:64], constant=0.0)
        wuq4 = wuq[:, :, :].rearrange("p c (h e) -> p c h e", e=96)
        for hb in range(2):
            o = 64 + hb * 16
            S.op('dve', 'tensor_scalar', r=["wuq"], w=["wuqr"], out=wuqr[:, :, :, o:o + 8], in0=wuq4[:, :, :, o + 8:o + 16],
                 scalar1=-1.0, scalar2=None, op0=ALU.mult)
            S.op('dve', 'tensor_copy', r=["wuq"], w=["wuqr"], out=wuqr[:, :, :, o + 8:o + 16], in_=wuq4[:, :, :, o:o + 8])
        M.region(self.offB, self.offC)
        PT = [M.alloc("PT%d" % i, [128, 18, 512], BF16) for i in range(2)]
        ytok = M.alloc("ytok", [128, 18, 512], BF16)
        Vx = M.alloc("Vx", [128, 18, 8, 65], BF16)
        QT = [M.alloc("QT%d" % i, [128, TOK], BF16) for i in range(2)]
        KT = [M.alloc("KT%d" % i, [128, TOK], BF16) for i in range(2)]
        KR = 128 if cfg.get('mla_k128', 0) else 96
        if KR == 128:
            for i in range(2):
                S.op('pool', 'memset', w=["QT%d:%d" % (i, t) for t in range(0, TOK, 512)], ap=QT[i][96:128, :], constant=0.0)
                S.op('pool', 'memset', w=["KT%d:%d" % (i, t) for t in range(0, TOK, 512)], ap=KT[i][96:128, :], constant=0.0)
        S.op('pool', 'memset', w=["Vx"], ap=Vx[:, :, :, 64:65], constant=1.0)
        for kc in range(18):
            ps, pk = self.psum(0, 3)
            S.op('pe', 'matmul', r=["kvn:%d" % (kc // 4 * 512), "wv"], w=[pk], out=ps[:, :], lhsT=kvn[:, kc * 128:(kc + 1) * 128],
                 rhs=wv[:, :], start=True, stop=True)
            S.op('dve', 'tensor_copy', r=[pk], w=["Vx"], out=Vx[:, kc, :, 0:64], in_=ps[:, :].rearrange("p (h e) -> p h e", e=64))
        scale = 96.0 ** -0.5

        def proj(h, only=None):
            qt, kt = QT[h % 2], KT[h % 2]
            for ti, (t0, tw, cv) in enumerate(tl):
                if only is not None and ti != only:
                    continue
                psq, pkq = self.psum(6, 8)
                for c in range(2):
                    S.op('pe', 'matmul', r=["wuq", "qn:%d" % t0], w=[pkq], out=psq[0:96, :tw], lhsT=wuq[:, c, h * 96:(h + 1) * 96],
                         rhs=qn[:, c, t0:t0 + tw], start=(c == 0), stop=(c == 1))
                sq, sk = nx_sq()
                S.op('act', 'activation', r=[pkq], w=[sk], out=sq[0:96, :tw], in_=psq[0:96, :tw], func=AF.Square)
                qr, qrk = nx_raw()
                S.op('dve', 'tensor_copy', r=[pkq], w=[qrk], out=qr[0:96, :tw], in_=psq[0:96, :tw])
                psr, pkr = self.psum(6, 8)
                for c in range(2):
                    S.op('pe', 'matmul', r=["wuqr", "qn:%d" % t0], w=[pkr], out=psr[0:96, :tw], lhsT=wuqr[:, c, h, :],
                         rhs=qn[:, c, t0:t0 + tw], start=(c == 0), stop=(c == 1))
                t2, t2k = nx_t()
                S.op('dve', 'tensor_tensor', r=[pkr, "css"], w=[t2k], out=t2[64:96, :tw], in0=psr[64:96, :tw],
                     in1=css[64:96, 1, t0:t0 + tw], op=ALU.mult)
                pss, pks = self.psum(6, 8)
                S.op('pe', 'matmul', r=[sk, "consts"], w=[pks], out=pss[0:96, :tw], lhsT=self.ones_bf[0:96, 0:96], rhs=sq[0:96, :tw],
                     start=True, stop=True)
                rs, rk = self.rstd_of(pss[0:96, :tw], pks, 96, tw, 1.0 / 96, nx_ln, nx_rs)
                S.op('dve', 'scalar_tensor_tensor', r=[qrk, rk, "evvec"], w=["QT%d:%d" % (h % 2, t0)], out=qt[0:64, t0:t0 + tw],
                     in0=qr[0:64, :tw], scalar=ev[0:64, 3:4], in1=rs[0:64, :tw], op0=ALU.mult, op1=ALU.mult)
                t1, t1k = nx_t()
                S.op('pool', 'tensor_tensor', r=[qrk, "css"], w=[t1k], out=t1[64:96, :tw], in0=qr[64:96, :tw],
                     in1=css[64:96, 0, t0:t0 + tw], op=ALU.mult)
                S.op('pool', 'tensor_tensor', r=[t1k, t2k], w=[t1k], out=t1[64:96, :tw], in0=t1[64:96, :tw], in1=t2[64:96, :tw],
                     op=ALU.add)
                S.op('pool', 'tensor_tensor', r=[t1k, rk], w=["QT%d:%d" % (h % 2, t0)], out=qt[64:96, t0:t0 + tw], in0=t1[64:96, :tw],
                     in1=rs[64:96, :tw], op=ALU.mult)
                psk, pkk = self.psum(6, 8)
                S.op('pe', 'matmul', r=["wkn", "kvn:%d" % t0], w=[pkk], out=psk[0:96, :tw], lhsT=wkn[:, h, :], rhs=kvn[:, t0:t0 + tw],
                     start=True, stop=False)
                S.op('pe', 'matmul', r=["shk", "krb:%d" % t0], w=[pkk], out=psk[0:96, :tw], lhsT=shk[0:32, :], rhs=krb[0:32, t0:t0 + tw],
                     start=False, stop=True)
                sq, sk = nx_sq()
                S.op('act', 'activation', r=[pkk], w=[sk], out=sq[0:64, :tw], in_=psk[0:64, :tw], func=AF.Square)
                kr_, krk = nx_raw()
                S.op('dve', 'tensor_copy', r=[pkk], w=[krk], out=kr_[0:96, :tw], in_=psk[0:96, :tw])
                pss, pks = self.psum(6, 8)
                S.op('pe', 'matmul', r=[sk, "consts"], w=[pks], out=pss[0:96, :tw], lhsT=self.ones_bf[0:64, 0:96], rhs=sq[0:64, :tw],
                     start=True, stop=False)
                S.op('pe', 'matmul', r=["sqkr:%d" % t0, "consts"], w=[pks], out=pss[0:96, :tw], lhsT=self.ones_bf[0:32, 0:96],
                     rhs=sqkr[0:32, t0:t0 + tw], start=False, stop=True)
                rs, rk = self.rstd_of(pss[0:96, :tw], pks, 96, tw, 1.0 / 96, nx_ln, nx_rs)
                S.op('dve', 'scalar_tensor_tensor', r=[krk, rk, "evvec"], w=["KT%d:%d" % (h % 2, t0)], out=kt[0:96, t0:t0 + tw],
                     in0=kr_[0:96, :tw], scalar=ev[0:96, 4:5], in1=rs[0:96, :tw], op0=ALU.mult, op1=ALU.mult)

        qtiles = [(t0, tw, list(range(18))) for (t0, tw, cv) in tiles_for(b=bi, with_ctx=False)] + [(SEQ, CTX, [16, 17])]

        NH = cfg.get('mla_heads', 8)
        seq_ = [(h, qi) for h in range(NH) for qi in range(len(qtiles))]
        steps = []
        for it, (h, qi) in enumerate(seq_):
            kcs = qtiles[qi][2]
            for m in range(len(kcs) // 2):
                steps.append((it, h, qi, m, kcs[2 * m], kcs[2 * m + 1], len(kcs) // 2))

        def emit_S(si, step):
            it, h, qi, m, kc0, kc1, npair = step
            t0, tw, kcs = qtiles[qi]
            pt, ptk = PT[it % 2], "PT%d" % (it % 2)
            if m == 0 and qi == 0 and h + 1 < NH:
                proj(h + 1)
            if m == 0 and qi in (1, 3) and mbufs is not None and self.mod_todo:
                self.mod_group(*self.mod_todo.pop(0), *mbufs, banks=(6, 8))
            P = self.pp[si % 2]
            pks = ["ps%d" % (2 * (si % 2)), "ps%d" % (2 * (si % 2) + 1)]
            for j, kc in enumerate((kc0, kc1)):
                S.op('pe', 'matmul', r=["KT%d:%d" % (h % 2, kc // 4 * 512), "QT%d:%d" % (h % 2, t0)], w=[pks[j]], out=P[:, j * 512:j * 512 + tw],
                     lhsT=KT[h % 2][0:96, kc * 128:(kc + 1) * 128], rhs=QT[h % 2][0:96, t0:t0 + tw], start=True, stop=True)
            S.op('act', 'activation', r=pks, w=[ptk + ":%d" % kc0, ptk + ":%d" % kc1], out=pt[:, kc0:kc0 + 2, :tw],
                 in_=P[:, :].rearrange("p (b c) -> p b c", c=512)[:, :, :tw], func=AF.Exp, scale=scale)

        def emit_PV(step):
            it, h, qi, m, kc0, kc1, npair = step
            t0, tw, kcs = qtiles[qi]
            pt, ptk = PT[it % 2], "PT%d" % (it % 2)
            pso, pko = self.ps[4 + it % 2], "ps%d" % (4 + it % 2)
            for j, kc in enumerate((kc0, kc1)):
                n = 2 * m + j
                for sub in range(tw // 128):
                    S.op('pe', 'matmul', r=[ptk + ":%d" % kc, "Vx"], w=[pko], out=pso[:, sub * 65:(sub + 1) * 65],
                         lhsT=pt[:, kc, sub * 128:(sub + 1) * 128], rhs=Vx[:, kc, h, :], start=(n == 0 and sub == 0),
                         stop=(n == 2 * npair - 1), skip_group_check=True)
            if m == npair - 1:
                for sub in range(tw // 128):
                    st = (t0 // 128) + sub
                    rec, rck = nx_rec()
                    S.op('dve', 'reciprocal', r=[pko], w=[rck], out=rec[:, 0:1], in_=pso[:, sub * 65 + 64:sub * 65 + 65])
                    S.op('dve', 'tensor_scalar', r=[pko, rck], w=["ytok:%d" % st], out=ytok[:, st, h * 64:(h + 1) * 64],
                         in0=pso[:, sub * 65:sub * 65 + 64], scalar1=rec[:, 0:1], scalar2=None, op0=ALU.mult)

        LAG = cfg.get('lag', 2)
        proj(0)
        for i, stp in enumerate(steps):
            emit_S(i, stp)
            if i >= LAG:
                emit_PV(steps[i - LAG])
        for stp in steps[max(0, len(steps) - LAG):]:
            emit_PV(stp)
        while mbufs is not None and self.mod_todo:
            self.mod_group(*self.mod_todo.pop(0), *mbufs, banks=(6, 8))
        if cfg.get('debug') and bi == 0:
            self.dump("ytok", ytok[:, :, :], [128, 18, 512], ["ytok:%d" % st for st in range(18)])
        for st in range(18):
            pb = self.psb[st % 2]
            pbk = "ps7"
            for c in range(4):
                S.op('pe', 'transpose', r=["ytok:%d" % st, "consts"], w=[pbk], out=pb[:, c * 128:(c + 1) * 128],
                     in_=ytok[:, st, c * 128:(c + 1) * 128], identity=self.ident_bf[:, :])
            S.op('act', 'activation', r=[pbk], w=["yT:%d" % (st // 4 * 512)], out=yT[:, 0:4, st * 128:(st + 1) * 128],
                 in_=pb[:, 0:512].rearrange("p (c t) -> p c t", t=128), func=AF.Copy)

    def odd_mixer(self, bi, hT, yT, xT, regD):
        nc, S, M, cfg = self.nc, self.S, self.M, self.cfg
        tl = tiles_for(b=bi)
        TK = ["%d" % t for t in range(0, TOK, 512)]
        ov = self.odvec
        for p in range(2 if not cfg.get('skip_gla') else 0):
            M.region(self.offD, Mem.HI)
            wq = M.alloc("gwq", [128, 8, 128], BF16)
            wk = M.alloc("gwk", [128, 8, 128], BF16)
            wvv = M.alloc("gwv", [128, 8, 256], BF16)
            wg = M.alloc("gwg", [128, 8, 256], BF16)
            wlr = M.alloc("gwlr", [128, 8, 32], BF16)
            glau = M.alloc("glau", [128, 4, 128], F32)
            gmask = M.alloc("gmask", [128, 2, 128], BF16)
            glaw = M.alloc("glaw", [64, 2, 256], F32)
            nx_qk = self.rot("qkraw", [128, 512], BF16, 4)
            lrTs = [M.alloc("lrT%d" % i, [64, 512], F32) for i in range(2)]
            gmask4 = M.alloc("gmask4", [128, 2, 4, 128], BF16)
            nx_e = self.rot("ge", [128, 512], F32, 2)
            nx_l = self.rot("gl", [128, 512], F32, 2)
            nx_x = self.rot("gx", [128, 512], F32, 4)
            nx_kt = self.rot("gkt", [128, 512], F32, 2)
            nx_am = self.rot("attm", [128, 512], BF16, 4)
            nx_sq = self.rot("gsq", [128, 512], BF16, 2)
            nx_ln = self.rot("glnt", [128, 512], F32, 2)
            od = self.odwin_d
            S.dma('pool', out=wq[:, :, :], in_=od[:, :, p * 128:(p + 1) * 128], w=["gwq"])
            S.dma('pool', out=wk[:, :, :], in_=od[:, :, 256 + p * 128:256 + (p + 1) * 128], w=["gwk"])
            S.dma('pool', out=wvv[:, :, :], in_=od[:, :, 512 + p * 256:512 + (p + 1) * 256], w=["gwv"])
            S.dma('pool', out=wg[:, :, :], in_=od[:, :, 1024 + p * 256:1024 + (p + 1) * 256], w=["gwg"])
            S.dma('pool', out=wlr[:, :, :], in_=od[:, :, 1536:1568], w=["gwlr"])
            S.dma('sp', out=glau[:, :, :], in_=self.glau_d[:, 0:4, :], w=["glau"])
            S.dma('pool', out=gmask[:, :, :], in_=self.glau_d[:, 4:6, :], w=["gmask"])
            for j in range(4):
                S.dma('pool', out=gmask4[:, :, j, :], in_=self.glau_d[:, 4:6, :], w=["gmask4"])
            S.dma('sp', out=glaw[:, :, :], in_=self.glaw_d[:, :, :], w=["glaw"])
            for i_ in range(2):
                S.op('dve', 'memset', w=["lrT%d" % i_], ap=lrTs[i_][:, :], constant=1.0)
            M.region(self.offA, self.offC)
            kd = [M.alloc("kd%d" % d, [128, 18, 128], BF16) for d in range(2)]
            vt = M.alloc("vtok", [128, 18, 256], BF16)
            qd = [M.alloc("qdec%d" % d, [128, SEQ], BF16) for d in range(2)]
            ki = [M.alloc("kinv%d" % d, [128, SEQ], BF16) for d in range(2)]
            sh = [M.alloc("shist%d" % d, [128, 32, 128], BF16) for d in range(2)]
            sg = M.alloc("sg", [128, 2, SEQ], BF16)
            dec = M.alloc("dec", [128, 2, 36], F32)
            sst = [[M.alloc("sst%d%d" % (d, i), [128, 128], F32) for i in range(2)] for d in range(2)]
            nx_rs = self.rot("grstd", [128, 512], F32, 2)
            nx_t = self.rot("gt1", [128, 512], F32, 2)
            def g1_a(t0, tw, cv):
                lrT = lrTs[(t0 // 512) % 2]
                lrk = "lrT%d" % ((t0 // 512) % 2)
                qraw = kraw = qk_ = kk_ = None
                hk = "hT:%d" % t0
                lat = t0 < SEQ
                if lat:
                    qraw, qk_ = nx_qk()
                    ps, pk = self.psum()
                    for k in range(8):
                        S.op('pe', 'matmul', r=["gwq", hk], w=[pk], out=ps[:, :tw], lhsT=wq[:, k, :], rhs=hT[:, k, t0:t0 + tw],
                             start=(k == 0), stop=(k == 7))
                    S.op('act', 'activation', r=[pk], w=[qk_], out=qraw[:, :tw], in_=ps[:, :tw], func=AF.Copy, scale=0.125)
                    kraw, kk_ = nx_qk()
                    ps, pk = self.psum()
                    for k in range(8):
                        S.op('pe', 'matmul', r=["gwk", hk], w=[pk], out=ps[:, :tw], lhsT=wk[:, k, :], rhs=hT[:, k, t0:t0 + tw],
                             start=(k == 0), stop=(k == 7))
                    S.op('act', 'activation', r=[pk], w=[kk_], out=kraw[:, :tw], in_=ps[:, :tw], func=AF.Copy)
                    for hl in range(2):
                        ps, pk = self.psum()
                        for k in range(8):
                            S.op('pe', 'matmul', r=["gwg", hk], w=[pk], out=ps[:, :tw], lhsT=wg[:, k, hl * 128:(hl + 1) * 128],
                                 rhs=hT[:, k, t0:t0 + tw], start=(k == 0), stop=(k == 7))
                        S.op('act', 'activation', r=[pk], w=["sg:%d" % t0], out=sg[:, hl, t0:t0 + tw], in_=ps[:, :tw], func=AF.Silu)
                ps, pk = self.psum()
                for k in range(8):
                    S.op('pe', 'matmul', r=["gwlr", hk], w=[pk], out=ps[0:32, :tw], lhsT=wlr[:, k, :],
                         rhs=hT[:, k, t0:t0 + tw], start=(k == 0), stop=(k == 7))
                S.op('act', 'activation', r=[pk], w=[lrk], out=lrT[0:32, :tw], in_=ps[0:32, :tw], func=AF.Copy)
                nsub = tw // 128
                st0 = t0 // 128
                psk, pkk = self.psum()
                for sub in range(nsub):
                    c0 = sub * 128
                    for k in range(8):
                        S.op('pe', 'matmul', r=["gwk", hk], w=[pkk], out=psk[:, c0:c0 + 128], lhsT=hT[:, k, t0 + c0:t0 + c0 + 128], rhs=wk[:, k, :],
                             start=(k == 0), stop=(k == 7))
                ktk, ktkk = nx_kt()
                S.op('act', 'activation', r=[pkk], w=[ktkk], out=ktk[:, :tw], in_=psk[:, :tw], func=AF.Copy)
                for s2 in range(0, nsub, 2):
                    psv, pkv = self.psum()
                    for sub in range(s2, min(s2 + 2, nsub)):
                        c0 = sub * 128
                        for k in range(8):
                            S.op('pe', 'matmul', r=["gwv", hk], w=[pkv], out=psv[:, (sub - s2) * 256:(sub - s2 + 1) * 256],
                                 lhsT=hT[:, k, t0 + c0:t0 + c0 + 128], rhs=wvv[:, k, :], start=(k == 0), stop=(k == 7))
                    ns = min(2, nsub - s2)
                    S.op('act', 'activation', r=[pkv], w=["vtok:%d" % (st0 + s2 + i) for i in range(ns)],
                         out=vt[:, st0 + s2:st0 + s2 + ns, :], in_=psv[:, 0:ns * 256].rearrange("p (a b) -> p a b", b=256), func=AF.Copy)
                return dict(qraw=qraw, kraw=kraw, qk_=qk_, kk_=kk_, lrT=lrT, lrk=lrk, ktk=ktk, ktkk=ktkk, hk=hk, lat=lat, nsub=nsub, st0=st0)

            def g1_b(t0, tw, cv, st_):
                qraw, kraw, qk_, kk_, lrT, lrk, ktk, ktkk, hk, lat, nsub, st0 = (st_[k_] for k_ in ('qraw', 'kraw', 'qk_', 'kk_', 'lrT', 'lrk', 'ktk', 'ktkk', 'hk', 'lat', 'nsub', 'st0'))
                for d in range(2):
                    psz, pkz = self.psum()
                    for sub in range(nsub):
                        c0 = sub * 128
                        S.op('pe', 'matmul', r=[lrk, "glaw"], w=[pkz], out=psz[:, c0:c0 + 128], lhsT=lrT[0:33, c0:c0 + 128],
                             rhs=glaw[0:33, d, p * 128:(p + 1) * 128], start=True, stop=True)
                    e, ek = nx_e()
                    l, lk = nx_l()
                    S.op('act', 'activation', r=[pkz], w=[ek], out=e[:, :tw], in_=psz[:, :tw], func=AF.Exp, scale=-1.0)
                    S.op('act', 'activation', r=[ek], w=[lk], out=l[:, :tw], in_=e[:, :tw], func=AF.Ln, bias=self.one_t[:, 0:1], scale=1.0)
                    psr, pkr = self.psum()
                    for sub in range(nsub):
                        c0 = sub * 128
                        S.op('pe', 'matmul', r=[lk, "glau"], w=[pkr], out=psr[:, c0:c0 + 128], lhsT=glau[:, 2 + d, :], rhs=l[:, c0:c0 + 128],
                             start=True, stop=True)
                    er, erk = nx_x()
                    S.op('act', 'activation', r=[pkr], w=[erk], out=er[:, :tw], in_=psr[:, :tw], func=AF.Exp)
                    S.op('dve', 'tensor_tensor', r=[ktkk, erk], w=["kd%d:%d" % (d, st0 + i) for i in range(nsub)],
                         out=kd[d][:, st0:st0 + nsub, :], in0=ktk[:, :tw].rearrange("p (a b) -> p a b", b=128),
                         in1=er[:, :tw].rearrange("p (a b) -> p a b", b=128), op=ALU.mult)
                    psc, pkc = self.psum()
                    for sub in range(nsub):
                        c0 = sub * 128
                        S.op('pe', 'matmul', r=[lk, "glau"], w=[pkc], out=psc[:, c0:c0 + 128], lhsT=l[:, c0:c0 + 128], rhs=glau[:, d, :],
                             start=True, stop=True)
                    ep, epk = nx_x()
                    S.op('act', 'activation', r=[pkc], w=[epk], out=ep[:, :tw], in_=psc[:, :tw], func=AF.Exp)
                    g0_ = 2 * st0
                    S.op('dve', 'tensor_copy', r=[epk], w=["dec"], out=dec[:, d, g0_:g0_ + 2 * nsub],
                         in_=ep[:, :tw].rearrange("p (a b) -> p a b", b=64)[:, :, 63 if d == 0 else 0])
                    if lat:
                        em, emk = nx_x()
                        S.op('act', 'activation', r=[pkc], w=[emk], out=em[:, :tw], in_=psc[:, :tw], func=AF.Exp, scale=-1.0)
                        S.op('dve', 'tensor_tensor', r=[qk_, epk], w=["qdec%d:%d" % (d, st0 + i) for i in range(nsub)], out=qd[d][:, t0:t0 + tw],
                             in0=qraw[:, :tw], in1=ep[:, :tw], op=ALU.mult)
                        S.op('pool', 'tensor_tensor', r=[kk_, emk], w=["kinv%d:%d" % (d, st0 + i) for i in range(nsub)], out=ki[d][:, t0:t0 + tw],
                             in0=kraw[:, :tw], in1=em[:, :tw], op=ALU.mult)

            pend_ = g1_a(*tl[0])
            for ti_, tile_ in enumerate(tl):
                nxt2_ = g1_a(*tl[ti_ + 1]) if ti_ + 1 < len(tl) else None
                g1_b(*tile_, pend_)
                pend_ = nxt2_
            if cfg.get('debug') and bi == 0 and p == 0:
                self.dump("dec", dec[:, :, :], [128, 2, 36], ["dec"])
                for d in range(2):
                    self.dump("kd%d" % d, kd[d][:, :, :], [128, 18, 128], ["kd%d:%d" % (d, st) for st in range(18)])
                    self.dump("qd%d" % d, qd[d][:, :], [128, SEQ], ["qdec%d:%d" % (d, st) for st in range(16)])
                    self.dump("ki%d" % d, ki[d][:, :], [128, SEQ], ["kinv%d:%d" % (d, st) for st in range(16)])
                self.dump("vt", vt[:, :, :], [128, 18, 256], ["vtok:%d" % st for st in range(18)])
            self.cut(11)
            orders = [[32, 33, 34, 35] + list(range(32)), [35, 34, 33, 32] + list(range(31, -1, -1))]
            for d in range(2):
                S.op('dve', 'memset', w=["sst%d0" % d], ap=sst[d][0][:, :], constant=0.0)
            for n in range(36):
                for d in range(2):
                    g = orders[d][n]
                    st, half = g // 2, g % 2
                    cur, nxt_ = sst[d][n % 2], sst[d][(n + 1) % 2]
                    ck, nk_ = "sst%d%d" % (d, n % 2), "sst%d%d" % (d, (n + 1) % 2)
                    if g < 32:
                        S.op('pool', 'tensor_copy', r=[ck], w=["shist%d:%d" % (d, g)], out=sh[d][:, g, :], in_=cur[:, :])
                    if n == 35:
                        continue
                    ps, pk = self.psum()
                    r0_ = half * 64
                    S.op('pe', 'matmul', r=["kd%d:%d" % (d, st), "vtok:%d" % st], w=[pk], out=ps[:, 0:256],
                         lhsT=kd[d][r0_:r0_ + 64, st, :], rhs=vt[r0_:r0_ + 64, st, :], start=True, stop=True)
                    for hl in range(2):
                        rr = slice(hl * 64, (hl + 1) * 64)
                        S.op('dve', 'scalar_tensor_tensor', r=[ck, pk, "dec"], w=[nk_], out=nxt_[rr, :], in0=cur[rr, :],
                             scalar=dec[rr, d, g:g + 1], in1=ps[rr, hl * 128:(hl + 1) * 128], op0=ALU.mult, op1=ALU.add)
            if cfg.get('debug') and bi == 0 and p == 0:
                for d in range(2):
                    self.dump("sh%d" % d, sh[d][:, :, :], [128, 32, 128], ["shist%d:%d" % (d, g) for g in range(32)])
            self.cut(12)
            for grp in range(4):
                T0 = grp * 512
                for hl in range(2):
                    rr = slice(hl * 64, (hl + 1) * 64)
                    ams = []
                    for d in range(2):
                        psa, pka = self.psum()
                        for j in range(4):
                            st = grp * 4 + j
                            tt0 = st * 128
                            S.op('pe', 'matmul', r=["kinv%d:%d" % (d, st), "qdec%d:%d" % (d, st)], w=[pka], out=psa[:, j * 128:(j + 1) * 128],
                                 lhsT=ki[d][rr, tt0:tt0 + 128], rhs=qd[d][rr, tt0:tt0 + 128], start=True, stop=True)
                        am, amk = nx_am()
                        S.op('dve', 'tensor_tensor', r=[pka, "gmask4"], w=[amk], out=am[:, :], in0=psa[:, :],
                             in1=gmask4[:, d, :, :].rearrange("p a b -> p (a b)"), op=ALU.mult)
                        ams.append((am, amk))
                    pso, pko = self.psum()
                    for j in range(4):
                        st = grp * 4 + j
                        tt0 = st * 128
                        for d in range(2):
                            S.op('pe', 'matmul', r=[ams[d][1], "vtok:%d" % st], w=[pko], out=pso[:, j * 128:(j + 1) * 128],
                                 lhsT=vt[:, st, hl * 128:(hl + 1) * 128], rhs=ams[d][0][:, j * 128:(j + 1) * 128], start=(d == 0), stop=False)
                        for d in range(2):
                            for half in range(2):
                                g = 2 * st + half
                                S.op('pe', 'matmul', r=["shist%d:%d" % (d, g), "qdec%d:%d" % (d, st)], w=[pko],
                                     out=pso[:, j * 128 + half * 64:j * 128 + (half + 1) * 64], lhsT=sh[d][rr, g, :],
                                     rhs=qd[d][rr, tt0 + half * 64:tt0 + (half + 1) * 64], start=False, stop=(d == 1 and half == 1))
                    sq, sk = nx_sq()
                    S.op('act', 'activation', r=[pko], w=[sk], out=sq[:, :], in_=pso[:, :], func=AF.Square)
                    pss, pks = self.psum()
                    S.op('pe', 'matmul', r=[sk, "consts"], w=[pks], out=pss[:, :], lhsT=self.ones_bf[:, :], rhs=sq[:, :], start=True, stop=True)
                    rs, rk = self.rstd_of(pss[:, :], pks, 128, 512, 1.0 / 128, nx_ln, nx_rs)
                    t1, t1k = nx_t()
                    S.op('dve', 'scalar_tensor_tensor', r=[pko, rk, "odvec"], w=[t1k], out=t1[:, :], in0=pso[:, :], scalar=ov[:, 0:1],
                         in1=rs[:, :], op0=ALU.mult, op1=ALU.mult)
                    S.op('pool', 'tensor_tensor', r=[t1k, "sg:%d" % T0], w=["yT:%d" % T0],
                         out=yT[:, 2 * p + hl, T0:T0 + 512], in0=t1[:, :], in1=sg[:, hl, T0:T0 + 512], op=ALU.mult)
            S.barrier()
        if cfg.get('skip_nat'):
            return
        M.region(self.offD, Mem.HI)
        tb = M.alloc("tbig", [128, 8, 16, 64], BF16)
        wnq_ = [M.alloc("wnq%d" % i, [128, 8, 128], BF16) for i in range(2)]
        wnk_ = [M.alloc("wnk%d" % i, [128, 8, 128], BF16) for i in range(2)]
        wnv = M.alloc("wnv", [128, 8, 512], BF16)
        ones2 = M.alloc("ones2", [128, 128], BF16)
        PTn = [M.alloc("PTn%d" % i, [128, 7, 512], BF16) for i in range(2)]
        nx_sb = self.rot("nsb", [128, 512], F32, 2)
        nx_sq = self.rot("nsq", [128, 512], BF16, 2)
        nx_ln = self.rot("nlnt", [128, 512], F32, 1)
        nx_rs = self.rot("nrstd", [128, 512], F32, 2)
        nx_rec = self.rot("nrec", [128, 4], F32, 4)
        M.region(self.offA, self.offC)
        QnT = M.alloc("QnT", [128, 4, SEQ], BF16)
        KnT = M.alloc("KnT", [128, 4, TOK], BF16)
        Vn = M.alloc("Vn", [128, 18, 8, 65], BF16)
        ytok = M.alloc("nytok", [128, 16, 512], BF16)
        od = self.odwin_d
        for hh in range(4):
            S.dma('pool', out=tb[:, 2 * hh:2 * hh + 2, :, :], in_=self.tbig_d[:, 2 * hh:2 * hh + 2, :, :], w=["tbig"])
        S.op('dve', 'tensor_scalar', r=["tbig"], w=["tbig"], out=tb[:, :, :, :], in0=tb[:, :, :, :], scalar1=8.0, scalar2=None, op0=ALU.mult)
        S.dma('pool', out=wnv[:, :, :], in_=od[:, :, 2592:3104], w=["wnv"])
        S.op('dve', 'memset', w=["ones2"], ap=ones2[:, :], constant=0.0)
        S.op('dve', 'memset', w=["ones2"], ap=ones2[0:64, 0:64], constant=1.0)
        S.op('dve', 'memset', w=["ones2"], ap=ones2[64:128, 64:128], constant=1.0)
        S.op('pool', 'memset', w=["Vn"], ap=Vn[:, :, :, :], constant=1.0)
        for st in range(18):
            ps, pk = self.psum()
            for k in range(8):
                S.op('pe', 'matmul', r=["wnv", "hT:%d" % (st // 4 * 512)], w=[pk], out=ps[:, :], lhsT=hT[:, k, st * 128:(st + 1) * 128],
                     rhs=wnv[:, k, :], start=(k == 0), stop=(k == 7))
            S.op('act', 'activation', r=[pk], w=["Vn"], out=Vn[:, st, :, 0:64], in_=ps[:, :].rearrange("p (h e) -> p h e", e=64), func=AF.Copy)
        groups = []
        for pr in range(4):
            for (t0, tw, cv) in tl:
                for which in range(2):
                    if which == 0 and t0 >= SEQ:
                        continue
                    groups.append((pr, t0, tw, which))
        loaded = set()

        def nproj_a(g):
            pr, t0, tw, which = g
            wnq, wnk = wnq_[pr % 2], wnk_[pr % 2]
            if pr not in loaded:
                loaded.add(pr)
                S.dma('pool', out=wnq[:, :, :], in_=od[:, :, 1568 + pr * 128:1568 + (pr + 1) * 128], w=["wnq%d" % (pr % 2)])
                S.dma('pool', out=wnk[:, :, :], in_=od[:, :, 2080 + pr * 128:2080 + (pr + 1) * 128], w=["wnk%d" % (pr % 2)])
            wt, wkey = (wnq, "wnq%d" % (pr % 2)) if which == 0 else (wnk, "wnk%d" % (pr % 2))
            ps, pk = self.psum()
            for k in range(8):
                S.op('pe', 'matmul', r=[wkey, "hT:%d" % t0], w=[pk], out=ps[:, :tw], lhsT=wt[:, k, :], rhs=hT[:, k, t0:t0 + tw],
                     start=(k == 0), stop=(k == 7))
            sq, sk = nx_sq()
            S.op('act', 'activation', r=[pk], w=[sk], out=sq[:, :tw], in_=ps[:, :tw], func=AF.Square)
            return ps, pk, sq, sk

        def nproj_b(g, st_):
            pr, t0, tw, which = g
            ps, pk, sq, sk = st_
            pss, pks = self.psum()
            S.op('pe', 'matmul', r=[sk, "ones2"], w=[pks], out=pss[:, :tw], lhsT=ones2[:, :], rhs=sq[:, :tw], start=True, stop=True)
            rs, rk = self.rstd_of(pss[:, :tw], pks, 128, tw, 1.0 / 64, nx_ln, nx_rs)
            if which == 0:
                S.op('dve', 'scalar_tensor_tensor', r=[pk, rk, "odvec"], w=["QnT:%d" % t0], out=QnT[:, pr, t0:t0 + tw], in0=ps[:, :tw],
                     scalar=ov[:, 1:2], in1=rs[:, :tw], op0=ALU.mult, op1=ALU.mult)
            else:
                S.op('dve', 'scalar_tensor_tensor', r=[pk, rk, "odvec"], w=["KnT:%d" % t0], out=KnT[:, pr, t0:t0 + tw], in0=ps[:, :tw],
                     scalar=ov[:, 2:3], in1=rs[:, :tw], op0=ALU.mult, op1=ALU.mult)

        pend = nproj_a(groups[0])
        for gi_, g in enumerate(groups):
            nxt_ = nproj_a(groups[gi_ + 1]) if gi_ + 1 < len(groups) else None
            nproj_b(g, pend)
            pend = nxt_
        self.cut(21)

        def r0f(r):
            return min(max(r - 4, 0), 24)
        iters = [(qt, hg) for qt in range(cfg.get('nat_tiles', 16)) for hg in range(2)]

        def kcs_of(qt):
            rlo = r0f(2 * qt)
            rhi = r0f(2 * qt + 1) + 8
            return list(range(rlo // 2, (rhi - 1) // 2 + 1)) + [16, 17]

        def nat_S(it):
            qt, hg = iters[it]
            kcs = kcs_of(qt)
            pt, ptk = PTn[it % 2], "PTn%d" % (it % 2)
            for j, kc in enumerate(kcs):
                ps, pk = self.psum(0, 3)
                for hh in range(4):
                    h = hh * 2 + hg
                    pr, rr = h // 2, slice((h % 2) * 64, (h % 2) * 64 + 64)
                    S.op('pe', 'matmul', r=["KnT:%d" % (kc // 4 * 512), "QnT:%d" % (qt // 4 * 512)], w=[pk], out=ps[:, hh * 128:(hh + 1) * 128],
                         lhsT=KnT[rr, pr, kc * 128:(kc + 1) * 128], rhs=QnT[rr, pr, qt * 128:(qt + 1) * 128], start=True, stop=True)
                if kc < 16:
                    Dd = 2 * kc - 2 * qt + 7
                    sb, sbk = nx_sb()
                    S.op('dve', 'tensor_tensor', r=[pk, "tbig"], w=[sbk], out=sb[:, :].rearrange("p (h a c) -> p h a c", a=2, c=64),
                         in0=ps[:, :].rearrange("p (h a c) -> p h a c", a=2, c=64),
                         in1=tb[:, hg:8:2, Dd - 1:Dd + 1, :][:, :, ::-1, :], op=ALU.add)
                    S.op('act', 'activation', r=[sbk], w=[ptk + ":%d" % j], out=pt[:, j, :], in_=sb[:, :], func=AF.Exp, scale=0.125)
                    for i in range(2):
                        for a in range(2):
                            krow, qrow = 2 * kc + i, 2 * qt + a
                            if not (r0f(qrow) <= krow < r0f(qrow) + 8) and not cfg.get('nat_nomemset'):
                                S.op('pool', 'memset', r=[], w=[ptk + ":%d" % j],
                                     ap=pt[i * 64:(i + 1) * 64, j, :].rearrange("p (h q) -> p h q", q=128)[:, :, a * 64:(a + 1) * 64], constant=0.0)
                else:
                    S.op('act', 'activation', r=[pk], w=[ptk + ":%d" % j], out=pt[:, j, :], in_=ps[:, :], func=AF.Exp, scale=0.125)

        def nat_PV(it):
            qt, hg = iters[it]
            if cfg.get('nat_nopv'):
                return
            kcs = kcs_of(qt)
            pt, ptk = PTn[it % 2], "PTn%d" % (it % 2)
            pso, pko = self.psum(3, 6)
            for hh in range(4):
                h = hh * 2 + hg
                for j, kc in enumerate(kcs):
                    S.op('pe', 'matmul', r=[ptk + ":%d" % j, "Vn"], w=[pko], out=pso[:, hh * 65:(hh + 1) * 65], lhsT=pt[:, j, hh * 128:(hh + 1) * 128],
                         rhs=Vn[:, kc, h, :], start=(j == 0), stop=(j == len(kcs) - 1))
            rec, rck = nx_rec()
            S.op('dve', 'reciprocal', r=[pko], w=[rck], out=rec[:, 0:4], in_=pso[:, 0:260].rearrange("p (h e) -> p h e", e=65)[:, :, 64])
            for hh in range(4):
                h = hh * 2 + hg
                S.op('dve', 'tensor_scalar', r=[pko, rck], w=["nytok:%d" % qt], out=ytok[:, qt, h * 64:(h + 1) * 64], in0=pso[:, hh * 65:hh * 65 + 64],
                     scalar1=rec[:, hh:hh + 1], scalar2=None, op0=ALU.mult)
            if hg == 1:
                pb = self.psb[qt % 2]
                pbk = "ps7"
                for c in range(4):
                    S.op('pe', 'transpose', r=["nytok:%d" % qt, "consts"], w=[pbk], out=pb[:, c * 128:(c + 1) * 128],
                         in_=ytok[:, qt, c * 128:(c + 1) * 128], identity=self.ident_bf[:, :])
                S.op('act', 'activation', r=[pbk], w=["yT:%d" % (qt // 4 * 512)], out=yT[:, 4:8, qt * 128:(qt + 1) * 128],
                     in_=pb[:, 0:512].rearrange("p (c t) -> p c t", t=128), func=AF.Copy)

        if iters:
            nat_S(0)
        for it in range(len(iters)):
            if it + 1 < len(iters):
                nat_S(it + 1)
            nat_PV(it)


def fm(w):
    K = w.shape[0] // 128
    return np.ascontiguousarray(w.reshape(K, 128, -1).transpose(1, 0, 2))


def vec_fm(v):
    return np.ascontiguousarray(v.reshape(-1, 128).T)


def host_inputs(inp, core, extra=None):
    b0 = core * NB
    f32 = np.float32
    m = {}
    x = inp['x'][b0:b0 + NB]
    m['xT'] = np.ascontiguousarray(x.transpose(0, 2, 1).reshape(NB, 8, 128, SEQ).transpose(0, 2, 1, 3)).astype(f32)
    cx = inp['ctx'][b0:b0 + NB]
    m['ctxT'] = np.ascontiguousarray(cx.transpose(0, 2, 1).reshape(NB, 8, 128, CTX).transpose(0, 2, 1, 3)).astype(f32)
    cv = np.stack([inp['c'][b0], inp['c'][b0 + 1], inp['c_ctx'], inp['c_ctx']], axis=-1)
    m['cvT'] = fm(cv).astype(f32)
    for l in range(2):
        m['adaw%d' % l] = fm(inp['ada_w'][l]).astype(f32)
        m['wout%d' % l] = fm(inp['w_out'][l]).astype(f32)
        m['w1_%d' % l] = fm(inp['mlp_w1'][l]).astype(f32)
        m['w2_%d' % l] = fm(inp['mlp_w2'][l]).astype(f32)
    m['adab'] = np.ascontiguousarray(np.stack([vec_fm(inp['ada_b'][l]) for l in range(2)], axis=1)).astype(f32)
    m['nmix'] = np.ascontiguousarray(np.stack([vec_fm(inp['norm_mix'][l]) for l in range(2)], axis=1)).astype(f32)
    m['nmlp'] = np.ascontiguousarray(np.stack([vec_fm(inp['norm_mlp'][l]) for l in range(2)], axis=1)).astype(f32)
    m['ident'] = np.eye(128, dtype=f32)
    m['evwin'] = fm(inp['ev_w_in'][0]).astype(f32)
    m['wuq'] = fm(inp['mla_w_uq'][0]).astype(f32)
    wukv = inp['mla_w_ukv'][0].reshape(128, 8, 128)
    m['wukvn'] = np.ascontiguousarray(wukv[:, :, 0:64]).astype(f32)
    m['wukvv'] = np.ascontiguousarray(wukv[:, :, 64:128].reshape(128, 512)).astype(f32)
    qg = inp['mla_q_gain'][0]
    kg = inp['mla_k_gain'][0]
    perm = np.array([(i + 8) if (i % 16) < 8 else (i - 8) for i in range(32)])
    ev = np.ones((128, 16), f32)
    ev[:, 0:2] = vec_fm(inp['mla_q_norm'][0])
    ev[:, 2] = inp['mla_kv_norm'][0]
    ev[0:96, 3] = qg
    ev[0:64, 4] = kg[0:64]
    ev[0:32, 5] = kg[64:96]
    ev[64:96, 5] = qg[64:96]
    ev[0:32, 6] = kg[64:96][perm]
    ev[64:96, 6] = qg[64:96][perm]
    m['evvec'] = ev
    lv = np.zeros((128, 4, 12), f32)
    for j in range(4):
        lv[:, :, j] = vec_fm(inp['lru_conv_w'][0][j])
    lv[:, :, 4] = vec_fm(inp['lru_conv_b'][0])
    for d in range(2):
        lv[:, :, 5 + d] = vec_fm(inp['lru_b_a'][0][d])
        lv[:, :, 7 + d] = vec_fm(inp['lru_b_x'][0][d])
        lv[:, :, 9 + d] = vec_fm(inp['lru_lam'][0][d])
    m['lruvec'] = lv
    m['convwb'] = np.ascontiguousarray(np.broadcast_to(inp['lru_conv_w'][0][None, :, :], (128, 4, 512))).astype(f32)
    lw = np.zeros((128, 4, 2, 2, 128), f32)
    for c in range(4):
        for d in range(2):
            for g, nm in enumerate(('lru_w_a', 'lru_w_x')):
                for hb in range(2):
                    lw[hb * 64:(hb + 1) * 64, c, d, g, hb * 64:(hb + 1) * 64] = inp[nm][0][d][2 * c + hb]
    m['lruw'] = lw
    m['ropecs'] = rope_tables()
    shk = np.zeros((128, 96), f32)
    for k in range(32):
        shk[k, 64 + k] = 1.0
    m['shk'] = shk
    m['odwin'] = fm(inp['od_w_in'][0]).astype(f32)
    gw = np.zeros((64, 2, 256), f32)
    for d in range(2):
        gw[d * 16:(d + 1) * 16, d, :] = inp['gla_w_a'][0][d]
        gw[32, d, :] = inp['gla_b_a'][0][d]
    m['glaw'] = gw
    m['glau'] = gla_consts()
    ovv = np.ones((128, 4), f32)
    ovv[:, 0] = inp['gla_o_gain'][0]
    ovv[:, 1] = np.tile(inp['na_q_gain'][0], 2)
    ovv[:, 2] = np.tile(inp['na_k_gain'][0], 2)
    m['odvec'] = ovv
    m['tbig'] = natten_table(inp['na_rpb'][0])
    return m


def gla_consts():
    f32 = np.float32
    u = np.zeros((128, 6, 128), f32)
    sp = np.arange(128)[:, None]
    t = np.arange(128)[None, :]
    same = (sp // 64) == (t // 64)
    u[:, 0, :] = np.where(same & (sp <= t), -1.0 / 16, 0.0)
    u[:, 1, :] = np.where(same & (sp >= t), -1.0 / 16, 0.0)
    u[:, 2, :] = np.where(same & (sp > t), -1.0 / 16, 0.0)
    u[:, 3, :] = np.where(same & (sp < t), -1.0 / 16, 0.0)
    u[:, 4, :] = np.where(same & (sp <= t), 1.0, 0.0)
    u[:, 5, :] = np.where(same & (sp >= t), 1.0, 0.0)
    return u


def natten_table(rpb):
    f32 = np.float32
    tbl = np.full((128, 8, 16, 64), -30000.0, f32)
    p = np.arange(128)
    i = p // 64
    kcol = p % 64
    c = np.arange(64)
    c0 = np.clip(c - 8, 0, 48)
    valid = (kcol[:, None] >= c0[None, :]) & (kcol[:, None] < c0[None, :] + 16)
    dj = np.clip(kcol[:, None] - c[None, :] + 15, 0, 30)
    for e in range(16):
        di = e + i
        ok = valid & (di[:, None] <= 14)
        dic = np.clip(di, 0, 14)
        for h in range(8):
            g = rpb[h][dic[:, None], dj]
            tbl[:, h, e, :] = np.where(ok, g, tbl[:, h, e, :])
    return tbl


def rope_tables():
    f32 = np.float32
    half = 16
    inv = 10000.0 ** (-np.arange(0, half, 2, dtype=np.float64) / half)
    pos = np.arange(SEQ)
    cs = np.zeros((128, 2, TOK), f32)
    tab = np.zeros((32, 2, TOK), np.float64)
    tab[:, 0, :] = 1.0
    for mm in range(32):
        p = (pos // 64) if mm < 16 else (pos % 64)
        ang = p.astype(np.float64) * inv[(mm % 16) % 8]
        tab[mm, 0, :SEQ] = np.cos(ang)
        tab[mm, 1, :SEQ] = np.sin(ang)
    cs[0:32] = tab.astype(f32)
    cs[64:96] = tab.astype(f32)
    return cs


def unpack_out(outT):
    return np.ascontiguousarray(outT.transpose(0, 3, 2, 1).reshape(NB, SEQ, D))


def kernel(**inputs):
    inp = {k: np.asarray(v) for k, v in inputs.items()}
    bld = Builder(dict())
    nc = bld.build()
    in_maps = [host_inputs(inp, c) for c in range(8)]
    res = run_bass_kernel_spmd(nc, in_maps, core_ids=list(range(8)))
    outs = [unpack_out(r["outT"]) for r in res.results]
    return np.concatenate(outs, axis=0).astype(np.float32)
```

```python
import numpy as np
import concourse.bass as bass
import concourse.mybir as mybir
from concourse.bass_utils import run_bass_kernel_spmd

F32 = mybir.dt.float32
BF16 = mybir.dt.bfloat16
AF = mybir.ActivationFunctionType
ALU = mybir.AluOpType

D = 1024
SEQ = 2048
CTX = 256
TOK = SEQ + CTX
NB = 2
EPS = 1e-6
NPOOL = 40


class Sched:
    def __init__(self, nc):
        self.nc = nc
        self.eng = {'pe': nc.tensor, 'act': nc.scalar, 'dve': nc.vector, 'pool': nc.gpsimd, 'sp': nc.sync}
        self.ins = []
        self.last_w = {}
        self.rd_eng = {}
        self.rd_dma = {}
        self.bar = {e: [] for e in self.eng}
        self.last_on = {e: None for e in self.eng}
        self.open_dma = set()
        self.ps_last = {}

    def _add(self, eng, meth, kw, r, w, dma):
        i = len(self.ins)
        deps = set()
        for k in list(r) + list(w):
            if k.startswith("ps"):
                pl = self.ps_last.setdefault(k, {})
                for e2, i2 in pl.items():
                    if e2 != eng:
                        deps.add(i2)
                pl[eng] = i
        for k in r:
            lw = self.last_w.get(k)
            if lw is not None:
                deps.add(lw)
        for k in w:
            lw = self.last_w.get(k)
            if lw is not None:
                deps.add(lw)
            deps.update(self.rd_eng.get(k, {}).values())
            deps.update(self.rd_dma.get(k, ()))
        if self.bar[eng]:
            deps.update(self.bar[eng])
            self.bar[eng] = []
        self.ins.append(dict(eng=eng, meth=meth, kw=kw, deps=deps, dma=dma))
        for k in r:
            if dma:
                self.rd_dma.setdefault(k, []).append(i)
            else:
                self.rd_eng.setdefault(k, {})[eng] = i
        for k in w:
            self.last_w[k] = i
            self.rd_eng[k] = {}
            self.rd_dma[k] = []
        if not dma:
            self.last_on[eng] = i
        for d in deps:
            self.open_dma.discard(d)
        if dma:
            self.open_dma.add(i)
        return i

    def op(self, eng, meth, r=(), w=(), **kw):
        return self._add(eng, meth, kw, r, w, False)

    def dma(self, q, out, in_, r=(), w=(), **kw):
        kw = dict(kw, out=out, in_=in_)
        return self._add(q, 'dma_start', kw, r, w, True)

    def barrier(self):
        deps = [i for i in self.last_on.values() if i is not None] + list(self.open_dma)
        for e in self.eng:
            self.bar[e] = list(deps)
        self.last_w.clear()
        self.rd_eng.clear()
        self.rd_dma.clear()
        self.ps_last.clear()

    def finish(self):
        self.barrier()
        self._add('sp', None, {}, (), (), False)

    def emit(self):
        nc = self.nc
        ins = self.ins
        need = [False] * len(ins)
        for i, I in enumerate(ins):
            for d in I['deps']:
                Dd = ins[d]
                if Dd['dma']:
                    continue
                if Dd['eng'] == 'pe' and I['eng'] == 'pe' and not I['dma']:
                    continue
                need[d] = True
        esem = {e: nc.alloc_semaphore("sem_" + e) for e in self.eng}
        npool = {'sp': 28, 'pool': 12, 'act': 2, 'pe': 2, 'dve': 2}
        dsem = {q: [nc.alloc_semaphore("dsem_%s%d" % (q, j)) for j in range(npool[q])] for q in ('sp', 'pool')}
        cnt = {e: 0 for e in self.eng}
        ndq = {'sp': 0, 'pool': 0}
        nd = 0
        for i, I in enumerate(ins):
            if I['dma']:
                q = I['eng']
                I['sem'] = dsem[q][ndq[q] % npool[q]]
                I['val'] = 16 * (ndq[q] // npool[q] + 1)
                ndq[q] += 1
                nd += 1
            elif need[i]:
                cnt[I['eng']] += 1
                I['sem'] = esem[I['eng']]
                I['val'] = cnt[I['eng']]
        known = {e: {} for e in self.eng}
        nwait = 0
        for i, I in enumerate(ins):
            e = I['eng']
            E = self.eng[e]
            waits = {}
            for d in I['deps']:
                Dd = ins[d]
                if (not Dd['dma']) and Dd['eng'] == 'pe' and e == 'pe' and not I['dma']:
                    continue
                s, v = Dd['sem'], Dd['val']
                if waits.get(id(s), (None, 0))[1] < v:
                    waits[id(s)] = (s, v)
            if I['dma'] and I['val'] > 16:
                s = I['sem']
                if waits.get(id(s), (None, 0))[1] < I['val'] - 16:
                    waits[id(s)] = (s, I['val'] - 16)
            for sid, (s, v) in waits.items():
                if known[e].get(sid, 0) >= v:
                    continue
                E.wait_ge(s, v)
                nwait += 1
                known[e][sid] = v
            if I['meth'] is None:
                continue
            inst = getattr(E, I['meth'])(**I['kw'])
            if I['dma']:
                inst.then_inc(I['sem'], 16)
            elif need[i]:
                inst.then_inc(I['sem'], 1)
        self.stats = dict(n=len(ins), nwait=nwait, nsig=sum(need), ndma=nd, cnt=cnt)


class StopBuild(Exception):
    pass


class Mem:
    LO = 16512
    HI = 229344

    def __init__(self, nc):
        self.nc = nc
        self.top = Mem.LO
        self.limit = Mem.HI
        self.n = 0

    def region(self, lo, hi):
        self.top = lo
        self.limit = hi

    def alloc(self, name, shape, dtype):
        nbytes = int(np.prod(shape[1:])) * (4 if dtype == F32 else 2)
        nbytes = (nbytes + 63) // 64 * 64
        off = self.top
        assert off + nbytes <= self.limit, (name, off, nbytes, self.limit)
        self.top += nbytes
        self.n += 1
        return self.nc.alloc_sbuf_tensor_at("%s_%d" % (name, self.n), list(shape), dtype, offset=off)

    def mark(self):
        return self.top

    def release(self, m):
        self.top = m


def tiles_for(ntok_lat=SEQ, with_ctx=True, b=0, tw=512):
    out = []
    for t0 in range(0, ntok_lat, tw):
        out.append((t0, tw, b))
    if with_ctx:
        out.append((SEQ, CTX, 2))
    return out


class Builder:
    def __init__(self, cfg):
        self.cfg = cfg
        self.nc = nc = bass.Bass("TRN2", target_bir_lowering=False)
        self.S = Sched(nc)
        self.M = Mem(nc)
        self.dr = {}
        self.dbg_outs = []
        self.pp = [nc.alloc_psum_tensor("pp%d" % i, [128, 1024], F32) for i in range(4)]
        self.ps = [self.pp[i // 2][:, (i % 2) * 512:(i % 2 + 1) * 512] for i in range(8)]
        v7 = self.ps[7].bitcast(BF16)
        self.psb = [v7[:, 0:512], v7[:, 512:1024]]
        self.ps_i = 0

    def din(self, name, shape, dtype=F32):
        t = self.nc.dram_tensor(name, list(shape), dtype, kind="ExternalInput").ap()
        self.dr[name] = t
        return t

    def dout(self, name, shape, dtype=F32):
        t = self.nc.dram_tensor(name, list(shape), dtype, kind="ExternalOutput").ap()
        self.dr[name] = t
        return t

    def psum(self, lo=0, hi=6):
        n = hi - lo
        i = lo + (self.ps_i % n)
        self.ps_i += 1
        return self.ps[i], "ps%d" % i

    def dump(self, name, ap, shape, rkeys):
        if not self.cfg.get('debug'):
            return
        t = self.dout("dbg_" + name, shape, ap.dtype)
        self.S.dma('sp', out=t, in_=ap, r=rkeys, w=["dbgout_" + name])
        self.dbg_outs.append("dbg_" + name)

    def modulate(self, xT, xkey, g, s, outT, okey, tiles, sc):
        S = self.S
        ones = self.ones_bf
        for (t0, tw, cv) in tiles:
            xk = xkey + ":%d" % t0
            S.op('pool', 'tensor_tensor', r=[xk], w=["sq"], out=sc['sq'][:, :, :tw], in0=xT[:, :, t0:t0 + tw], in1=xT[:, :, t0:t0 + tw],
                 op=ALU.mult)
            ps, pk = self.psum()
            for k in range(8):
                S.op('pe', 'matmul', r=["sq", "consts"], w=[pk], out=ps[:, :tw], lhsT=ones[:, :], rhs=sc['sq'][:, k, :tw],
                     start=(k == 0), stop=(k == 7))
            S.op('act', 'activation', r=[pk], w=["lnt"], out=sc['lnt'][:, :tw], in_=ps[:, :tw], func=AF.Ln,
                 scale=1.0 / D, bias=self.eps_t[:, 0:1])
            pr, prk = self.psum()
            S.op('act', 'activation', r=["lnt"], w=[prk], out=pr[:, :tw], in_=sc['lnt'][:, :tw], func=AF.Exp, scale=-0.5)
            for k in range(8):
                S.op('dve', 'tensor_tensor', r=[xk, prk], w=["mtmp%d" % k], out=sc['mtmp'][:, k, :tw], in0=xT[:, k, t0:t0 + tw],
                     in1=pr[:, :tw], op=ALU.mult)
                S.op('act', 'activation', r=["mtmp%d" % k, "mod"], w=[okey + ":%d" % t0],
                     out=outT[:, k, t0:t0 + tw], in_=sc['mtmp'][:, k, :tw], func=AF.Identity,
                     bias=s[:, k, cv:cv + 1], scale=g[:, k, cv:cv + 1])

    def mod_group(self, l, j, bufs, rowt, banks=(0, 6)):
        S = self.S
        for half in range(2):
            gi = self.mod_gi
            self.mod_gi += 1
            buf, bk = bufs[gi % 2], "awb%d" % (gi % 2)
            rt, rtk = rowt[gi % len(rowt)], "rowt%d" % (gi % len(rowt))
            c0 = j * 1024 + half * 512
            S.dma('pool', out=buf[:, :, :], in_=self.adaw_d[l][:, :, c0:c0 + 512], w=[bk])
            ps, pk = self.psum(*banks)
            for k in range(8):
                S.op('pe', 'matmul', r=[bk, "scv"], w=[pk], out=ps[0:4, :], lhsT=self.scvb[:, k, :], rhs=buf[:, k, :],
                     start=(k == 0), stop=(k == 7))
            S.op('act', 'activation', r=[pk], w=[rtk], out=rt[0:4, :], in_=ps[0:4, :], func=AF.Copy)
            pt_, ptk_ = self.psum(*banks)
            for blk in range(4):
                S.op('pe', 'transpose', r=[rtk, "consts"], w=[ptk_], out=pt_[:, blk * 4:(blk + 1) * 4], in_=rt[0:4, blk * 128:(blk + 1) * 128],
                     identity=self.ident_f[0:4, 0:4])
            f0 = j * 8 + half * 4
            for cv in range(4):
                S.op('dve', 'tensor_tensor', r=[ptk_, "nm"], w=["mod"], out=self.mod[:, l, f0:f0 + 4, cv],
                     in0=pt_[:, 0:16].rearrange("p (f c) -> p f c", c=4)[:, :, cv], in1=self.adab[:, l, f0:f0 + 4], op=ALU.add)
        if j == 1:
            for cv in range(4):
                S.op('dve', 'scalar_tensor_tensor', r=["mod", "nm"], w=["mod"], out=self.gmix[:, l, :, cv],
                     in0=self.mod[:, l, 8:16, cv], scalar=1.0, in1=self.nmix_t[:, l, :], op0=ALU.add, op1=ALU.mult)
        if j == 4:
            for cv in range(4):
                S.op('dve', 'scalar_tensor_tensor', r=["mod", "nm"], w=["mod"], out=self.gmlp[:, l, :, cv],
                     in0=self.mod[:, l, 32:40, cv], scalar=1.0, in1=self.nmlp_t[:, l, :], op0=ALU.add, op1=ALU.mult)

    def build(self):
        nc, S, M, cfg = self.nc, self.S, self.M, self.cfg
        layers = cfg.get('layers', [0, 1])
        nb = cfg.get('nb', NB)
        xT_d = self.din("xT", [NB, 128, 8, SEQ])
        cT_d = self.din("ctxT", [NB, 128, 8, CTX])
        cv_d = self.din("cvT", [128, 8, 4])
        adaw_d = [self.din("adaw%d" % l, [128, 8, 6 * D]) for l in range(2)]
        adab_d = self.din("adab", [128, 2, 48])
        nmix_d = self.din("nmix", [128, 2, 8])
        nmlp_d = self.din("nmlp", [128, 2, 8])
        wout_d = [self.din("wout%d" % l, [128, 8, D]) for l in range(2)]
        w1_d = [self.din("w1_%d" % l, [128, 8, 4 * D]) for l in range(2)]
        w2_d = [self.din("w2_%d" % l, [128, 32, D]) for l in range(2)]
        ident_d = self.din("ident", [128, 128])
        outT_d = self.dout("outT", [NB, 128, 8, SEQ])
        xs_d = self.nc.dram_tensor("xs", [128, 8, TOK], F32, kind="Internal").ap()
        if cfg.get('inject_y'):
            yinj_d = [self.din("yinj%d" % l, [NB, 128, 8, TOK]) for l in range(2)]
        self.declare_mixer_dram()

        self.ident_bf = M.alloc("ident_bf", [128, 128], BF16)
        self.ident_f = M.alloc("ident_f", [128, 128], F32)
        self.ones_bf = M.alloc("ones_bf", [128, 128], BF16)
        self.eps_t = M.alloc("eps_t", [128, 1], F32)
        self.mod = M.alloc("mod", [128, 2, 48, 4], F32)
        self.gmix = M.alloc("gmix", [128, 2, 8, 4], F32)
        self.gmlp = M.alloc("gmlp", [128, 2, 8, 4], F32)
        nmix = M.alloc("nmix", [128, 2, 8], F32)
        nmlp = M.alloc("nmlp", [128, 2, 8], F32)
        adab = M.alloc("adab", [128, 2, 48], F32)
        self.alloc_mixer_persistent()
        base = M.mark()

        S.dma('sp', out=self.ident_f[:, :], in_=ident_d[:, :], w=["consts"])
        S.dma('pool', out=self.ident_bf[:, :], in_=ident_d[:, :], w=["consts"])
        S.op('dve', 'memset', w=["consts"], ap=self.ones_bf[:, :], constant=1.0)
        S.op('dve', 'memset', w=["consts"], ap=self.eps_t[:, :], constant=EPS)
        S.dma('sp', out=nmix[:, :, :], in_=nmix_d[:, :, :], w=["nm"])
        S.dma('sp', out=nmlp[:, :, :], in_=nmlp_d[:, :, :], w=["nm"])
        S.dma('sp', out=adab[:, :, :], in_=adab_d[:, :, :], w=["nm"])
        self.load_mixer_persistent()

        cvs = M.alloc("cvs", [128, 8, 4], F32)
        self.scvb = M.alloc("scvb", [128, 8, 4], BF16)
        self.adab = adab
        self.nmix_t, self.nmlp_t = nmix, nmlp
        self.adaw_d = adaw_d
        S.dma('sp', out=cvs[:, :, :], in_=cv_d[:, :, :], w=["cvs"])
        S.op('act', 'activation', r=["cvs"], w=["scv"], out=self.scvb[:, :, :], in_=cvs[:, :, :], func=AF.Silu)
        base = M.mark()
        bufs = [M.alloc("awb%d" % i, [128, 8, 512], BF16) for i in range(2)]
        rowt = [M.alloc("rowt%d" % i, [4, 512], F32) for i in range(2)]
        self.mod_gi = 0
        for j in range(2):
            self.mod_group(0, j, bufs, rowt)
        self.mod_todo = [(0, j) for j in range(2, 6)] + [(1, j) for j in range(6)]
        if cfg.get('mod_all') or 0 not in layers:
            while self.mod_todo:
                self.mod_group(*self.mod_todo.pop(0), bufs, rowt)
        self.dump("mod", self.mod[:, :, :, :], [128, 2, 48, 4], ["mod"])
        S.barrier()
        M.release(base)

        self.offB = M.mark()
        hT = M.alloc("hT", [128, 8, TOK], BF16)
        self.offA = M.mark()
        xT = M.alloc("xT", [128, 8, TOK], F32)
        self.offC = M.mark()
        yT = M.alloc("yT", [128, 8, TOK], BF16)
        regD = M.mark()
        self.offD = regD

        for bi in range(nb):
            for t0 in range(0, SEQ, 512):
                S.dma('sp', out=xT[:, :, t0:t0 + 512], in_=xT_d[bi, :, :, t0:t0 + 512], w=["xT:%d" % t0])
            S.dma('sp', out=xT[:, :, SEQ:TOK], in_=cT_d[bi, :, :, :], w=["xT:%d" % SEQ])
            for l in layers:
                last = (l == 1)
                m0 = M.mark()
                sc = dict(sq=M.alloc("sq", [128, 8, 512], BF16), lnt=M.alloc("lnt", [128, 512], F32),
                          rstd=M.alloc("rstd", [128, 512], F32), mtmp=M.alloc("mtmp", [128, 8, 512], F32))
                self.modulate(xT, "xT", self.gmix[:, l], self.mod[:, l, 0:8], hT, "hT", tiles_for(b=bi), sc)
                if cfg.get('debug') and bi == 0:
                    self.dump("hT%d" % l, hT[:, :, :], [128, 8, TOK], ["hT:%d" % t for t in range(0, TOK, 512)])
                for (t0, tw, cv) in tiles_for(b=bi):
                    S.dma('sp', out=xs_d[:, :, t0:t0 + tw], in_=xT[:, :, t0:t0 + tw], r=["xT:%d" % t0], w=["xs:%d" % t0])
                S.barrier()
                M.release(m0)
                if cfg.get('inject_y'):
                    for k in range(8):
                        S.dma('pool', out=yT[:, k, :], in_=yinj_d[l][bi, :, k, :], w=["yT:%d" % t for t in range(0, TOK, 512)])
                else:
                    try:
                        if l == 0:
                            self.even_mixer(bi, hT, yT, xT, regD)
                        else:
                            self.odd_mixer(bi, hT, yT, xT, regD)
                    except StopBuild:
                        pass
                S.barrier()
                M.region(regD, Mem.HI)
                if cfg.get('debug') and bi == 0:
                    self.dump("yT%d" % l, yT[:, :, :], [128, 8, TOK], ["yT:%d" % t for t in range(0, TOK, 512)])
                for (t0, tw, cv) in tiles_for(b=bi, with_ctx=not last):
                    S.dma('sp', out=xT[:, :, t0:t0 + tw], in_=xs_d[:, :, t0:t0 + tw], r=["xs:%d" % t0], w=["xT:%d" % t0])
                m0 = M.mark()
                wo = M.alloc("wo", [128, 8, D], BF16)
                sc = dict(sq=M.alloc("sq", [128, 8, 512], BF16), lnt=M.alloc("lnt", [128, 512], F32),
                          mtmp=M.alloc("mtmp", [128, 8, 512], F32))
                assert M.top <= Mem.HI - 8 * D * 2
                M.region(Mem.HI - 8 * D * 2, Mem.HI)
                w1pre = M.alloc("w1pre", [128, 8, D], BF16)
                M.region(m0, Mem.HI)
                S.dma('pool', out=wo[:, :, :], in_=wout_d[l][:, :, :], w=["wo"])
                S.dma('pool', out=w1pre[:, :, :], in_=w1_d[l][:, :, 0:D], w=["w1g0"])
                tl = tiles_for(b=bi, with_ctx=not last)

                def wout_tile(t0, tw, cv):
                    for fo in range(8):
                        ps, pk = self.psum()
                        for k in range(8):
                            S.op('pe', 'matmul', r=["wo", "yT:%d" % t0], w=[pk], out=ps[:, :tw],
                                 lhsT=wo[:, k, fo * 128:(fo + 1) * 128], rhs=yT[:, k, t0:t0 + tw],
                                 start=(k == 0), stop=(k == 7))
                        S.op('dve', 'scalar_tensor_tensor', r=[pk, "mod"], w=["xT:%d" % t0],
                             out=xT[:, fo, t0:t0 + tw], in0=ps[:, :tw], scalar=self.mod[:, l, 16 + fo, cv:cv + 1],
                             in1=xT[:, fo, t0:t0 + tw], op0=ALU.mult, op1=ALU.add)

                wout_tile(*tl[0])
                for ti, (t0, tw, cv) in enumerate(tl):
                    if ti + 1 < len(tl):
                        wout_tile(*tl[ti + 1])
                    self.modulate(xT, "xT", self.gmlp[:, l], self.mod[:, l, 24:32], hT, "hT", [(t0, tw, cv)], sc)
                if cfg.get('debug') and bi == 0:
                    self.dump("xmid%d" % l, xT[:, :, :], [128, 8, TOK], ["xT:%d" % t for t in range(0, TOK, 512)])
                S.barrier()
                M.release(m0)
                M.release(M.mark())
                M.region(self.offC, Mem.HI)
                m0 = M.mark()
                w1g = [w1pre, M.alloc("w1g1", [128, 8, D], BF16)]
                w2g = [M.alloc("w2g%d" % i, [128, 8, D], BF16) for i in range(2)]
                hid = [M.alloc("hid%d" % i, [128, 8, 512], BF16) for i in range(2)]
                rl = [M.alloc("rl%d" % i, [128, 512], BF16) for i in range(2)]
                S.dma('pool', out=w2g[0][:, :, :], in_=w2_d[l][:, 0:8, :], w=["w2g0"])

                def pref(fg):
                    nb_ = fg % 2
                    S.dma('pool', out=w1g[nb_][:, :, :], in_=w1_d[l][:, :, fg * D:(fg + 1) * D], w=["w1g%d" % nb_])
                    S.dma('pool', out=w2g[nb_][:, :, :], in_=w2_d[l][:, fg * 8:(fg + 1) * 8, :], w=["w2g%d" % nb_])

                seq = [(fg, ti) for fg in range(4) for ti in range(len(tl))]

                def w1s(i):
                    fg, ti = seq[i]
                    t0, tw, cv = tl[ti]
                    wb, hb = fg % 2, i % 2
                    for hc in range(8):
                        ps, pk = self.psum()
                        for k in range(8):
                            S.op('pe', 'matmul', r=["w1g%d" % wb, "hT:%d" % t0], w=[pk], out=ps[:, :tw],
                                 lhsT=w1g[wb][:, k, hc * 128:(hc + 1) * 128], rhs=hT[:, k, t0:t0 + tw],
                                 start=(k == 0), stop=(k == 7))
                        rb = hc % 2
                        S.op('act', 'activation', r=[pk], w=["rl%d" % rb], out=rl[rb][:, :tw], in_=ps[:, :tw],
                             func=AF.Relu)
                        S.op('pool', 'tensor_tensor', r=["rl%d" % rb], w=["hid%d:%d" % (hb, hc)],
                             out=hid[hb][:, hc, :tw], in0=rl[rb][:, :tw], in1=rl[rb][:, :tw], op=ALU.mult)

                def w2s(i):
                    fg, ti = seq[i]
                    t0, tw, cv = tl[ti]
                    wb, hb = fg % 2, i % 2
                    for fo in range(8):
                        ps, pk = self.psum()
                        for hc in range(8):
                            S.op('pe', 'matmul', r=["w2g%d" % wb, "hid%d:%d" % (hb, hc)], w=[pk], out=ps[:, :tw],
                                 lhsT=w2g[wb][:, hc, fo * 128:(fo + 1) * 128], rhs=hid[hb][:, hc, :tw],
                                 start=(hc == 0), stop=(hc == 7))
                        S.op('dve', 'scalar_tensor_tensor', r=[pk, "mod"], w=["xT:%d" % t0],
                             out=xT[:, fo, t0:t0 + tw], in0=ps[:, :tw], scalar=self.mod[:, l, 40 + fo, cv:cv + 1],
                             in1=xT[:, fo, t0:t0 + tw], op0=ALU.mult, op1=ALU.add)

                pref(1)
                w1s(0)
                for i in range(len(seq)):
                    if i + 1 < len(seq):
                        w1s(i + 1)
                    w2s(i)
                    fg, ti = seq[i]
                    if ti == len(tl) - 1 and fg + 2 < 4:
                        pref(fg + 2)
                if cfg.get('debug') and bi == 0:
                    self.dump("x%d" % l, xT[:, :, :], [128, 8, TOK], ["xT:%d" % t for t in range(0, TOK, 512)])
                S.barrier()
                M.region(regD, Mem.HI)
            for t0 in range(0, SEQ, 512):
                S.dma('sp', out=outT_d[bi, :, :, t0:t0 + 512], in_=xT[:, :, t0:t0 + 512], r=["xT:%d" % t0], w=["outT:%d" % t0])
            S.barrier()
        S.finish()
        S.emit()
        return nc

    def cut(self, n):
        if self.cfg.get('cut') == n:
            raise StopBuild()

    def rot(self, name, shape, dtype, n):
        tiles = [self.M.alloc(name, shape, dtype) for _ in range(n)]
        uid = self.M.n
        st = {'i': 0}

        def nxt():
            i = st['i'] % n
            st['i'] += 1
            return tiles[i], "%s@%d#%d" % (name, uid, i)
        return nxt

    def rstd_of(self, ps_ap, pk, rows, tw, inv_n, nx_ln, nx_rs):
        S = self.S
        lnt, lk = nx_ln()
        rs, rk = nx_rs()
        S.op('act', 'activation', r=[pk], w=[lk], out=lnt[0:rows, :tw], in_=ps_ap, func=AF.Ln, scale=inv_n,
             bias=self.eps_t[0:rows, 0:1])
        S.op('act', 'activation', r=[lk], w=[rk], out=rs[0:rows, :tw], in_=lnt[0:rows, :tw], func=AF.Exp, scale=-0.5)
        return rs, rk

    def declare_mixer_dram(self):
        self.evwin_d = self.din("evwin", [128, 8, 1440])
        self.wuq_d = self.din("wuq", [128, 2, 768])
        self.wukvn_d = self.din("wukvn", [128, 8, 64])
        self.wukvv_d = self.din("wukvv", [128, 512])
        self.evvec_d = self.din("evvec", [128, 16])
        self.lruvec_d = self.din("lruvec", [128, 4, 12])
        self.lruw_d = self.din("lruw", [128, 4, 2, 2, 128])
        self.ropecs_d = self.din("ropecs", [128, 2, TOK])
        self.shk_d = self.din("shk", [128, 96])
        self.convwb_d = self.din("convwb", [128, 4, 512])
        self.odwin_d = self.din("odwin", [128, 8, 3104])
        self.glaw_d = self.din("glaw", [64, 2, 256])
        self.glau_d = self.din("glau", [128, 6, 128])
        self.odvec_d = self.din("odvec", [128, 4])
        self.tbig_d = self.din("tbig", [128, 8, 16, 64])

    def alloc_mixer_persistent(self):
        M = self.M
        self.evvec = M.alloc("evvec", [128, 16], F32)
        self.lruvec = M.alloc("lruvec", [128, 4, 12], F32)
        self.nsp = M.alloc("nsp", [128, 4, 2], F32)
        self.one_t = M.alloc("one_t", [128, 1], F32)
        self.spt = [M.alloc("spt%d" % i, [128, 4, 2], F32) for i in range(3)]
        self.odvec = M.alloc("odvec", [128, 4], F32)

    def load_mixer_persistent(self):
        S = self.S
        S.dma('sp', out=self.evvec[:, :], in_=self.evvec_d[:, :], w=["evvec"])
        S.dma('sp', out=self.lruvec[:, :, :], in_=self.lruvec_d[:, :, :], w=["lruvec"])
        S.op('dve', 'memset', w=["consts"], ap=self.one_t[:, :], constant=1.0)
        S.dma('sp', out=self.odvec[:, :], in_=self.odvec_d[:, :], w=["odvec"])
        e, t, u = self.spt
        S.op('act', 'activation', r=["lruvec"], w=["spe"], out=e[:, :, :], in_=self.lruvec[:, :, 9:11], func=AF.Exp, scale=-1.0)
        S.op('dve', 'tensor_scalar', r=["spe"], w=["spt"], out=t[:, :, :], in0=e[:, :, :], scalar1=-0.25, scalar2=1.0 / 3.0,
             op0=ALU.mult, op1=ALU.add)
        S.op('dve', 'tensor_tensor', r=["spe", "spt"], w=["spu"], out=u[:, :, :], in0=t[:, :, :], in1=e[:, :, :], op=ALU.mult)
        S.op('dve', 'tensor_scalar', r=["spu"], w=["spt"], out=t[:, :, :], in0=u[:, :, :], scalar1=-1.0, scalar2=0.5,
             op0=ALU.mult, op1=ALU.add)
        S.op('dve', 'tensor_tensor', r=["spe", "spt"], w=["spu"], out=u[:, :, :], in0=t[:, :, :], in1=e[:, :, :], op=ALU.mult)
        S.op('dve', 'tensor_scalar', r=["spu"], w=["spt"], out=t[:, :, :], in0=u[:, :, :], scalar1=8.0, scalar2=-8.0,
             op0=ALU.mult, op1=ALU.add)
        S.op('dve', 'tensor_tensor', r=["spe", "spt"], w=["nsp"], out=self.nsp[:, :, :], in0=t[:, :, :], in1=e[:, :, :], op=ALU.mult)

    def even_mixer(self, bi, hT, yT, xT, regD):
        nc, S, M, cfg = self.nc, self.S, self.M, self.cfg
        ev = self.evvec
        tl = tiles_for(b=bi)
        TK = ["%d" % t for t in range(0, TOK, 512)]
        M.region(self.offD, Mem.HI)
        qn = M.alloc("qn", [128, 2, TOK], BF16)
        kvn = M.alloc("kvn", [128, TOK], BF16)
        krb = M.alloc("krb", [128, TOK], BF16)
        sqkr = M.alloc("sqkr", [128, TOK], BF16)
        css = M.alloc("css", [128, 2, TOK], BF16)
        D1 = M.mark()
        winA = M.alloc("winA", [128, 8, 416], BF16)
        winR = M.alloc("winR", [128, 8, 32], BF16)
        nx_uqr = self.rot("uqr", [128, 2, 512], F32, 1)
        nx_sq2 = self.rot("sq2", [128, 2, 512], BF16, 1)
        nx_ln = self.rot("lnt", [128, 512], F32, 2)
        nx_rs = self.rot("rstd", [128, 512], F32, 2)
        nx_t = self.rot("tt", [128, 512], F32, 3)
        M.region(self.offA, self.offC)
        csf = M.alloc("csf", [128, 2, TOK], F32)
        M.region(D1, Mem.HI)
        for j in range(2):
            S.dma('sp', out=csf[:, j, :], in_=self.ropecs_d[:, j, :], w=["csf%d" % j])
            S.op('dve', 'tensor_scalar', r=["csf%d" % j, "evvec"], w=["css"], out=css[0:96, j, :], in0=csf[0:96, j, :],
                 scalar1=ev[0:96, 5 + j:6 + j], scalar2=None, op0=ALU.mult)
        S.dma('pool', out=winA[:, :, :], in_=self.evwin_d[:, :, 0:416], w=["winA"])
        for hb in range(2):
            o = hb * 16
            S.op('dve', 'tensor_scalar', r=["winA"], w=["winR"], out=winR[:, :, o:o + 8], in0=winA[:, :, 384 + o + 8:384 + o + 16],
                 scalar1=-1.0, scalar2=None, op0=ALU.mult)
            S.op('dve', 'tensor_copy', r=["winA"], w=["winR"], out=winR[:, :, o + 8:o + 16], in_=winA[:, :, 384 + o:384 + o + 8])
        self.cut(1)
        for (t0, tw, cv) in tl:
            hk = "hT:%d" % t0
            uqr, uk = nx_uqr()
            sq2, sk = nx_sq2()
            for c in range(2):
                ps, pk = self.psum()
                for k in range(8):
                    S.op('pe', 'matmul', r=["winA", hk], w=[pk], out=ps[:, :tw], lhsT=winA[:, k, c * 128:(c + 1) * 128],
                         rhs=hT[:, k, t0:t0 + tw], start=(k == 0), stop=(k == 7))
                S.op('act', 'activation', r=[pk], w=[sk + ":%d" % c], out=sq2[:, c, :tw], in_=ps[:, :tw], func=AF.Square)
                S.op('dve', 'tensor_copy', r=[pk], w=[uk + ":%d" % c], out=uqr[:, c, :tw], in_=ps[:, :tw])
            ps, pk = self.psum()
            for c in range(2):
                S.op('pe', 'matmul', r=[sk + ":%d" % c, "consts"], w=[pk], out=ps[:, :tw], lhsT=self.ones_bf[:, :],
                     rhs=sq2[:, c, :tw], start=(c == 0), stop=(c == 1))
            rs, rk = self.rstd_of(ps[:, :tw], pk, 128, tw, 1.0 / 256, nx_ln, nx_rs)
            for c in range(2):
                S.op('dve', 'scalar_tensor_tensor', r=[uk + ":%d" % c, rk, "evvec"], w=["qn:%d" % t0], out=qn[:, c, t0:t0 + tw],
                     in0=uqr[:, c, :tw], scalar=ev[:, c:c + 1], in1=rs[:, :tw], op0=ALU.mult, op1=ALU.mult)
            self.cut(2)
            ps, pk = self.psum()
            for k in range(8):
                S.op('pe', 'matmul', r=["winA", hk], w=[pk], out=ps[:, :tw], lhsT=winA[:, k, 256:384],
                     rhs=hT[:, k, t0:t0 + tw], start=(k == 0), stop=(k == 7))
            sq1, s1k = nx_sq2()
            ur, u1k = nx_t()
            S.op('act', 'activation', r=[pk], w=[s1k + ":0"], out=sq1[:, 0, :tw], in_=ps[:, :tw], func=AF.Square)
            S.op('dve', 'tensor_copy', r=[pk], w=[u1k], out=ur[:, :tw], in_=ps[:, :tw])
            ps2, pk2 = self.psum()
            S.op('pe', 'matmul', r=[s1k + ":0", "consts"], w=[pk2], out=ps2[:, :tw], lhsT=self.ones_bf[:, :], rhs=sq1[:, 0, :tw],
                 start=True, stop=True)
            rs, rk = self.rstd_of(ps2[:, :tw], pk2, 128, tw, 1.0 / 128, nx_ln, nx_rs)
            S.op('dve', 'scalar_tensor_tensor', r=[u1k, rk, "evvec"], w=["kvn:%d" % t0], out=kvn[:, t0:t0 + tw],
                 in0=ur[:, :tw], scalar=ev[:, 2:3], in1=rs[:, :tw], op0=ALU.mult, op1=ALU.mult)
            self.cut(3)
            psA, pkA = self.psum()
            for k in range(8):
                S.op('pe', 'matmul', r=["winA", hk], w=[pkA], out=psA[0:32, :tw], lhsT=winA[:, k, 384:416],
                     rhs=hT[:, k, t0:t0 + tw], start=(k == 0), stop=(k == 7))
            psB, pkB = self.psum()
            for k in range(8):
                S.op('pe', 'matmul', r=["winR", hk], w=[pkB], out=psB[0:32, :tw], lhsT=winR[:, k, :],
                     rhs=hT[:, k, t0:t0 + tw], start=(k == 0), stop=(k == 7))
            S.op('act', 'activation', r=[pkA], w=["sqkr:%d" % t0], out=sqkr[0:32, t0:t0 + tw], in_=psA[0:32, :tw], func=AF.Square)
            t1, t1k = nx_t()
            t2, t2k = nx_t()
            S.op('dve', 'tensor_tensor', r=[pkA, "css"], w=[t1k], out=t1[0:32, :tw], in0=psA[0:32, :tw],
                 in1=css[0:32, 0, t0:t0 + tw], op=ALU.mult)
            S.op('dve', 'tensor_tensor', r=[pkB, "css"], w=[t2k], out=t2[0:32, :tw], in0=psB[0:32, :tw],
                 in1=css[0:32, 1, t0:t0 + tw], op=ALU.mult)
            S.op('pool', 'tensor_tensor', r=[t1k, t2k], w=["krb:%d" % t0], out=krb[0:32, t0:t0 + tw], in0=t1[0:32, :tw],
                 in1=t2[0:32, :tw], op=ALU.add)
        self.cut(4)
        if cfg.get('debug') and bi == 0:
            self.dump("qn", qn[:, :, :], [128, 2, TOK], ["qn:" + t for t in TK])
            self.dump("kvn", kvn[:, :], [128, TOK], ["kvn:" + t for t in TK])
            self.dump("krb", krb[0:32, :], [32, TOK], ["krb:" + t for t in TK])
        S.barrier()
        mbufs = None
        if self.mod_todo:
            keep = (M.top, M.limit)
            M.region(self.offC, self.offC + 4 * TOK * 2)
            mbufs = ([M.alloc("awb%d" % i, [128, 8, 512], BF16) for i in range(2)], [M.alloc("rowt%d" % i, [4, 512], F32) for i in range(1)])
            M.region(*keep)
            for _ in range(3):
                self.mod_group(*self.mod_todo.pop(0), *mbufs)
        if cfg.get('even_stop') == 1:
            return
        M.region(D1, Mem.HI)
        wug = M.alloc("wug", [128, 8, 512], BF16)
        lw = M.alloc("lruw", [128, 4, 2, 2, 128], BF16)
        cwb = M.alloc("cwb", [128, 4, 512], BF16)
        wux = [M.alloc("wux%d" % i, [128, 8, 128], BF16) for i in range(2)]
        S.dma('pool', out=wug[:, :, :], in_=self.evwin_d[:, :, 928:1440], w=["wug"])
        S.dma('pool', out=lw[:, :, :, :, :], in_=self.lruw_d[:, :, :, :, :], w=["lruw"])
        S.dma('pool', out=cwb[:, :, :], in_=self.convwb_d[:, :, :], w=["cwb"])
        M.region(self.offA, self.offC)
        xl = M.alloc("xl", [128, TOK], F32)
        xlb = M.alloc("xlb", [128, TOK], BF16)
        at = M.alloc("at", [128, TOK], F32)
        bx = M.alloc("bx", [128, TOK], F32)
        hf = M.alloc("hf", [128, TOK], F32)
        hb_ = M.alloc("hb", [128, TOK], F32)
        gel = M.alloc("gel", [128, TOK], BF16)
        wj = M.alloc("wj", [128, 4, 8, 128], BF16)
        lv = self.lruvec
        S.dma('pool', out=wux[0][:, :, :], in_=self.evwin_d[:, :, 416:416 + 128], w=["wux0"])
        for c in range(4):
            wx, wxk = wux[c % 2], "wux%d" % (c % 2)
            if c + 1 < 4:
                S.dma('pool', out=wux[(c + 1) % 2][:, :, :], in_=self.evwin_d[:, :, 416 + (c + 1) * 128:416 + (c + 2) * 128],
                      w=["wux%d" % ((c + 1) % 2)])
            for j in range(4):
                for k in range(8):
                    S.op('pool', 'tensor_tensor', r=[wxk, "cwb"], w=["wj"], out=wj[:, j, k, :], in0=wx[:, k, :],
                         in1=cwb[:, j, c * 128:(c + 1) * 128], op=ALU.mult)
            for (t0, tw, cv) in tl:
                s0, e0 = (0, SEQ) if t0 < SEQ else (SEQ, TOK)
                ps, pk = self.psum()
                mm = []
                for j in (2, 1, 0, 3):
                    sft = j - 2
                    lo = max(t0, s0 - sft)
                    hi = min(t0 + tw, e0 - sft)
                    for k in range(8):
                        mm.append((j, k, lo, hi, sft))
                for n, (j, k, lo, hi, sft) in enumerate(mm):
                    S.op('pe', 'matmul', r=["wj"] + ["hT:%d" % t for t in range(0, TOK, 512)], w=[pk], out=ps[:, lo - t0:hi - t0],
                         lhsT=wj[:, j, k, :], rhs=hT[:, k, lo + sft:hi + sft], start=(n == 0), stop=(n == len(mm) - 1))
                S.op('act', 'activation', r=[pk, "lruvec"], w=["xl"], out=xl[:, t0:t0 + tw], in_=ps[:, :tw], func=AF.Identity,
                     bias=lv[:, c, 4:5], scale=1.0)
            for (t0, tw, cv) in tl:
                ps, pk = self.psum()
                for k in range(8):
                    S.op('pe', 'matmul', r=["wug", "hT:%d" % t0], w=[pk], out=ps[:, :tw], lhsT=wug[:, k, c * 128:(c + 1) * 128],
                         rhs=hT[:, k, t0:t0 + tw], start=(k == 0), stop=(k == 7))
                S.op('act', 'activation', r=[pk], w=["gel"], out=gel[:, t0:t0 + tw], in_=ps[:, :tw], func=AF.Gelu_apprx_tanh)
            S.op('pool', 'tensor_copy', r=["xl"], w=["xlb:%d" % t for t in range(0, TOK, 512)], out=xlb[:, :], in_=xl[:, :])
            for d in range(2):
                hx, hxk = (hf, "hf") if d == 0 else (hb_, "hb")
                for (t0, tw, cv) in tl:
                    psa, pka = self.psum()
                    S.op('pe', 'matmul', r=["lruw", "xlb:%d" % t0], w=[pka], out=psa[:, :tw], lhsT=lw[:, c, d, 0, :], rhs=xlb[:, t0:t0 + tw],
                         start=True, stop=True)
                    S.op('act', 'activation', r=[pka, "lruvec"], w=["at"], out=at[:, t0:t0 + tw], in_=psa[:, :tw], func=AF.Sigmoid,
                         bias=lv[:, c, 5 + d:6 + d], scale=1.0)
                    psx, pkx = self.psum()
                    S.op('pe', 'matmul', r=["lruw", "xlb:%d" % t0], w=[pkx], out=psx[:, :tw], lhsT=lw[:, c, d, 1, :], rhs=xlb[:, t0:t0 + tw],
                         start=True, stop=True)
                    S.op('act', 'activation', r=[pkx, "lruvec"], w=["bx"], out=bx[:, t0:t0 + tw], in_=psx[:, :tw], func=AF.Sigmoid,
                         bias=lv[:, c, 7 + d:8 + d], scale=1.0)
                S.op('act', 'activation', r=["at", "nsp"], w=["at"], out=at[:, :], in_=at[:, :], func=AF.Exp, scale=self.nsp[:, c, d:d + 1])
                S.op('dve', 'tensor_tensor', r=["at"], w=[hxk], out=hx[:, :], in0=at[:, :], in1=at[:, :], op=ALU.mult)
                S.op('act', 'activation', r=[hxk], w=[hxk], out=hx[:, :], in_=hx[:, :], func=AF.Sqrt, scale=-1.0, bias=self.one_t[:, 0:1])
                S.op('dve', 'tensor_tensor', r=["bx", "xl"], w=["bx"], out=bx[:, :], in0=bx[:, :], in1=xl[:, :], op=ALU.mult)
                S.op('dve', 'tensor_tensor', r=["bx", hxk], w=["bx"], out=bx[:, :], in0=bx[:, :], in1=hx[:, :], op=ALU.mult)
                if d == 0:
                    S.op('dve', 'tensor_tensor_scan', r=["at", "bx"], w=[hxk], out=hf[:, SEQ:TOK], data0=at[:, SEQ:TOK],
                         data1=bx[:, SEQ:TOK], initial=0.0, op0=ALU.mult, op1=ALU.add)
                    S.op('dve', 'tensor_tensor_scan', r=["at", "bx", hxk], w=[hxk], out=hf[:, 0:SEQ], data0=at[:, 0:SEQ],
                         data1=bx[:, 0:SEQ], initial=hf[:, TOK - 1:TOK], op0=ALU.mult, op1=ALU.add)
                else:
                    S.op('dve', 'tensor_tensor_scan', r=["at", "bx"], w=[hxk], out=hb_[:, SEQ:TOK][:, ::-1],
                         data0=at[:, SEQ:TOK][:, ::-1], data1=bx[:, SEQ:TOK][:, ::-1], initial=0.0, op0=ALU.mult, op1=ALU.add)
                    S.op('dve', 'tensor_tensor_scan', r=["at", "bx", hxk], w=[hxk], out=hb_[:, 0:SEQ][:, ::-1],
                         data0=at[:, 0:SEQ][:, ::-1], data1=bx[:, 0:SEQ][:, ::-1], initial=hb_[:, SEQ:SEQ + 1],
                         op0=ALU.mult, op1=ALU.add)
            S.op('dve', 'tensor_tensor', r=["hf", "hb"], w=["hf"], out=hf[:, :], in0=hf[:, :], in1=hb_[:, :], op=ALU.add)
            S.op('dve', 'tensor_tensor', r=["hf", "gel"], w=["yT:" + t for t in TK], out=yT[:, 4 + c, :], in0=hf[:, :], in1=gel[:, :],
                 op=ALU.mult)
        S.barrier()
        if cfg.get('even_stop') == 2:
            return
        M.region(D1, Mem.HI)
        wuq = M.alloc("wuq", [128, 2, 768], BF16)
        wuqr = M.alloc("wuqr", [128, 2, 8, 96], BF16)
        wkn = M.alloc("wkn", [128, 8, 96], BF16)
        wv = M.alloc("wv", [128, 512], BF16)
        shk = M.alloc("shk", [128, 96], BF16)
        nx_sq = self.rot("sq", [128, 512], BF16, 2)
        nx_ln = self.rot("lnt", [128, 512], F32, 2)
        nx_rs = self.rot("rstd", [128, 512], F32, 2)
        nx_t = self.rot("tt", [128, 512], F32, 2)
        nx_raw = self.rot("qkr", [128, 512], F32, 2)
        nx_rec = self.rot("rec", [128, 1], F32, 4)
        S.dma('pool', out=wuq[:, :, :], in_=self.wuq_d[:, :, :], w=["wuq"])
        S.dma('pool', out=wkn[:, :, 0:64], in_=self.wukvn_d[:, :, :], w=["wkn"])
        S.op('dve', 'memset', w=["wkn"], ap=wkn[:, :, 64:96], constant=0.0)
        S.dma('pool', out=wv[:, :], in_=self.wukvv_d[:, :], w=["wv"])
        S.dma('pool', out=shk[:, :], in_=self.shk_d[:, :], w=["shk"])
        S.op('dve', 'memset', w=["wuqr"], ap=wuqr[:, :, :, 0:64], constant=0.0)
        wuq4 = wuq[:, :, :].rearrange("p c (h e) -> p c h e", e=96)
        for hb in range(2):
            o = 64 + hb * 16
            S.op('dve', 'tensor_scalar', r=["wuq"], w=["wuqr"], out=wuqr[:, :, :, o:o + 8], in0=wuq4[:, :, :, o + 8:o + 16],
                 scalar1=-1.0, scalar2=None, op0=ALU.mult)
            S.op('dve', 'tensor_copy', r=["wuq"], w=["wuqr"], out=wuqr[:, :, :, o + 8:o + 16], in_=wuq4[:, :, :, o:o + 8])
        M.region(self.offB, self.offC)
        PT = [M.alloc("PT%d" % i, [128, 18, 512], BF16) for i in range(2)]
        ytok = M.alloc("ytok", [128, 18, 512], BF16)
        Vx = M.alloc("Vx", [128, 18, 8, 65], BF16)
        QT = [M.alloc("QT%d" % i, [128, TOK], BF16) for i in range(2)]
        KT = [M.alloc("KT%d" % i, [128, TOK], BF16) for i in range(2)]
        KR = 128 if cfg.get('mla_k128', 0) else 96
        if KR == 128:
            for i in range(2):
                S.op('pool', 'memset', w=["QT%d:%d" % (i, t) for t in range(0, TOK, 512)], ap=QT[i][96:128, :], constant=0.0)
                S.op('pool', 'memset', w=["KT%d:%d" % (i, t) for t in range(0, TOK, 512)], ap=KT[i][96:128, :], constant=0.0)
        S.op('pool', 'memset', w=["Vx"], ap=Vx[:, :, :, 64:65], constant=1.0)
        for kc in range(18):
            ps, pk = self.psum(0, 3)
            S.op('pe', 'matmul', r=["kvn:%d" % (kc // 4 * 512), "wv"], w=[pk], out=ps[:, :], lhsT=kvn[:, kc * 128:(kc + 1) * 128],
                 rhs=wv[:, :], start=True, stop=True)
            S.op('dve', 'tensor_copy', r=[pk], w=["Vx"], out=Vx[:, kc, :, 0:64], in_=ps[:, :].rearrange("p (h e) -> p h e", e=64))
        scale = 96.0 ** -0.5

        def proj(h, only=None):
            qt, kt = QT[h % 2], KT[h % 2]
            for ti, (t0, tw, cv) in enumerate(tl):
                if only is not None and ti != only:
                    continue
                psq, pkq = self.psum(6, 8)
                for c in range(2):
                    S.op('pe', 'matmul', r=["wuq", "qn:%d" % t0], w=[pkq], out=psq[0:96, :tw], lhsT=wuq[:, c, h * 96:(h + 1) * 96],
                         rhs=qn[:, c, t0:t0 + tw], start=(c == 0), stop=(c == 1))
                sq, sk = nx_sq()
                S.op('act', 'activation', r=[pkq], w=[sk], out=sq[0:96, :tw], in_=psq[0:96, :tw], func=AF.Square)
                qr, qrk = nx_raw()
                S.op('dve', 'tensor_copy', r=[pkq], w=[qrk], out=qr[0:96, :tw], in_=psq[0:96, :tw])
                psr, pkr = self.psum(6, 8)
                for c in range(2):
                    S.op('pe', 'matmul', r=["wuqr", "qn:%d" % t0], w=[pkr], out=psr[0:96, :tw], lhsT=wuqr[:, c, h, :],
                         rhs=qn[:, c, t0:t0 + tw], start=(c == 0), stop=(c == 1))
                t2, t2k = nx_t()
                S.op('dve', 'tensor_tensor', r=[pkr, "css"], w=[t2k], out=t2[64:96, :tw], in0=psr[64:96, :tw],
                     in1=css[64:96, 1, t0:t0 + tw], op=ALU.mult)
                pss, pks = self.psum(6, 8)
                S.op('pe', 'matmul', r=[sk, "consts"], w=[pks], out=pss[0:96, :tw], lhsT=self.ones_bf[0:96, 0:96], rhs=sq[0:96, :tw],
                     start=True, stop=True)
                rs, rk = self.rstd_of(pss[0:96, :tw], pks, 96, tw, 1.0 / 96, nx_ln, nx_rs)
                S.op('dve', 'scalar_tensor_tensor', r=[qrk, rk, "evvec"], w=["QT%d:%d" % (h % 2, t0)], out=qt[0:64, t0:t0 + tw],
                     in0=qr[0:64, :tw], scalar=ev[0:64, 3:4], in1=rs[0:64, :tw], op0=ALU.mult, op1=ALU.mult)
                t1, t1k = nx_t()
                S.op('pool', 'tensor_tensor', r=[qrk, "css"], w=[t1k], out=t1[64:96, :tw], in0=qr[64:96, :tw],
                     in1=css[64:96, 0, t0:t0 + tw], op=ALU.mult)
                S.op('pool', 'tensor_tensor', r=[t1k, t2k], w=[t1k], out=t1[64:96, :tw], in0=t1[64:96, :tw], in1=t2[64:96, :tw],
                     op=ALU.add)
                S.op('pool', 'tensor_tensor', r=[t1k, rk], w=["QT%d:%d" % (h % 2, t0)], out=qt[64:96, t0:t0 + tw], in0=t1[64:96, :tw],
                     in1=rs[64:96, :tw], op=ALU.mult)
                psk, pkk = self.psum(6, 8)
                S.op('pe', 'matmul', r=["wkn", "kvn:%d" % t0], w=[pkk], out=psk[0:96, :tw], lhsT=wkn[:, h, :], rhs=kvn[:, t0:t0 + tw],
                     start=True, stop=False)
                S.op('pe', 'matmul', r=["shk", "krb:%d" % t0], w=[pkk], out=psk[0:96, :tw], lhsT=shk[0:32, :], rhs=krb[0:32, t0:t0 + tw],
                     start=False, stop=True)
                sq, sk = nx_sq()
                S.op('act', 'activation', r=[pkk], w=[sk], out=sq[0:64, :tw], in_=psk[0:64, :tw], func=AF.Square)
                kr_, krk = nx_raw()
                S.op('dve', 'tensor_copy', r=[pkk], w=[krk], out=kr_[0:96, :tw], in_=psk[0:96, :tw])
                pss, pks = self.psum(6, 8)
                S.op('pe', 'matmul', r=[sk, "consts"], w=[pks], out=pss[0:96, :tw], lhsT=self.ones_bf[0:64, 0:96], rhs=sq[0:64, :tw],
                     start=True, stop=False)
                S.op('pe', 'matmul', r=["sqkr:%d" % t0, "consts"], w=[pks], out=pss[0:96, :tw], lhsT=self.ones_bf[0:32, 0:96],
                     rhs=sqkr[0:32, t0:t0 + tw], start=False, stop=True)
                rs, rk = self.rstd_of(pss[0:96, :tw], pks, 96, tw, 1.0 / 96, nx_ln, nx_rs)
                S.op('dve', 'scalar_tensor_tensor', r=[krk, rk, "evvec"], w=["KT%d:%d" % (h % 2, t0)], out=kt[0:96, t0:t0 + tw],
                     in0=kr_[0:96, :tw], scalar=ev[0:96, 4:5], in1=rs[0:96, :tw], op0=ALU.mult, op1=ALU.mult)

        qtiles = [(t0, tw, list(range(18))) for (t0, tw, cv) in tiles_for(b=bi, with_ctx=False)] + [(SEQ, CTX, [16, 17])]

        NH = cfg.get('mla_heads', 8)
        seq_ = [(h, qi) for h in range(NH) for qi in range(len(qtiles))]
        steps = []
        for it, (h, qi) in enumerate(seq_):
            kcs = qtiles[qi][2]
            for m in range(len(kcs) // 2):
                steps.append((it, h, qi, m, kcs[2 * m], kcs[2 * m + 1], len(kcs) // 2))

        def emit_S(si, step):
            it, h, qi, m, kc0, kc1, npair = step
            t0, tw, kcs = qtiles[qi]
            pt, ptk = PT[it % 2], "PT%d" % (it % 2)
            if m == 0 and qi == 0 and h + 1 < NH:
                proj(h + 1)
            if m == 0 and qi in (1, 3) and mbufs is not None and self.mod_todo:
                self.mod_group(*self.mod_todo.pop(0), *mbufs, banks=(6, 8))
            P = self.pp[si % 2]
            pks = ["ps%d" % (2 * (si % 2)), "ps%d" % (2 * (si % 2) + 1)]
            for j, kc in enumerate((kc0, kc1)):
                S.op('pe', 'matmul', r=["KT%d:%d" % (h % 2, kc // 4 * 512), "QT%d:%d" % (h % 2, t0)], w=[pks[j]], out=P[:, j * 512:j * 512 + tw],
                     lhsT=KT[h % 2][0:96, kc * 128:(kc + 1) * 128], rhs=QT[h % 2][0:96, t0:t0 + tw], start=True, stop=True)
            S.op('act', 'activation', r=pks, w=[ptk + ":%d" % kc0, ptk + ":%d" % kc1], out=pt[:, kc0:kc0 + 2, :tw],
                 in_=P[:, :].rearrange("p (b c) -> p b c", c=512)[:, :, :tw], func=AF.Exp, scale=scale)

        def emit_PV(step):
            it, h, qi, m, kc0, kc1, npair = step
            t0, tw, kcs = qtiles[qi]
            pt, ptk = PT[it % 2], "PT%d" % (it % 2)
            pso, pko = self.ps[4 + it % 2], "ps%d" % (4 + it % 2)
            for j, kc in enumerate((kc0, kc1)):
                n = 2 * m + j
                for sub in range(tw // 128):
                    S.op('pe', 'matmul', r=[ptk + ":%d" % kc, "Vx"], w=[pko], out=pso[:, sub * 65:(sub + 1) * 65],
                         lhsT=pt[:, kc, sub * 128:(sub + 1) * 128], rhs=Vx[:, kc, h, :], start=(n == 0 and sub == 0),
                         stop=(n == 2 * npair - 1), skip_group_check=True)
            if m == npair - 1:
                for sub in range(tw // 128):
                    st = (t0 // 128) + sub
                    rec, rck = nx_rec()
                    S.op('dve', 'reciprocal', r=[pko], w=[rck], out=rec[:, 0:1], in_=pso[:, sub * 65 + 64:sub * 65 + 65])
                    S.op('dve', 'tensor_scalar', r=[pko, rck], w=["ytok:%d" % st], out=ytok[:, st, h * 64:(h + 1) * 64],
                         in0=pso[:, sub * 65:sub * 65 + 64], scalar1=rec[:, 0:1], scalar2=None, op0=ALU.mult)

        LAG = cfg.get('lag', 2)
        proj(0)
        for i, stp in enumerate(steps):
            emit_S(i, stp)
            if i >= LAG:
                emit_PV(steps[i - LAG])
        for stp in steps[max(0, len(steps) - LAG):]:
            emit_PV(stp)
        while mbufs is not None and self.mod_todo:
            self.mod_group(*self.mod_todo.pop(0), *mbufs, banks=(6, 8))
        if cfg.get('debug') and bi == 0:
            self.dump("ytok", ytok[:, :, :], [128, 18, 512], ["ytok:%d" % st for st in range(18)])
        for st in range(18):
            pb = self.psb[st % 2]
            pbk = "ps7"
            for c in range(4):
                S.op('pe', 'transpose', r=["ytok:%d" % st, "consts"], w=[pbk], out=pb[:, c * 128:(c + 1) * 128],
                     in_=ytok[:, st, c * 128:(c + 1) * 128], identity=self.ident_bf[:, :])
            S.op('act', 'activation', r=[pbk], w=["yT:%d" % (st // 4 * 512)], out=yT[:, 0:4, st * 128:(st + 1) * 128],
                 in_=pb[:, 0:512].rearrange("p (c t) -> p c t", t=128), func=AF.Copy)

    def odd_mixer(self, bi, hT, yT, xT, regD):
        nc, S, M, cfg = self.nc, self.S, self.M, self.cfg
        tl = tiles_for(b=bi)
        TK = ["%d" % t for t in range(0, TOK, 512)]
        ov = self.odvec
        for p in range(2 if not cfg.get('skip_gla') else 0):
            M.region(self.offD, Mem.HI)
            wq = M.alloc("gwq", [128, 8, 128], BF16)
            wk = M.alloc("gwk", [128, 8, 128], BF16)
            wvv = M.alloc("gwv", [128, 8, 256], BF16)
            wg = M.alloc("gwg", [128, 8, 256], BF16)
            wlr = M.alloc("gwlr", [128, 8, 32], BF16)
            glau = M.alloc("glau", [128, 4, 128], F32)
            gmask = M.alloc("gmask", [128, 2, 128], BF16)
            glaw = M.alloc("glaw", [64, 2, 256], F32)
            nx_qk = self.rot("qkraw", [128, 512], BF16, 4)
            lrTs = [M.alloc("lrT%d" % i, [64, 512], F32) for i in range(2)]
            gmask4 = M.alloc("gmask4", [128, 2, 4, 128], BF16)
            nx_e = self.rot("ge", [128, 512], F32, 2)
            nx_l = self.rot("gl", [128, 512], F32, 2)
            nx_x = self.rot("gx", [128, 512], F32, 4)
            nx_kt = self.rot("gkt", [128, 512], F32, 2)
            nx_am = self.rot("attm", [128, 512], BF16, 4)
            nx_sq = self.rot("gsq", [128, 512], BF16, 2)
            nx_ln = self.rot("glnt", [128, 512], F32, 2)
            od = self.odwin_d
            S.dma('pool', out=wq[:, :, :], in_=od[:, :, p * 128:(p + 1) * 128], w=["gwq"])
            S.dma('pool', out=wk[:, :, :], in_=od[:, :, 256 + p * 128:256 + (p + 1) * 128], w=["gwk"])
            S.dma('pool', out=wvv[:, :, :], in_=od[:, :, 512 + p * 256:512 + (p + 1) * 256], w=["gwv"])
            S.dma('pool', out=wg[:, :, :], in_=od[:, :, 1024 + p * 256:1024 + (p + 1) * 256], w=["gwg"])
            S.dma('pool', out=wlr[:, :, :], in_=od[:, :, 1536:1568], w=["gwlr"])
            S.dma('sp', out=glau[:, :, :], in_=self.glau_d[:, 0:4, :], w=["glau"])
            S.dma('pool', out=gmask[:, :, :], in_=self.glau_d[:, 4:6, :], w=["gmask"])
            for j in range(4):
                S.dma('pool', out=gmask4[:, :, j, :], in_=self.glau_d[:, 4:6, :], w=["gmask4"])
            S.dma('sp', out=glaw[:, :, :], in_=self.glaw_d[:, :, :], w=["glaw"])
            for i_ in range(2):
                S.op('dve', 'memset', w=["lrT%d" % i_], ap=lrTs[i_][:, :], constant=1.0)
            M.region(self.offA, self.offC)
            kd = [M.alloc("kd%d" % d, [128, 18, 128], BF16) for d in range(2)]
            vt = M.alloc("vtok", [128, 18, 256], BF16)
            qd = [M.alloc("qdec%d" % d, [128, SEQ], BF16) for d in range(2)]
            ki = [M.alloc("kinv%d" % d, [128, SEQ], BF16) for d in range(2)]
            sh = [M.alloc("shist%d" % d, [128, 32, 128], BF16) for d in range(2)]
            sg = M.alloc("sg", [128, 2, SEQ], BF16)
            dec = M.alloc("dec", [128, 2, 36], F32)
            sst = [[M.alloc("sst%d%d" % (d, i), [128, 128], F32) for i in range(2)] for d in range(2)]
            nx_rs = self.rot("grstd", [128, 512], F32, 2)
            nx_t = self.rot("gt1", [128, 512], F32, 2)
            def g1_a(t0, tw, cv):
                lrT = lrTs[(t0 // 512) % 2]
                lrk = "lrT%d" % ((t0 // 512) % 2)
                qraw = kraw = qk_ = kk_ = None
                hk = "hT:%d" % t0
                lat = t0 < SEQ
                if lat:
                    qraw, qk_ = nx_qk()
                    ps, pk = self.psum()
                    for k in range(8):
                        S.op('pe', 'matmul', r=["gwq", hk], w=[pk], out=ps[:, :tw], lhsT=wq[:, k, :], rhs=hT[:, k, t0:t0 + tw],
                             start=(k == 0), stop=(k == 7))
                    S.op('act', 'activation', r=[pk], w=[qk_], out=qraw[:, :tw], in_=ps[:, :tw], func=AF.Copy, scale=0.125)
                    kraw, kk_ = nx_qk()
                    ps, pk = self.psum()
                    for k in range(8):
                        S.op('pe', 'matmul', r=["gwk", hk], w=[pk], out=ps[:, :tw], lhsT=wk[:, k, :], rhs=hT[:, k, t0:t0 + tw],
                             start=(k == 0), stop=(k == 7))
                    S.op('act', 'activation', r=[pk], w=[kk_], out=kraw[:, :tw], in_=ps[:, :tw], func=AF.Copy)
                    for hl in range(2):
                        ps, pk = self.psum()
                        for k in range(8):
                            S.op('pe', 'matmul', r=["gwg", hk], w=[pk], out=ps[:, :tw], lhsT=wg[:, k, hl * 128:(hl + 1) * 128],
                                 rhs=hT[:, k, t0:t0 + tw], start=(k == 0), stop=(k == 7))
                        S.op('act', 'activation', r=[pk], w=["sg:%d" % t0], out=sg[:, hl, t0:t0 + tw], in_=ps[:, :tw], func=AF.Silu)
                ps, pk = self.psum()
                for k in range(8):
                    S.op('pe', 'matmul', r=["gwlr", hk], w=[pk], out=ps[0:32, :tw], lhsT=wlr[:, k, :],
                         rhs=hT[:, k, t0:t0 + tw], start=(k == 0), stop=(k == 7))
                S.op('act', 'activation', r=[pk], w=[lrk], out=lrT[0:32, :tw], in_=ps[0:32, :tw], func=AF.Copy)
                nsub = tw // 128
                st0 = t0 // 128
                psk, pkk = self.psum()
                for sub in range(nsub):
                    c0 = sub * 128
                    for k in range(8):
                        S.op('pe', 'matmul', r=["gwk", hk], w=[pkk], out=psk[:, c0:c0 + 128], lhsT=hT[:, k, t0 + c0:t0 + c0 + 128], rhs=wk[:, k, :],
                             start=(k == 0), stop=(k == 7))
                ktk, ktkk = nx_kt()
                S.op('act', 'activation', r=[pkk], w=[ktkk], out=ktk[:, :tw], in_=psk[:, :tw], func=AF.Copy)
                for s2 in range(0, nsub, 2):
                    psv, pkv = self.psum()
                    for sub in range(s2, min(s2 + 2, nsub)):
                        c0 = sub * 128
                        for k in range(8):
                            S.op('pe', 'matmul', r=["gwv", hk], w=[pkv], out=psv[:, (sub - s2) * 256:(sub - s2 + 1) * 256],
                                 lhsT=hT[:, k, t0 + c0:t0 + c0 + 128], rhs=wvv[:, k, :], start=(k == 0), stop=(k == 7))
                    ns = min(2, nsub - s2)
                    S.op('act', 'activation', r=[pkv], w=["vtok:%d" % (st0 + s2 + i) for i in range(ns)],
                         out=vt[:, st0 + s2:st0 + s2 + ns, :], in_=psv[:, 0:ns * 256].rearrange("p (a b) -> p a b", b=256), func=AF.Copy)
                return dict(qraw=qraw, kraw=kraw, qk_=qk_, kk_=kk_, lrT=lrT, lrk=lrk, ktk=ktk, ktkk=ktkk, hk=hk, lat=lat, nsub=nsub, st0=st0)

            def g1_b(t0, tw, cv, st_):
                qraw, kraw, qk_, kk_, lrT, lrk, ktk, ktkk, hk, lat, nsub, st0 = (st_[k_] for k_ in ('qraw', 'kraw', 'qk_', 'kk_', 'lrT', 'lrk', 'ktk', 'ktkk', 'hk', 'lat', 'nsub', 'st0'))
                for d in range(2):
                    psz, pkz = self.psum()
                    for sub in range(nsub):
                        c0 = sub * 128
                        S.op('pe', 'matmul', r=[lrk, "glaw"], w=[pkz], out=psz[:, c0:c0 + 128], lhsT=lrT[0:33, c0:c0 + 128],
                             rhs=glaw[0:33, d, p * 128:(p + 1) * 128], start=True, stop=True)
                    e, ek = nx_e()
                    l, lk = nx_l()
                    S.op('act', 'activation', r=[pkz], w=[ek], out=e[:, :tw], in_=psz[:, :tw], func=AF.Exp, scale=-1.0)
                    S.op('act', 'activation', r=[ek], w=[lk], out=l[:, :tw], in_=e[:, :tw], func=AF.Ln, bias=self.one_t[:, 0:1], scale=1.0)
                    psr, pkr = self.psum()
                    for sub in range(nsub):
                        c0 = sub * 128
                        S.op('pe', 'matmul', r=[lk, "glau"], w=[pkr], out=psr[:, c0:c0 + 128], lhsT=glau[:, 2 + d, :], rhs=l[:, c0:c0 + 128],
                             start=True, stop=True)
                    er, erk = nx_x()
                    S.op('act', 'activation', r=[pkr], w=[erk], out=er[:, :tw], in_=psr[:, :tw], func=AF.Exp)
                    S.op('dve', 'tensor_tensor', r=[ktkk, erk], w=["kd%d:%d" % (d, st0 + i) for i in range(nsub)],
                         out=kd[d][:, st0:st0 + nsub, :], in0=ktk[:, :tw].rearrange("p (a b) -> p a b", b=128),
                         in1=er[:, :tw].rearrange("p (a b) -> p a b", b=128), op=ALU.mult)
                    psc, pkc = self.psum()
                    for sub in range(nsub):
                        c0 = sub * 128
                        S.op('pe', 'matmul', r=[lk, "glau"], w=[pkc], out=psc[:, c0:c0 + 128], lhsT=l[:, c0:c0 + 128], rhs=glau[:, d, :],
                             start=True, stop=True)
                    ep, epk = nx_x()
                    S.op('act', 'activation', r=[pkc], w=[epk], out=ep[:, :tw], in_=psc[:, :tw], func=AF.Exp)
                    g0_ = 2 * st0
                    S.op('dve', 'tensor_copy', r=[epk], w=["dec"], out=dec[:, d, g0_:g0_ + 2 * nsub],
                         in_=ep[:, :tw].rearrange("p (a b) -> p a b", b=64)[:, :, 63 if d == 0 else 0])
                    if lat:
                        em, emk = nx_x()
                        S.op('act', 'activation', r=[pkc], w=[emk], out=em[:, :tw], in_=psc[:, :tw], func=AF.Exp, scale=-1.0)
                        S.op('dve', 'tensor_tensor', r=[qk_, epk], w=["qdec%d:%d" % (d, st0 + i) for i in range(nsub)], out=qd[d][:, t0:t0 + tw],
                             in0=qraw[:, :tw], in1=ep[:, :tw], op=ALU.mult)
                        S.op('pool', 'tensor_tensor', r=[kk_, emk], w=["kinv%d:%d" % (d, st0 + i) for i in range(nsub)], out=ki[d][:, t0:t0 + tw],
                             in0=kraw[:, :tw], in1=em[:, :tw], op=ALU.mult)

            pend_ = g1_a(*tl[0])
            for ti_, tile_ in enumerate(tl):
                nxt2_ = g1_a(*tl[ti_ + 1]) if ti_ + 1 < len(tl) else None
                g1_b(*tile_, pend_)
                pend_ = nxt2_
            if cfg.get('debug') and bi == 0 and p == 0:
                self.dump("dec", dec[:, :, :], [128, 2, 36], ["dec"])
                for d in range(2):
                    self.dump("kd%d" % d, kd[d][:, :, :], [128, 18, 128], ["kd%d:%d" % (d, st) for st in range(18)])
                    self.dump("qd%d" % d, qd[d][:, :], [128, SEQ], ["qdec%d:%d" % (d, st) for st in range(16)])
                    self.dump("ki%d" % d, ki[d][:, :], [128, SEQ], ["kinv%d:%d" % (d, st) for st in range(16)])
                self.dump("vt", vt[:, :, :], [128, 18, 256], ["vtok:%d" % st for st in range(18)])
            self.cut(11)
            orders = [[32, 33, 34, 35] + list(range(32)), [35, 34, 33, 32] + list(range(31, -1, -1))]
            for d in range(2):
                S.op('dve', 'memset', w=["sst%d0" % d], ap=sst[d][0][:, :], constant=0.0)
            for n in range(36):
                for d in range(2):
                    g = orders[d][n]
                    st, half = g // 2, g % 2
                    cur, nxt_ = sst[d][n % 2], sst[d][(n + 1) % 2]
                    ck, nk_ = "sst%d%d" % (d, n % 2), "sst%d%d" % (d, (n + 1) % 2)
                    if g < 32:
                        S.op('pool', 'tensor_copy', r=[ck], w=["shist%d:%d" % (d, g)], out=sh[d][:, g, :], in_=cur[:, :])
                    if n == 35:
                        continue
                    ps, pk = self.psum()
                    r0_ = half * 64
                    S.op('pe', 'matmul', r=["kd%d:%d" % (d, st), "vtok:%d" % st], w=[pk], out=ps[:, 0:256],
                         lhsT=kd[d][r0_:r0_ + 64, st, :], rhs=vt[r0_:r0_ + 64, st, :], start=True, stop=True)
                    for hl in range(2):
                        rr = slice(hl * 64, (hl + 1) * 64)
                        S.op('dve', 'scalar_tensor_tensor', r=[ck, pk, "dec"], w=[nk_], out=nxt_[rr, :], in0=cur[rr, :],
                             scalar=dec[rr, d, g:g + 1], in1=ps[rr, hl * 128:(hl + 1) * 128], op0=ALU.mult, op1=ALU.add)
            if cfg.get('debug') and bi == 0 and p == 0:
                for d in range(2):
                    self.dump("sh%d" % d, sh[d][:, :, :], [128, 32, 128], ["shist%d:%d" % (d, g) for g in range(32)])
            self.cut(12)
            for grp in range(4):
                T0 = grp * 512
                for hl in range(2):
                    rr = slice(hl * 64, (hl + 1) * 64)
                    ams = []
                    for d in range(2):
                        psa, pka = self.psum()
                        for j in range(4):
                            st = grp * 4 + j
                            tt0 = st * 128
                            S.op('pe', 'matmul', r=["kinv%d:%d" % (d, st), "qdec%d:%d" % (d, st)], w=[pka], out=psa[:, j * 128:(j + 1) * 128],
                                 lhsT=ki[d][rr, tt0:tt0 + 128], rhs=qd[d][rr, tt0:tt0 + 128], start=True, stop=True)
                        am, amk = nx_am()
                        S.op('dve', 'tensor_tensor', r=[pka, "gmask4"], w=[amk], out=am[:, :], in0=psa[:, :],
                             in1=gmask4[:, d, :, :].rearrange("p a b -> p (a b)"), op=ALU.mult)
                        ams.append((am, amk))
                    pso, pko = self.psum()
                    for j in range(4):
                        st = grp * 4 + j
                        tt0 = st * 128
                        for d in range(2):
                            S.op('pe', 'matmul', r=[ams[d][1], "vtok:%d" % st], w=[pko], out=pso[:, j * 128:(j + 1) * 128],
                                 lhsT=vt[:, st, hl * 128:(hl + 1) * 128], rhs=ams[d][0][:, j * 128:(j + 1) * 128], start=(d == 0), stop=False)
                        for d in range(2):
                            for half in range(2):
                                g = 2 * st + half
                                S.op('pe', 'matmul', r=["shist%d:%d" % (d, g), "qdec%d:%d" % (d, st)], w=[pko],
                                     out=pso[:, j * 128 + half * 64:j * 128 + (half + 1) * 64], lhsT=sh[d][rr, g, :],
                                     rhs=qd[d][rr, tt0 + half * 64:tt0 + (half + 1) * 64], start=False, stop=(d == 1 and half == 1))
                    sq, sk = nx_sq()
                    S.op('act', 'activation', r=[pko], w=[sk], out=sq[:, :], in_=pso[:, :], func=AF.Square)
                    pss, pks = self.psum()
                    S.op('pe', 'matmul', r=[sk, "consts"], w=[pks], out=pss[:, :], lhsT=self.ones_bf[:, :], rhs=sq[:, :], start=True, stop=True)
                    rs, rk = self.rstd_of(pss[:, :], pks, 128, 512, 1.0 / 128, nx_ln, nx_rs)
                    t1, t1k = nx_t()
                    S.op('dve', 'scalar_tensor_tensor', r=[pko, rk, "odvec"], w=[t1k], out=t1[:, :], in0=pso[:, :], scalar=ov[:, 0:1],
                         in1=rs[:, :], op0=ALU.mult, op1=ALU.mult)
                    S.op('pool', 'tensor_tensor', r=[t1k, "sg:%d" % T0], w=["yT:%d" % T0],
                         out=yT[:, 2 * p + hl, T0:T0 + 512], in0=t1[:, :], in1=sg[:, hl, T0:T0 + 512], op=ALU.mult)
            S.barrier()
        if cfg.get('skip_nat'):
            return
        M.region(self.offD, Mem.HI)
        tb = M.alloc("tbig", [128, 8, 16, 64], BF16)
        wnq_ = [M.alloc("wnq%d" % i, [128, 8, 128], BF16) for i in range(2)]
        wnk_ = [M.alloc("wnk%d" % i, [128, 8, 128], BF16) for i in range(2)]
        wnv = M.alloc("wnv", [128, 8, 512], BF16)
        ones2 = M.alloc("ones2", [128, 128], BF16)
        PTn = [M.alloc("PTn%d" % i, [128, 7, 512], BF16) for i in range(2)]
        nx_sb = self.rot("nsb", [128, 512], F32, 2)
        nx_sq = self.rot("nsq", [128, 512], BF16, 2)
        nx_ln = self.rot("nlnt", [128, 512], F32, 1)
        nx_rs = self.rot("nrstd", [128, 512], F32, 2)
        nx_rec = self.rot("nrec", [128, 4], F32, 4)
        M.region(self.offA, self.offC)
        QnT = M.alloc("QnT", [128, 4, SEQ], BF16)
        KnT = M.alloc("KnT", [128, 4, TOK], BF16)
        Vn = M.alloc("Vn", [128, 18, 8, 65], BF16)
        ytok = M.alloc("nytok", [128, 16, 512], BF16)
        od = self.odwin_d
        for hh in range(4):
            S.dma('pool', out=tb[:, 2 * hh:2 * hh + 2, :, :], in_=self.tbig_d[:, 2 * hh:2 * hh + 2, :, :], w=["tbig"])
        S.op('dve', 'tensor_scalar', r=["tbig"], w=["tbig"], out=tb[:, :, :, :], in0=tb[:, :, :, :], scalar1=8.0, scalar2=None, op0=ALU.mult)
        S.dma('pool', out=wnv[:, :, :], in_=od[:, :, 2592:3104], w=["wnv"])
        S.op('dve', 'memset', w=["ones2"], ap=ones2[:, :], constant=0.0)
        S.op('dve', 'memset', w=["ones2"], ap=ones2[0:64, 0:64], constant=1.0)
        S.op('dve', 'memset', w=["ones2"], ap=ones2[64:128, 64:128], constant=1.0)
        S.op('pool', 'memset', w=["Vn"], ap=Vn[:, :, :, :], constant=1.0)
        for st in range(18):
            ps, pk = self.psum()
            for k in range(8):
                S.op('pe', 'matmul', r=["wnv", "hT:%d" % (st // 4 * 512)], w=[pk], out=ps[:, :], lhsT=hT[:, k, st * 128:(st + 1) * 128],
                     rhs=wnv[:, k, :], start=(k == 0), stop=(k == 7))
            S.op('act', 'activation', r=[pk], w=["Vn"], out=Vn[:, st, :, 0:64], in_=ps[:, :].rearrange("p (h e) -> p h e", e=64), func=AF.Copy)
        groups = []
        for pr in range(4):
            for (t0, tw, cv) in tl:
                for which in range(2):
                    if which == 0 and t0 >= SEQ:
                        continue
                    groups.append((pr, t0, tw, which))
        loaded = set()

        def nproj_a(g):
            pr, t0, tw, which = g
            wnq, wnk = wnq_[pr % 2], wnk_[pr % 2]
            if pr not in loaded:
                loaded.add(pr)
                S.dma('pool', out=wnq[:, :, :], in_=od[:, :, 1568 + pr * 128:1568 + (pr + 1) * 128], w=["wnq%d" % (pr % 2)])
                S.dma('pool', out=wnk[:, :, :], in_=od[:, :, 2080 + pr * 128:2080 + (pr + 1) * 128], w=["wnk%d" % (pr % 2)])
            wt, wkey = (wnq, "wnq%d" % (pr % 2)) if which == 0 else (wnk, "wnk%d" % (pr % 2))
            ps, pk = self.psum()
            for k in range(8):
                S.op('pe', 'matmul', r=[wkey, "hT:%d" % t0], w=[pk], out=ps[:, :tw], lhsT=wt[:, k, :], rhs=hT[:, k, t0:t0 + tw],
                     start=(k == 0), stop=(k == 7))
            sq, sk = nx_sq()
            S.op('act', 'activation', r=[pk], w=[sk], out=sq[:, :tw], in_=ps[:, :tw], func=AF.Square)
            return ps, pk, sq, sk

        def nproj_b(g, st_):
            pr, t0, tw, which = g
            ps, pk, sq, sk = st_
            pss, pks = self.psum()
            S.op('pe', 'matmul', r=[sk, "ones2"], w=[pks], out=pss[:, :tw], lhsT=ones2[:, :], rhs=sq[:, :tw], start=True, stop=True)
            rs, rk = self.rstd_of(pss[:, :tw], pks, 128, tw, 1.0 / 64, nx_ln, nx_rs)
            if which == 0:
                S.op('dve', 'scalar_tensor_tensor', r=[pk, rk, "odvec"], w=["QnT:%d" % t0], out=QnT[:, pr, t0:t0 + tw], in0=ps[:, :tw],
                     scalar=ov[:, 1:2], in1=rs[:, :tw], op0=ALU.mult, op1=ALU.mult)
            else:
                S.op('dve', 'scalar_tensor_tensor', r=[pk, rk, "odvec"], w=["KnT:%d" % t0], out=KnT[:, pr, t0:t0 + tw], in0=ps[:, :tw],
                     scalar=ov[:, 2:3], in1=rs[:, :tw], op0=ALU.mult, op1=ALU.mult)

        pend = nproj_a(groups[0])
        for gi_, g in enumerate(groups):
            nxt_ = nproj_a(groups[gi_ + 1]) if gi_ + 1 < len(groups) else None
            nproj_b(g, pend)
            pend = nxt_
        self.cut(21)

        def r0f(r):
            return min(max(r - 4, 0), 24)
        iters = [(qt, hg) for qt in range(cfg.get('nat_tiles', 16)) for hg in range(2)]

        def kcs_of(qt):
            rlo = r0f(2 * qt)
            rhi = r0f(2 * qt + 1) + 8
            return list(range(rlo // 2, (rhi - 1) // 2 + 1)) + [16, 17]

        def nat_S(it):
            qt, hg = iters[it]
            kcs = kcs_of(qt)
            pt, ptk = PTn[it % 2], "PTn%d" % (it % 2)
            for j, kc in enumerate(kcs):
                ps, pk = self.psum(0, 3)
                for hh in range(4):
                    h = hh * 2 + hg
                    pr, rr = h // 2, slice((h % 2) * 64, (h % 2) * 64 + 64)
                    S.op('pe', 'matmul', r=["KnT:%d" % (kc // 4 * 512), "QnT:%d" % (qt // 4 * 512)], w=[pk], out=ps[:, hh * 128:(hh + 1) * 128],
                         lhsT=KnT[rr, pr, kc * 128:(kc + 1) * 128], rhs=QnT[rr, pr, qt * 128:(qt + 1) * 128], start=True, stop=True)
                if kc < 16:
                    Dd = 2 * kc - 2 * qt + 7
                    sb, sbk = nx_sb()
                    S.op('dve', 'tensor_tensor', r=[pk, "tbig"], w=[sbk], out=sb[:, :].rearrange("p (h a c) -> p h a c", a=2, c=64),
                         in0=ps[:, :].rearrange("p (h a c) -> p h a c", a=2, c=64),
                         in1=tb[:, hg:8:2, Dd - 1:Dd + 1, :][:, :, ::-1, :], op=ALU.add)
                    S.op('act', 'activation', r=[sbk], w=[ptk + ":%d" % j], out=pt[:, j, :], in_=sb[:, :], func=AF.Exp, scale=0.125)
                    for i in range(2):
                        for a in range(2):
                            krow, qrow = 2 * kc + i, 2 * qt + a
                            if not (r0f(qrow) <= krow < r0f(qrow) + 8) and not cfg.get('nat_nomemset'):
                                S.op('pool', 'memset', r=[], w=[ptk + ":%d" % j],
                                     ap=pt[i * 64:(i + 1) * 64, j, :].rearrange("p (h q) -> p h q", q=128)[:, :, a * 64:(a + 1) * 64], constant=0.0)
                else:
                    S.op('act', 'activation', r=[pk], w=[ptk + ":%d" % j], out=pt[:, j, :], in_=ps[:, :], func=AF.Exp, scale=0.125)

        def nat_PV(it):
            qt, hg = iters[it]
            if cfg.get('nat_nopv'):
                return
            kcs = kcs_of(qt)
            pt, ptk = PTn[it % 2], "PTn%d" % (it % 2)
            pso, pko = self.psum(3, 6)
            for hh in range(4):
                h = hh * 2 + hg
                for j, kc in enumerate(kcs):
                    S.op('pe', 'matmul', r=[ptk + ":%d" % j, "Vn"], w=[pko], out=pso[:, hh * 65:(hh + 1) * 65], lhsT=pt[:, j, hh * 128:(hh + 1) * 128],
                         rhs=Vn[:, kc, h, :], start=(j == 0), stop=(j == len(kcs) - 1))
            rec, rck = nx_rec()
            S.op('dve', 'reciprocal', r=[pko], w=[rck], out=rec[:, 0:4], in_=pso[:, 0:260].rearrange("p (h e) -> p h e", e=65)[:, :, 64])
            for hh in range(4):
                h = hh * 2 + hg
                S.op('dve', 'tensor_scalar', r=[pko, rck], w=["nytok:%d" % qt], out=ytok[:, qt, h * 64:(h + 1) * 64], in0=pso[:, hh * 65:hh * 65 + 64],
                     scalar1=rec[:, hh:hh + 1], scalar2=None, op0=ALU.mult)
            if hg == 1:
                pb = self.psb[qt % 2]
                pbk = "ps7"
                for c in range(4):
                    S.op('pe', 'transpose', r=["nytok:%d" % qt, "consts"], w=[pbk], out=pb[:, c * 128:(c + 1) * 128],
                         in_=ytok[:, qt, c * 128:(c + 1) * 128], identity=self.ident_bf[:, :])
                S.op('act', 'activation', r=[pbk], w=["yT:%d" % (qt // 4 * 512)], out=yT[:, 4:8, qt * 128:(qt + 1) * 128],
                     in_=pb[:, 0:512].rearrange("p (c t) -> p c t", t=128), func=AF.Copy)

        if iters:
            nat_S(0)
        for it in range(len(iters)):
            if it + 1 < len(iters):
                nat_S(it + 1)
            nat_PV(it)


def fm(w):
    K = w.shape[0] // 128
    return np.ascontiguousarray(w.reshape(K, 128, -1).transpose(1, 0, 2))


def vec_fm(v):
    return np.ascontiguousarray(v.reshape(-1, 128).T)


def host_inputs(inp, core, extra=None):
    b0 = core * NB
    f32 = np.float32
    m = {}
    x = inp['x'][b0:b0 + NB]
    m['xT'] = np.ascontiguousarray(x.transpose(0, 2, 1).reshape(NB, 8, 128, SEQ).transpose(0, 2, 1, 3)).astype(f32)
    cx = inp['ctx'][b0:b0 + NB]
    m['ctxT'] = np.ascontiguousarray(cx.transpose(0, 2, 1).reshape(NB, 8, 128, CTX).transpose(0, 2, 1, 3)).astype(f32)
    cv = np.stack([inp['c'][b0], inp['c'][b0 + 1], inp['c_ctx'], inp['c_ctx']], axis=-1)
    m['cvT'] = fm(cv).astype(f32)
    for l in range(2):
        m['adaw%d' % l] = fm(inp['ada_w'][l]).astype(f32)
        m['wout%d' % l] = fm(inp['w_out'][l]).astype(f32)
        m['w1_%d' % l] = fm(inp['mlp_w1'][l]).astype(f32)
        m['w2_%d' % l] = fm(inp['mlp_w2'][l]).astype(f32)
    m['adab'] = np.ascontiguousarray(np.stack([vec_fm(inp['ada_b'][l]) for l in range(2)], axis=1)).astype(f32)
    m['nmix'] = np.ascontiguousarray(np.stack([vec_fm(inp['norm_mix'][l]) for l in range(2)], axis=1)).astype(f32)
    m['nmlp'] = np.ascontiguousarray(np.stack([vec_fm(inp['norm_mlp'][l]) for l in range(2)], axis=1)).astype(f32)
    m['ident'] = np.eye(128, dtype=f32)
    m['evwin'] = fm(inp['ev_w_in'][0]).astype(f32)
    m['wuq'] = fm(inp['mla_w_uq'][0]).astype(f32)
    wukv = inp['mla_w_ukv'][0].reshape(128, 8, 128)
    m['wukvn'] = np.ascontiguousarray(wukv[:, :, 0:64]).astype(f32)
    m['wukvv'] = np.ascontiguousarray(wukv[:, :, 64:128].reshape(128, 512)).astype(f32)
    qg = inp['mla_q_gain'][0]
    kg = inp['mla_k_gain'][0]
    perm = np.array([(i + 8) if (i % 16) < 8 else (i - 8) for i in range(32)])
    ev = np.ones((128, 16), f32)
    ev[:, 0:2] = vec_fm(inp['mla_q_norm'][0])
    ev[:, 2] = inp['mla_kv_norm'][0]
    ev[0:96, 3] = qg
    ev[0:64, 4] = kg[0:64]
    ev[0:32, 5] = kg[64:96]
    ev[64:96, 5] = qg[64:96]
    ev[0:32, 6] = kg[64:96][perm]
    ev[64:96, 6] = qg[64:96][perm]
    m['evvec'] = ev
    lv = np.zeros((128, 4, 12), f32)
    for j in range(4):
        lv[:, :, j] = vec_fm(inp['lru_conv_w'][0][j])
    lv[:, :, 4] = vec_fm(inp['lru_conv_b'][0])
    for d in range(2):
        lv[:, :, 5 + d] = vec_fm(inp['lru_b_a'][0][d])
        lv[:, :, 7 + d] = vec_fm(inp['lru_b_x'][0][d])
        lv[:, :, 9 + d] = vec_fm(inp['lru_lam'][0][d])
    m['lruvec'] = lv
    m['convwb'] = np.ascontiguousarray(np.broadcast_to(inp['lru_conv_w'][0][None, :, :], (128, 4, 512))).astype(f32)
    lw = np.zeros((128, 4, 2, 2, 128), f32)
    for c in range(4):
        for d in range(2):
            for g, nm in enumerate(('lru_w_a', 'lru_w_x')):
                for hb in range(2):
                    lw[hb * 64:(hb + 1) * 64, c, d, g, hb * 64:(hb + 1) * 64] = inp[nm][0][d][2 * c + hb]
    m['lruw'] = lw
    m['ropecs'] = rope_tables()
    shk = np.zeros((128, 96), f32)
    for k in range(32):
        shk[k, 64 + k] = 1.0
    m['shk'] = shk
    m['odwin'] = fm(inp['od_w_in'][0]).astype(f32)
    gw = np.zeros((64, 2, 256), f32)
    for d in range(2):
        gw[d * 16:(d + 1) * 16, d, :] = inp['gla_w_a'][0][d]
        gw[32, d, :] = inp['gla_b_a'][0][d]
    m['glaw'] = gw
    m['glau'] = gla_consts()
    ovv = np.ones((128, 4), f32)
    ovv[:, 0] = inp['gla_o_gain'][0]
    ovv[:, 1] = np.tile(inp['na_q_gain'][0], 2)
    ovv[:, 2] = np.tile(inp['na_k_gain'][0], 2)
    m['odvec'] = ovv
    m['tbig'] = natten_table(inp['na_rpb'][0])
    return m


def gla_consts():
    f32 = np.float32
    u = np.zeros((128, 6, 128), f32)
    sp = np.arange(128)[:, None]
    t = np.arange(128)[None, :]
    same = (sp // 64) == (t // 64)
    u[:, 0, :] = np.where(same & (sp <= t), -1.0 / 16, 0.0)
    u[:, 1, :] = np.where(same & (sp >= t), -1.0 / 16, 0.0)
    u[:, 2, :] = np.where(same & (sp > t), -1.0 / 16, 0.0)
    u[:, 3, :] = np.where(same & (sp < t), -1.0 / 16, 0.0)
    u[:, 4, :] = np.where(same & (sp <= t), 1.0, 0.0)
    u[:, 5, :] = np.where(same & (sp >= t), 1.0, 0.0)
    return u


def natten_table(rpb):
    f32 = np.float32
    tbl = np.full((128, 8, 16, 64), -30000.0, f32)
    p = np.arange(128)
    i = p // 64
    kcol = p % 64
    c = np.arange(64)
    c0 = np.clip(c - 8, 0, 48)
    valid = (kcol[:, None] >= c0[None, :]) & (kcol[:, None] < c0[None, :] + 16)
    dj = np.clip(kcol[:, None] - c[None, :] + 15, 0, 30)
    for e in range(16):
        di = e + i
        ok = valid & (di[:, None] <= 14)
        dic = np.clip(di, 0, 14)
        for h in range(8):
            g = rpb[h][dic[:, None], dj]
            tbl[:, h, e, :] = np.where(ok, g, tbl[:, h, e, :])
    return tbl


def rope_tables():
    f32 = np.float32
    half = 16
    inv = 10000.0 ** (-np.arange(0, half, 2, dtype=np.float64) / half)
    pos = np.arange(SEQ)
    cs = np.zeros((128, 2, TOK), f32)
    tab = np.zeros((32, 2, TOK), np.float64)
    tab[:, 0, :] = 1.0
    for mm in range(32):
        p = (pos // 64) if mm < 16 else (pos % 64)
        ang = p.astype(np.float64) * inv[(mm % 16) % 8]
        tab[mm, 0, :SEQ] = np.cos(ang)
        tab[mm, 1, :SEQ] = np.sin(ang)
    cs[0:32] = tab.astype(f32)
    cs[64:96] = tab.astype(f32)
    return cs


def unpack_out(outT):
    return np.ascontiguousarray(outT.transpose(0, 3, 2, 1).reshape(NB, SEQ, D))


def kernel(**inputs):
    inp = {k: np.asarray(v) for k, v in inputs.items()}
    bld = Builder(dict())
    nc = bld.build()
    in_maps = [host_inputs(inp, c) for c in range(8)]
    res = run_bass_kernel_spmd(nc, in_maps, core_ids=list(range(8)))
    outs = [unpack_out(r["outT"]) for r in res.results]
    return np.concatenate(outs, axis=0).astype(np.float32)
```
